# Optimizing a Trainium2 kernel written in Bass

```python
import math
import jax, jax.numpy as jnp
from jax import lax
import numpy as np

D_MODEL = 1024
BATCH = 16
SEQ = 256
DEPTH = 2
DEC_BATCH = 2
DEC_SEQ = 2048
PAST_LEN = 256

GRID_W = 64
HEAD_DIM = 64
N_MIXERS = 4
GROUP_W = D_MODEL // N_MIXERS
N_HG = GROUP_W // HEAD_DIM
D_FF = 4 * D_MODEL
CHUNK = 64
CONV_K = 5
RWKV_W_RANK = 64
RWKV_A_RANK = 64
RWKV_G_RANK = 128
ROPE_BASE = 10000.0
LN_EPS = 1e-5
DEEPNORM_ALPHA = (2 * DEPTH) ** 0.25
DEEPNORM_BETA = (8 * DEPTH) ** -0.25

A_SIZES = (GROUP_W, GROUP_W, GROUP_W, GROUP_W, 2 * N_HG, 2 * N_HG)
B_SIZES = (3 * GROUP_W, GROUP_W, 2 * N_HG, 2 * N_HG)
C_SIZES = (GROUP_W, GROUP_W, GROUP_W, GROUP_W)
D_SIZES = (GROUP_W, GROUP_W, GROUP_W, 2 * RWKV_W_RANK, 2 * RWKV_A_RANK, RWKV_G_RANK)
GROUP_COLS = (sum(A_SIZES), sum(B_SIZES), sum(C_SIZES), sum(D_SIZES))
P_IN = sum(GROUP_COLS)

kernel_name = 'hybrid_bidir_recurrent_diffusion_step'


def split_cols(x, sizes):
    offsets = [sum(sizes[:i + 1]) for i in range(len(sizes) - 1)]
    return jnp.split(x, offsets, axis=-1)


def layer_norm(x, gain=None, bias=None):
    xf = x.astype(jnp.float32)
    mu = jnp.mean(xf, axis=-1, keepdims=True)
    var = jnp.mean(jnp.square(xf - mu), axis=-1, keepdims=True)
    y = (xf - mu) * lax.rsqrt(var + LN_EPS)
    if gain is not None:
        y = y * gain + bias
    return y.astype(x.dtype)


def head_norm(y, gain, center):
    if center:
        y = y - jnp.mean(y, axis=-1, keepdims=True)
    y = y * lax.rsqrt(jnp.mean(jnp.square(y), axis=-1, keepdims=True) + LN_EPS)
    return y * gain.reshape(N_HG, HEAD_DIM)


def l2norm(x):
    return x * lax.rsqrt(jnp.sum(jnp.square(x), axis=-1, keepdims=True) + 1e-6)


def heads(x):
    return x.reshape(x.shape[:2] + (N_HG, HEAD_DIM))


def bidir(x):
    x = jnp.swapaxes(x, 1, 2)
    return jnp.stack([x, jnp.flip(x, axis=2)], axis=0)


def merge_dirs(y):
    return jnp.swapaxes(y[0] + jnp.flip(y[1], axis=2), 1, 2)


def dir_gate(g):
    g = jnp.transpose(g.reshape(g.shape[:2] + (2, N_HG)), (2, 0, 3, 1))
    return jnp.stack([g[0], jnp.flip(g[1], axis=2)], axis=0)


def dir_vec(x):
    x = jnp.transpose(x, (2, 0, 3, 1, 4))
    return jnp.stack([x[0], jnp.flip(x[1], axis=2)], axis=0)


def to_chunks(x):
    s = x.shape
    x = x.reshape(s[:3] + (s[3] // CHUNK, CHUNK) + s[4:])
    return jnp.moveaxis(x, 3, 0)


def from_chunks(y):
    y = jnp.moveaxis(y, 0, 3)
    s = y.shape
    return y.reshape(s[:3] + (s[3] * s[4],) + s[5:])


def chunk_masks():
    idx = jnp.arange(CHUNK)
    return idx[:, None] >= idx[None, :], idx[:, None] > idx[None, :]


def short_conv(x, w):
    return lax.conv_general_dilated(x, w[:, None, :].astype(x.dtype), window_strides=(1,),
                                    padding=[(CONV_K // 2, CONV_K // 2)],
                                    dimension_numbers=('NWC', 'WIO', 'NWC'),
                                    feature_group_count=x.shape[-1])


def centred_shift(x):
    pad = jnp.pad(x, ((0, 0), (1, 1), (0, 0)))
    return 0.5 * (pad[:, :-2] + pad[:, 2:])


def grid_rope(x, pos):
    half = HEAD_DIM // 2
    nf = half // 2
    inv = ROPE_BASE ** (-jnp.arange(nf, dtype=jnp.float32) / nf)

    def rot(xp, p):
        ang = p.astype(jnp.float32)[:, None] * inv[None, :]
        cos, sin = jnp.cos(ang)[:, None, :], jnp.sin(ang)[:, None, :]
        x1, x2 = xp[..., :nf], xp[..., nf:]
        return jnp.concatenate([x1 * cos - x2 * sin, x1 * sin + x2 * cos], axis=-1)

    return jnp.concatenate([rot(x[..., :half], pos[0]), rot(x[..., half:], pos[1])], axis=-1)


def mlstm_scan(q, k, v, li, lf, c0, n0, m0):
    incl, _ = chunk_masks()

    def step(carry, xs):
        cm, nv, m = carry
        qc, kc, vc, ic, fc = xs
        b = jnp.cumsum(fc, axis=-1)
        bl = b[..., -1]
        logd = jnp.where(incl, b[..., :, None] - b[..., None, :] + ic[..., None, :], -jnp.inf)
        inter = b + m[..., None]
        mt = jnp.maximum(inter, jnp.max(logd, axis=-1))
        s = jnp.einsum('zbhld,zbhsd->zbhls', qc, kc) * jnp.exp(logd - mt[..., None])
        ew = jnp.exp(inter - mt)
        num = jnp.einsum('zbhls,zbhse->zbhle', s, vc) + jnp.einsum('zbhld,zbhde->zbhle', qc, cm) * ew[..., None]
        den = jnp.sum(s, axis=-1) + jnp.einsum('zbhld,zbhd->zbhl', qc, nv) * ew
        h = num / jnp.maximum(jnp.abs(den), jnp.exp(-mt))[..., None]
        wk = bl[..., None] - b + ic
        m_new = jnp.maximum(bl + m, jnp.max(wk, axis=-1))
        sk = jnp.exp(wk - m_new[..., None])
        sp = jnp.exp(bl + m - m_new)
        c_new = cm * sp[..., None, None] + jnp.einsum('zbhs,zbhsd,zbhse->zbhde', sk, kc, vc)
        n_new = nv * sp[..., None] + jnp.einsum('zbhs,zbhsd->zbhd', sk, kc)
        return (c_new, n_new, m_new), h

    xs = (to_chunks(q), to_chunks(k), to_chunks(v), to_chunks(li), to_chunks(lf))
    (c1, n1, m1), h = lax.scan(step, (c0, n0, m0), xs)
    return from_chunks(h), c1, n1, m1


def delta_scan(q, k, v, beta, g, s0):
    incl, strict = chunk_masks()
    eye = jnp.eye(CHUNK, dtype=jnp.float32)

    def step(s, xs):
        qc, kc, vc, bc, gc = xs
        gcum = jnp.cumsum(gc, axis=-1)
        decay = jnp.exp(jnp.where(incl, gcum[..., :, None] - gcum[..., None, :], -jnp.inf))
        a = jnp.where(strict, jnp.einsum('zbhld,zbhsd->zbhls', kc, kc) * decay, 0.0) * bc[..., :, None]
        rhs = jnp.concatenate([vc * bc[..., None], kc * (bc * jnp.exp(gcum))[..., None]], axis=-1)
        x = lax.linalg.triangular_solve(eye + a, rhs, left_side=True, lower=True, unit_diagonal=True)
        u, w = x[..., :HEAD_DIM], x[..., HEAD_DIM:]
        v_new = u - jnp.einsum('zbhld,zbhde->zbhle', w, s)
        qk = jnp.einsum('zbhld,zbhsd->zbhls', qc, kc) * decay
        o = jnp.einsum('zbhld,zbhde->zbhle', qc * jnp.exp(gcum)[..., None], s) + jnp.einsum('zbhls,zbhse->zbhle', qk, v_new)
        gl = gcum[..., -1]
        s_new = s * jnp.exp(gl)[..., None, None] + jnp.einsum('zbhs,zbhsd,zbhse->zbhde', jnp.exp(gl[..., None] - gcum), kc, v_new)
        return s_new, o

    xs = (to_chunks(q), to_chunks(k), to_chunks(v), to_chunks(beta), to_chunks(g))
    s1, o = lax.scan(step, s0, xs)
    return from_chunks(o), s1


def retention_scan(q, k, v, log_gamma, s0):
    incl, _ = chunk_masks()
    idx = jnp.arange(CHUNK, dtype=jnp.float32)
    lg = log_gamma.astype(jnp.float32)[:, None, :, None]
    dmat = jnp.exp(jnp.where(incl, (idx[:, None] - idx[None, :]) * lg[..., None], -jnp.inf))
    xi = jnp.exp((idx + 1.0) * lg)
    zeta = jnp.exp((CHUNK - 1.0 - idx) * lg)
    gl = jnp.exp(CHUNK * lg[..., 0])

    def step(s, xs):
        qc, kc, vc = xs
        sc = jnp.einsum('zbhld,zbhsd->zbhls', qc, kc) * dmat
        o = jnp.einsum('zbhls,zbhse->zbhle', sc, vc) + jnp.einsum('zbhld,zbhde->zbhle', qc, s) * xi[..., None]
        s_new = s * gl[..., None, None] + jnp.einsum('zbhsd,zbhse->zbhde', kc * zeta[..., None], vc)
        return s_new, o

    s1, o = lax.scan(step, s0, (to_chunks(q), to_chunks(k), to_chunks(v)))
    return from_chunks(o), s1


def rwkv_scan(r, kh, kt, v, w, a, s0):
    def step(s, xs):
        rt, kht, ktt, vt, wt, at = xs
        sk = jnp.einsum('zbhij,zbhj->zbhi', s, kht)
        s = s * wt[..., None, :] - sk[..., :, None] * (at * kht)[..., None, :] + vt[..., :, None] * ktt[..., None, :]
        return s, jnp.einsum('zbhij,zbhj->zbhi', s, rt)

    xs = (jnp.moveaxis(r, 3, 0), jnp.moveaxis(kh, 3, 0), jnp.moveaxis(kt, 3, 0),
          jnp.moveaxis(v, 3, 0), jnp.moveaxis(w, 3, 0), jnp.moveaxis(a, 3, 0))
    s1, y = lax.scan(step, s0, xs)
    return jnp.moveaxis(y, 0, 3), s1


def zero_states(bsz):
    f = jnp.float32
    mat = (2, bsz, N_HG, HEAD_DIM, HEAD_DIM)
    return (jnp.zeros(mat, f), jnp.zeros((2, bsz, N_HG, HEAD_DIM), f), jnp.zeros((2, bsz, N_HG), f),
            jnp.zeros(mat, f), jnp.zeros(mat, f), jnp.zeros(mat, f))


def mixing_block(u, lp, init, pos):
    bsz, t = u.shape[0], u.shape[1]
    c0, n0, m0, sb0, sc0, sd0 = init
    pa, pb, pc, pd = split_cols((u @ lp['w_in']).astype(jnp.float32), GROUP_COLS)

    aq, ak, av, ao, ai, af = split_cols(pa, A_SIZES)
    li = dir_gate(ai + lp['mlstm_i_bias'].reshape(-1))
    lf = jax.nn.log_sigmoid(dir_gate(af + lp['mlstm_f_bias'].reshape(-1)))
    ha, c1, n1, m1 = mlstm_scan(bidir(heads(aq)), bidir(heads(ak) * HEAD_DIM ** -0.5), bidir(heads(av)),
                                li, lf, c0, n0, m0)
    ya = jax.nn.sigmoid(heads(ao)) * head_norm(merge_dirs(ha), lp['mlstm_norm'], True)

    bqkv, bz, bbeta, balpha = split_cols(pb, B_SIZES)
    bq, bk, bv = jnp.split(jax.nn.silu(short_conv(bqkv, lp['delta_conv'])), 3, axis=-1)
    beta = jax.nn.sigmoid(dir_gate(bbeta))
    g = -jnp.exp(lp['delta_a_log'])[:, None, :, None] * jax.nn.softplus(dir_gate(balpha) + lp['delta_dt_bias'][:, None, :, None])
    hb, sb1 = delta_scan(bidir(l2norm(heads(bq)) * HEAD_DIM ** -0.5), bidir(l2norm(heads(bk))),
                         bidir(heads(bv)), beta, g, sb0)
    yb = head_norm(merge_dirs(hb), lp['delta_norm'], False) * jax.nn.silu(heads(bz))

    cq, ck, cv, cg = split_cols(pc, C_SIZES)
    q, k = heads(cq) * HEAD_DIM ** -0.5, heads(ck)
    if pos is not None:
        q, k = grid_rope(q, pos), grid_rope(k, pos)
    hc, sc1 = retention_scan(bidir(q), bidir(k), bidir(heads(cv)), -jnp.exp(lp['ret_decay']), sc0)
    yc = jax.nn.silu(heads(cg)) * head_norm(merge_dirs(hc), lp['ret_norm'], True)

    pd = pd + lp['rwkv_mu'] * (centred_shift(pd) - pd)
    dr, dk, dv, dw, da, dg = split_cols(pd, D_SIZES)
    r, k, v = heads(dr), heads(dk), heads(dv)
    w_pre = lp['rwkv_w0'] + jnp.einsum('btzr,zrc->btzc', jnp.tanh(dw.reshape(bsz, t, 2, RWKV_W_RANK)), lp['rwkv_w2'])
    decay = jnp.exp(-jnp.exp(-jax.nn.softplus(-w_pre) - 0.5))
    a = jax.nn.sigmoid(lp['rwkv_a0'] + jnp.einsum('btzr,zrc->btzc', da.reshape(bsz, t, 2, RWKV_A_RANK), lp['rwkv_a2']))
    a = a.reshape(bsz, t, 2, N_HG, HEAD_DIM)
    kh = l2norm(k * lp['rwkv_kk'].reshape(N_HG, HEAD_DIM))
    kt = k[:, :, None] * (1.0 + (a - 1.0) * lp['rwkv_ka'].reshape(N_HG, HEAD_DIM))
    hd, sd1 = rwkv_scan(bidir(r), bidir(kh), dir_vec(kt), bidir(v),
                        dir_vec(decay.reshape(bsz, t, 2, N_HG, HEAD_DIM)), dir_vec(a), sd0)
    bonus = jnp.sum(r[:, :, None] * kt * lp['rwkv_rk'], axis=-1, keepdims=True) * v[:, :, None]
    yd = heads(jax.nn.sigmoid(dg) @ lp['rwkv_g2']) * (head_norm(merge_dirs(hd), lp['rwkv_norm'], True) + jnp.sum(bonus, axis=2))

    y = jnp.concatenate([ya, yb, yc, yd], axis=2).reshape(bsz, t, D_MODEL)
    return y.astype(u.dtype) @ lp['w_out'], (c1, n1, m1, sb1, sc1, sd1)


def trunk_layer(h, mod, lp, init, pos):
    sh1, sc1, g1, sh2, sc2, g2 = jnp.split(mod, 6, axis=-1)
    y, states = mixing_block(layer_norm(h) * (1.0 + sc1) + sh1, lp, init, pos)
    h = layer_norm(DEEPNORM_ALPHA * h + g1 * y, lp['ln1_g'], lp['ln1_b'])
    u = layer_norm(h) * (1.0 + sc2) + sh2
    f = jnp.square(jax.nn.relu(u @ lp['w_ff1'])) @ lp['w_ff2']
    h = layer_norm(DEEPNORM_ALPHA * h + g2 * f, lp['ln2_g'], lp['ln2_b'])
    return h, states


def setup_inputs(seed: int = 0) -> dict:
    key = jax.random.key(seed)
    keys = iter(jax.random.split(key, 64))
    f32 = jnp.float32
    H, d, G = N_HG, HEAD_DIM, GROUP_W

    def nrm(shape, scale):
        return scale * jax.random.normal(next(keys), shape, f32)

    def gain(shape):
        return 1.0 + nrm(shape, 0.02)

    ret_base = jnp.log(-jnp.log(1.0 - 2.0 ** (-5.0 - jnp.arange(H, dtype=f32))))
    dt = jnp.exp(jax.random.uniform(next(keys), (DEPTH, 2, H), f32, math.log(1e-3), math.log(1e-1)))
    mat = (DEC_BATCH, DEPTH, 2, H, d, d)
    return {
        'x_prompt': nrm((BATCH, SEQ, D_MODEL), 1.0),
        'x_sample': nrm((DEC_BATCH, DEC_SEQ, D_MODEL), 1.0),
        'c': nrm((DEC_BATCH, D_MODEL), 1.0),
        'state_mlstm_C': nrm(mat, 0.1),
        'state_mlstm_n': nrm((DEC_BATCH, DEPTH, 2, H, d), 0.1),
        'state_mlstm_m': nrm((DEC_BATCH, DEPTH, 2, H), 0.5),
        'state_delta': nrm(mat, 0.1),
        'state_ret': nrm(mat, 0.3),
        'state_rwkv': nrm(mat, 0.1),
        'c_ctx': nrm((D_MODEL,), 1.0),
        'w_mod': nrm((DEPTH, D_MODEL, 6 * D_MODEL), 0.5 * D_MODEL ** -0.5),
        'b_mod': nrm((DEPTH, 6 * D_MODEL), 0.02),
        'w_in': nrm((DEPTH, D_MODEL, P_IN), D_MODEL ** -0.5),
        'w_out': nrm((DEPTH, D_MODEL, D_MODEL), DEEPNORM_BETA * D_MODEL ** -0.5),
        'ln1_g': gain((DEPTH, D_MODEL)),
        'ln1_b': nrm((DEPTH, D_MODEL), 0.02),
        'ln2_g': gain((DEPTH, D_MODEL)),
        'ln2_b': nrm((DEPTH, D_MODEL), 0.02),
        'w_ff1': nrm((DEPTH, D_MODEL, D_FF), D_MODEL ** -0.5),
        'w_ff2': nrm((DEPTH, D_FF, D_MODEL), DEEPNORM_BETA * D_FF ** -0.5),
        'mlstm_i_bias': nrm((DEPTH, 2, H), 0.1),
        'mlstm_f_bias': jnp.linspace(3.0, 6.0, H, dtype=f32) + nrm((DEPTH, 2, H), 0.1),
        'mlstm_norm': gain((DEPTH, G)),
        'delta_conv': nrm((DEPTH, CONV_K, 3 * G), CONV_K ** -0.5),
        'delta_a_log': jnp.log(jax.random.uniform(next(keys), (DEPTH, 2, H), f32, 1.0, 16.0)),
        'delta_dt_bias': dt + jnp.log(-jnp.expm1(-dt)),
        'delta_norm': gain((DEPTH, G)),
        'ret_decay': ret_base + nrm((DEPTH, 2, H), 0.05),
        'ret_norm': gain((DEPTH, G)),
        'rwkv_mu': jax.random.uniform(next(keys), (DEPTH, GROUP_COLS[3]), f32, 0.2, 0.8),
        'rwkv_w0': jnp.linspace(-6.5, -1.5, G, dtype=f32) + nrm((DEPTH, 2, G), 0.1),
        'rwkv_w2': nrm((DEPTH, 2, RWKV_W_RANK, G), 0.5 * RWKV_W_RANK ** -0.5),
        'rwkv_a0': nrm((DEPTH, 2, G), 0.1),
        'rwkv_a2': nrm((DEPTH, 2, RWKV_A_RANK, G), 0.5 * RWKV_A_RANK ** -0.5),
        'rwkv_g2': nrm((DEPTH, RWKV_G_RANK, G), RWKV_G_RANK ** -0.5),
        'rwkv_kk': 0.85 + nrm((DEPTH, G), 0.02),
        'rwkv_ka': 1.0 + nrm((DEPTH, G), 0.02),
        'rwkv_rk': nrm((DEPTH, H, d), 0.1),
        'rwkv_norm': gain((DEPTH, G)),
    }


def reference(x_prompt, x_sample, c, state_mlstm_C, state_mlstm_n, state_mlstm_m, state_delta, state_ret,
              state_rwkv, c_ctx, w_mod, b_mod, w_in, w_out, ln1_g, ln1_b, ln2_g, ln2_b, w_ff1, w_ff2,
              mlstm_i_bias, mlstm_f_bias, mlstm_norm, delta_conv, delta_a_log, delta_dt_bias, delta_norm,
              ret_decay, ret_norm, rwkv_mu, rwkv_w0, rwkv_w2, rwkv_a0, rwkv_a2, rwkv_g2, rwkv_kk, rwkv_ka,
              rwkv_rk, rwkv_norm):
    n_rows = x_sample.shape[1] // GRID_W
    pos = (jnp.repeat(jnp.arange(n_rows), GRID_W), jnp.tile(jnp.arange(GRID_W), n_rows))
    caches = (state_mlstm_C, state_mlstm_n, state_mlstm_m, state_delta, state_ret, state_rwkv)
    hp, hs = x_prompt, x_sample
    ctx_states = []
    for l in range(DEPTH):
        lp = {'w_in': w_in[l], 'w_out': w_out[l], 'ln1_g': ln1_g[l], 'ln1_b': ln1_b[l],
              'ln2_g': ln2_g[l], 'ln2_b': ln2_b[l], 'w_ff1': w_ff1[l], 'w_ff2': w_ff2[l],
              'mlstm_i_bias': mlstm_i_bias[l], 'mlstm_f_bias': mlstm_f_bias[l], 'mlstm_norm': mlstm_norm[l],
              'delta_conv': delta_conv[l], 'delta_a_log': delta_a_log[l], 'delta_dt_bias': delta_dt_bias[l],
              'delta_norm': delta_norm[l], 'ret_decay': ret_decay[l], 'ret_norm': ret_norm[l],
              'rwkv_mu': rwkv_mu[l], 'rwkv_w0': rwkv_w0[l], 'rwkv_w2': rwkv_w2[l], 'rwkv_a0': rwkv_a0[l],
              'rwkv_a2': rwkv_a2[l], 'rwkv_g2': rwkv_g2[l], 'rwkv_kk': rwkv_kk[l], 'rwkv_ka': rwkv_ka[l],
              'rwkv_rk': rwkv_rk[l], 'rwkv_norm': rwkv_norm[l]}
        mod_ctx = jax.nn.silu(c_ctx) @ w_mod[l] + b_mod[l]
        hp, st = trunk_layer(hp, mod_ctx, lp, zero_states(hp.shape[0]), None)
        ctx_states.append(st)
        mod_lat = (jax.nn.silu(c) @ w_mod[l] + b_mod[l])[:, None, :]
        init = tuple(jnp.swapaxes(s[:, l], 0, 1).astype(jnp.float32) for s in caches)
        hs, _ = trunk_layer(hs, mod_lat, lp, init, pos)
    new_mlstm_C, new_mlstm_n, new_mlstm_m, new_delta, new_ret, new_rwkv = [
        jnp.stack([jnp.swapaxes(st[i], 0, 1) for st in ctx_states], axis=1) for i in range(6)]
    return (hp, hs, new_mlstm_C, new_mlstm_n, new_mlstm_m, new_delta, new_ret, new_rwkv)
```

```python
import contextlib
import numpy as np
import concourse.bass as bass
import concourse.mybir as mybir
from concourse.bass_utils import run_bass_kernel_spmd

F32 = mybir.dt.float32
BF16 = mybir.dt.bfloat16
AF = mybir.ActivationFunctionType
ALU = mybir.AluOpType
AX = mybir.AxisListType

D = 1024
NT = 16
TOK = 2048
DEPTH = 2
H = 4
HD = 64
ALPHA = (2 * DEPTH) ** 0.25
LN_EPS = 1e-5
N_CORES = 8

ENGS = ("pe", "act", "dve", "pool", "sp")
NSLOT = 6

CO = {}
_off = 0
for _n, _w in (("INCL", 64), ("STRICT", 64), ("ONES", 128), ("SEL", 512), ("PIDX", 1), ("NPIDX", 1), ("BLK", 128), ("PP1", 1)):
    CO[_n] = (_off, _w)
    _off += _w
NCONST = _off
ARENA_WORDS = 53200


def build_consts():
    c = np.zeros((128, NCONST), np.float32)
    s = np.arange(64)
    incl = (s[:, None] <= s[None, :]).astype(np.float32)
    strict = (s[:, None] < s[None, :]).astype(np.float32)
    for half in range(2):
        c[64 * half:64 * half + 64, CO["INCL"][0]:CO["INCL"][0] + 64] = incl
        c[64 * half:64 * half + 64, CO["STRICT"][0]:CO["STRICT"][0] + 64] = strict
    c[:, CO["ONES"][0]:CO["ONES"][0] + 128] = 1.0
    sel = np.zeros((64, 4, 128), np.float32)
    for z in range(2):
        for ch in range(2):
            for lp in range(64):
                n = 64 * ch + lp if z == 0 else 127 - 64 * ch - lp
                sel[lp, 2 * z + ch, n] = 1.0
    c[0:64, CO["SEL"][0]:CO["SEL"][0] + 512] = sel.reshape(64, 512)
    p = np.arange(128) % 64
    c[:, CO["PIDX"][0]] = p
    c[:, CO["NPIDX"][0]] = -p
    c[:, CO["PP1"][0]] = p + 1
    blk = np.zeros((128, 128), np.float32)
    blk[0:64, 0:64] = 1.0
    blk[64:128, 64:128] = 1.0
    c[:, CO["BLK"][0]:CO["BLK"][0] + 128] = blk
    return c


class Op:
    __slots__ = ("eng", "fn", "reads", "writes", "chan", "val", "is_dma", "idx", "signal")

    def __init__(self, eng, fn, reads, writes, is_dma):
        self.eng = eng
        self.fn = fn
        self.reads = reads
        self.writes = writes
        self.is_dma = is_dma
        self.chan = None
        self.val = 0
        self.signal = False


class _Rec:
    def __getattr__(self, name):
        def f(*a, **kw):
            self.call = (name, a, kw)
            return self
        return f


class Prog:
    def __init__(self):
        self.ops = []

    def op(self, eng, fn, reads=(), writes=()):
        r = _Rec()
        fn(r)
        o = Op(eng, r.call, tuple(reads), tuple(writes), False)
        self.ops.append(o)
        return o

    def dma(self, eng, fn, reads=(), writes=()):
        r = _Rec()
        fn(r)
        o = Op(eng, r.call, tuple(reads), tuple(writes), True)
        self.ops.append(o)
        return o

    def barrier(self):
        self.ops.append(None)

    def emit(self, nc, stack):
        raw = self.ops
        ops = []
        barrier_at = set()
        for o in raw:
            if o is None:
                barrier_at.add(len(ops))
            else:
                ops.append(o)
        dma_n = {e: 0 for e in ENGS}
        slot_prev = {}
        for i, o in enumerate(ops):
            o.idx = i
            if o.is_dma:
                j = dma_n[o.eng] % NSLOT
                dma_n[o.eng] += 1
                o.chan = ("dma", o.eng, j)
            else:
                o.chan = o.eng
        lastw = {}
        readers = {}
        deps = []
        last_chan = {}
        bar_deps = set()
        for o in ops:
            if o.idx in barrier_at:
                bar_deps = set(last_chan.values())
            last_chan[o.chan] = o.idx
            d = set(bar_deps)
            for k in o.reads:
                w = lastw.get(k)
                if w is not None:
                    d.add(w)
            for k in o.writes:
                w = lastw.get(k)
                if w is not None:
                    d.add(w)
                for r in readers.get(k, ()):
                    d.add(r)
            if o.is_dma:
                p = slot_prev.get(o.chan)
                if p is not None:
                    d.add(p)
                slot_prev[o.chan] = o.idx
            d.discard(o.idx)
            deps.append(d)
            for k in o.reads:
                readers.setdefault(k, []).append(o.idx)
            for k in o.writes:
                lastw[k] = o.idx
                readers[k] = []
        pos = {}
        cnt = {}
        for o in ops:
            c = o.chan
            cnt[c] = cnt.get(c, 0) + 1
            pos[o.idx] = cnt[c]
        know_stream = {e: {} for e in ENGS}
        know_op = [None] * len(ops)
        needed = [None] * len(ops)
        for o in ops:
            ks = know_stream[o.eng]
            need = []
            for p in sorted(deps[o.idx], reverse=True):
                po = ops[p]
                if po.eng == "pe" and o.eng == "pe" and not po.is_dma and not o.is_dma:
                    continue
                if ks.get(po.chan, 0) >= pos[p]:
                    continue
                need.append(p)
                for c, v in know_op[p].items():
                    if ks.get(c, 0) < v:
                        ks[c] = v
                if ks.get(po.chan, 0) < pos[p]:
                    ks[po.chan] = pos[p]
            needed[o.idx] = need
            know_op[o.idx] = dict(ks)
            for p in need:
                ops[p].signal = True
        for o in ops:
            if o.is_dma:
                o.signal = True
        cnt = {}
        for o in ops:
            if o.signal:
                cnt[o.chan] = cnt.get(o.chan, 0) + 1
                o.val = cnt[o.chan] * (16 if o.is_dma else 1)
        sems = {}
        for c in cnt:
            name = "s_" + ("_".join(str(x) for x in c) if isinstance(c, tuple) else c)
            sems[c] = stack.enter_context(nc.semaphore(name))
        self.maxval = dict(cnt)
        block = stack.enter_context(nc.Block())
        streams = {e: [] for e in ENGS}
        for o in ops:
            streams[o.eng].append(o)

        def run_stream(engname, engobj):
            for o in streams[engname]:
                for p in needed[o.idx]:
                    po = ops[p]
                    engobj.wait_ge(sems[po.chan], po.val)
                name, a, kw = o.fn
                inst = getattr(engobj, name)(*a, **kw)
                if o.signal:
                    inst.then_inc(sems[o.chan], 16 if o.is_dma else 1)

        @block.tensor
        def _(e):
            run_stream("pe", e)

        @block.scalar
        def _(e):
            run_stream("act", e)

        @block.vector
        def _(e):
            run_stream("dve", e)

        @block.gpsimd
        def _(e):
            run_stream("pool", e)

        @block.sync
        def _(e):
            run_stream("sp", e)
            for c, n in cnt.items():
                if isinstance(c, tuple):
                    e.wait_ge(sems[c], n * 16)
        return len(ops)


class K:
    def __init__(self, debug=None):
        self.debug = debug or {}
        self.nc = bass.Bass("TRN2", target_bir_lowering=False)
        self.P = Prog()
        self.ins = {}
        self.outs = {}
        self.uid = 0

    def din(self, name, shape):
        t = self.nc.dram_tensor(name, list(shape), F32, kind="ExternalInput").ap()
        self.ins[name] = t
        return t

    def dout(self, name, shape):
        t = self.nc.dram_tensor(name, list(shape), F32, kind="ExternalOutput").ap()
        self.outs[name] = t
        return t

    def sb(self, name, shape, dt=F32):
        shape = list(shape)
        nelem = 1
        for d in shape[1:]:
            nelem *= d
        words = (nelem * (2 if dt == BF16 else 4) + 3) // 4
        words = (words + 7) // 8 * 8
        off = self.arena_off
        assert off + words <= ARENA_WORDS, ("SBUF arena overflow", name, off, words)
        self.arena_off = off + words
        self.arena_peak = max(self.arena_peak, self.arena_off)
        ap = self.arena[0:shape[0], off:off + words]
        if dt == BF16:
            ap = ap.bitcast(BF16)
        ap = ap[:, 0:nelem]
        if len(shape) == 3:
            ap = ap.rearrange("p (a b) -> p a b", b=shape[2])
        elif len(shape) == 4:
            ap = ap.rearrange("p (a b c) -> p a b c", b=shape[2], c=shape[3])
        elif len(shape) == 5:
            ap = ap.rearrange("p (a b c d) -> p a b c d", b=shape[2], c=shape[3], d=shape[4])
        return ap

    def mark(self):
        return self.arena_off

    def release(self, m):
        self.arena_off = m
        self.P.barrier()

    def build(self):
        nc, P = self.nc, self.P
        with contextlib.ExitStack() as st:
            self.stack = st
            self.arena = st.enter_context(nc.sbuf_tensor("arena", [128, ARENA_WORDS], F32))
            self.arena_off = 0
            self.arena_peak = 0
            self.declare_io()
            self.alloc_global()
            self.load_consts()
            for l in range(self.debug.get("layers", DEPTH)):
                self.layer(l)
            self.finish()
            n = P.emit(nc, st)
            self.n_ops = n
        return nc

    def declare_io(self):
        self.x_in = self.din("x", [TOK, D])
        self.cond = self.din("cond", [8, 128])
        self.ident_in = self.din("ident", [128, 128])
        self.w_mod = self.din("w_mod", [DEPTH, D, 6 * D])
        self.b_mod = self.din("b_mod", [DEPTH, 48, 128])
        self.y_out = self.dout("y", [TOK, D])
        self.consts_in = self.din("consts", [128, NCONST])
        self.keep_in = self.din("keep", [1, 1])
        self.rope_in = self.din("rope", [NT, 128, 2, 128])
        self.w_out = self.din("w_out", [DEPTH, D, D])
        self.wC = self.din("wC", [DEPTH, D, 1536])
        self.wA = self.din("wA", [DEPTH, D, 1040])
        self.ml_ib = self.din("ml_ib", [DEPTH, 8])
        self.ml_fb = self.din("ml_fb", [DEPTH, 8])
        self.ml_norm = self.din("ml_norm", [DEPTH, 256])
        self.init_mC = self.din("init_mC", [DEPTH, 2, H, HD, HD])
        self.init_mn = self.din("init_mn", [DEPTH, 2, H, HD])
        self.init_mm = self.din("init_mm", [DEPTH, 2, H])
        self.out_mC = self.dout("out_mC", [8, DEPTH, 2, H, HD, HD])
        self.out_mn = self.dout("out_mn", [8, DEPTH, 2, H, HD])
        self.out_mm = self.dout("out_mm", [8, DEPTH, 2, H])
        self.wB = self.din("wB", [DEPTH, D, 1040])
        self.dl_conv = self.din("dl_conv", [DEPTH, 128, 6, 5])
        self.dl_alog = self.din("dl_alog", [DEPTH, 8])
        self.dl_dtb = self.din("dl_dtb", [DEPTH, 8])
        self.dl_norm = self.din("dl_norm", [DEPTH, 256])
        self.init_delta = self.din("init_delta", [DEPTH, 2, H, HD, HD])
        self.out_delta = self.dout("out_delta", [8, DEPTH, 2, H, HD, HD])
        self.wD = self.din("wD", [DEPTH, D, 1152])
        self.rw_mu = self.din("rw_mu", [DEPTH, 128, 9])
        self.rw_w2p = self.din("rw_w2p", [DEPTH, 128, 2, 256])
        self.rw_a2p = self.din("rw_a2p", [DEPTH, 128, 2, 256])
        self.rw_w0 = self.din("rw_w0", [DEPTH, 512])
        self.rw_a0 = self.din("rw_a0", [DEPTH, 512])
        self.rw_kk = self.din("rw_kk", [DEPTH, 256])
        self.rw_ka = self.din("rw_ka", [DEPTH, 256])
        self.rw_rk = self.din("rw_rk", [DEPTH, 256])
        self.rw_norm = self.din("rw_norm", [DEPTH, 256])
        self.rw_g2 = self.din("rw_g2", [DEPTH, 128, 256])
        self.init_rwkv = self.din("init_rwkv", [DEPTH, 2, H, HD, HD])
        self.out_rwkv = self.dout("out_rwkv", [8, DEPTH, 2, H, HD, HD])
        self.ln1_g = self.din("ln1_g", [DEPTH, D])
        self.ln1_b = self.din("ln1_b", [DEPTH, D])
        self.ln2_g = self.din("ln2_g", [DEPTH, D])
        self.ln2_b = self.din("ln2_b", [DEPTH, D])
        self.w_ff1 = self.din("w_ff1", [DEPTH, D, 4 * D])
        self.w_ff2 = self.din("w_ff2", [DEPTH, 4 * D, D])
        self.ret_decay = self.din("ret_decay", [DEPTH, 8])
        self.ret_norm = self.din("ret_norm", [DEPTH, 256])
        self.init_ret = self.din("init_ret", [DEPTH, 2, H, HD, HD])
        self.out_ret = self.dout("out_ret", [8, DEPTH, 2, H, HD, HD])
        if "yacc" in self.debug:
            self.dbg_yacc = self.dout("dbg_yacc", [128, NT, 256])
        if "uT" in self.debug:
            self.dbg_uT = self.dout("dbg_uT", [128, 8, TOK])

    def alloc_global(self):
        nc = self.nc
        self.xres = self.sb("xres", [128, NT, D], F32)
        self.ident = self.sb("ident", [128, 128], F32)
        self.psum = self.stack.enter_context(nc.psum_tensor("psum", [128, 8, 512], F32))
        self.modT = self.sb("modT", [128, 48], F32)
        self.sc1p = self.sb("sc1p", [128, 8], F32)
        self.sc2p = self.sb("sc2p", [128, 8], F32)
        self.scT = self.sb("scT", [128, 8], F32)
        self.bmodT = self.sb("bmodT", [128, 48], F32)
        self.consts = self.sb("consts", [128, NCONST], F32)
        self.ident_bf = self.sb("ident_bf", [128, 128], BF16)
        self.keep = self.sb("keep", [128, 1], F32)
        self.g_bc = {"g1": self.sb("g1_bc", [128, D], F32), "g2": self.sb("g2_bc", [128, D], F32)}

    def load_consts(self):
        P = self.P
        xv = self.x_in.rearrange("(i p) d -> p i d", p=128)
        for q in range(4):
            P.dma("sp", lambda e, q=q: e.dma_start(out=self.xres[:, 4 * q:4 * q + 4, :], in_=xv[:, 4 * q:4 * q + 4, :]),
                  writes=[("xres", i) for i in range(4 * q, 4 * q + 4)])
        P.dma("sp", lambda e: e.dma_start(out=self.ident[:], in_=self.ident_in), writes=["ident"])
        P.dma("sp", lambda e: e.dma_start(out=self.consts[:], in_=self.consts_in), writes=["consts"])
        P.dma("sp", lambda e: e.dma_start(out=self.keep[:], in_=self.keep_in.partition_broadcast(128)), writes=["keep"])
        P.op("dve", lambda e: e.tensor_copy(self.ident_bf[:], self.ident[:]), reads=["ident"], writes=["ident_bf"])
        c8 = self.sb("c8", [8, 128], F32)
        P.dma("sp", lambda e: e.dma_start(out=c8[:], in_=self.cond), writes=["c8"])
        ps = self.psum
        P.op("pe", lambda e: e.transpose(ps[:, 0, 0:8], c8[:], self.ident[0:8, 0:8]), reads=["c8", "ident"], writes=["ps0"])
        P.op("act", lambda e: e.activation(self.scT[:], ps[:, 0, 0:8], AF.Silu), writes=["ps0", "scT"])

    def compute_mod(self, l):
        P, ps = self.P, self.psum
        m = self.mark()
        self.scbc = self.sb("scbc", [128, 8, 128], F32)
        P.op("dve", lambda e: e.tensor_copy(self.scbc[:], self.scT[:].unsqueeze(2).to_broadcast([128, 8, 128])),
             reads=["scT"], writes=["scbc"])
        b48 = self.sb("b48", [48, 128], F32)
        P.dma("sp", lambda e: e.dma_start(out=b48[:], in_=self.b_mod[l]), writes=["b48"])
        P.op("pe", lambda e: e.transpose(ps[:, 1, 0:48], b48[:], self.ident[0:48, 0:48]), reads=["b48", "ident"], writes=["ps1"])
        P.op("dve", lambda e: e.tensor_copy(self.bmodT[:], ps[:, 1, 0:48]), writes=["ps1", "bmodT"])
        wv = self.w_mod[l].rearrange("(k p) n -> p k n", p=128)
        wblk = [self.sb("wmodblk%d" % i, [128, 8, 512], F32) for i in range(2)]
        bb = self.sb("g_bb", [128, D], F32)
        for b in range(12):
            wb = wblk[b % 2]
            key = "wmodblk%d" % (b % 2)
            P.dma("sp", lambda e, b=b, wb=wb: e.dma_start(out=wb[:], in_=wv[:, :, 512 * b:512 * b + 512]), writes=[key])
            for jj in range(4):
                j = 4 * b + jj
                for k in range(8):
                    P.op("pe", lambda e, wb=wb, jj=jj, j=j, k=k: e.matmul(
                        ps[:, 2, j:j + 1], wb[:, k, 128 * jj:128 * jj + 128], self.scT[:, k:k + 1],
                        start=(k == 0), stop=(k == 7)), reads=[key, "scT"], writes=["ps2"])
            if b in (4, 5, 10, 11):
                which = "g1" if b < 6 else "g2"
                half = b % 2
                if half == 0:
                    off = 2048 if which == "g1" else 5120
                    bm = self.b_mod[l].rearrange("a b -> (a b)")[off:off + 1024].unsqueeze(0)
                    P.dma("sp", lambda e, bm=bm: e.dma_start(out=bb[:], in_=bm.partition_broadcast(128)), writes=["g_bb"])
                gt = self.g_bc[which]
                for k in range(8):
                    P.op("pe", lambda e, wb=wb, k=k: e.matmul(ps[:, 3, :], self.scbc[:, k, :], wb[:, k, :], start=(k == 0), stop=(k == 7)),
                         reads=[key, "scbc"], writes=["ps3"])
                P.op("dve", lambda e, gt=gt, half=half: e.tensor_tensor(
                    gt[:, 512 * half:512 * half + 512], ps[:, 3, :], bb[:, 512 * half:512 * half + 512], ALU.add),
                    reads=["g_bb"], writes=["ps3", which + "_bc"])
        P.op("dve", lambda e: e.tensor_tensor(self.modT[:], ps[:, 2, 0:48], self.bmodT[:], ALU.add), reads=["bmodT"], writes=["ps2", "modT"])
        P.op("dve", lambda e: e.tensor_scalar(self.sc1p[:], self.modT[:, 8:16], 1.0, None, ALU.add), reads=["modT"], writes=["sc1p"])
        P.op("dve", lambda e: e.tensor_scalar(self.sc2p[:], self.modT[:, 32:40], 1.0, None, ALU.add), reads=["modT"], writes=["sc2p"])
        self.release(m)

    def ln_mod_T(self, tile, dst_fn, scp, sh_off, tmp, dst_keys):
        P, ps = self.P, self.psum
        stats, mv, rstd, xhat = tmp
        xk = ("xres", tile)
        src = self.xres[:, tile, :]
        for hh in range(2):
            P.op("dve", lambda e, hh=hh: e.bn_stats(stats[:, hh, :], src[:, 512 * hh:512 * hh + 512]), reads=[xk], writes=["lnstats"])
        P.op("dve", lambda e: e.bn_aggr(mv[:], stats[:]), reads=["lnstats"], writes=["lnmv"])
        P.op("act", lambda e: e.activation(rstd[:], mv[:, 1:2], AF.Ln, bias=LN_EPS, scale=1.0), reads=["lnmv"], writes=["lnrstd"])
        P.op("act", lambda e: e.activation(rstd[:], rstd[:], AF.Exp, scale=-0.5), reads=["lnrstd"], writes=["lnrstd"])
        P.op("dve", lambda e: e.tensor_scalar(xhat[:], src, mv[:, 0:1], rstd[:, 0:1], ALU.subtract, ALU.mult),
             reads=[xk, "lnmv", "lnrstd"], writes=["xhat"])
        for k in range(8):
            b = 4 + (k // 4)
            P.op("pe", lambda e, k=k, b=b: e.transpose(ps[:, b, 128 * (k % 4):128 * (k % 4) + 128], xhat[:, 128 * k:128 * k + 128], self.ident[:]),
                 reads=["xhat", "ident"], writes=["ps%d" % b])
        for k in range(8):
            b = 4 + (k // 4)
            P.op("act", lambda e, k=k, b=b: e.activation(dst_fn(k), ps[:, b, 128 * (k % 4):128 * (k % 4) + 128], AF.Identity,
                                                       bias=self.modT[:, sh_off + k:sh_off + k + 1], scale=scp[:, k:k + 1]),
                 reads=["modT", "sc1p", "sc2p"], writes=["ps%d" % b] + dst_keys)

    def dump(self, name, ap, keys, shape):
        want = self.debug.get("dump", ())
        if name not in want or ("dmp_" + name) in self.outs:
            return
        P = self.P
        out = self.dout("dmp_" + name, shape)
        m = self.mark()
        tmp = self.sb("dmp_" + name, shape, F32)
        P.op("dve", lambda e: e.tensor_copy(tmp[:], ap), reads=keys, writes=["dmp_" + name])
        P.dma("sp", lambda e: e.dma_start(out=out, in_=tmp[:]), reads=["dmp_" + name])
        self.release(m)

    def cst(self, name, rows=64):
        o, w = CO[name]
        return self.consts[0:rows, o:o + w]

    def load_w(self, name, src, ncols, kchunks=8):
        P = self.P
        wb = self.sb(name, [128, kchunks, ncols], BF16)
        wv = src.rearrange("(k p) n -> p k n", p=128)
        step = 2 if kchunks % 2 == 0 else 1
        for k0 in range(0, kchunks, step):
            P.dma("pool", lambda e, k0=k0: e.dma_start(out=wb[:, k0:k0 + step, :], in_=wv[:, k0:k0 + step, :]),
                  writes=[(name, k0 // step)])
        return wb, [(name, i) for i in range(kchunks // step)]

    def emit_out_tile(self, t, osb, osb_keys, bank):
        P, ps = self.P, self.psum
        sel = self.cst("SEL").rearrange("p (q n) -> p q n", n=128)
        pk = "ps%d" % bank
        for z in range(2):
            tile = t if z == 0 else NT - 1 - t
            for c in range(2):
                P.op("pe", lambda e, z=z, c=c: e.matmul(ps[:, bank, 256 * z:256 * z + 256], sel[:, 2 * z + c, :], osb[:, c, z, :],
                                                         start=(c == 0), stop=(c == 1)),
                     reads=["consts"] + osb_keys, writes=[pk])
            yk = ("yacc", tile)
            if t < NT // 2:
                P.op("act", lambda e, z=z, tile=tile: e.activation(self.yacc[:, tile, :], ps[:, bank, 256 * z:256 * z + 256], AF.Copy),
                     writes=[pk, yk])
            else:
                P.op("dve", lambda e, z=z, tile=tile: e.tensor_tensor(self.yacc[:, tile, :], ps[:, bank, 256 * z:256 * z + 256],
                                                                     self.yacc[:, tile, :], ALU.add),
                     writes=[pk, yk])

    def mixer_ret(self, l, first):
        P, ps = self.P, self.psum
        m0 = self.mark()
        wb, wkeys = self.load_w("wC", self.wC[l], 1536)
        lg = self.sb("ret_lg", [128, 8], F32)
        P.dma("sp", lambda e: e.dma_start(out=lg[:], in_=self.ret_decay[l:l + 1, :].partition_broadcast(128)), writes=["ret_lg"])
        P.op("act", lambda e: e.activation(lg[:], lg[:], AF.Exp), reads=["ret_lg"], writes=["ret_lg"])
        P.op("dve", lambda e: e.tensor_scalar(lg[:], lg[:], -1.0, None, ALU.mult), reads=["ret_lg"], writes=["ret_lg"])
        gs = self.sb("ret_gs", [128, 8], F32)
        ga = self.sb("ret_ga", [128, 8], F32)
        pidx = self.consts[:, CO["PIDX"][0]:CO["PIDX"][0] + 1]
        npidx = self.consts[:, CO["NPIDX"][0]:CO["NPIDX"][0] + 1]
        P.op("act", lambda e: e.activation(gs[:], lg[:], AF.Exp, scale=npidx), reads=["ret_lg", "consts"], writes=["ret_gs"])
        P.op("act", lambda e: e.activation(ga[:], lg[:], AF.Exp, scale=pidx), reads=["ret_lg", "consts"], writes=["ret_ga"])
        G64 = self.sb("ret_g64", [128, 4], F32)
        GAM = self.sb("ret_gam", [128, 4], F32)
        GAMI = self.sb("ret_gami", [128, 4], F32)
        for hq in range(2):
            rows = slice(64 * hq, 64 * hq + 64)
            P.op("act", lambda e, rows=rows, hq=hq: e.activation(G64[rows, :], lg[rows, hq::2], AF.Exp, scale=64.0), reads=["ret_lg"], writes=["ret_g64"])
            P.op("act", lambda e, rows=rows, hq=hq: e.activation(GAM[rows, :], lg[rows, hq::2], AF.Exp, scale=1.0), reads=["ret_lg"], writes=["ret_gam"])
            P.op("act", lambda e, rows=rows, hq=hq: e.activation(GAMI[rows, :], lg[rows, hq::2], AF.Exp, scale=-1.0), reads=["ret_lg"], writes=["ret_gami"])
        S = self.sb("ret_S", [128, 4, 64], F32)
        Sb = self.sb("ret_Sb", [128, 4, 64], BF16)
        Sout = self.sb("ret_Sout", [128, 4, 64], F32)
        P.dma("sp", lambda e: e.dma_start(out=S[:], in_=self.init_ret[l].rearrange("z (hp hq) d e -> (hq d) (z hp) e", hq=2)), writes=["ret_S"])
        P.op("dve", lambda e: e.tensor_tensor(S[:], S[:], GAM[:].unsqueeze(2).to_broadcast([128, 4, 64]), ALU.mult), reads=["ret_gam"], writes=["ret_S"])
        P.op("act", lambda e: e.activation(Sb[:], S[:], AF.Copy), reads=["ret_S"], writes=["ret_Sb"])
        rc = [self.sb("ret_rc%d" % i, [128, 2, 2, 128], F32) for i in range(2)]
        ta = self.sb("ret_ta", [128, 2, 128], F32)
        tb = self.sb("ret_tb", [128, 2, 128], F32)
        qT = self.sb("ret_qT", [128, 2, 2, 2, 128], BF16)
        P.op("dve", lambda e: e.memset(qT[:], 0.0), writes=["ret_qT"])
        kT = self.sb("ret_kT", [128, 2, 2, 128], BF16)
        vT = self.sb("ret_vT", [128, 2, 2, 128], BF16)
        ktm = self.sb("ret_ktm", [64, 8, 64], BF16)
        vtm = self.sb("ret_vtm", [64, 8, 64], BF16)
        pm = self.sb("ret_pm", [64, 8, 64], BF16)
        osb = self.sb("ret_osb", [64, 2, 2, 256], F32)
        tmpS = self.sb("ret_tmpS", [128, 4, 64], F32)
        incl = self.cst("INCL")
        psb = ps.bitcast(BF16) if False else None

        def bfview(bank):
            return ps[:, bank, :].bitcast(BF16)

        stop = self.debug.get("stop", 99)
        for t in range(NT if stop > 1 else 0):
            tiles = (t, NT - 1 - t)
            r = rc[t % 2]
            rk = "ret_rc%d" % (t % 2)
            for z in range(2):
                P.dma("sp", lambda e, z=z, r=r: e.dma_start(out=r[:, z, :, :], in_=self.rope_in[tiles[z]]), writes=[rk])
            def proj(j, bank, z):
                tok0 = 2 + 128 * tiles[z]
                for k in range(8):
                    P.op("pe", lambda e, j=j, k=k, z=z, tok0=tok0, bank=bank: e.matmul(
                        ps[:, bank, 128 * z:128 * z + 128], wb[:, k, 128 * j:128 * j + 128], self.uT[:, k, tok0:tok0 + 128],
                        start=(k == 0), stop=(k == 7)),
                        reads=wkeys + [("uT", tiles[z])], writes=["ps%d" % bank])
            for which, dst, dkey in ((0, qT, "ret_qT"), (1, kT, "ret_kT")):
                for hp in range(2):
                    j = 2 * which + hp
                    for z in range(2):
                        proj(j, 0, z)
                        proj(6 + j, 1, z)
                    scale = 0.125 if which == 0 else 1.0
                    P.op("dve", lambda e, scale=scale, r=r: e.scalar_tensor_tensor(
                        ta[:], ps[:, 0, 0:256].rearrange("p (z n) -> p z n", z=2), scale, r[:, :, 0, :], ALU.mult, ALU.mult),
                        reads=[rk], writes=["ps0", "ret_ta"])
                    P.op("dve", lambda e, scale=scale, r=r: e.scalar_tensor_tensor(
                        tb[:], ps[:, 1, 0:256].rearrange("p (z n) -> p z n", z=2), scale, r[:, :, 1, :], ALU.mult, ALU.mult),
                        reads=[rk], writes=["ps1", "ret_tb"])
                    if which == 1:
                        P.op("dve", lambda e, dst=dst, hp=hp: e.tensor_tensor(dst[:, hp, 0, :], ta[:, 0, :], tb[:, 0, :], ALU.add),
                             reads=["ret_ta", "ret_tb"], writes=[dkey])
                        P.op("dve", lambda e, dst=dst, hp=hp: e.tensor_tensor(dst[:, hp, 1, ::-1], ta[:, 1, :], tb[:, 1, :], ALU.add),
                             reads=["ret_ta", "ret_tb"], writes=[dkey])
                    else:
                        for hq in range(2):
                            rws = slice(64 * hq, 64 * hq + 64)
                            P.op("dve", lambda e, dst=dst, hp=hp, hq=hq, rws=rws: e.tensor_tensor(
                                dst[rws, hp, hq, 0, :], ta[rws, 0, :], tb[rws, 0, :], ALU.add),
                                reads=["ret_ta", "ret_tb"], writes=[dkey])
                            P.op("dve", lambda e, dst=dst, hp=hp, hq=hq, rws=rws: e.tensor_tensor(
                                dst[rws, hp, hq, 1, ::-1], ta[rws, 1, :], tb[rws, 1, :], ALU.add),
                                reads=["ret_ta", "ret_tb"], writes=[dkey])
            for hp in range(2):
                bank = hp
                for z in range(2):
                    proj(4 + hp, bank, z)
                P.op("act", lambda e, hp=hp, bank=bank: e.activation(vT[:, hp, 0, :], ps[:, bank, 0:128], AF.Copy), writes=["ps%d" % bank, "ret_vT"])
                P.op("act", lambda e, hp=hp, bank=bank: e.activation(vT[:, hp, 1, ::-1], ps[:, bank, 128:256], AF.Copy), writes=["ps%d" % bank, "ret_vT"])
            self.dump("ret_qT", qT[:], ["ret_qT"], [128, 2, 2, 2, 128])
            self.dump("ret_kT", kT[:], ["ret_kT"], [128, 2, 2, 128])
            self.dump("ret_vT", vT[:], ["ret_vT"], [128, 2, 2, 128])
            for c in range(2 if stop > 2 else 0):
                cs = slice(64 * c, 64 * c + 64)
                for (src, skey, bank) in ((kT, "ret_kT", 2), (vT, "ret_vT", 3)):
                    bv = bfview(bank)
                    for z in range(2):
                        for hp in range(2):
                            col = (4 * z + 2 * hp) * 64
                            P.op("pe", lambda e, src=src, z=z, hp=hp, col=col, bv=bv: e.transpose(
                                bv[0:64, col:col + 128], src[:, hp, z, cs], self.ident_bf[:]),
                                reads=[skey, "ident_bf"], writes=["ps%d" % bank])
                P.op("act", lambda e: e.activation(ktm[:], bfview(2)[0:64, 0:512].rearrange("p (u d) -> p u d", d=64), AF.Copy),
                     writes=["ps2", "ret_ktm"])
                P.op("dve", lambda e: e.tensor_tensor(vtm[:], bfview(3)[0:64, 0:512].rearrange("p (u d) -> p u d", d=64),
                                                      gs[0:64, :].unsqueeze(2).to_broadcast([64, 8, 64]), ALU.mult),
                     reads=["ret_gs"], writes=["ps3", "ret_vtm"])
                self.dump("ret_ktm", ktm[:], ["ret_ktm"], [64, 8, 64])
                self.dump("ret_vtm", vtm[:], ["ret_vtm"], [64, 8, 64])
                if stop <= 3:
                    continue
                for z in range(2):
                    for h in range(4):
                        hp, hq = h // 2, h % 2
                        rows = slice(64 * hq, 64 * hq + 64)
                        u = 4 * z + h
                        P.op("pe", lambda e, z=z, hp=hp, hq=hq, u=u: e.matmul(
                            ps[0:64, 4, 64 * u:64 * u + 64], kT[:, hp, z, cs], qT[:, hp, hq, z, cs], start=True, stop=True),
                            reads=["ret_kT", "ret_qT"], writes=["ps4"])
                if self.debug.get("sub") == "a":
                    continue
                P.op("dve", lambda e: e.tensor_tensor(pm[:], ps[0:64, 4, :].rearrange("p (u l) -> p u l", l=64),
                                                      incl.unsqueeze(1).to_broadcast([64, 8, 64]), ALU.mult),
                     reads=["consts"], writes=["ps4", "ret_pm"])
                self.dump("ret_pm", pm[:], ["ret_pm"], [64, 8, 64])
                if stop <= 4:
                    continue
                for z in range(2):
                    for h in range(4):
                        hp, hq = h // 2, h % 2
                        rows = slice(64 * hq, 64 * hq + 64)
                        u = 4 * z + h
                        P.op("pe", lambda e, u=u: e.matmul(ps[0:64, 5, 64 * u:64 * u + 64], pm[:, u, :], vtm[:, u, :], start=True, stop=False),
                             reads=["ret_pm", "ret_vtm"], writes=["ps5"])
                        P.op("pe", lambda e, z=z, hp=hp, hq=hq, u=u: e.matmul(
                            ps[0:64, 5, 64 * u:64 * u + 64], qT[:, hp, hq, z, cs], Sb[:, 2 * z + hp, :], start=False, stop=True),
                            reads=["ret_qT", "ret_Sb"], writes=["ps5"])
                P.op("dve", lambda e, c=c: e.tensor_tensor(
                    osb[:, c, :, :].rearrange("p z (h e) -> p (z h) e", e=64), ps[0:64, 5, :].rearrange("p (u e) -> p u e", e=64),
                    ga[0:64, :].unsqueeze(2).to_broadcast([64, 8, 64]), ALU.mult),
                    reads=["ret_ga"], writes=["ps5", ("ret_osb", c)])
                self.dump("ret_osb", osb[:, 0, :, :], [("ret_osb", 0)], [64, 2, 256])
                if stop <= 5:
                    continue
                for z in range(2):
                    for h in range(4):
                        hp, hq = h // 2, h % 2
                        rows = slice(64 * hq, 64 * hq + 64)
                        u = 4 * z + h
                        col = (2 * z + hp) * 64
                        P.op("pe", lambda e, rows=rows, u=u, col=col: e.matmul(ps[rows, 6, col:col + 64], ktm[:, u, :], vtm[:, u, :], start=True, stop=True),
                             reads=["ret_ktm", "ret_vtm"], writes=["ps6"])
                P.op("dve", lambda e: e.tensor_tensor(tmpS[:], ps[:, 6, 0:256].rearrange("p (a e) -> p a e", e=64), S[:], ALU.add),
                     reads=["ret_S"], writes=["ps6", "ret_tmpS"])
                P.op("dve", lambda e: e.tensor_tensor(S[:], tmpS[:], G64[:].unsqueeze(2).to_broadcast([128, 4, 64]), ALU.mult),
                     reads=["ret_tmpS", "ret_g64"], writes=["ret_S"])
                self.dump("ret_S1", S[:], ["ret_S"], [128, 4, 64])
                if not (t % 2 == 1 and c == 1):
                    P.op("act", lambda e: e.activation(Sb[:], S[:], AF.Copy), reads=["ret_S"], writes=["ret_Sb"])
            if stop > 6:
                self.emit_out_tile(t, osb, [("ret_osb", 0), ("ret_osb", 1)], 7)
            if t % 2 == 1 and stop > 7:
                P.op("dve", lambda e: e.tensor_tensor(Sout[:], S[:], GAMI[:].unsqueeze(2).to_broadcast([128, 4, 64]), ALU.mult),
                     reads=["ret_S", "ret_gami"], writes=["ret_Sout"])
                for z in range(2):
                    seg = (t - 1) // 2 if z == 0 else (NT - 1 - t) // 2
                    P.dma("sp", lambda e, z=z, seg=seg: e.dma_start(
                        out=self.out_ret[seg, l, z].rearrange("(hp hq) d e -> (hq d) hp e", hq=2), in_=Sout[:, 2 * z:2 * z + 2, :]),
                        reads=["ret_Sout"])
                P.op("dve", lambda e: e.tensor_scalar(S[:], S[:], self.keep[:, 0:1], None, ALU.mult), reads=["ret_S", "keep"], writes=["ret_S"])
                P.op("act", lambda e: e.activation(Sb[:], S[:], AF.Copy), reads=["ret_S"], writes=["ret_Sb"])
        if self.debug.get("yacc") == "ret":
            P.dma("sp", lambda e: e.dma_start(out=self.dbg_yacc, in_=self.yacc[:]), reads=[("yacc", i) for i in range(NT)])
        if self.debug.get("post", True):
            self.post_simple(l, "ret", wb, wkeys, 1280, AF.Silu, self.ret_norm, True, 512, first)
        self.release(m0)

    def proj_fm(self, wb, wkeys, col, M, tiles, bank, halo=0):
        P, ps = self.P, self.psum
        W = 128 + 2 * halo
        for z in range(2):
            tok0 = 2 + 128 * tiles[z] - halo
            for k in range(8):
                P.op("pe", lambda e, k=k, z=z, tok0=tok0: e.matmul(
                    ps[0:M, bank, W * z:W * z + W], wb[:, k, col:col + M], self.uT[:, k, tok0:tok0 + W],
                    start=(k == 0), stop=(k == 7)),
                    reads=wkeys + [("uT", tiles[z])], writes=["ps%d" % bank])

    def evac_fm(self, dst_fn, bank, M, dkey, scale=1.0, W=128, eng="act", rows=None):
        P, ps = self.P, self.psum
        rs = slice(0, M) if rows is None else rows
        for z in range(2):
            src = ps[rs, bank, W * z:W * z + W]
            dst = dst_fn(z)
            if z == 1:
                dst = dst[:, ::-1]
            if eng == "act":
                P.op("act", lambda e, dst=dst, src=src: e.activation(dst, src, AF.Copy, scale=scale), writes=["ps%d" % bank, dkey])
            else:
                P.op("dve", lambda e, dst=dst, src=src: e.tensor_scalar(dst, src, scale, None, ALU.mult), writes=["ps%d" % bank, dkey])

    def tm_transposes(self, srcT, skey, cs, bank, col0):
        P, ps = self.P, self.psum
        if srcT.dtype == BF16:
            bv = ps[:, bank, :].bitcast(BF16)
            idn = self.ident_bf
        else:
            bv = ps[:, bank:bank + 2, :].rearrange("p a b -> p (a b)")
            idn = self.ident
        for z in range(2):
            for hp in range(2):
                col = col0 + (4 * z + 2 * hp) * 64
                P.op("pe", lambda e, z=z, hp=hp, col=col: e.transpose(bv[0:64, col:col + 128], srcT[:, hp, z, cs], idn[:]),
                     reads=[skey, "ident_bf", "ident"], writes=["ps%d" % bank, "ps%d" % (bank + (0 if srcT.dtype == BF16 else 1))])
        return bv[0:64, col0:col0 + 512].rearrange("p (u d) -> p u d", d=64)

    def mixer_mlstm(self, l, first):
        P, ps = self.P, self.psum
        SDT = F32 if self.debug.get("mlf32") else BF16
        m0 = self.mark()
        wb, wkeys = self.load_w("wA", self.wA[l], 1040)
        incl = self.cst("INCL")
        ones = self.consts[0:64, CO["ONES"][0]:CO["ONES"][0] + 128]
        ib = self.sb("ml_ib", [128, 8], F32)
        fb = self.sb("ml_fb", [128, 8], F32)
        P.dma("sp", lambda e: e.dma_start(out=ib[:], in_=self.ml_ib[l:l + 1, :].partition_broadcast(128)), writes=["ml_ib"])
        P.dma("sp", lambda e: e.dma_start(out=fb[:], in_=self.ml_fb[l:l + 1, :].partition_broadcast(128)), writes=["ml_fb"])
        Cg = self.sb("ml_C", [128, 4, 65], F32)
        Cb = self.sb("ml_Cb", [128, 4, 65], SDT)
        Cout = self.sb("ml_Cout", [128, 4, 65], F32)
        tmpC = self.sb("ml_tmpC", [128, 4, 65], F32)
        mst = self.sb("ml_m", [8, 1], F32)
        msc = self.sb("ml_msc", [8, 4], F32)
        dg = self.sb("ml_dg", [8, 8], F32)
        esl = self.sb("ml_esl", [128, 4], F32)
        P.dma("sp", lambda e: e.dma_start(out=Cg[:, :, 0:64], in_=self.init_mC[l].rearrange("z (hp hq) d e -> (hq d) (z hp) e", hq=2)), writes=["ml_C"])
        P.dma("sp", lambda e: e.dma_start(out=Cg[:, :, 64:65], in_=self.init_mn[l].rearrange("z (hp hq) (d o) -> (hq d) (z hp) o", hq=2, o=1),
                                          allow_slow_non_contiguous=True), writes=["ml_C"])
        P.dma("sp", lambda e: e.dma_start(out=mst[:], in_=self.init_mm[l].rearrange("z (h o) -> (z h) o", o=1), allow_slow_non_contiguous=True), writes=["ml_m"])

        def bcast_units(src81, sign, dst_keys):
            P.op("dve", lambda e: e.tensor_scalar(dg[:], self.ident[0:8, 0:8], src81, None, ALU.mult), reads=["ident", "ml_m"], writes=["ml_dg"])
            P.op("pe", lambda e: e.matmul(ps[:, 7, 0:8], self.consts[0:8, CO["ONES"][0]:CO["ONES"][0] + 128], dg[:], start=True, stop=True),
                 reads=["consts", "ml_dg"], writes=["ps7"])
            for hq in range(2):
                rows = slice(64 * hq, 64 * hq + 64)
                P.op("act", lambda e, rows=rows, hq=hq: e.activation(esl[rows, :], ps[rows, 7, hq:8:2], AF.Exp, scale=sign), writes=["ps7", "ml_esl"])

        bcast_units(mst[:, 0:1], 1.0, None)
        P.op("dve", lambda e: e.tensor_tensor(Cg[:], Cg[:], esl[:].unsqueeze(2).to_broadcast([128, 4, 65]), ALU.mult), reads=["ml_esl", "ml_C"], writes=["ml_C"])
        P.op("act", lambda e: e.activation(Cb[:], Cg[:], AF.Copy), reads=["ml_C"], writes=["ml_Cb"])
        qT = self.sb("ml_qT", [128, 2, 2, 2, 128], SDT)
        P.op("dve", lambda e: e.memset(qT[:], 0.0), writes=["ml_qT"])
        kT = self.sb("ml_kT", [128, 2, 2, 128], SDT)
        vT = self.sb("ml_vT", [128, 2, 2, 128], SDT)
        gT = self.sb("ml_gT", [8, 2, 128], F32)
        ktm = self.sb("ml_ktm", [64, 8, 64], SDT)
        vaug = self.sb("ml_vaug", [64, 8, 65], SDT)
        pm = self.sb("ml_pm", [64, 8, 64], SDT)
        osb = self.sb("ml_osb", [64, 2, 2, 256], F32)
        gtm = self.sb("ml_gtm", [64, 2, 8], F32)
        li = self.sb("ml_li", [64, 8], F32)
        sp = self.sb("ml_sp", [64, 8], F32)
        lib = self.sb("ml_lib", [64, 8], F32)
        e1 = self.sb("ml_e1", [64, 8], F32)
        eb = self.sb("ml_eb", [64, 8], F32)
        wk = self.sb("ml_wk", [64, 8], F32)
        nbl = self.sb("ml_nbl", [64, 8], F32)
        ebls = self.sb("ml_ebls", [128, 4], F32)
        dn = self.sb("ml_dn", [64, 8], F32)
        for t in range(NT):
            tiles = (t, NT - 1 - t)
            for hp in range(2):
                self.proj_fm(wb, wkeys, 128 * hp, 128, tiles, 0)
                for hq in range(2):
                    rows = slice(64 * hq, 64 * hq + 64)
                    self.evac_fm(lambda z, hp=hp, hq=hq, rows=rows: qT[rows, hp, hq, z, :], 0, 128, "ml_qT", rows=rows, eng="dve" if hq else "act")
                self.proj_fm(wb, wkeys, 256 + 128 * hp, 128, tiles, 1)
                self.evac_fm(lambda z, hp=hp: kT[:, hp, z, :], 1, 128, "ml_kT", scale=0.125)
                self.proj_fm(wb, wkeys, 512 + 128 * hp, 128, tiles, 0)
                self.evac_fm(lambda z, hp=hp: vT[:, hp, z, :], 0, 128, "ml_vT", eng="dve")
            for z in range(2):
                tok0 = 2 + 128 * tiles[z]
                for k in range(8):
                    P.op("pe", lambda e, k=k, z=z, tok0=tok0: e.matmul(ps[0:8, 1, 128 * z:128 * z + 128], wb[:, k, 768 + 8 * z:768 + 8 * z + 8],
                                                                    self.uT[:, k, tok0:tok0 + 128], start=(k == 0), stop=(k == 7)),
                         reads=wkeys + [("uT", tiles[z])], writes=["ps1"])
            self.evac_fm(lambda z: gT[:, z, :], 1, 8, "ml_gT", eng="dve")
            for c in range(2):
                cs = slice(64 * c, 64 * c + 64)
                for z in range(2):
                    P.op("pe", lambda e, z=z: e.transpose(ps[0:64, 7, 8 * z:8 * z + 8], gT[:, z, cs], self.ident[0:8, 0:8]), reads=["ml_gT", "ident"], writes=["ps7"])
                P.op("dve", lambda e: e.tensor_copy(gtm[:], ps[0:64, 7, 0:16].rearrange("p (z g) -> p z g", g=8)), writes=["ps7", "ml_gtm"])
                P.op("dve", lambda e: e.tensor_tensor(li[:].rearrange("p (z h) -> p z h", h=4), gtm[:, :, 0:4],
                                                      ib[0:64, :].rearrange("p (z h) -> p z h", h=4), ALU.add), reads=["ml_gtm", "ml_ib"], writes=["ml_li"])
                P.op("dve", lambda e: e.tensor_tensor(sp[:].rearrange("p (z h) -> p z h", h=4), gtm[:, :, 4:8],
                                                      fb[0:64, :].rearrange("p (z h) -> p z h", h=4), ALU.add), reads=["ml_gtm", "ml_fb"], writes=["ml_sp"])
                P.op("act", lambda e: e.activation(sp[:], sp[:], AF.Exp, scale=-1.0), reads=["ml_sp"], writes=["ml_sp"])
                P.op("act", lambda e: e.activation(sp[:], sp[:], AF.Ln, bias=1.0, scale=1.0), reads=["ml_sp"], writes=["ml_sp"])
                P.op("pe", lambda e: e.matmul(ps[0:64, 7, 16:24], incl, sp[:], start=True, stop=True), reads=["consts", "ml_sp"], writes=["ps7"])
                P.op("pe", lambda e: e.matmul(ps[:, 7, 24:32], ones, sp[:], start=True, stop=True), reads=["consts", "ml_sp"], writes=["ps7"])
                P.op("dve", lambda e: e.tensor_tensor(lib[:], ps[0:64, 7, 16:24], li[:], ALU.add), reads=["ml_li"], writes=["ps7", "ml_lib"])
                P.op("act", lambda e: e.activation(eb[:], ps[0:64, 7, 16:24], AF.Exp, scale=-1.0), writes=["ps7", "ml_eb"])
                P.op("dve", lambda e: e.tensor_copy(nbl[:], ps[0:64, 7, 24:32]), writes=["ps7", "ml_nbl"])
                for hq in range(2):
                    rows = slice(64 * hq, 64 * hq + 64)
                    P.op("act", lambda e, rows=rows, hq=hq: e.activation(ebls[rows, :], ps[rows, 7, 24 + hq:32:2], AF.Exp, scale=-1.0), writes=["ps7", "ml_ebls"])
                P.op("act", lambda e: e.activation(e1[:], lib[:], AF.Exp), reads=["ml_lib"], writes=["ml_e1"])
                P.op("dve", lambda e: e.tensor_tensor(wk[:], lib[:], nbl[:], ALU.subtract), reads=["ml_lib", "ml_nbl"], writes=["ml_wk"])
                ktv = self.tm_transposes(kT, "ml_kT", cs, 2, 0)
                vtv = self.tm_transposes(vT, "ml_vT", cs, 2, 512)
                P.op("act", lambda e: e.activation(ktm[:], ktv, AF.Copy), writes=["ps2", "ps3", "ml_ktm"])
                P.op("dve", lambda e: e.tensor_tensor(vaug[:, :, 0:64], vtv, e1[:].unsqueeze(2).to_broadcast([64, 8, 64]), ALU.mult),
                     reads=["ml_e1"], writes=["ps2", "ps3", "ml_vaug"])
                P.op("dve", lambda e: e.tensor_copy(vaug[:, :, 64:65], e1[:].unsqueeze(2)), reads=["ml_e1"], writes=["ml_vaug"])
                for z in range(2):
                    for h in range(4):
                        hp, hq = h // 2, h % 2
                        u = 4 * z + h
                        P.op("pe", lambda e, z=z, hp=hp, hq=hq, u=u: e.matmul(ps[0:64, 4, 64 * u:64 * u + 64], kT[:, hp, z, cs], qT[:, hp, hq, z, cs], start=True, stop=True),
                             reads=["ml_kT", "ml_qT"], writes=["ps4"])
                P.op("dve", lambda e: e.tensor_tensor(pm[:], ps[0:64, 4, :].rearrange("p (u l) -> p u l", l=64), incl.unsqueeze(1).to_broadcast([64, 8, 64]), ALU.mult),
                     reads=["consts"], writes=["ps4", "ml_pm"])
                for z in range(2):
                    bank = 5 + z
                    for h in range(4):
                        hp, hq = h // 2, h % 2
                        u = 4 * z + h
                        P.op("pe", lambda e, u=u, h=h, bank=bank: e.matmul(ps[0:64, bank, 65 * h:65 * h + 65], pm[:, u, :], vaug[:, u, :], start=True, stop=False),
                             reads=["ml_pm", "ml_vaug"], writes=["ps%d" % bank])
                        P.op("pe", lambda e, z=z, hp=hp, hq=hq, h=h, bank=bank: e.matmul(ps[0:64, bank, 65 * h:65 * h + 65], qT[:, hp, hq, z, cs], Cb[:, 2 * z + hp, :], start=False, stop=True),
                             reads=["ml_qT", "ml_Cb"], writes=["ps%d" % bank])
                for z in range(2):
                    bank = 5 + z
                    o3 = ps[0:64, bank, 0:260].rearrange("p (h e) -> p h e", e=65)
                    P.op("dve", lambda e, z=z, o3=o3: e.tensor_tensor(dn[:, 4 * z:4 * z + 4], o3[:, :, 64], eb[:, 4 * z:4 * z + 4], ALU.mult),
                         reads=["ml_eb"], writes=["ps%d" % bank, "ml_dn"])
                P.op("act", lambda e: e.activation(dn[:], dn[:], AF.Abs), reads=["ml_dn"], writes=["ml_dn"])
                P.op("dve", lambda e: e.tensor_scalar(dn[:], dn[:], 1.0, None, ALU.max), reads=["ml_dn"], writes=["ml_dn"])
                P.op("dve", lambda e: e.reciprocal(dn[:], dn[:]), reads=["ml_dn"], writes=["ml_dn"])
                P.op("dve", lambda e: e.tensor_tensor(dn[:], dn[:], eb[:], ALU.mult), reads=["ml_dn", "ml_eb"], writes=["ml_dn"])
                for z in range(2):
                    bank = 5 + z
                    o3 = ps[0:64, bank, 0:260].rearrange("p (h e) -> p h e", e=65)
                    P.op("dve", lambda e, z=z, o3=o3, c=c: e.tensor_tensor(
                        osb[:, c, z, :].rearrange("p (h e) -> p h e", e=64), o3[:, :, 0:64],
                        dn[:, 4 * z:4 * z + 4].unsqueeze(2).to_broadcast([64, 4, 64]), ALU.mult),
                        reads=["ml_dn"], writes=["ps%d" % bank, ("ml_osb", c)])
                for z in range(2):
                    for h in range(4):
                        hp, hq = h // 2, h % 2
                        rows = slice(64 * hq, 64 * hq + 64)
                        u = 4 * z + h
                        col = (2 * z + hp) * 65
                        P.op("pe", lambda e, rows=rows, u=u, col=col: e.matmul(ps[rows, 3, col:col + 65], ktm[:, u, :], vaug[:, u, :], start=True, stop=True),
                             reads=["ml_ktm", "ml_vaug"], writes=["ps3"])
                P.op("dve", lambda e: e.tensor_tensor(tmpC[:], ps[:, 3, 0:260].rearrange("p (a e) -> p a e", e=65), Cg[:], ALU.add),
                     reads=["ml_C"], writes=["ps3", "ml_tmpC"])
                P.op("dve", lambda e: e.tensor_tensor(Cg[:], tmpC[:], ebls[:].unsqueeze(2).to_broadcast([128, 4, 65]), ALU.mult),
                     reads=["ml_tmpC", "ml_ebls"], writes=["ml_C"])
                if not (t % 2 == 1 and c == 1):
                    P.op("act", lambda e: e.activation(Cb[:], Cg[:], AF.Copy), reads=["ml_C"], writes=["ml_Cb"])
                P.op("pe", lambda e: e.transpose(ps[0:8, 7, 64:128], wk[:], self.ident[0:64, 0:64]), reads=["ml_wk", "ident"], writes=["ps7"])
                P.op("pe", lambda e: e.transpose(ps[0:8, 7, 128:192], nbl[:], self.ident[0:64, 0:64]), reads=["ml_nbl", "ident"], writes=["ps7"])
                P.op("dve", lambda e: e.tensor_reduce(msc[:, 0:1], ps[0:8, 7, 64:128], AX.X, ALU.max), writes=["ps7", "ml_msc"])
                P.op("dve", lambda e: e.tensor_tensor(msc[:, 1:2], mst[:], ps[0:8, 7, 128:129], ALU.subtract), reads=["ml_m"], writes=["ps7", "ml_msc"])
                P.op("dve", lambda e: e.tensor_tensor(mst[:], msc[:, 0:1], msc[:, 1:2], ALU.max), reads=["ml_msc"], writes=["ml_m"])
            self.emit_out_tile(t, osb, [("ml_osb", 0), ("ml_osb", 1)], 7)
            if t % 2 == 1:
                bcast_units(mst[:, 0:1], -1.0, None)
                P.op("dve", lambda e: e.tensor_tensor(Cout[:], Cg[:], esl[:].unsqueeze(2).to_broadcast([128, 4, 65]), ALU.mult),
                     reads=["ml_C", "ml_esl"], writes=["ml_Cout"])
                for z in range(2):
                    seg = (t - 1) // 2 if z == 0 else (NT - 1 - t) // 2
                    P.dma("sp", lambda e, z=z, seg=seg: e.dma_start(
                        out=self.out_mC[seg, l, z].rearrange("(hp hq) d e -> (hq d) hp e", hq=2), in_=Cout[:, 2 * z:2 * z + 2, 0:64]), reads=["ml_Cout"])
                    P.dma("sp", lambda e, z=z, seg=seg: e.dma_start(
                        out=self.out_mn[seg, l, z].rearrange("(hp hq) (d o) -> (hq d) hp o", hq=2, o=1), in_=Cout[:, 2 * z:2 * z + 2, 64:65],
                        allow_slow_non_contiguous=True), reads=["ml_Cout"])
                    P.dma("sp", lambda e, z=z, seg=seg: e.dma_start(
                        out=self.out_mm[seg, l, z].rearrange("(h o) -> h o", o=1), in_=mst[4 * z:4 * z + 4, :], allow_slow_non_contiguous=True), reads=["ml_m"])
                P.op("dve", lambda e: e.tensor_scalar(Cg[:], Cg[:], self.keep[:, 0:1], None, ALU.mult), reads=["ml_C", "keep"], writes=["ml_C"])
                P.op("dve", lambda e: e.tensor_scalar(mst[:], mst[:], self.keep[0:8, 0:1], None, ALU.mult), reads=["ml_m", "keep"], writes=["ml_m"])
                P.op("act", lambda e: e.activation(Cb[:], Cg[:], AF.Copy), reads=["ml_C"], writes=["ml_Cb"])
        if self.debug.get("yacc") == "mlstm":
            P.dma("sp", lambda e: e.dma_start(out=self.dbg_yacc, in_=self.yacc[:]), reads=[("yacc", i) for i in range(NT)])
        if self.debug.get("post", True):
            self.post_simple(l, "ml", wb, wkeys, 784, AF.Sigmoid, self.ml_norm, True, 0, first)
        self.release(m0)

    def neumann_inverse(self, pfx, Pm, Qm, Rm, banks, keys=None):
        P, ps = self.P, self.psum
        bP, bQ, bR = banks
        kP, kQ, kR = keys if keys is not None else (pfx + "P", pfx + "Q", pfx + "R")
        ident64 = self.ident[0:64, 0:64]
        for u in range(8):
            P.op("pe", lambda e, u=u: e.transpose(ps[0:64, bQ, 64 * u:64 * u + 64], Pm[:, u, :], ident64), reads=[kP, "ident"], writes=["ps%d" % bQ])
        P.op("act", lambda e: e.activation(Qm[:], ps[0:64, bQ, :].rearrange("p (u l) -> p u l", l=64), AF.Copy), writes=["ps%d" % bQ, kQ])
        P.op("dve", lambda e: e.tensor_tensor(Rm[:], Pm[:], ident64.unsqueeze(1).to_broadcast([64, 8, 64]), ALU.add), reads=[kP, "ident"], writes=[kR])
        for lvl in range(5):
            last = lvl == 4
            if not last:
                for u in range(8):
                    P.op("pe", lambda e, u=u: e.matmul(ps[0:64, bP, 64 * u:64 * u + 64], Qm[:, u, :], Pm[:, u, :], start=True, stop=True),
                         reads=[kP, kQ], writes=["ps%d" % bP])
            for u in range(8):
                P.op("pe", lambda e, u=u: e.matmul(ps[0:64, bQ, 64 * u:64 * u + 64], Pm[:, u, :], Qm[:, u, :], start=True, stop=True),
                     reads=[kP, kQ], writes=["ps%d" % bQ])
            if not last:
                P.op("dve", lambda e: e.tensor_copy(Pm[:], ps[0:64, bP, :].rearrange("p (u l) -> p u l", l=64)), writes=["ps%d" % bP, kP])
            P.op("act", lambda e: e.activation(Qm[:], ps[0:64, bQ, :].rearrange("p (u l) -> p u l", l=64), AF.Copy), writes=["ps%d" % bQ, kQ])
            for u in range(8):
                P.op("pe", lambda e, u=u: e.matmul(ps[0:64, bR, 64 * u:64 * u + 64], Qm[:, u, :], Rm[:, u, :], start=True, stop=True),
                     reads=[kQ, kR], writes=["ps%d" % bR])
            P.op("dve", lambda e: e.tensor_tensor(Rm[:], ps[0:64, bR, :].rearrange("p (u l) -> p u l", l=64), Rm[:], ALU.add),
                 reads=[kR], writes=["ps%d" % bR, kR])

    def mixer_delta(self, l, first):
        P, ps = self.P, self.psum
        m0 = self.mark()
        wb, wkeys = self.load_w("wB", self.wB[l], 1040)
        incl = self.cst("INCL")
        strict = self.cst("STRICT")
        ones64 = self.consts[0:64, CO["ONES"][0]:CO["ONES"][0] + 64]
        ones128 = self.consts[0:64, CO["ONES"][0]:CO["ONES"][0] + 128]
        blk = self.consts[:, CO["BLK"][0]:CO["BLK"][0] + 128]
        ident64 = self.ident[0:64, 0:64]
        cw = self.sb("dl_cw", [128, 6, 5], F32)
        P.dma("sp", lambda e: e.dma_start(out=cw[:], in_=self.dl_conv[l]), writes=["dl_cw"])
        Au = self.sb("dl_A", [128, 8], F32)
        dtb = self.sb("dl_dtb", [128, 8], F32)
        P.dma("sp", lambda e: e.dma_start(out=Au[:], in_=self.dl_alog[l:l + 1, :].partition_broadcast(128)), writes=["dl_A"])
        P.dma("sp", lambda e: e.dma_start(out=dtb[:], in_=self.dl_dtb[l:l + 1, :].partition_broadcast(128)), writes=["dl_dtb"])
        P.op("act", lambda e: e.activation(Au[:], Au[:], AF.Exp), reads=["dl_A"], writes=["dl_A"])
        S = self.sb("dl_S", [128, 4, 64], F32)
        Sb = self.sb("dl_Sb", [128, 4, 64], BF16)
        Sout = self.sb("dl_Sout", [128, 4, 64], F32)
        tmpS = self.sb("dl_tmpS", [128, 4, 64], F32)
        P.dma("sp", lambda e: e.dma_start(out=S[:], in_=self.init_delta[l].rearrange("z (hp hq) d e -> (hq d) (z hp) e", hq=2)), writes=["dl_S"])
        P.op("act", lambda e: e.activation(Sb[:], S[:], AF.Copy), reads=["dl_S"], writes=["dl_Sb"])
        pre = self.sb("dl_pre", [128, 2, 132], F32)
        acc = self.sb("dl_acc", [128, 2, 128], F32)
        sq = self.sb("dl_sq", [128, 2, 128], F32)
        rn = self.sb("dl_rn", [128, 2, 128], F32)
        qTp = self.sb("dl_qTp", [128, 2, 2, 2, 128], BF16)
        kTp = self.sb("dl_kTp", [128, 2, 2, 2, 128], BF16)
        P.op("dve", lambda e: e.memset(qTp[:], 0.0), writes=["dl_qTp"])
        P.op("dve", lambda e: e.memset(kTp[:], 0.0), writes=["dl_kTp"])
        kT = self.sb("dl_kT", [128, 2, 2, 128], BF16)
        vT = self.sb("dl_vT", [128, 2, 2, 128], BF16)
        gT = self.sb("dl_gT", [8, 2, 128], F32)
        gtm = self.sb("dl_gtm", [64, 4, 8], F32)
        beta2 = self.sb("dl_beta", [64, 2, 8], F32)
        nbeta2 = self.sb("dl_nbeta", [64, 2, 8], F32)
        ng2 = self.sb("dl_ng", [64, 2, 8], F32)
        ngc2 = self.sb("dl_ngc", [64, 2, 8], F32)
        eg2 = self.sb("dl_eg", [64, 2, 8], F32)
        eglt2 = self.sb("dl_eglt", [64, 2, 8], F32)
        egls2 = self.sb("dl_egls", [128, 2, 4], F32)
        dgm = self.sb("dl_dgm", [64, 8, 64], F32)
        dT = self.sb("dl_dT", [64, 8, 64], F32)
        dTs = self.sb("dl_dTs", [64, 8, 64], F32)
        Pm = self.sb("dl_P", [64, 8, 64], F32)
        Qm = self.sb("dl_Q", [64, 8, 64], F32)
        Rm = self.sb("dl_R", [64, 8, 64], F32)
        qkd = self.sb("dl_qkd", [64, 8, 64], BF16)
        ktm = self.sb("dl_ktm", [64, 8, 64], BF16)
        vtm = self.sb("dl_vtm", [64, 8, 64], F32)
        kd = self.sb("dl_kd", [64, 8, 64], BF16)
        rr = self.sb("dl_r", [64, 8, 64], F32)
        vnew = self.sb("dl_vnew", [64, 8, 64], BF16)
        vnf = self.sb("dl_vnf", [64, 8, 64], F32)
        t1 = self.sb("dl_t1", [64, 8, 64], F32)
        osb = self.sb("dl_osb", [64, 2, 2, 256], F32)
        for t in range(NT):
            tiles = (t, NT - 1 - t)
            edge = slice(0, 2) if t % 2 == 0 else slice(130, 132)
            for j in range(6):
                bank = j % 2
                self.proj_fm(wb, wkeys, 128 * j, 128, tiles, bank, halo=2)
                self.evac_fm(lambda z: pre[:, z, :], bank, 128, "dl_pre", W=132, eng="act")
                P.op("dve", lambda e: e.tensor_scalar(pre[:, :, edge], pre[:, :, edge], self.keep[:, 0:1], None, ALU.mult), reads=["dl_pre", "keep"], writes=["dl_pre"])
                for z in range(2):
                    eng = "dve"
                    for k in range(5):
                        wk_ = cw[:, j, k:k + 1] if z == 0 else cw[:, j, 4 - k:5 - k]
                        if k == 0:
                            P.op(eng, lambda e, z=z, wk_=wk_: e.tensor_scalar(acc[:, z, :], pre[:, z, 0:128], wk_, None, ALU.mult),
                                 reads=["dl_pre", "dl_cw"], writes=[("dl_acc", z)])
                        else:
                            P.op(eng, lambda e, z=z, k=k, wk_=wk_: e.scalar_tensor_tensor(acc[:, z, :], pre[:, z, k:k + 128], wk_, acc[:, z, :], ALU.mult, ALU.add),
                                 reads=["dl_pre", "dl_cw", ("dl_acc", z)], writes=[("dl_acc", z)])
                akeys = [("dl_acc", 0), ("dl_acc", 1)]
                if j >= 4:
                    P.op("act", lambda e, j=j: e.activation(vT[:, j - 4, :, :], acc[:], AF.Silu), reads=akeys, writes=["dl_vT"])
                    continue
                P.op("act", lambda e: e.activation(acc[:], acc[:], AF.Silu), reads=akeys, writes=akeys)
                P.op("act", lambda e: e.activation(sq[:], acc[:], AF.Square), reads=akeys, writes=["dl_sq"])
                P.op("pe", lambda e: e.matmul(ps[:, 2, 0:256], blk, sq[:].rearrange("p z n -> p (z n)"), start=True, stop=True), reads=["consts", "dl_sq"], writes=["ps2"])
                P.op("act", lambda e: e.activation(rn[:], ps[:, 2, 0:256].rearrange("p (z n) -> p z n", z=2), AF.Ln, bias=1e-6, scale=1.0), writes=["ps2", "dl_rn"])
                P.op("act", lambda e: e.activation(rn[:], rn[:], AF.Exp, scale=-0.5), reads=["dl_rn"], writes=["dl_rn"])
                hp = j % 2
                if j < 2:
                    for hq in range(2):
                        rows = slice(64 * hq, 64 * hq + 64)
                        P.op("dve", lambda e, rows=rows, hp=hp, hq=hq: e.scalar_tensor_tensor(qTp[rows, hp, hq, :, :], acc[rows, :, :], 0.125, rn[rows, :, :], ALU.mult, ALU.mult),
                             reads=akeys + ["dl_rn"], writes=["dl_qTp"])
                else:
                    P.op("dve", lambda e, hp=hp: e.tensor_tensor(kT[:, hp, :, :], acc[:], rn[:], ALU.mult), reads=akeys + ["dl_rn"], writes=["dl_kT"])
                    for hq in range(2):
                        rows = slice(64 * hq, 64 * hq + 64)
                        P.op("pool", lambda e, rows=rows, hp=hp, hq=hq: e.tensor_copy(kTp[rows, hp, hq, :, :], kT[rows, hp, :, :]), reads=["dl_kT"], writes=["dl_kTp"])
            for z in range(2):
                tok0 = 2 + 128 * tiles[z]
                for k in range(8):
                    P.op("pe", lambda e, k=k, z=z, tok0=tok0: e.matmul(ps[0:8, 1, 128 * z:128 * z + 128], wb[:, k, 768 + 8 * z:768 + 8 * z + 8],
                                                                    self.uT[:, k, tok0:tok0 + 128], start=(k == 0), stop=(k == 7)),
                         reads=wkeys + [("uT", tiles[z])], writes=["ps1"])
            self.evac_fm(lambda z: gT[:, z, :], 1, 8, "dl_gT", eng="dve")
            for c in range(2):
                for z in range(2):
                    q_ = 2 * c + z
                    P.op("pe", lambda e, z=z, c=c, q_=q_: e.transpose(ps[0:64, 7, 8 * q_:8 * q_ + 8], gT[:, z, 64 * c:64 * c + 64], self.ident[0:8, 0:8]),
                         reads=["dl_gT", "ident"], writes=["ps7"])
            P.op("dve", lambda e: e.tensor_copy(gtm[:], ps[0:64, 7, 0:32].rearrange("p (q g) -> p q g", g=8)), writes=["ps7", "dl_gtm"])
            b4 = beta2[:].rearrange("p c (z h) -> p (c z) h", h=4)
            P.op("act", lambda e: e.activation(b4, gtm[:, :, 0:4], AF.Exp, scale=-1.0), reads=["dl_gtm"], writes=["dl_beta"])
            P.op("act", lambda e: e.activation(beta2[:], beta2[:], AF.Ln, bias=1.0, scale=1.0), reads=["dl_beta"], writes=["dl_beta"])
            P.op("act", lambda e: e.activation(beta2[:], beta2[:], AF.Exp, scale=-1.0), reads=["dl_beta"], writes=["dl_beta"])
            P.op("dve", lambda e: e.tensor_scalar(nbeta2[:], beta2[:], -1.0, None, ALU.mult), reads=["dl_beta"], writes=["dl_nbeta"])
            for c in range(2):
                P.op("dve", lambda e, c=c: e.tensor_tensor(ng2[:, c, :].rearrange("p (z h) -> p z h", h=4), gtm[:, 2 * c:2 * c + 2, 4:8],
                                                           dtb[0:64, :].rearrange("p (z h) -> p z h", h=4), ALU.add),
                     reads=["dl_gtm", "dl_dtb"], writes=["dl_ng"])
            P.op("act", lambda e: e.activation(ng2[:], ng2[:], AF.Exp), reads=["dl_ng"], writes=["dl_ng"])
            P.op("act", lambda e: e.activation(ng2[:], ng2[:], AF.Ln, bias=1.0, scale=1.0), reads=["dl_ng"], writes=["dl_ng"])
            P.op("dve", lambda e: e.tensor_tensor(ng2[:], ng2[:], Au[0:64, :].unsqueeze(1).to_broadcast([64, 2, 8]), ALU.mult), reads=["dl_ng", "dl_A"], writes=["dl_ng"])
            for c in range(2):
                P.op("pe", lambda e, c=c: e.matmul(ps[0:64, 7, 32 + 8 * c:40 + 8 * c], incl, ng2[:, c, :], start=True, stop=True), reads=["consts", "dl_ng"], writes=["ps7"])
                P.op("pe", lambda e, c=c: e.matmul(ps[:, 7, 48 + 8 * c:56 + 8 * c], ones128, ng2[:, c, :], start=True, stop=True), reads=["consts", "dl_ng"], writes=["ps7"])
            ngcp = ps[0:64, 7, 32:48].rearrange("p (c u) -> p c u", u=8)
            nglp = ps[0:64, 7, 48:64].rearrange("p (c u) -> p c u", u=8)
            P.op("dve", lambda e: e.tensor_copy(ngc2[:], ngcp), writes=["ps7", "dl_ngc"])
            P.op("act", lambda e: e.activation(eg2[:], ngcp, AF.Exp, scale=-1.0), writes=["ps7", "dl_eg"])
            P.op("dve", lambda e: e.tensor_tensor(eglt2[:], ngc2[:], nglp, ALU.subtract), reads=["dl_ngc"], writes=["ps7", "dl_eglt"])
            P.op("act", lambda e: e.activation(eglt2[:], eglt2[:], AF.Exp), reads=["dl_eglt"], writes=["dl_eglt"])
            for hq in range(2):
                rows = slice(64 * hq, 64 * hq + 64)
                P.op("act", lambda e, rows=rows, hq=hq: e.activation(egls2[rows, :, :], ps[rows, 7, 48:64].rearrange("p (c u) -> p c u", u=8)[:, :, hq:8:2], AF.Exp, scale=-1.0),
                     writes=["ps7", "dl_egls"])
            for c in range(2):
                cs = slice(64 * c, 64 * c + 64)
                beta, nbeta, ngc, eg, eglt, egls = beta2[:, c, :], nbeta2[:, c, :], ngc2[:, c, :], eg2[:, c, :], eglt2[:, c, :], egls2[:, c, :]
                P.op("dve", lambda e: e.tensor_tensor(dgm[:], ident64.unsqueeze(1).to_broadcast([64, 8, 64]), ngc[:].unsqueeze(2).to_broadcast([64, 8, 64]), ALU.mult),
                     reads=["ident", "dl_ngc"], writes=["dl_dgm"])
                P.op("pe", lambda e: e.matmul(ps[0:64, 3, :], ones64, dgm[:].rearrange("p u l -> p (u l)"), start=True, stop=True), reads=["consts", "dl_dgm"], writes=["ps3"])
                P.op("dve", lambda e: e.tensor_tensor(dT[:], ngc[:].unsqueeze(2).to_broadcast([64, 8, 64]), ps[0:64, 3, :].rearrange("p (u l) -> p u l", l=64), ALU.subtract),
                     reads=["dl_ngc"], writes=["ps3", "dl_dT"])
                P.op("dve", lambda e: e.tensor_scalar(dT[:], dT[:], 0.0, None, ALU.min), reads=["dl_dT"], writes=["dl_dT"])
                P.op("act", lambda e: e.activation(dT[:], dT[:], AF.Exp), reads=["dl_dT"], writes=["dl_dT"])
                P.op("pool", lambda e: e.tensor_tensor(dTs[:], dT[:], strict.unsqueeze(1).to_broadcast([64, 8, 64]), ALU.mult), reads=["dl_dT", "consts"], writes=["dl_dTs"])
                P.op("pool", lambda e: e.tensor_tensor(dT[:], dT[:], incl.unsqueeze(1).to_broadcast([64, 8, 64]), ALU.mult), reads=["dl_dT", "consts"], writes=["dl_dT"])
                ktv = self.tm_transposes(kT, "dl_kT", cs, 2, 0)
                vtv = self.tm_transposes(vT, "dl_vT", cs, 2, 512)
                P.op("act", lambda e: e.activation(ktm[:], ktv, AF.Copy), writes=["ps2", "dl_ktm"])
                P.op("dve", lambda e: e.tensor_copy(vtm[:], vtv), writes=["ps2", "dl_vtm"])
                P.op("dve", lambda e: e.tensor_tensor(kd[:], ktm[:], eglt[:].unsqueeze(2).to_broadcast([64, 8, 64]), ALU.mult), reads=["dl_ktm", "dl_eglt"], writes=["dl_kd"])
                for z in range(2):
                    for h in range(4):
                        hp, hq = h // 2, h % 2
                        u = 4 * z + h
                        P.op("pe", lambda e, z=z, hp=hp, hq=hq, u=u: e.matmul(ps[0:64, 4, 64 * u:64 * u + 64], kT[:, hp, z, cs], kTp[:, hp, hq, z, cs], start=True, stop=True),
                             reads=["dl_kT", "dl_kTp"], writes=["ps4"])
                        P.op("pe", lambda e, z=z, hp=hp, hq=hq, u=u: e.matmul(ps[0:64, 5, 64 * u:64 * u + 64], kT[:, hp, z, cs], qTp[:, hp, hq, z, cs], start=True, stop=True),
                             reads=["dl_kT", "dl_qTp"], writes=["ps5"])
                P.op("dve", lambda e: e.tensor_tensor(Pm[:], ps[0:64, 4, :].rearrange("p (u l) -> p u l", l=64), dTs[:], ALU.mult), reads=["dl_dTs"], writes=["ps4", "dl_P"])
                P.op("dve", lambda e: e.tensor_tensor(Pm[:], Pm[:], nbeta[:].unsqueeze(2).to_broadcast([64, 8, 64]), ALU.mult), reads=["dl_P", "dl_nbeta"], writes=["dl_P"])
                P.op("dve", lambda e: e.tensor_tensor(qkd[:], ps[0:64, 5, :].rearrange("p (u l) -> p u l", l=64), dT[:], ALU.mult), reads=["dl_dT"], writes=["ps5", "dl_qkd"])
                self.neumann_inverse("dl_", Pm, Qm, Rm, (4, 5, 6))
                for z in range(2):
                    bank = 3 + z
                    for h in range(4):
                        hp, hq = h // 2, h % 2
                        P.op("pe", lambda e, z=z, hp=hp, hq=hq, h=h, bank=bank: e.matmul(ps[0:64, bank, 128 * h:128 * h + 64], kTp[:, hp, hq, z, cs], Sb[:, 2 * z + hp, :], start=True, stop=True),
                             reads=["dl_kTp", "dl_Sb"], writes=["ps%d" % bank])
                        P.op("pe", lambda e, z=z, hp=hp, hq=hq, h=h, bank=bank: e.matmul(ps[0:64, bank, 128 * h + 64:128 * h + 128], qTp[:, hp, hq, z, cs], Sb[:, 2 * z + hp, :], start=True, stop=True),
                             reads=["dl_qTp", "dl_Sb"], writes=["ps%d" % bank])
                for z in range(2):
                    bank = 3 + z
                    ks4 = ps[0:64, bank, :].rearrange("p (h two e) -> p h two e", two=2, e=64)
                    us = slice(4 * z, 4 * z + 4)
                    P.op("dve", lambda e, ks4=ks4, us=us: e.tensor_tensor(t1[:, us, :], ks4[:, :, 0, :], eg[:, us].unsqueeze(2).to_broadcast([64, 4, 64]), ALU.mult),
                         reads=["dl_eg"], writes=["ps%d" % bank, ("dl_t1", z)])
                    P.op("dve", lambda e, us=us, z=z: e.tensor_tensor(rr[:, us, :], vtm[:, us, :], t1[:, us, :], ALU.subtract), reads=["dl_vtm", ("dl_t1", z)], writes=[("dl_r", z)])
                    P.op("dve", lambda e, ks4=ks4, us=us: e.tensor_tensor(t1[:, us, :], ks4[:, :, 1, :], eg[:, us].unsqueeze(2).to_broadcast([64, 4, 64]), ALU.mult),
                         reads=["dl_eg", ("dl_r", z)], writes=["ps%d" % bank, ("dl_t1", z)])
                rkeys = [("dl_r", 0), ("dl_r", 1)]
                for u in range(8):
                    P.op("pe", lambda e, u=u: e.matmul(ps[0:64, 5, 64 * u:64 * u + 64], Rm[:, u, :], rr[:, u, :], start=True, stop=True), reads=["dl_R"] + rkeys, writes=["ps5"])
                P.op("dve", lambda e: e.tensor_tensor(vnew[:], ps[0:64, 5, :].rearrange("p (u l) -> p u l", l=64), beta[:].unsqueeze(2).to_broadcast([64, 8, 64]), ALU.mult),
                     reads=["dl_beta"], writes=["ps5", "dl_vnew"])
                for u in range(8):
                    P.op("pe", lambda e, u=u: e.matmul(ps[0:64, 6, 64 * u:64 * u + 64], qkd[:, u, :], vnew[:, u, :], start=True, stop=True), reads=["dl_qkd", "dl_vnew"], writes=["ps6"])
                P.op("dve", lambda e, c=c: e.tensor_tensor(osb[:, c, :, :].rearrange("p z (h e) -> p (z h) e", e=64), ps[0:64, 6, :].rearrange("p (u e) -> p u e", e=64), t1[:], ALU.add),
                     reads=[("dl_t1", 0), ("dl_t1", 1)], writes=["ps6", ("dl_osb", c)])
                for z in range(2):
                    for h in range(4):
                        hp, hq = h // 2, h % 2
                        rows = slice(64 * hq, 64 * hq + 64)
                        u = 4 * z + h
                        col = (2 * z + hp) * 64
                        P.op("pe", lambda e, rows=rows, u=u, col=col: e.matmul(ps[rows, 3, col:col + 64], kd[:, u, :], vnew[:, u, :], start=True, stop=True),
                             reads=["dl_kd", "dl_vnew"], writes=["ps3"])
                P.op("dve", lambda e: e.tensor_tensor(tmpS[:], S[:], egls[:].unsqueeze(2).to_broadcast([128, 4, 64]), ALU.mult), reads=["dl_S", "dl_egls"], writes=["dl_tmpS"])
                P.op("dve", lambda e: e.tensor_tensor(S[:], ps[:, 3, 0:256].rearrange("p (a e) -> p a e", e=64), tmpS[:], ALU.add), reads=["dl_tmpS"], writes=["ps3", "dl_S"])
                if not (t % 2 == 1 and c == 1):
                    P.op("act", lambda e: e.activation(Sb[:], S[:], AF.Copy), reads=["dl_S"], writes=["dl_Sb"])
            self.emit_out_tile(t, osb, [("dl_osb", 0), ("dl_osb", 1)], 7)
            if t % 2 == 1:
                P.op("dve", lambda e: e.tensor_copy(Sout[:], S[:]), reads=["dl_S"], writes=["dl_Sout"])
                for z in range(2):
                    seg = (t - 1) // 2 if z == 0 else (NT - 1 - t) // 2
                    P.dma("sp", lambda e, z=z, seg=seg: e.dma_start(
                        out=self.out_delta[seg, l, z].rearrange("(hp hq) d e -> (hq d) hp e", hq=2), in_=Sout[:, 2 * z:2 * z + 2, :]), reads=["dl_Sout"])
                P.op("dve", lambda e: e.tensor_scalar(S[:], S[:], self.keep[:, 0:1], None, ALU.mult), reads=["dl_S", "keep"], writes=["dl_S"])
                P.op("act", lambda e: e.activation(Sb[:], S[:], AF.Copy), reads=["dl_S"], writes=["dl_Sb"])
        if self.debug.get("yacc") == "delta":
            P.dma("sp", lambda e: e.dma_start(out=self.dbg_yacc, in_=self.yacc[:]), reads=[("yacc", i) for i in range(NT)])
        if self.debug.get("post", True):
            self.post_simple(l, "dl", wb, wkeys, 784, AF.Silu, self.dl_norm, False, 256, first)
        self.release(m0)

    def rwkv_mix_fm(self, pre, mu_col, dst, keys_r, keys_w, eng="dve"):
        P = self.P
        n = dst.shape[-1]
        P.op("dve", lambda e: e.tensor_tensor(dst, pre[:, :, 0:n], pre[:, :, 2:n + 2], ALU.add), reads=keys_r, writes=keys_w)
        P.op("dve", lambda e: e.scalar_tensor_tensor(dst, dst, 0.5, pre[:, :, 1:n + 1], ALU.mult, ALU.subtract), reads=keys_r + keys_w, writes=keys_w)
        P.op("dve", lambda e: e.scalar_tensor_tensor(dst, dst, mu_col, pre[:, :, 1:n + 1], ALU.mult, ALU.add), reads=keys_r + keys_w + ["rw_mu"], writes=keys_w)

    def mixer_rwkv(self, l, first):
        P, ps = self.P, self.psum
        m0 = self.mark()
        wb, wkeys = self.load_w("wD", self.wD[l], 1152)
        incl = self.cst("INCL")
        strict = self.cst("STRICT")
        ones64 = self.consts[0:64, CO["ONES"][0]:CO["ONES"][0] + 64]
        ident64 = self.ident[0:64, 0:64]
        bacc = self.sb("rw_bacc", [128, NT, 256], BF16)
        mu = self.sb("rw_mu", [128, 9], F32)
        P.dma("sp", lambda e: e.dma_start(out=mu[:], in_=self.rw_mu[l]), writes=["rw_mu"])
        m1 = self.mark()
        w2p = self.sb("rw_w2p", [128, 2, 256], F32)
        a2p = self.sb("rw_a2p", [128, 2, 256], F32)
        P.dma("sp", lambda e: e.dma_start(out=w2p[:], in_=self.rw_w2p[l]), writes=["rw_w2p"])
        P.dma("sp", lambda e: e.dma_start(out=a2p[:], in_=self.rw_a2p[l]), writes=["rw_a2p"])
        bcs = {}
        for nm, src, width in (("w0", self.rw_w0, 512), ("a0", self.rw_a0, 512), ("kk", self.rw_kk, 256), ("ka", self.rw_ka, 256), ("rk", self.rw_rk, 256)):
            bcs[nm] = self.sb("rw_bc_" + nm, [64, width], F32)
            P.dma("sp", lambda e, nm=nm, src=src: e.dma_start(out=bcs[nm][:], in_=src[l:l + 1, :].partition_broadcast(64)), writes=["rw_bc"])
        omka = self.sb("rw_omka", [64, 256], F32)
        P.op("dve", lambda e: e.tensor_scalar(omka[:], bcs["ka"][:], -1.0, 1.0, ALU.mult, ALU.add), reads=["rw_bc"], writes=["rw_omka"])
        M = self.sb("rw_M", [64, 8, 64], F32)
        pre = self.sb("rw_pre", [128, 2, 130], F32)
        rT = self.sb("rw_rT", [128, 2, 2, 128], BF16)
        kT = self.sb("rw_kT", [128, 2, 2, 128], BF16)
        vT = self.sb("rw_vT", [128, 2, 2, 128], BF16)
        twT = self.sb("rw_twT", [128, 2, 128], F32)
        daT = self.sb("rw_daT", [128, 2, 128], F32)
        rtm = self.sb("rw_rtm", [64, 2, 256], F32)
        ktm = self.sb("rw_ktm", [64, 2, 256], F32)
        vtm = self.sb("rw_vtm", [64, 2, 256], F32)
        lw = self.sb("rw_lw", [64, 2, 256], F32)
        av = self.sb("rw_a", [64, 2, 256], F32)
        ecl = self.sb("rw_ecl", [64, 2, 256], F32)
        encl = self.sb("rw_encl", [64, 2, 256], F32)
        ecw = self.sb("rw_ecw", [64, 2, 256], F32)
        kh = self.sb("rw_kh", [64, 2, 256], F32)
        kx = self.sb("rw_kx", [64, 2, 256], F32)
        ss8 = self.sb("rw_ss8", [64, 8], F32)
        bs8 = self.sb("rw_bs8", [64, 8], F32)
        ktz = self.sb("rw_ktz", [64, 2, 256], F32)
        al = self.sb("rw_al", [64, 2, 256], F32)
        be = self.sb("rw_be", [64, 2, 256], F32)
        kti = self.sb("rw_kti", [64, 2, 256], F32)
        rti = self.sb("rw_rti", [64, 2, 256], F32)
        beT = self.sb("rw_beT", [64, 8, 64], F32)
        Pm = lw[:].rearrange("p z (h j) -> p (z h) j", j=64)
        Qm = av[:].rearrange("p z (h j) -> p (z h) j", j=64)
        Aak = ecl[:].rearrange("p z (h j) -> p (z h) j", j=64)
        Ara = encl[:].rearrange("p z (h j) -> p (z h) j", j=64)
        Ark = ecw[:].rearrange("p z (h j) -> p (z h) j", j=64)
        X1 = kh[:].rearrange("p z (h j) -> p (z h) j", j=64)
        Uu = kx[:].rearrange("p z (h j) -> p (z h) j", j=64)
        tmpM = ktz[:].rearrange("p z (h j) -> p (z h) j", j=64)
        alT = rtm[:].rearrange("p z (h j) -> p (z h) j", j=64)
        ktT = ktm[:].rearrange("p z (h j) -> p (z h) j", j=64)
        rtT = be[:].rearrange("p z (h j) -> p (z h) j", j=64)
        Mt = kh[:].rearrange("p z (h j) -> p (z h) j", j=64)
        Rm = rti[:].rearrange("p z (h j) -> p (z h) j", j=64)
        WL = self.sb("rw_WL", [64, 8], F32)
        P.dma("sp", lambda e: e.dma_start(out=Mt[:], in_=self.init_rwkv[l].rearrange("z h i j -> i (z h) j")), writes=["rw_kh"])
        for u in range(8):
            P.op("pe", lambda e, u=u: e.transpose(ps[0:64, 2, 64 * u:64 * u + 64], Mt[:, u, :], ident64), reads=["rw_kh", "ident"], writes=["ps2"])
        P.op("dve", lambda e: e.tensor_copy(M[:], ps[0:64, 2, :].rearrange("p (u e) -> p u e", e=64)), writes=["ps2", "rw_M"])
        osb = self.sb("rw_osb", [64, 2, 512], F32)
        v3 = lambda x_: x_[:].rearrange("p z (h j) -> p (z h) j", j=64)
        for t in range(NT):
            tiles = (t, NT - 1 - t)
            edge = slice(0, 1) if t % 2 == 0 else slice(129, 130)
            for j in range(9):
                bank = j % 2
                self.proj_fm(wb, wkeys, 128 * j, 128, tiles, bank, halo=1)
                self.evac_fm(lambda z: pre[:, z, :], bank, 128, "rw_pre", W=130, eng="act")
                P.op("dve", lambda e: e.tensor_scalar(pre[:, :, edge], pre[:, :, edge], self.keep[:, 0:1], None, ALU.mult), reads=["rw_pre", "keep"], writes=["rw_pre"])
                if j < 6:
                    dst, dk = ((rT, "rw_rT"), (kT, "rw_kT"), (vT, "rw_vT"))[j // 2]
                    dst = dst[:, j % 2, :, :]
                elif j == 6:
                    dst, dk = twT[:], "rw_twT"
                elif j == 7:
                    dst, dk = daT[:], "rw_daT"
                else:
                    continue
                self.rwkv_mix_fm(pre, mu[:, j:j + 1], dst, ["rw_pre"], [dk])
                if j == 6:
                    P.op("act", lambda e: e.activation(twT[:], twT[:], AF.Tanh), reads=["rw_twT"], writes=["rw_twT"])
            for c in range(2):
                cs = slice(64 * c, 64 * c + 64)
                for z in range(2):
                    P.op("pe", lambda e, z=z: e.matmul(ps[0:64, 2, 256 * z:256 * z + 256], twT[:, z, cs], w2p[:, z, :], start=True, stop=True), reads=["rw_twT", "rw_w2p"], writes=["ps2"])
                    P.op("pe", lambda e, z=z: e.matmul(ps[0:64, 3, 256 * z:256 * z + 256], daT[:, z, cs], a2p[:, z, :], start=True, stop=True), reads=["rw_daT", "rw_a2p"], writes=["ps3"])
                lwf = lw[:].rearrange("p z n -> p (z n)")
                avf = av[:].rearrange("p z n -> p (z n)")
                P.op("dve", lambda e: e.tensor_tensor(lwf, ps[0:64, 2, :], bcs["w0"][:], ALU.add), reads=["rw_bc"], writes=["ps2", "rw_lw"])
                P.op("act", lambda e: e.activation(lwf, lwf, AF.Sigmoid), reads=["rw_lw"], writes=["rw_lw"])
                P.op("dve", lambda e: e.tensor_scalar(lwf, lwf, -float(np.exp(-0.5)), None, ALU.mult), reads=["rw_lw"], writes=["rw_lw"])
                P.op("dve", lambda e: e.tensor_tensor(avf, ps[0:64, 3, :], bcs["a0"][:], ALU.add), reads=["rw_bc"], writes=["ps3", "rw_a"])
                P.op("act", lambda e: e.activation(avf, avf, AF.Sigmoid), reads=["rw_a"], writes=["rw_a"])
                P.op("pe", lambda e: e.matmul(ps[0:64, 4, :], incl, lwf, start=True, stop=True), reads=["consts", "rw_lw"], writes=["ps4"])
                for u in range(8):
                    z, h = u // 4, u % 4
                    P.op("pe", lambda e, u=u, z=z, h=h: e.matmul(ps[0:64, 7, 64 + u:65 + u], lw[:, z, 64 * h:64 * h + 64], ones64[:, 0:1], start=True, stop=True),
                         reads=["rw_lw", "consts"], writes=["ps7"])
                P.op("act", lambda e: e.activation(WL[:], ps[0:64, 7, 64:72], AF.Exp), writes=["ps7", "rw_WL"])
                eclf = ecl[:].rearrange("p z n -> p (z n)")
                P.op("act", lambda e: e.activation(eclf, ps[0:64, 4, :], AF.Exp), writes=["ps4", "rw_ecl"])
                P.op("act", lambda e: e.activation(encl[:].rearrange("p z n -> p (z n)"), ps[0:64, 4, :], AF.Exp, scale=-1.0), writes=["ps4", "rw_encl"])
                P.op("dve", lambda e: e.tensor_tensor(ecw[:].rearrange("p z n -> p (z n)"), ps[0:64, 4, :], lwf, ALU.subtract), reads=["rw_lw"], writes=["ps4", "rw_ecw"])
                P.op("act", lambda e: e.activation(ecw[:], ecw[:], AF.Exp), reads=["rw_ecw"], writes=["rw_ecw"])
                for (src, skey, dstt, dkey, bank) in ((rT, "rw_rT", rtm, "rw_rtm", 5), (kT, "rw_kT", ktm, "rw_ktm", 6), (vT, "rw_vT", vtm, "rw_vtm", 5)):
                    bv = ps[:, bank, :].bitcast(BF16)
                    for z in range(2):
                        for hp in range(2):
                            col = (4 * z + 2 * hp) * 64
                            P.op("pe", lambda e, src=src, z=z, hp=hp, col=col, bv=bv: e.transpose(bv[0:64, col:col + 128], src[:, hp, z, cs], self.ident_bf[:]),
                                 reads=[skey, "ident_bf"], writes=["ps%d" % bank])
                    P.op("act", lambda e, dstt=dstt, bv=bv: e.activation(dstt[:].rearrange("p z n -> p (z n)"), bv[0:64, 0:512], AF.Copy), writes=["ps%d" % bank, dkey])
                kk2 = bcs["kk"][:].unsqueeze(1).to_broadcast([64, 2, 256])
                ka2 = bcs["ka"][:].unsqueeze(1).to_broadcast([64, 2, 256])
                P.op("dve", lambda e: e.tensor_tensor(kx[:], ktm[:], kk2, ALU.mult), reads=["rw_ktm", "rw_bc"], writes=["rw_kx"])
                P.op("act", lambda e: e.activation(kh[:], kx[:], AF.Square), reads=["rw_kx"], writes=["rw_kh"])
                P.op("dve", lambda e: e.tensor_reduce(ss8[:], v3(kh), AX.X, ALU.add), reads=["rw_kh"], writes=["rw_ss8"])
                P.op("act", lambda e: e.activation(ss8[:], ss8[:], AF.Ln, bias=1e-6, scale=1.0), reads=["rw_ss8"], writes=["rw_ss8"])
                P.op("act", lambda e: e.activation(ss8[:], ss8[:], AF.Exp, scale=-0.5), reads=["rw_ss8"], writes=["rw_ss8"])
                P.op("dve", lambda e: e.tensor_tensor(v3(kh), v3(kx), ss8[:].unsqueeze(2).to_broadcast([64, 8, 64]), ALU.mult), reads=["rw_kx", "rw_ss8"], writes=["rw_kh"])
                P.op("dve", lambda e: e.tensor_tensor(ktz[:], av[:], ka2, ALU.mult), reads=["rw_a", "rw_bc"], writes=["rw_ktz"])
                P.op("dve", lambda e: e.tensor_tensor(ktz[:], ktz[:], omka[:].unsqueeze(1).to_broadcast([64, 2, 256]), ALU.add), reads=["rw_ktz", "rw_omka"], writes=["rw_ktz"])
                P.op("dve", lambda e: e.tensor_tensor(ktz[:], ktz[:], ktm[:], ALU.mult), reads=["rw_ktz", "rw_ktm"], writes=["rw_ktz"])
                P.op("dve", lambda e: e.tensor_tensor(al[:], av[:], kh[:], ALU.mult), reads=["rw_a", "rw_kh"], writes=["rw_al"])
                P.op("dve", lambda e: e.scalar_tensor_tensor(al[:], al[:], -1.0, encl[:], ALU.mult, ALU.mult), reads=["rw_al", "rw_encl"], writes=["rw_al"])
                P.op("dve", lambda e: e.tensor_tensor(be[:], kh[:], ecw[:], ALU.mult), reads=["rw_kh", "rw_ecw"], writes=["rw_be"])
                P.op("dve", lambda e: e.tensor_tensor(kti[:], ktz[:], encl[:], ALU.mult), reads=["rw_ktz", "rw_encl"], writes=["rw_kti"])
                P.op("dve", lambda e: e.tensor_tensor(rti[:], rtm[:], ecl[:], ALU.mult), reads=["rw_rtm", "rw_ecl"], writes=["rw_rti"])
                P.op("dve", lambda e: e.tensor_tensor(kx[:], rtm[:], ktz[:], ALU.mult), reads=["rw_rtm", "rw_ktz"], writes=["rw_kx"])
                P.op("dve", lambda e: e.tensor_tensor(kx[:], kx[:], bcs["rk"][:].unsqueeze(1).to_broadcast([64, 2, 256]), ALU.mult), reads=["rw_kx", "rw_bc"], writes=["rw_kx"])
                P.op("dve", lambda e: e.tensor_reduce(bs8[:], v3(kx), AX.X, ALU.add), reads=["rw_kx"], writes=["rw_bs8"])
                for z in range(2):
                    P.op("dve", lambda e, z=z, c=c: e.tensor_tensor(osb[:, z, 256:512].rearrange("p (h j) -> p h j", j=64), vtm[:, z, :].rearrange("p (h j) -> p h j", j=64),
                                                               bs8[:, 4 * z:4 * z + 4].unsqueeze(2).to_broadcast([64, 4, 64]), ALU.mult),
                         reads=["rw_vtm", "rw_bs8"], writes=["rw_osb"])
                for (src, skey, dstT, dkey, bank) in ((al, "rw_al", alT, "rw_rtm", 2), (be, "rw_be", beT, "rw_beT", 3), (kti, "rw_kti", ktT, "rw_ktm", 4), (rti, "rw_rti", rtT, "rw_be", 5)):
                    for u in range(8):
                        z, h = u // 4, u % 4
                        P.op("pe", lambda e, src=src, u=u, z=z, h=h, bank=bank: e.transpose(ps[0:64, bank, 64 * u:64 * u + 64], src[:, z, 64 * h:64 * h + 64], ident64),
                             reads=[skey, "ident"], writes=["ps%d" % bank])
                    P.op("act", lambda e, dstT=dstT, bank=bank: e.activation(dstT[:], ps[0:64, bank, :].rearrange("p (u l) -> p u l", l=64), AF.Copy), writes=["ps%d" % bank, dkey])
                for (lhs, lkey, rhs, rkey, bank, msk, dst, dkey) in ((alT, "rw_rtm", beT, "rw_beT", 2, strict, Pm, "rw_lw"), (ktT, "rw_ktm", beT, "rw_beT", 3, strict, Aak, "rw_ecl"),
                                                                      (alT, "rw_rtm", rtT, "rw_be", 4, incl, Ara, "rw_encl"), (ktT, "rw_ktm", rtT, "rw_be", 5, incl, Ark, "rw_ecw")):
                    for u in range(8):
                        P.op("pe", lambda e, lhs=lhs, rhs=rhs, u=u, bank=bank: e.matmul(ps[0:64, bank, 64 * u:64 * u + 64], lhs[:, u, :], rhs[:, u, :], start=True, stop=True),
                             reads=[lkey, rkey], writes=["ps%d" % bank])
                    P.op("dve", lambda e, bank=bank, msk=msk, dst=dst: e.tensor_tensor(dst[:], ps[0:64, bank, :].rearrange("p (u l) -> p u l", l=64),
                                                                                     msk.unsqueeze(1).to_broadcast([64, 8, 64]), ALU.mult),
                         reads=["consts"], writes=["ps%d" % bank, dkey])
                self.neumann_inverse("rw_", Pm, Qm, Rm, (2, 3, 4), keys=("rw_lw", "rw_a", "rw_rti"))
                for u in range(8):
                    z, h = u // 4, u % 4
                    P.op("pe", lambda e, u=u: e.matmul(ps[0:64, 5, 64 * u:64 * u + 64], beT[:, u, :], M[:, u, :], start=True, stop=False), reads=["rw_beT", "rw_M"], writes=["ps5"])
                    P.op("pe", lambda e, u=u, z=z, h=h: e.matmul(ps[0:64, 5, 64 * u:64 * u + 64], Aak[:, u, :], vtm[:, z, 64 * h:64 * h + 64], start=False, stop=True),
                         reads=["rw_ecl", "rw_vtm"], writes=["ps5"])
                P.op("act", lambda e: e.activation(X1[:], ps[0:64, 5, :].rearrange("p (u e) -> p u e", e=64), AF.Copy), writes=["ps5", "rw_kh"])
                for u in range(8):
                    P.op("pe", lambda e, u=u: e.matmul(ps[0:64, 6, 64 * u:64 * u + 64], Rm[:, u, :], X1[:, u, :], start=True, stop=True), reads=["rw_rti", "rw_kh"], writes=["ps6"])
                P.op("act", lambda e: e.activation(Uu[:], ps[0:64, 6, :].rearrange("p (u e) -> p u e", e=64), AF.Copy), writes=["ps6", "rw_kx"])
                for u in range(8):
                    z, h = u // 4, u % 4
                    vu = vtm[:, z, 64 * h:64 * h + 64]
                    P.op("pe", lambda e, u=u: e.matmul(ps[0:64, 5, 64 * u:64 * u + 64], rtT[:, u, :], M[:, u, :], start=True, stop=False), reads=["rw_be", "rw_M"], writes=["ps5"])
                    P.op("pe", lambda e, u=u: e.matmul(ps[0:64, 5, 64 * u:64 * u + 64], Ara[:, u, :], Uu[:, u, :], start=False, stop=False), reads=["rw_encl", "rw_kx"], writes=["ps5"])
                    P.op("pe", lambda e, u=u, vu=vu: e.matmul(ps[0:64, 5, 64 * u:64 * u + 64], Ark[:, u, :], vu, start=False, stop=True), reads=["rw_ecw", "rw_vtm"], writes=["ps5"])
                for z in range(2):
                    P.op("act", lambda e, z=z, c=c: e.activation(osb[:, z, 0:256], ps[0:64, 5, 256 * z:256 * z + 256], AF.Copy), writes=["ps5", "rw_osb"])
                for u in range(8):
                    z, h = u // 4, u % 4
                    vu = vtm[:, z, 64 * h:64 * h + 64]
                    P.op("pe", lambda e, u=u, z=z, h=h: e.matmul(ps[0:64, 6, 64 * u:64 * u + 64], al[:, z, 64 * h:64 * h + 64], Uu[:, u, :], start=True, stop=False), reads=["rw_al", "rw_kx"], writes=["ps6"])
                    P.op("pe", lambda e, u=u, z=z, h=h, vu=vu: e.matmul(ps[0:64, 6, 64 * u:64 * u + 64], kti[:, z, 64 * h:64 * h + 64], vu, start=False, stop=True), reads=["rw_kti", "rw_vtm"], writes=["ps6"])
                P.op("dve", lambda e: e.tensor_tensor(tmpM[:], ps[0:64, 6, :].rearrange("p (u e) -> p u e", e=64), M[:], ALU.add), reads=["rw_M"], writes=["ps6", "rw_ktz"])
                P.op("dve", lambda e: e.tensor_tensor(M[:], tmpM[:], WL[:].unsqueeze(2).to_broadcast([64, 8, 64]), ALU.mult), reads=["rw_ktz", "rw_WL"], writes=["rw_M"])
                self.emit_out_tile2(t, c, osb, ["rw_osb"], bacc)
            if t % 2 == 1:
                for u in range(8):
                    P.op("pe", lambda e, u=u: e.transpose(ps[0:64, 2, 64 * u:64 * u + 64], M[:, u, :], ident64), reads=["rw_M", "ident"], writes=["ps2"])
                P.op("dve", lambda e: e.tensor_copy(Mt[:], ps[0:64, 2, :].rearrange("p (u e) -> p u e", e=64)), writes=["ps2", "rw_kh"])
                for z in range(2):
                    seg = (t - 1) // 2 if z == 0 else (NT - 1 - t) // 2
                    P.dma("sp", lambda e, z=z, seg=seg: e.dma_start(out=self.out_rwkv[seg, l, z].rearrange("h i j -> i h j"), in_=Mt[:, 4 * z:4 * z + 4, :]), reads=["rw_kh"])
                P.op("dve", lambda e: e.tensor_scalar(M[:], M[:], self.keep[0:64, 0:1], None, ALU.mult), reads=["rw_M", "keep"], writes=["rw_M"])
        if self.debug.get("yacc") == "rwkv":
            P.dma("sp", lambda e: e.dma_start(out=self.dbg_yacc, in_=self.yacc[:]), reads=[("yacc", i) for i in range(NT)])
        if self.debug.get("yacc") == "rwkv_bonus":
            P.dma("sp", lambda e: e.dma_start(out=self.dbg_yacc, in_=bacc[:]), reads=[("bacc", i) for i in range(NT)])
        self.release(m1)
        if self.debug.get("post", True):
            self.post_rwkv(l, wb, wkeys, bacc, mu, first)
        self.release(m0)

    def emit_out_tile2(self, t, c, osb, osb_keys, bacc):
        P, ps = self.P, self.psum
        sel = self.cst("SEL").rearrange("p (q n) -> p q n", n=128)
        for z in range(2):
            bank = z
            pk = "ps%d" % bank
            tile = t if z == 0 else NT - 1 - t
            P.op("pe", lambda e, z=z, bank=bank: e.matmul(ps[:, bank, :], sel[:, 2 * z + c, :], osb[:, z, :], start=(c == 0), stop=(c == 1)),
                 reads=["consts"] + osb_keys, writes=[pk])
            if c == 0:
                continue
            yk = ("yacc", tile)
            bk = ("bacc", tile)
            if t < NT // 2:
                P.op("act", lambda e, tile=tile, bank=bank: e.activation(self.yacc[:, tile, :], ps[:, bank, 0:256], AF.Copy), writes=[pk, yk])
                P.op("act", lambda e, tile=tile, bank=bank: e.activation(bacc[:, tile, :], ps[:, bank, 256:512], AF.Copy), writes=[pk, bk])
            else:
                P.op("dve", lambda e, tile=tile, bank=bank: e.tensor_tensor(self.yacc[:, tile, :], ps[:, bank, 0:256], self.yacc[:, tile, :], ALU.add), writes=[pk, yk])
                P.op("dve", lambda e, tile=tile, bank=bank: e.tensor_tensor(bacc[:, tile, :], ps[:, bank, 256:512], bacc[:, tile, :], ALU.add), writes=[pk, bk])

    def post_rwkv(self, l, wb, wkeys, bacc, mu, first):
        P, ps = self.P, self.psum
        m0 = self.mark()
        wo, wokeys = self.load_w("wo_rw", self.w_out[l][768:1024, :], D, kchunks=2)
        gain_bc = self.sb("hn_gain", [128, 256], F32)
        P.dma("sp", lambda e: e.dma_start(out=gain_bc[:], in_=self.rw_norm[l:l + 1, :].partition_broadcast(128)), writes=["hn_gain"])
        g2 = self.sb("rw_g2", [128, 256], F32)
        P.dma("sp", lambda e: e.dma_start(out=g2[:], in_=self.rw_g2[l]), writes=["rw_g2"])
        tmp = (self.sb("hn_yc", [128, 256], F32), self.sb("hn_sq", [128, 256], F32), self.sb("hn_st", [128, 2, 4], F32),
               self.sb("yact", [128, 256], F32))
        gate = self.sb("hn_gate", [128, 256], F32)
        otmp = (self.sb("yTm", [128, 2, 128], BF16), self.sb("gtmp", [128, D], F32))
        pre1 = self.sb("rwp_pre", [128, 1, 130], F32)
        sg = self.sb("rwp_sg", [128, 1, 128], F32)
        for i in range(NT):
            tok0 = 2 + 128 * i - 1
            for k in range(8):
                P.op("pe", lambda e, k=k, tok0=tok0: e.matmul(ps[:, 0, 0:130], wb[:, k, 1024:1152], self.uT[:, k, tok0:tok0 + 130], start=(k == 0), stop=(k == 7)),
                     reads=wkeys + [("uT", i)] + ([("uT", i - 1)] if i > 0 else []) + ([("uT", i + 1)] if i < NT - 1 else []), writes=["ps0"])
            P.op("act", lambda e: e.activation(pre1[:, 0, :], ps[:, 0, 0:130], AF.Copy), writes=["ps0", "rwp_pre"])
            edge = slice(0, 1) if i % 2 == 0 else slice(129, 130)
            P.op("dve", lambda e, edge=edge: e.tensor_scalar(pre1[:, :, edge], pre1[:, :, edge], self.keep[:, 0:1], None, ALU.mult), reads=["rwp_pre", "keep"], writes=["rwp_pre"])
            self.rwkv_mix_fm(pre1, mu[:, 8:9], sg[:], ["rwp_pre"], ["rwp_sg"])
            P.op("act", lambda e: e.activation(sg[:], sg[:], AF.Sigmoid), reads=["rwp_sg"], writes=["rwp_sg"])
            P.op("pe", lambda e: e.matmul(ps[:, 1, 0:256], sg[:, 0, :], g2[:], start=True, stop=True), reads=["rwp_sg", "rw_g2"], writes=["ps1"])
            P.op("act", lambda e: e.activation(gate[:], ps[:, 1, 0:256], AF.Copy), writes=["ps1", "hn_gate"])
            self.head_norm_tile(i, True, gain_bc, gate[:], ["hn_gate"], tmp, extra=(bacc[:, i, :], [("bacc", i)]))
            self.out_proj_tile(i, tmp[3], wo, wokeys, first, otmp)
        self.release(m0)

    def head_norm_tile(self, i, center, gain_bc, gate_ap, gate_keys, tmp, extra=None):
        P = self.P
        yc, sq, st4, yact = tmp
        y3 = self.yacc[:, i, :].rearrange("p (h e) -> p h e", e=64)
        yk = ("yacc", i)
        yc3 = yc[:].rearrange("p (h e) -> p h e", e=64)
        if center:
            P.op("dve", lambda e: e.tensor_reduce(st4[:, 0, :], y3, AX.X, ALU.add), reads=[yk], writes=["hn_st"])
            P.op("dve", lambda e: e.tensor_scalar(st4[:, 0, :], st4[:, 0, :], -1.0 / 64, None, ALU.mult), reads=["hn_st"], writes=["hn_st"])
            P.op("dve", lambda e: e.tensor_tensor(yc3, y3, st4[:, 0, :].unsqueeze(2).to_broadcast([128, 4, 64]), ALU.add),
                 reads=[yk, "hn_st"], writes=["hn_yc"])
        else:
            P.op("dve", lambda e: e.tensor_copy(yc[:], self.yacc[:, i, :]), reads=[yk], writes=["hn_yc"])
        P.op("act", lambda e: e.activation(sq[:], yc[:], AF.Square), reads=["hn_yc"], writes=["hn_sq"])
        P.op("dve", lambda e: e.tensor_reduce(st4[:, 1, :], sq[:].rearrange("p (h e) -> p h e", e=64), AX.X, ALU.add), reads=["hn_sq"], writes=["hn_st1"])
        P.op("act", lambda e: e.activation(st4[:, 1, :], st4[:, 1, :], AF.Ln, bias=LN_EPS, scale=1.0 / 64), reads=["hn_st1"], writes=["hn_st1"])
        P.op("act", lambda e: e.activation(st4[:, 1, :], st4[:, 1, :], AF.Exp, scale=-0.5), reads=["hn_st1"], writes=["hn_st1"])
        P.op("dve", lambda e: e.tensor_tensor(yc3, yc3, st4[:, 1, :].unsqueeze(2).to_broadcast([128, 4, 64]), ALU.mult),
             reads=["hn_yc", "hn_st1"], writes=["hn_yc"])
        P.op("dve", lambda e: e.tensor_tensor(yc[:], yc[:], gain_bc[:], ALU.mult), reads=["hn_yc", "hn_gain"], writes=["hn_yc"])
        if extra is not None:
            P.op("dve", lambda e: e.tensor_tensor(yc[:], yc[:], extra[0], ALU.add), reads=["hn_yc"] + extra[1], writes=["hn_yc"])
        P.op("dve", lambda e: e.tensor_tensor(yact[:], yc[:], gate_ap, ALU.mult), reads=["hn_yc"] + gate_keys, writes=["yact"])

    def out_proj_tile(self, i, yact, wo, wokeys, first, tmp):
        P, ps = self.P, self.psum
        yTm, gtmp = tmp
        for kk in range(2):
            P.op("pe", lambda e, kk=kk: e.transpose(ps[:, 2, 128 * kk:128 * kk + 128], yact[:, 128 * kk:128 * kk + 128], self.ident[:]),
                 reads=["yact", "ident"], writes=["ps2"])
        P.op("act", lambda e: e.activation(yTm[:], ps[:, 2, 0:256].rearrange("p (k n) -> p k n", n=128), AF.Copy), writes=["ps2", "yTm"])
        for n in range(2):
            for kk in range(2):
                P.op("pe", lambda e, n=n, kk=kk: e.matmul(ps[:, 4 + n, :], yTm[:, kk, :], wo[:, kk, 512 * n:512 * n + 512],
                                                           start=(kk == 0), stop=(kk == 1)),
                     reads=["yTm"] + wokeys, writes=["ps%d" % (4 + n)])
        xk = ("xres", i)
        for n in range(2):
            cols = slice(512 * n, 512 * n + 512)
            P.op("dve", lambda e, n=n, cols=cols: e.tensor_tensor(gtmp[:, cols], ps[:, 4 + n, :], self.g_bc["g1"][:, cols], ALU.mult),
                 reads=["g1_bc"], writes=["ps%d" % (4 + n), "gtmp"])
        P.op("dve", lambda e: e.scalar_tensor_tensor(self.xres[:, i, :], self.xres[:, i, :], ALPHA if first else 1.0, gtmp[:], ALU.mult, ALU.add),
             reads=["gtmp"], writes=[xk])

    def post_simple(self, l, name, wb, wkeys, gcol0, gate_func, norm_dram, center, wo_row0, first):
        P, ps = self.P, self.psum
        m0 = self.mark()
        wo, wokeys = self.load_w("wo_" + name, self.w_out[l][wo_row0:wo_row0 + 256, :], D, kchunks=2)
        gain_bc = self.sb("hn_gain", [128, 256], F32)
        P.dma("sp", lambda e: e.dma_start(out=gain_bc[:], in_=norm_dram[l:l + 1, :].partition_broadcast(128)), writes=["hn_gain"])
        tmp = (self.sb("hn_yc", [128, 256], F32), self.sb("hn_sq", [128, 256], F32), self.sb("hn_st", [128, 2, 4], F32),
               self.sb("yact", [128, 256], F32))
        gate = self.sb("hn_gate", [128, 256], F32)
        otmp = (self.sb("yTm", [128, 2, 128], BF16), self.sb("gtmp", [128, D], F32))
        for i in range(NT):
            for k in range(8):
                P.op("pe", lambda e, k=k, i=i: e.matmul(ps[:, 0, 0:256], self.uT[:, k, 2 + 128 * i:2 + 128 * i + 128], wb[:, k, gcol0:gcol0 + 256],
                                                         start=(k == 0), stop=(k == 7)),
                     reads=wkeys + [("uT", i)], writes=["ps0"])
            P.op("act", lambda e: e.activation(gate[:], ps[:, 0, 0:256], gate_func), writes=["ps0", "hn_gate"])
            self.head_norm_tile(i, center, gain_bc, gate[:], ["hn_gate"], tmp)
            self.out_proj_tile(i, tmp[3], wo, wokeys, first, otmp)
        self.release(m0)

    def ln_affine_tile(self, i, g_bc, b_bc):
        P = self.P
        stats, mv, rstd, xhat = self.lntmp
        xk = ("xres", i)
        src = self.xres[:, i, :]
        for hh in range(2):
            P.op("dve", lambda e, hh=hh: e.bn_stats(stats[:, hh, :], src[:, 512 * hh:512 * hh + 512]), reads=[xk], writes=["lnstats"])
        P.op("dve", lambda e: e.bn_aggr(mv[:], stats[:]), reads=["lnstats"], writes=["lnmv"])
        P.op("act", lambda e: e.activation(rstd[:], mv[:, 1:2], AF.Ln, bias=LN_EPS, scale=1.0), reads=["lnmv"], writes=["lnrstd"])
        P.op("act", lambda e: e.activation(rstd[:], rstd[:], AF.Exp, scale=-0.5), reads=["lnrstd"], writes=["lnrstd"])
        P.op("dve", lambda e: e.tensor_scalar(xhat[:], src, mv[:, 0:1], rstd[:, 0:1], ALU.subtract, ALU.mult),
             reads=[xk, "lnmv", "lnrstd"], writes=["xhat"])
        P.op("pool", lambda e: e.tensor_tensor(xhat[:], xhat[:], g_bc[:], ALU.mult), reads=["xhat", "lnbc"], writes=["xhat"])
        P.op("pool", lambda e: e.tensor_tensor(src, xhat[:], b_bc[:], ALU.add), reads=["xhat", "lnbc"], writes=[xk])

    def phase_c(self, l):
        P, ps = self.P, self.psum
        m0 = self.mark()
        self.lntmp = (self.sb("lnstats", [128, 2, 6], F32), self.sb("lnmv", [128, 2], F32),
                      self.sb("lnrstd", [128, 1], F32), self.sb("xhat", [128, D], F32))
        bc = {}
        for nm, src in (("ln1_g", self.ln1_g), ("ln1_b", self.ln1_b), ("ln2_g", self.ln2_g), ("ln2_b", self.ln2_b)):
            bc[nm] = self.sb("bc_" + nm, [128, D], F32)
            P.dma("sp", lambda e, nm=nm, src=src: e.dma_start(out=bc[nm][:], in_=src[l:l + 1, :].partition_broadcast(128)), writes=["lnbc"])
        u2T = self.sb("u2T", [128, 8, TOK], BF16)
        hT = [self.sb("hT%d" % i, [128, 4, 512], BF16) for i in range(2)]
        rtmp = [self.sb("ffn_rtmp%d" % i, [128, 512], F32) for i in range(2)]
        gtmp = [self.sb("ffn_gtmp%d" % i, [128, 512], F32) for i in range(2)]
        w1b = [self.sb("w1b%d" % i, [128, 8, 512], BF16) for i in range(2)]
        w2b = [self.sb("w2b%d" % i, [128, 4, 1024], BF16) for i in range(2)]
        w1v = self.w_ff1[l].rearrange("(k p) n -> p k n", p=128)
        w2v = self.w_ff2[l].rearrange("(c p) n -> p c n", p=128)
        for i in range(NT):
            self.ln_affine_tile(i, bc["ln1_g"], bc["ln1_b"])
            self.ln_mod_T(i, lambda k, i=i: u2T[:, k, 128 * i:128 * i + 128], self.sc2p, 24, self.lntmp, [("u2T", i)])
        nh = 0
        ng_ = 0
        for sl in range(8):
            wa = w1b[sl % 2]
            wbk = w2b[sl % 2]
            ka = "w1b%d" % (sl % 2)
            kb = "w2b%d" % (sl % 2)
            for kh in range(2):
                P.dma("pool", lambda e, wa=wa, sl=sl, kh=kh: e.dma_start(out=wa[:, 4 * kh:4 * kh + 4, :], in_=w1v[:, 4 * kh:4 * kh + 4, 512 * sl:512 * sl + 512]), writes=[(ka, kh)])
            for kh in range(2):
                P.dma("pool", lambda e, wbk=wbk, sl=sl, kh=kh: e.dma_start(out=wbk[:, 2 * kh:2 * kh + 2, :], in_=w2v[:, 4 * sl + 2 * kh:4 * sl + 2 * kh + 2, :]), writes=[(kb, kh)])
            wakeys = [(ka, 0), (ka, 1)]
            wbkeys = [(kb, 0), (kb, 1)]
            for blk in range(4):
                hb = hT[nh % 2]
                hk = "hT%d" % (nh % 2)
                nh += 1
                u2keys = [("u2T", i) for i in range(4 * blk, 4 * blk + 4)]
                for cc in range(4):
                    bank = cc % 2
                    for k in range(8):
                        P.op("pe", lambda e, wa=wa, cc=cc, k=k, bank=bank, blk=blk: e.matmul(
                            ps[:, bank, :], wa[:, k, 128 * cc:128 * cc + 128], u2T[:, k, 512 * blk:512 * blk + 512], start=(k == 0), stop=(k == 7)),
                            reads=wakeys + u2keys, writes=["ps%d" % bank])
                    rt = rtmp[cc % 2]
                    rk = "ffn_rtmp%d" % (cc % 2)
                    P.op("act", lambda e, rt=rt, bank=bank: e.activation(rt[:], ps[:, bank, :], AF.Relu), writes=["ps%d" % bank, rk])
                    P.op("pool", lambda e, rt=rt, cc=cc, hb=hb: e.tensor_tensor(hb[:, cc, :], rt[:], rt[:], ALU.mult), reads=[rk], writes=[(hk, cc)])
                hkeys = [(hk, cc) for cc in range(4)]
                for j in range(4):
                    i = 4 * blk + j
                    for n in range(2):
                        bank = 2 + (ng_ % 4)
                        gt = gtmp[ng_ % 2]
                        gk = "ffn_gtmp%d" % (ng_ % 2)
                        ng_ += 1
                        cols = slice(512 * n, 512 * n + 512)
                        for hc in range(4):
                            P.op("pe", lambda e, wbk=wbk, hc=hc, j=j, bank=bank, hb=hb, cols=cols: e.matmul(
                                ps[:, bank, :], hb[:, hc, 128 * j:128 * j + 128], wbk[:, hc, cols], start=(hc == 0), stop=(hc == 3)),
                                reads=wbkeys + hkeys, writes=["ps%d" % bank])
                        P.op("dve", lambda e, bank=bank, cols=cols, gt=gt: e.tensor_tensor(gt[:], ps[:, bank, :], self.g_bc["g2"][:, cols], ALU.mult),
                             reads=["g2_bc"], writes=["ps%d" % bank, gk])
                        P.op("dve", lambda e, i=i, cols=cols, gt=gt, sl=sl: e.scalar_tensor_tensor(
                            self.xres[:, i, cols], self.xres[:, i, cols], ALPHA if sl == 0 else 1.0, gt[:], ALU.mult, ALU.add),
                            reads=[gk], writes=[("xres", i)])
        for i in range(NT):
            self.ln_affine_tile(i, bc["ln2_g"], bc["ln2_b"])
        self.release(m0)

    def alloc_lntmp(self):
        self.lntmp = (self.sb("lnstats", [128, 2, 6], F32), self.sb("lnmv", [128, 2], F32),
                      self.sb("lnrstd", [128, 1], F32), self.sb("xhat", [128, D], F32))

    def layer(self, l):
        P = self.P
        self.compute_mod(l)
        mL = self.mark()
        self.uT = self.sb("uT", [128, 8, TOK + 4], BF16)
        self.yacc = self.sb("yacc", [128, NT, 256], F32)
        P.op("dve", lambda e: e.memset(self.uT[:, :, 0:2], 0.0), writes=["uTpadL"])
        P.op("dve", lambda e: e.memset(self.uT[:, :, TOK + 2:TOK + 4], 0.0), writes=["uTpadR"])
        mA = self.mark()
        self.alloc_lntmp()
        for i in range(NT):
            self.ln_mod_T(i, lambda k, i=i: self.uT[:, k, 2 + 128 * i:2 + 128 * i + 128], self.sc1p, 0, self.lntmp, [("uT", i)])
        self.release(mA)
        if "uT" in self.debug and l == self.debug["uT"]:
            mm_ = self.mark()
            dbgf = self.sb("dbgf", [128, 8, 512], F32)
            for q in range(4):
                P.op("dve", lambda e, q=q: e.tensor_copy(dbgf[:], self.uT[:, :, 2 + 512 * q:2 + 512 * q + 512]),
                     reads=[("uT", i) for i in range(4 * q, 4 * q + 4)], writes=["dbgf"])
                P.dma("sp", lambda e, q=q: e.dma_start(out=self.dbg_uT[:, :, 512 * q:512 * q + 512], in_=dbgf[:]), reads=["dbgf"])
            self.release(mm_)
        mixers = self.debug.get("mixers", ["mlstm", "delta", "ret", "rwkv"])
        first = True
        for mx in mixers:
            getattr(self, "mixer_" + mx)(l, first)
            first = False
        self.release(mL)
        if self.debug.get("phase_c", True):
            self.phase_c(l)

    def finish(self):
        P = self.P
        yv = self.y_out.rearrange("(i p) d -> p i d", p=128)
        for q in range(4):
            P.dma("sp", lambda e, q=q: e.dma_start(out=yv[:, 4 * q:4 * q + 4, :], in_=self.xres[:, 4 * q:4 * q + 4, :]),
                  reads=[("xres", i) for i in range(4 * q, 4 * q + 4)])


PROMPT_ASSIGN = [[0, 1, 2], [3, 4, 5], [6, 7, 8], [9, 10, 11], [12, 13], [14, 15]]

OFF_A, OFF_B, OFF_C, OFF_D = 0, 1040, 2080, 3104


def rope_tables(is_sample):
    tab = np.zeros((TOK, 64, 2), np.float32)
    tab[:, :, 0] = 1.0
    if is_sample:
        n = np.arange(TOK)
        posv = (n // 64, n % 64)
        inv = 10000.0 ** (-np.arange(16, dtype=np.float32) / 16)
        for half in range(2):
            ang = posv[half].astype(np.float32)[:, None] * inv[None, :]
            cos, sin = np.cos(ang), np.sin(ang)
            base = 32 * half
            tab[:, base:base + 16, 0] = cos
            tab[:, base + 16:base + 32, 0] = cos
            tab[:, base:base + 16, 1] = -sin
            tab[:, base + 16:base + 32, 1] = sin
    t = tab.reshape(NT, 128, 64, 2).transpose(0, 2, 3, 1)
    t = np.concatenate([t, t], axis=1)
    return np.ascontiguousarray(t.astype(np.float32))


def swap_perm():
    idx = []
    for h in range(4):
        for half in range(2):
            b = 64 * h + 32 * half
            idx += list(range(b + 16, b + 32)) + list(range(b, b + 16))
    return np.array(idx)


def make_in_maps(inp, kern):
    f32 = np.float32
    g = lambda k: np.asarray(inp[k], f32)
    maps = []
    ident = np.eye(128, dtype=f32)
    consts = build_consts()
    b_mod = np.ascontiguousarray(g("b_mod").reshape(DEPTH, 48, 128))
    w_mod = np.ascontiguousarray(g("w_mod"))
    w_in = g("w_in")
    sw = swap_perm()
    cq = w_in[:, :, OFF_C:OFF_C + 256]
    ck = w_in[:, :, OFF_C + 256:OFF_C + 512]
    cv = w_in[:, :, OFF_C + 512:OFF_C + 768]
    cg = w_in[:, :, OFF_C + 768:OFF_C + 1024]
    aq = w_in[:, :, OFF_A:OFF_A + 256]
    ak = w_in[:, :, OFF_A + 256:OFF_A + 512]
    av = w_in[:, :, OFF_A + 512:OFF_A + 768]
    ao = w_in[:, :, OFF_A + 768:OFF_A + 1024]
    ai = w_in[:, :, OFF_A + 1024:OFF_A + 1032]
    af = w_in[:, :, OFF_A + 1032:OFF_A + 1040]
    agate = np.concatenate([ai[:, :, 0:4], af[:, :, 0:4], ai[:, :, 4:8], af[:, :, 4:8]], axis=2)
    bqkv = w_in[:, :, OFF_B:OFF_B + 768]
    bz = w_in[:, :, OFF_B + 768:OFF_B + 1024]
    bbeta = w_in[:, :, OFF_B + 1024:OFF_B + 1032]
    balpha = w_in[:, :, OFF_B + 1032:OFF_B + 1040]
    bgate = np.concatenate([bbeta[:, :, 0:4], balpha[:, :, 0:4], bbeta[:, :, 4:8], balpha[:, :, 4:8]], axis=2)
    dconv = g("delta_conv")
    dconv = np.ascontiguousarray(dconv.reshape(DEPTH, 5, 6, 128).transpose(0, 3, 2, 1))
    w2 = g("rwkv_w2")
    a2 = g("rwkv_a2")
    w2p = np.zeros((DEPTH, 128, 2, 256), f32)
    a2p = np.zeros((DEPTH, 128, 2, 256), f32)
    for z_ in range(2):
        w2p[:, 64 * z_:64 * z_ + 64, z_, :] = w2[:, z_]
        a2p[:, 64 * z_:64 * z_ + 64, z_, :] = a2[:, z_]
    shared = {
        "wD": np.ascontiguousarray(w_in[:, :, OFF_D:OFF_D + 1152]),
        "rw_mu": np.ascontiguousarray(g("rwkv_mu").reshape(DEPTH, 9, 128).transpose(0, 2, 1)),
        "rw_w2p": w2p, "rw_a2p": a2p,
        "rw_w0": np.ascontiguousarray(g("rwkv_w0").reshape(DEPTH, 512)),
        "rw_a0": np.ascontiguousarray(g("rwkv_a0").reshape(DEPTH, 512)),
        "rw_kk": g("rwkv_kk"), "rw_ka": g("rwkv_ka"),
        "rw_rk": np.ascontiguousarray(g("rwkv_rk").reshape(DEPTH, 256)),
        "rw_norm": g("rwkv_norm"), "rw_g2": np.ascontiguousarray(g("rwkv_g2")),
        "wB": np.ascontiguousarray(np.concatenate([bqkv, bgate, bz], axis=2)),
        "dl_conv": dconv,
        "dl_alog": np.ascontiguousarray(g("delta_a_log").reshape(DEPTH, 8)),
        "dl_dtb": np.ascontiguousarray(g("delta_dt_bias").reshape(DEPTH, 8)),
        "dl_norm": np.ascontiguousarray(g("delta_norm")),
        "wA": np.ascontiguousarray(np.concatenate([aq, ak, av, agate, ao], axis=2)),
        "ml_ib": np.ascontiguousarray(g("mlstm_i_bias").reshape(DEPTH, 8)),
        "ml_fb": np.ascontiguousarray(g("mlstm_f_bias").reshape(DEPTH, 8)),
        "ml_norm": np.ascontiguousarray(g("mlstm_norm")),
        "ident": ident, "consts": consts, "w_mod": w_mod, "b_mod": b_mod,
        "w_out": np.ascontiguousarray(g("w_out")),
        "wC": np.ascontiguousarray(np.concatenate([cq, ck, cv, cq[:, :, sw], ck[:, :, sw], cg], axis=2)),
        "ret_decay": np.ascontiguousarray(g("ret_decay").reshape(DEPTH, 8)),
        "ret_norm": np.ascontiguousarray(g("ret_norm")),
        "ln1_g": g("ln1_g"), "ln1_b": g("ln1_b"), "ln2_g": g("ln2_g"), "ln2_b": g("ln2_b"),
        "w_ff1": np.ascontiguousarray(g("w_ff1")), "w_ff2": np.ascontiguousarray(g("w_ff2")),
    }
    ropes = {True: rope_tables(True), False: rope_tables(False)}
    zeros_mat = np.zeros((DEPTH, 2, H, HD, HD), f32)
    for c in range(N_CORES):
        m = dict(shared)
        if c < 2:
            x = g("x_sample")[c]
            cond = g("c")[c]
            m["init_ret"] = np.ascontiguousarray(g("state_ret")[c])
            m["init_mC"] = np.ascontiguousarray(g("state_mlstm_C")[c])
            m["init_delta"] = np.ascontiguousarray(g("state_delta")[c])
            m["init_rwkv"] = np.ascontiguousarray(g("state_rwkv")[c])
            m["init_mn"] = np.ascontiguousarray(g("state_mlstm_n")[c])
            m["init_mm"] = np.ascontiguousarray(g("state_mlstm_m")[c])
        else:
            x = np.zeros((TOK, D), f32)
            mine = PROMPT_ASSIGN[c - 2]
            for s_ in range(8):
                x[256 * s_:256 * s_ + 256] = inp["x_prompt"][mine[s_ % len(mine)]]
            cond = g("c_ctx")
            m["init_ret"] = zeros_mat
            m["init_mC"] = zeros_mat
            m["init_delta"] = zeros_mat
            m["init_rwkv"] = zeros_mat
            m["init_mn"] = np.zeros((DEPTH, 2, H, HD), f32)
            m["init_mm"] = np.zeros((DEPTH, 2, H), f32)
        m["x"] = np.ascontiguousarray(x)
        m["cond"] = np.ascontiguousarray(cond.reshape(8, 128))
        m["keep"] = np.full((1, 1), 1.0 if c < 2 else 0.0, f32)
        m["rope"] = ropes[c < 2]
        missing = [k for k in kern.ins if k not in m]
        assert not missing, missing
        maps.append({k: m[k] for k in kern.ins})
    return maps


def run(inp, debug=None, trace=False):
    kern = K(debug)
    nc = kern.build()
    maps = make_in_maps(inp, kern)
    res = run_bass_kernel_spmd(nc, maps, core_ids=list(range(N_CORES)), trace=trace)
    return kern, res


def gather_states(r, name, shape_tail):
    out = np.zeros((16, DEPTH, 2) + shape_tail, np.float32)
    for c in range(2, N_CORES):
        for s_, b in enumerate(PROMPT_ASSIGN[c - 2]):
            out[b] = r[c][name][s_]
    return out


def kernel(**inp):
    kern, res = run(inp)
    r = res.results
    BATCH, SEQ = 16, 256
    y_prompt = np.zeros((BATCH, SEQ, D), np.float32)
    y_sample = np.zeros((2, TOK, D), np.float32)
    for c in range(2):
        y_sample[c] = r[c]["y"]
    for c in range(2, N_CORES):
        for s_, b in enumerate(PROMPT_ASSIGN[c - 2]):
            y_prompt[b] = r[c]["y"][256 * s_:256 * s_ + 256]
    new_ret = gather_states(r, "out_ret", (H, HD, HD))
    new_mC = gather_states(r, "out_mC", (H, HD, HD))
    new_mn = gather_states(r, "out_mn", (H, HD))
    new_mm = gather_states(r, "out_mm", (H,))
    new_delta = gather_states(r, "out_delta", (H, HD, HD)) if "out_delta" in r[0] else np.zeros_like(new_ret)
    new_rwkv = gather_states(r, "out_rwkv", (H, HD, HD)) if "out_rwkv" in r[0] else np.zeros_like(new_ret)
    return (y_prompt, y_sample, new_mC, new_mn, new_mm, new_delta, new_ret, new_rwkv)
```

```python
import contextlib
import numpy as np
import concourse.bass as bass
import concourse.mybir as mybir
from concourse.bass_utils import run_bass_kernel_spmd

F32 = mybir.dt.float32
BF16 = mybir.dt.bfloat16
AF = mybir.ActivationFunctionType
ALU = mybir.AluOpType
AX = mybir.AxisListType

D = 1024
NT = 16
TOK = 2048
DEPTH = 2
H = 4
HD = 64
ALPHA = (2 * DEPTH) ** 0.25
LN_EPS = 1e-5
N_CORES = 8

ENGS = ("pe", "act", "dve", "pool", "sp")
NSLOT = 6

CO = {}
_off = 0
for _n, _w in (("INCL", 64), ("STRICT", 64), ("ONES", 128), ("SEL", 512), ("PIDX", 1), ("NPIDX", 1), ("BLK", 128), ("PP1", 1)):
    CO[_n] = (_off, _w)
    _off += _w
NCONST = _off
ARENA_WORDS = 53200


def build_consts():
    c = np.zeros((128, NCONST), np.float32)
    s = np.arange(64)
    incl = (s[:, None] <= s[None, :]).astype(np.float32)
    strict = (s[:, None] < s[None, :]).astype(np.float32)
    for half in range(2):
        c[64 * half:64 * half + 64, CO["INCL"][0]:CO["INCL"][0] + 64] = incl
        c[64 * half:64 * half + 64, CO["STRICT"][0]:CO["STRICT"][0] + 64] = strict
    c[:, CO["ONES"][0]:CO["ONES"][0] + 128] = 1.0
    sel = np.zeros((64, 4, 128), np.float32)
    for z in range(2):
        for ch in range(2):
            for lp in range(64):
                n = 64 * ch + lp if z == 0 else 127 - 64 * ch - lp
                sel[lp, 2 * z + ch, n] = 1.0
    c[0:64, CO["SEL"][0]:CO["SEL"][0] + 512] = sel.reshape(64, 512)
    p = np.arange(128) % 64
    c[:, CO["PIDX"][0]] = p
    c[:, CO["NPIDX"][0]] = -p
    c[:, CO["PP1"][0]] = p + 1
    blk = np.zeros((128, 128), np.float32)
    blk[0:64, 0:64] = 1.0
    blk[64:128, 64:128] = 1.0
    c[:, CO["BLK"][0]:CO["BLK"][0] + 128] = blk
    return c


class Op:
    __slots__ = ("eng", "fn", "reads", "writes", "chan", "val", "is_dma", "idx", "signal")

    def __init__(self, eng, fn, reads, writes, is_dma):
        self.eng = eng
        self.fn = fn
        self.reads = reads
        self.writes = writes
        self.is_dma = is_dma
        self.chan = None
        self.val = 0
        self.signal = False


class _Rec:
    def __getattr__(self, name):
        def f(*a, **kw):
            self.call = (name, a, kw)
            return self
        return f


class Prog:
    def __init__(self):
        self.ops = []

    def op(self, eng, fn, reads=(), writes=()):
        r = _Rec()
        fn(r)
        o = Op(eng, r.call, tuple(reads), tuple(writes), False)
        self.ops.append(o)
        return o

    def dma(self, eng, fn, reads=(), writes=()):
        r = _Rec()
        fn(r)
        o = Op(eng, r.call, tuple(reads), tuple(writes), True)
        self.ops.append(o)
        return o

    def barrier(self):
        self.ops.append(None)

    def emit(self, nc, stack):
        raw = self.ops
        ops = []
        barrier_at = set()
        for o in raw:
            if o is None:
                barrier_at.add(len(ops))
            else:
                ops.append(o)
        dma_n = {e: 0 for e in ENGS}
        slot_prev = {}
        for i, o in enumerate(ops):
            o.idx = i
            if o.is_dma:
                j = dma_n[o.eng] % NSLOT
                dma_n[o.eng] += 1
                o.chan = ("dma", o.eng, j)
            else:
                o.chan = o.eng
        lastw = {}
        readers = {}
        deps = []
        last_chan = {}
        bar_deps = set()
        for o in ops:
            if o.idx in barrier_at:
                bar_deps = set(last_chan.values())
            last_chan[o.chan] = o.idx
            d = set(bar_deps)
            for k in o.reads:
                w = lastw.get(k)
                if w is not None:
                    d.add(w)
            for k in o.writes:
                w = lastw.get(k)
                if w is not None:
                    d.add(w)
                for r in readers.get(k, ()):
                    d.add(r)
            if o.is_dma:
                p = slot_prev.get(o.chan)
                if p is not None:
                    d.add(p)
                slot_prev[o.chan] = o.idx
            d.discard(o.idx)
            deps.append(d)
            for k in o.reads:
                readers.setdefault(k, []).append(o.idx)
            for k in o.writes:
                lastw[k] = o.idx
                readers[k] = []
        pos = {}
        cnt = {}
        for o in ops:
            c = o.chan
            cnt[c] = cnt.get(c, 0) + 1
            pos[o.idx] = cnt[c]
        know_stream = {e: {} for e in ENGS}
        know_op = [None] * len(ops)
        needed = [None] * len(ops)
        for o in ops:
            ks = know_stream[o.eng]
            need = []
            for p in sorted(deps[o.idx], reverse=True):
                po = ops[p]
                if po.eng == "pe" and o.eng == "pe" and not po.is_dma and not o.is_dma:
                    continue
                if ks.get(po.chan, 0) >= pos[p]:
                    continue
                need.append(p)
                for c, v in know_op[p].items():
                    if ks.get(c, 0) < v:
                        ks[c] = v
                if ks.get(po.chan, 0) < pos[p]:
                    ks[po.chan] = pos[p]
            needed[o.idx] = need
            know_op[o.idx] = dict(ks)
            for p in need:
                ops[p].signal = True
        for o in ops:
            if o.is_dma:
                o.signal = True
        cnt = {}
        for o in ops:
            if o.signal:
                cnt[o.chan] = cnt.get(o.chan, 0) + 1
                o.val = cnt[o.chan] * (16 if o.is_dma else 1)
        sems = {}
        for c in cnt:
            name = "s_" + ("_".join(str(x) for x in c) if isinstance(c, tuple) else c)
            sems[c] = stack.enter_context(nc.semaphore(name))
        self.maxval = dict(cnt)
        block = stack.enter_context(nc.Block())
        streams = {e: [] for e in ENGS}
        for o in ops:
            streams[o.eng].append(o)

        def run_stream(engname, engobj):
            for o in streams[engname]:
                for p in needed[o.idx]:
                    po = ops[p]
                    engobj.wait_ge(sems[po.chan], po.val)
                name, a, kw = o.fn
                inst = getattr(engobj, name)(*a, **kw)
                if o.signal:
                    inst.then_inc(sems[o.chan], 16 if o.is_dma else 1)

        @block.tensor
        def _(e):
            run_stream("pe", e)

        @block.scalar
        def _(e):
            run_stream("act", e)

        @block.vector
        def _(e):
            run_stream("dve", e)

        @block.gpsimd
        def _(e):
            run_stream("pool", e)

        @block.sync
        def _(e):
            run_stream("sp", e)
            for c, n in cnt.items():
                if isinstance(c, tuple):
                    e.wait_ge(sems[c], n * 16)
        return len(ops)


class K:
    def __init__(self, debug=None):
        self.debug = debug or {}
        self.nc = bass.Bass("TRN2", target_bir_lowering=False)
        self.P = Prog()
        self.ins = {}
        self.outs = {}
        self.uid = 0

    def din(self, name, shape):
        t = self.nc.dram_tensor(name, list(shape), F32, kind="ExternalInput").ap()
        self.ins[name] = t
        return t

    def dout(self, name, shape):
        t = self.nc.dram_tensor(name, list(shape), F32, kind="ExternalOutput").ap()
        self.outs[name] = t
        return t

    def sb(self, name, shape, dt=F32):
        shape = list(shape)
        nelem = 1
        for d in shape[1:]:
            nelem *= d
        words = (nelem * (2 if dt == BF16 else 4) + 3) // 4
        words = (words + 7) // 8 * 8
        off = self.arena_off
        assert off + words <= ARENA_WORDS, ("SBUF arena overflow", name, off, words)
        self.arena_off = off + words
        self.arena_peak = max(self.arena_peak, self.arena_off)
        ap = self.arena[0:shape[0], off:off + words]
        if dt == BF16:
            ap = ap.bitcast(BF16)
        ap = ap[:, 0:nelem]
        if len(shape) == 3:
            ap = ap.rearrange("p (a b) -> p a b", b=shape[2])
        elif len(shape) == 4:
            ap = ap.rearrange("p (a b c) -> p a b c", b=shape[2], c=shape[3])
        elif len(shape) == 5:
            ap = ap.rearrange("p (a b c d) -> p a b c d", b=shape[2], c=shape[3], d=shape[4])
        return ap

    def mark(self):
        return self.arena_off

    def release(self, m):
        self.arena_off = m
        self.P.barrier()

    def build(self):
        nc, P = self.nc, self.P
        with contextlib.ExitStack() as st:
            self.stack = st
            self.arena = st.enter_context(nc.sbuf_tensor("arena", [128, ARENA_WORDS], F32))
            self.arena_off = 0
            self.arena_peak = 0
            self.declare_io()
            self.alloc_global()
            self.load_consts()
            for l in range(self.debug.get("layers", DEPTH)):
                self.layer(l)
            self.finish()
            n = P.emit(nc, st)
            self.n_ops = n
        return nc

    def declare_io(self):
        self.x_in = self.din("x", [TOK, D])
        self.cond = self.din("cond", [8, 128])
        self.ident_in = self.din("ident", [128, 128])
        self.w_mod = self.din("w_mod", [DEPTH, D, 6 * D])
        self.b_mod = self.din("b_mod", [DEPTH, 48, 128])
        self.y_out = self.dout("y", [TOK, D])
        self.consts_in = self.din("consts", [128, NCONST])
        self.keep_in = self.din("keep", [1, 1])
        self.rope_in = self.din("rope", [NT, 128, 2, 128])
        self.w_out = self.din("w_out", [DEPTH, D, D])
        self.wC = self.din("wC", [DEPTH, D, 1536])
        self.wA = self.din("wA", [DEPTH, D, 1040])
        self.ml_ib = self.din("ml_ib", [DEPTH, 8])
        self.ml_fb = self.din("ml_fb", [DEPTH, 8])
        self.ml_norm = self.din("ml_norm", [DEPTH, 256])
        self.init_mC = self.din("init_mC", [DEPTH, 2, H, HD, HD])
        self.init_mn = self.din("init_mn", [DEPTH, 2, H, HD])
        self.init_mm = self.din("init_mm", [DEPTH, 2, H])
        self.out_mC = self.dout("out_mC", [8, DEPTH, 2, H, HD, HD])
        self.out_mn = self.dout("out_mn", [8, DEPTH, 2, H, HD])
        self.out_mm = self.dout("out_mm", [8, DEPTH, 2, H])
        self.wB = self.din("wB", [DEPTH, D, 1040])
        self.dl_conv = self.din("dl_conv", [DEPTH, 128, 6, 5])
        self.dl_alog = self.din("dl_alog", [DEPTH, 8])
        self.dl_dtb = self.din("dl_dtb", [DEPTH, 8])
        self.dl_norm = self.din("dl_norm", [DEPTH, 256])
        self.init_delta = self.din("init_delta", [DEPTH, 2, H, HD, HD])
        self.out_delta = self.dout("out_delta", [8, DEPTH, 2, H, HD, HD])
        self.wD = self.din("wD", [DEPTH, D, 1152])
        self.rw_mu = self.din("rw_mu", [DEPTH, 128, 9])
        self.rw_w2p = self.din("rw_w2p", [DEPTH, 128, 2, 256])
        self.rw_a2p = self.din("rw_a2p", [DEPTH, 128, 2, 256])
        self.rw_w0 = self.din("rw_w0", [DEPTH, 512])
        self.rw_a0 = self.din("rw_a0", [DEPTH, 512])
        self.rw_kk = self.din("rw_kk", [DEPTH, 256])
        self.rw_ka = self.din("rw_ka", [DEPTH, 256])
        self.rw_rk = self.din("rw_rk", [DEPTH, 256])
        self.rw_norm = self.din("rw_norm", [DEPTH, 256])
        self.rw_g2 = self.din("rw_g2", [DEPTH, 128, 256])
        self.init_rwkv = self.din("init_rwkv", [DEPTH, 2, H, HD, HD])
        self.out_rwkv = self.dout("out_rwkv", [8, DEPTH, 2, H, HD, HD])
        self.ln1_g = self.din("ln1_g", [DEPTH, D])
        self.ln1_b = self.din("ln1_b", [DEPTH, D])
        self.ln2_g = self.din("ln2_g", [DEPTH, D])
        self.ln2_b = self.din("ln2_b", [DEPTH, D])
        self.w_ff1 = self.din("w_ff1", [DEPTH, D, 4 * D])
        self.w_ff2 = self.din("w_ff2", [DEPTH, 4 * D, D])
        self.ret_decay = self.din("ret_decay", [DEPTH, 8])
        self.ret_norm = self.din("ret_norm", [DEPTH, 256])
        self.init_ret = self.din("init_ret", [DEPTH, 2, H, HD, HD])
        self.out_ret = self.dout("out_ret", [8, DEPTH, 2, H, HD, HD])
        if "yacc" in self.debug:
            self.dbg_yacc = self.dout("dbg_yacc", [128, NT, 256])
        if "uT" in self.debug:
            self.dbg_uT = self.dout("dbg_uT", [128, 8, TOK])

    def alloc_global(self):
        nc = self.nc
        self.xres = self.sb("xres", [128, NT, D], F32)
        self.ident = self.sb("ident", [128, 128], F32)
        self.psum = self.stack.enter_context(nc.psum_tensor("psum", [128, 8, 512], F32))
        self.modT = self.sb("modT", [128, 48], F32)
        self.sc1p = self.sb("sc1p", [128, 8], F32)
        self.sc2p = self.sb("sc2p", [128, 8], F32)
        self.scT = self.sb("scT", [128, 8], F32)
        self.bmodT = self.sb("bmodT", [128, 48], F32)
        self.consts = self.sb("consts", [128, NCONST], F32)
        self.ident_bf = self.sb("ident_bf", [128, 128], BF16)
        self.keep = self.sb("keep", [128, 1], F32)
        self.g_bc = {"g1": self.sb("g1_bc", [128, D], F32), "g2": self.sb("g2_bc", [128, D], F32)}

    def load_consts(self):
        P = self.P
        xv = self.x_in.rearrange("(i p) d -> p i d", p=128)
        for q in range(4):
            P.dma("sp", lambda e, q=q: e.dma_start(out=self.xres[:, 4 * q:4 * q + 4, :], in_=xv[:, 4 * q:4 * q + 4, :]),
                  writes=[("xres", i) for i in range(4 * q, 4 * q + 4)])
        P.dma("sp", lambda e: e.dma_start(out=self.ident[:], in_=self.ident_in), writes=["ident"])
        P.dma("sp", lambda e: e.dma_start(out=self.consts[:], in_=self.consts_in), writes=["consts"])
        P.dma("sp", lambda e: e.dma_start(out=self.keep[:], in_=self.keep_in.partition_broadcast(128)), writes=["keep"])
        P.op("dve", lambda e: e.tensor_copy(self.ident_bf[:], self.ident[:]), reads=["ident"], writes=["ident_bf"])
        c8 = self.sb("c8", [8, 128], F32)
        P.dma("sp", lambda e: e.dma_start(out=c8[:], in_=self.cond), writes=["c8"])
        ps = self.psum
        P.op("pe", lambda e: e.transpose(ps[:, 0, 0:8], c8[:], self.ident[0:8, 0:8]), reads=["c8", "ident"], writes=["ps0"])
        P.op("act", lambda e: e.activation(self.scT[:], ps[:, 0, 0:8], AF.Silu), writes=["ps0", "scT"])

    def compute_mod(self, l):
        P, ps = self.P, self.psum
        m = self.mark()
        self.scbc = self.sb("scbc", [128, 8, 128], F32)
        P.op("dve", lambda e: e.tensor_copy(self.scbc[:], self.scT[:].unsqueeze(2).to_broadcast([128, 8, 128])),
             reads=["scT"], writes=["scbc"])
        b48 = self.sb("b48", [48, 128], F32)
        P.dma("sp", lambda e: e.dma_start(out=b48[:], in_=self.b_mod[l]), writes=["b48"])
        P.op("pe", lambda e: e.transpose(ps[:, 1, 0:48], b48[:], self.ident[0:48, 0:48]), reads=["b48", "ident"], writes=["ps1"])
        P.op("dve", lambda e: e.tensor_copy(self.bmodT[:], ps[:, 1, 0:48]), writes=["ps1", "bmodT"])
        wv = self.w_mod[l].rearrange("(k p) n -> p k n", p=128)
        wblk = [self.sb("wmodblk%d" % i, [128, 8, 512], F32) for i in range(2)]
        bb = self.sb("g_bb", [128, D], F32)
        for b in range(12):
            wb = wblk[b % 2]
            key = "wmodblk%d" % (b % 2)
            P.dma("sp", lambda e, b=b, wb=wb: e.dma_start(out=wb[:], in_=wv[:, :, 512 * b:512 * b + 512]), writes=[key])
            for jj in range(4):
                j = 4 * b + jj
                for k in range(8):
                    P.op("pe", lambda e, wb=wb, jj=jj, j=j, k=k: e.matmul(
                        ps[:, 2, j:j + 1], wb[:, k, 128 * jj:128 * jj + 128], self.scT[:, k:k + 1],
                        start=(k == 0), stop=(k == 7)), reads=[key, "scT"], writes=["ps2"])
            if b in (4, 5, 10, 11):
                which = "g1" if b < 6 else "g2"
                half = b % 2
                if half == 0:
                    off = 2048 if which == "g1" else 5120
                    bm = self.b_mod[l].rearrange("a b -> (a b)")[off:off + 1024].unsqueeze(0)
                    P.dma("sp", lambda e, bm=bm: e.dma_start(out=bb[:], in_=bm.partition_broadcast(128)), writes=["g_bb"])
                gt = self.g_bc[which]
                for k in range(8):
                    P.op("pe", lambda e, wb=wb, k=k: e.matmul(ps[:, 3, :], self.scbc[:, k, :], wb[:, k, :], start=(k == 0), stop=(k == 7)),
                         reads=[key, "scbc"], writes=["ps3"])
                P.op("dve", lambda e, gt=gt, half=half: e.tensor_tensor(
                    gt[:, 512 * half:512 * half + 512], ps[:, 3, :], bb[:, 512 * half:512 * half + 512], ALU.add),
                    reads=["g_bb"], writes=["ps3", which + "_bc"])
        P.op("dve", lambda e: e.tensor_tensor(self.modT[:], ps[:, 2, 0:48], self.bmodT[:], ALU.add), reads=["bmodT"], writes=["ps2", "modT"])
        P.op("dve", lambda e: e.tensor_scalar(self.sc1p[:], self.modT[:, 8:16], 1.0, None, ALU.add), reads=["modT"], writes=["sc1p"])
        P.op("dve", lambda e: e.tensor_scalar(self.sc2p[:], self.modT[:, 32:40], 1.0, None, ALU.add), reads=["modT"], writes=["sc2p"])
        self.release(m)

    def ln_mod_T(self, tile, dst_fn, scp, sh_off, tmp, dst_keys):
        P, ps = self.P, self.psum
        stats, mv, rstd, xhat = tmp
        xk = ("xres", tile)
        src = self.xres[:, tile, :]
        for hh in range(2):
            P.op("dve", lambda e, hh=hh: e.bn_stats(stats[:, hh, :], src[:, 512 * hh:512 * hh + 512]), reads=[xk], writes=["lnstats"])
        P.op("dve", lambda e: e.bn_aggr(mv[:], stats[:]), reads=["lnstats"], writes=["lnmv"])
        P.op("act", lambda e: e.activation(rstd[:], mv[:, 1:2], AF.Ln, bias=LN_EPS, scale=1.0), reads=["lnmv"], writes=["lnrstd"])
        P.op("act", lambda e: e.activation(rstd[:], rstd[:], AF.Exp, scale=-0.5), reads=["lnrstd"], writes=["lnrstd"])
        P.op("dve", lambda e: e.tensor_scalar(xhat[:], src, mv[:, 0:1], rstd[:, 0:1], ALU.subtract, ALU.mult),
             reads=[xk, "lnmv", "lnrstd"], writes=["xhat"])
        for k in range(8):
            b = 4 + (k // 4)
            P.op("pe", lambda e, k=k, b=b: e.transpose(ps[:, b, 128 * (k % 4):128 * (k % 4) + 128], xhat[:, 128 * k:128 * k + 128], self.ident[:]),
                 reads=["xhat", "ident"], writes=["ps%d" % b])
        for k in range(8):
            b = 4 + (k // 4)
            P.op("act", lambda e, k=k, b=b: e.activation(dst_fn(k), ps[:, b, 128 * (k % 4):128 * (k % 4) + 128], AF.Identity,
                                                       bias=self.modT[:, sh_off + k:sh_off + k + 1], scale=scp[:, k:k + 1]),
                 reads=["modT", "sc1p", "sc2p"], writes=["ps%d" % b] + dst_keys)

    def dump(self, name, ap, keys, shape):
        want = self.debug.get("dump", ())
        if name not in want or ("dmp_" + name) in self.outs:
            return
        P = self.P
        out = self.dout("dmp_" + name, shape)
        m = self.mark()
        tmp = self.sb("dmp_" + name, shape, F32)
        P.op("dve", lambda e: e.tensor_copy(tmp[:], ap), reads=keys, writes=["dmp_" + name])
        P.dma("sp", lambda e: e.dma_start(out=out, in_=tmp[:]), reads=["dmp_" + name])
        self.release(m)

    def act_sigmoid(self, out, in_, reads, writes):
        P = self.P
        P.op("act", lambda e: e.activation(out, in_, AF.Exp, scale=-1.0), reads=reads, writes=writes)
        P.op("act", lambda e: e.activation(out, out, AF.Ln, bias=1.0, scale=1.0), reads=writes, writes=writes)
        P.op("act", lambda e: e.activation(out, out, AF.Exp, scale=-1.0), reads=writes, writes=writes)

    def cst(self, name, rows=64):
        o, w = CO[name]
        return self.consts[0:rows, o:o + w]

    def load_w(self, name, src, ncols, kchunks=8):
        P = self.P
        wb = self.sb(name, [128, kchunks, ncols], BF16)
        wv = src.rearrange("(k p) n -> p k n", p=128)
        step = 2 if kchunks % 2 == 0 else 1
        for k0 in range(0, kchunks, step):
            P.dma("pool", lambda e, k0=k0: e.dma_start(out=wb[:, k0:k0 + step, :], in_=wv[:, k0:k0 + step, :]),
                  writes=[(name, k0 // step)])
        return wb, [(name, i) for i in range(kchunks // step)]

    def emit_out_tile(self, t, osb, osb_keys, bank):
        P, ps = self.P, self.psum
        sel = self.cst("SEL").rearrange("p (q n) -> p q n", n=128)
        pk = "ps%d" % bank
        for z in range(2):
            tile = t if z == 0 else NT - 1 - t
            for c in range(2):
                P.op("pe", lambda e, z=z, c=c: e.matmul(ps[:, bank, 256 * z:256 * z + 256], sel[:, 2 * z + c, :], osb[:, c, z, :],
                                                         start=(c == 0), stop=(c == 1)),
                     reads=["consts"] + osb_keys, writes=[pk])
            yk = ("yacc", tile)
            if t < NT // 2:
                P.op("act", lambda e, z=z, tile=tile: e.activation(self.yacc[:, tile, :], ps[:, bank, 256 * z:256 * z + 256], AF.Copy),
                     writes=[pk, yk])
            else:
                P.op("dve", lambda e, z=z, tile=tile: e.tensor_tensor(self.yacc[:, tile, :], ps[:, bank, 256 * z:256 * z + 256],
                                                                     self.yacc[:, tile, :], ALU.add),
                     writes=[pk, yk])

    def mixer_ret(self, l, first):
        P, ps = self.P, self.psum
        m0 = self.mark()
        wb, wkeys = self.load_w("wC", self.wC[l], 1536)
        lg = self.sb("ret_lg", [128, 8], F32)
        P.dma("sp", lambda e: e.dma_start(out=lg[:], in_=self.ret_decay[l:l + 1, :].partition_broadcast(128)), writes=["ret_lg"])
        P.op("act", lambda e: e.activation(lg[:], lg[:], AF.Exp), reads=["ret_lg"], writes=["ret_lg"])
        P.op("dve", lambda e: e.tensor_scalar(lg[:], lg[:], -1.0, None, ALU.mult), reads=["ret_lg"], writes=["ret_lg"])
        gs = self.sb("ret_gs", [128, 8], F32)
        ga = self.sb("ret_ga", [128, 8], F32)
        pidx = self.consts[:, CO["PIDX"][0]:CO["PIDX"][0] + 1]
        npidx = self.consts[:, CO["NPIDX"][0]:CO["NPIDX"][0] + 1]
        P.op("act", lambda e: e.activation(gs[:], lg[:], AF.Exp, scale=npidx), reads=["ret_lg", "consts"], writes=["ret_gs"])
        P.op("act", lambda e: e.activation(ga[:], lg[:], AF.Exp, scale=pidx), reads=["ret_lg", "consts"], writes=["ret_ga"])
        G64 = self.sb("ret_g64", [128, 4], F32)
        GAM = self.sb("ret_gam", [128, 4], F32)
        GAMI = self.sb("ret_gami", [128, 4], F32)
        for hq in range(2):
            rows = slice(64 * hq, 64 * hq + 64)
            P.op("act", lambda e, rows=rows, hq=hq: e.activation(G64[rows, :], lg[rows, hq::2], AF.Exp, scale=64.0), reads=["ret_lg"], writes=["ret_g64"])
            P.op("act", lambda e, rows=rows, hq=hq: e.activation(GAM[rows, :], lg[rows, hq::2], AF.Exp, scale=1.0), reads=["ret_lg"], writes=["ret_gam"])
            P.op("act", lambda e, rows=rows, hq=hq: e.activation(GAMI[rows, :], lg[rows, hq::2], AF.Exp, scale=-1.0), reads=["ret_lg"], writes=["ret_gami"])
        S = self.sb("ret_S", [128, 4, 64], F32)
        Sb = self.sb("ret_Sb", [128, 4, 64], BF16)
        Sout = self.sb("ret_Sout", [128, 4, 64], F32)
        P.dma("sp", lambda e: e.dma_start(out=S[:], in_=self.init_ret[l].rearrange("z (hp hq) d e -> (hq d) (z hp) e", hq=2)), writes=["ret_S"])
        P.op("dve", lambda e: e.tensor_tensor(S[:], S[:], GAM[:].unsqueeze(2).to_broadcast([128, 4, 64]), ALU.mult), reads=["ret_gam"], writes=["ret_S"])
        P.op("act", lambda e: e.activation(Sb[:], S[:], AF.Copy), reads=["ret_S"], writes=["ret_Sb"])
        rc = [self.sb("ret_rc%d" % i, [128, 2, 2, 128], F32) for i in range(2)]
        ta = self.sb("ret_ta", [128, 2, 128], F32)
        tb = self.sb("ret_tb", [128, 2, 128], F32)
        qT = self.sb("ret_qT", [128, 2, 2, 2, 128], BF16)
        P.op("dve", lambda e: e.memset(qT[:], 0.0), writes=["ret_qT"])
        kT = self.sb("ret_kT", [128, 2, 2, 128], BF16)
        vT = self.sb("ret_vT", [128, 2, 2, 128], BF16)
        ktm = self.sb("ret_ktm", [64, 8, 64], BF16)
        vtm = self.sb("ret_vtm", [64, 8, 64], BF16)
        pm = self.sb("ret_pm", [64, 8, 64], BF16)
        osb = self.sb("ret_osb", [64, 2, 2, 256], F32)
        tmpS = self.sb("ret_tmpS", [128, 4, 64], F32)
        incl = self.cst("INCL")
        psb = ps.bitcast(BF16) if False else None

        def bfview(bank):
            return ps[:, bank, :].bitcast(BF16)

        stop = self.debug.get("stop", 99)
        for t in range(NT if stop > 1 else 0):
            tiles = (t, NT - 1 - t)
            r = rc[t % 2]
            rk = "ret_rc%d" % (t % 2)
            for z in range(2):
                P.dma("sp", lambda e, z=z, r=r: e.dma_start(out=r[:, z, :, :], in_=self.rope_in[tiles[z]]), writes=[rk])
            def proj(j, bank, z):
                tok0 = 2 + 128 * tiles[z]
                for k in range(8):
                    P.op("pe", lambda e, j=j, k=k, z=z, tok0=tok0, bank=bank: e.matmul(
                        ps[:, bank, 128 * z:128 * z + 128], wb[:, k, 128 * j:128 * j + 128], self.uT[:, k, tok0:tok0 + 128],
                        start=(k == 0), stop=(k == 7)),
                        reads=wkeys + [("uT", tiles[z])], writes=["ps%d" % bank])
            for which, dst, dkey in ((0, qT, "ret_qT"), (1, kT, "ret_kT")):
                for hp in range(2):
                    j = 2 * which + hp
                    for z in range(2):
                        proj(j, 0, z)
                        proj(6 + j, 1, z)
                    scale = 0.125 if which == 0 else 1.0
                    P.op("dve", lambda e, scale=scale, r=r: e.scalar_tensor_tensor(
                        ta[:], ps[:, 0, 0:256].rearrange("p (z n) -> p z n", z=2), scale, r[:, :, 0, :], ALU.mult, ALU.mult),
                        reads=[rk], writes=["ps0", "ret_ta"])
                    P.op("dve", lambda e, scale=scale, r=r: e.scalar_tensor_tensor(
                        tb[:], ps[:, 1, 0:256].rearrange("p (z n) -> p z n", z=2), scale, r[:, :, 1, :], ALU.mult, ALU.mult),
                        reads=[rk], writes=["ps1", "ret_tb"])
                    if which == 1:
                        P.op("dve", lambda e, dst=dst, hp=hp: e.tensor_tensor(dst[:, hp, 0, :], ta[:, 0, :], tb[:, 0, :], ALU.add),
                             reads=["ret_ta", "ret_tb"], writes=[dkey])
                        P.op("dve", lambda e, dst=dst, hp=hp: e.tensor_tensor(dst[:, hp, 1, ::-1], ta[:, 1, :], tb[:, 1, :], ALU.add),
                             reads=["ret_ta", "ret_tb"], writes=[dkey])
                    else:
                        for hq in range(2):
                            rws = slice(64 * hq, 64 * hq + 64)
                            P.op("dve", lambda e, dst=dst, hp=hp, hq=hq, rws=rws: e.tensor_tensor(
                                dst[rws, hp, hq, 0, :], ta[rws, 0, :], tb[rws, 0, :], ALU.add),
                                reads=["ret_ta", "ret_tb"], writes=[dkey])
                            P.op("dve", lambda e, dst=dst, hp=hp, hq=hq, rws=rws: e.tensor_tensor(
                                dst[rws, hp, hq, 1, ::-1], ta[rws, 1, :], tb[rws, 1, :], ALU.add),
                                reads=["ret_ta", "ret_tb"], writes=[dkey])
            for hp in range(2):
                bank = hp
                for z in range(2):
                    proj(4 + hp, bank, z)
                P.op("act", lambda e, hp=hp, bank=bank: e.activation(vT[:, hp, 0, :], ps[:, bank, 0:128], AF.Copy), writes=["ps%d" % bank, "ret_vT"])
                P.op("act", lambda e, hp=hp, bank=bank: e.activation(vT[:, hp, 1, ::-1], ps[:, bank, 128:256], AF.Copy), writes=["ps%d" % bank, "ret_vT"])
            self.dump("ret_qT", qT[:], ["ret_qT"], [128, 2, 2, 2, 128])
            self.dump("ret_kT", kT[:], ["ret_kT"], [128, 2, 2, 128])
            self.dump("ret_vT", vT[:], ["ret_vT"], [128, 2, 2, 128])
            for c in range(2 if stop > 2 else 0):
                cs = slice(64 * c, 64 * c + 64)
                for (src, skey, bank) in ((kT, "ret_kT", 2), (vT, "ret_vT", 3)):
                    bv = bfview(bank)
                    for z in range(2):
                        for hp in range(2):
                            col = (4 * z + 2 * hp) * 64
                            P.op("pe", lambda e, src=src, z=z, hp=hp, col=col, bv=bv: e.transpose(
                                bv[0:64, col:col + 128], src[:, hp, z, cs], self.ident_bf[:]),
                                reads=[skey, "ident_bf"], writes=["ps%d" % bank])
                P.op("act", lambda e: e.activation(ktm[:], bfview(2)[0:64, 0:512].rearrange("p (u d) -> p u d", d=64), AF.Copy),
                     writes=["ps2", "ret_ktm"])
                P.op("dve", lambda e: e.tensor_tensor(vtm[:], bfview(3)[0:64, 0:512].rearrange("p (u d) -> p u d", d=64),
                                                      gs[0:64, :].unsqueeze(2).to_broadcast([64, 8, 64]), ALU.mult),
                     reads=["ret_gs"], writes=["ps3", "ret_vtm"])
                self.dump("ret_ktm", ktm[:], ["ret_ktm"], [64, 8, 64])
                self.dump("ret_vtm", vtm[:], ["ret_vtm"], [64, 8, 64])
                if stop <= 3:
                    continue
                for z in range(2):
                    for h in range(4):
                        hp, hq = h // 2, h % 2
                        rows = slice(64 * hq, 64 * hq + 64)
                        u = 4 * z + h
                        P.op("pe", lambda e, z=z, hp=hp, hq=hq, u=u: e.matmul(
                            ps[0:64, 4, 64 * u:64 * u + 64], kT[:, hp, z, cs], qT[:, hp, hq, z, cs], start=True, stop=True),
                            reads=["ret_kT", "ret_qT"], writes=["ps4"])
                if self.debug.get("sub") == "a":
                    continue
                P.op("dve", lambda e: e.tensor_tensor(pm[:], ps[0:64, 4, :].rearrange("p (u l) -> p u l", l=64),
                                                      incl.unsqueeze(1).to_broadcast([64, 8, 64]), ALU.mult),
                     reads=["consts"], writes=["ps4", "ret_pm"])
                self.dump("ret_pm", pm[:], ["ret_pm"], [64, 8, 64])
                if stop <= 4:
                    continue
                for z in range(2):
                    for h in range(4):
                        hp, hq = h // 2, h % 2
                        rows = slice(64 * hq, 64 * hq + 64)
                        u = 4 * z + h
                        P.op("pe", lambda e, u=u: e.matmul(ps[0:64, 5, 64 * u:64 * u + 64], pm[:, u, :], vtm[:, u, :], start=True, stop=False),
                             reads=["ret_pm", "ret_vtm"], writes=["ps5"])
                        P.op("pe", lambda e, z=z, hp=hp, hq=hq, u=u: e.matmul(
                            ps[0:64, 5, 64 * u:64 * u + 64], qT[:, hp, hq, z, cs], Sb[:, 2 * z + hp, :], start=False, stop=True),
                            reads=["ret_qT", "ret_Sb"], writes=["ps5"])
                P.op("dve", lambda e, c=c: e.tensor_tensor(
                    osb[:, c, :, :].rearrange("p z (h e) -> p (z h) e", e=64), ps[0:64, 5, :].rearrange("p (u e) -> p u e", e=64),
                    ga[0:64, :].unsqueeze(2).to_broadcast([64, 8, 64]), ALU.mult),
                    reads=["ret_ga"], writes=["ps5", ("ret_osb", c)])
                self.dump("ret_osb", osb[:, 0, :, :], [("ret_osb", 0)], [64, 2, 256])
                if stop <= 5:
                    continue
                for z in range(2):
                    for h in range(4):
                        hp, hq = h // 2, h % 2
                        rows = slice(64 * hq, 64 * hq + 64)
                        u = 4 * z + h
                        col = (2 * z + hp) * 64
                        P.op("pe", lambda e, rows=rows, u=u, col=col: e.matmul(ps[rows, 6, col:col + 64], ktm[:, u, :], vtm[:, u, :], start=True, stop=True),
                             reads=["ret_ktm", "ret_vtm"], writes=["ps6"])
                P.op("dve", lambda e: e.tensor_tensor(tmpS[:], ps[:, 6, 0:256].rearrange("p (a e) -> p a e", e=64), S[:], ALU.add),
                     reads=["ret_S"], writes=["ps6", "ret_tmpS"])
                P.op("dve", lambda e: e.tensor_tensor(S[:], tmpS[:], G64[:].unsqueeze(2).to_broadcast([128, 4, 64]), ALU.mult),
                     reads=["ret_tmpS", "ret_g64"], writes=["ret_S"])
                self.dump("ret_S1", S[:], ["ret_S"], [128, 4, 64])
                if not (t % 2 == 1 and c == 1):
                    P.op("act", lambda e: e.activation(Sb[:], S[:], AF.Copy), reads=["ret_S"], writes=["ret_Sb"])
            if stop > 6:
                self.emit_out_tile(t, osb, [("ret_osb", 0), ("ret_osb", 1)], 7)
            if t % 2 == 1 and stop > 7:
                P.op("dve", lambda e: e.tensor_tensor(Sout[:], S[:], GAMI[:].unsqueeze(2).to_broadcast([128, 4, 64]), ALU.mult),
                     reads=["ret_S", "ret_gami"], writes=["ret_Sout"])
                for z in range(2):
                    seg = (t - 1) // 2 if z == 0 else (NT - 1 - t) // 2
                    P.dma("sp", lambda e, z=z, seg=seg: e.dma_start(
                        out=self.out_ret[seg, l, z].rearrange("(hp hq) d e -> (hq d) hp e", hq=2), in_=Sout[:, 2 * z:2 * z + 2, :]),
                        reads=["ret_Sout"])
                P.op("dve", lambda e: e.tensor_scalar(S[:], S[:], self.keep[:, 0:1], None, ALU.mult), reads=["ret_S", "keep"], writes=["ret_S"])
                P.op("act", lambda e: e.activation(Sb[:], S[:], AF.Copy), reads=["ret_S"], writes=["ret_Sb"])
        if self.debug.get("yacc") == "ret":
            P.dma("sp", lambda e: e.dma_start(out=self.dbg_yacc, in_=self.yacc[:]), reads=[("yacc", i) for i in range(NT)])
        if self.debug.get("post", True):
            self.post_simple(l, "ret", wb, wkeys, 1280, AF.Silu, self.ret_norm, True, 512, first)
        self.release(m0)

    def proj_fm(self, wb, wkeys, col, M, tiles, bank, halo=0):
        P, ps = self.P, self.psum
        W = 128 + 2 * halo
        for z in range(2):
            tok0 = 2 + 128 * tiles[z] - halo
            for k in range(8):
                P.op("pe", lambda e, k=k, z=z, tok0=tok0: e.matmul(
                    ps[0:M, bank, W * z:W * z + W], wb[:, k, col:col + M], self.uT[:, k, tok0:tok0 + W],
                    start=(k == 0), stop=(k == 7)),
                    reads=wkeys + [("uT", tiles[z])], writes=["ps%d" % bank])

    def evac_fm(self, dst_fn, bank, M, dkey, scale=1.0, W=128, eng="act", rows=None):
        P, ps = self.P, self.psum
        rs = slice(0, M) if rows is None else rows
        for z in range(2):
            src = ps[rs, bank, W * z:W * z + W]
            dst = dst_fn(z)
            if z == 1:
                dst = dst[:, ::-1]
            if eng == "act":
                P.op("act", lambda e, dst=dst, src=src: e.activation(dst, src, AF.Copy, scale=scale), writes=["ps%d" % bank, dkey])
            else:
                P.op("dve", lambda e, dst=dst, src=src: e.tensor_scalar(dst, src, scale, None, ALU.mult), writes=["ps%d" % bank, dkey])

    def tm_transposes(self, srcT, skey, cs, bank, col0):
        P, ps = self.P, self.psum
        if srcT.dtype == BF16:
            bv = ps[:, bank, :].bitcast(BF16)
            idn = self.ident_bf
        else:
            bv = ps[:, bank:bank + 2, :].rearrange("p a b -> p (a b)")
            idn = self.ident
        for z in range(2):
            for hp in range(2):
                col = col0 + (4 * z + 2 * hp) * 64
                P.op("pe", lambda e, z=z, hp=hp, col=col: e.transpose(bv[0:64, col:col + 128], srcT[:, hp, z, cs], idn[:]),
                     reads=[skey, "ident_bf", "ident"], writes=["ps%d" % bank, "ps%d" % (bank + (0 if srcT.dtype == BF16 else 1))])
        return bv[0:64, col0:col0 + 512].rearrange("p (u d) -> p u d", d=64)

    def mixer_mlstm(self, l, first):
        P, ps = self.P, self.psum
        SDT = F32 if self.debug.get("mlf32") else BF16
        m0 = self.mark()
        wb, wkeys = self.load_w("wA", self.wA[l], 1040)
        incl = self.cst("INCL")
        ones = self.consts[0:64, CO["ONES"][0]:CO["ONES"][0] + 128]
        ib = self.sb("ml_ib", [128, 8], F32)
        fb = self.sb("ml_fb", [128, 8], F32)
        P.dma("sp", lambda e: e.dma_start(out=ib[:], in_=self.ml_ib[l:l + 1, :].partition_broadcast(128)), writes=["ml_ib"])
        P.dma("sp", lambda e: e.dma_start(out=fb[:], in_=self.ml_fb[l:l + 1, :].partition_broadcast(128)), writes=["ml_fb"])
        Cg = self.sb("ml_C", [128, 4, 65], F32)
        Cb = self.sb("ml_Cb", [128, 4, 65], SDT)
        Cout = self.sb("ml_Cout", [128, 4, 65], F32)
        tmpC = self.sb("ml_tmpC", [128, 4, 65], F32)
        mst = self.sb("ml_m", [8, 1], F32)
        msc = self.sb("ml_msc", [8, 4], F32)
        dg = self.sb("ml_dg", [8, 8], F32)
        esl = self.sb("ml_esl", [128, 4], F32)
        P.dma("sp", lambda e: e.dma_start(out=Cg[:, :, 0:64], in_=self.init_mC[l].rearrange("z (hp hq) d e -> (hq d) (z hp) e", hq=2)), writes=["ml_C"])
        P.dma("sp", lambda e: e.dma_start(out=Cg[:, :, 64:65], in_=self.init_mn[l].rearrange("z (hp hq) (d o) -> (hq d) (z hp) o", hq=2, o=1),
                                          allow_slow_non_contiguous=True), writes=["ml_C"])
        P.dma("sp", lambda e: e.dma_start(out=mst[:], in_=self.init_mm[l].rearrange("z (h o) -> (z h) o", o=1), allow_slow_non_contiguous=True), writes=["ml_m"])

        def bcast_units(src81, sign, dst_keys):
            P.op("dve", lambda e: e.tensor_scalar(dg[:], self.ident[0:8, 0:8], src81, None, ALU.mult), reads=["ident", "ml_m"], writes=["ml_dg"])
            P.op("pe", lambda e: e.matmul(ps[:, 7, 0:8], self.consts[0:8, CO["ONES"][0]:CO["ONES"][0] + 128], dg[:], start=True, stop=True),
                 reads=["consts", "ml_dg"], writes=["ps7"])
            for hq in range(2):
                rows = slice(64 * hq, 64 * hq + 64)
                P.op("act", lambda e, rows=rows, hq=hq: e.activation(esl[rows, :], ps[rows, 7, hq:8:2], AF.Exp, scale=sign), writes=["ps7", "ml_esl"])

        bcast_units(mst[:, 0:1], 1.0, None)
        P.op("dve", lambda e: e.tensor_tensor(Cg[:], Cg[:], esl[:].unsqueeze(2).to_broadcast([128, 4, 65]), ALU.mult), reads=["ml_esl", "ml_C"], writes=["ml_C"])
        P.op("act", lambda e: e.activation(Cb[:], Cg[:], AF.Copy), reads=["ml_C"], writes=["ml_Cb"])
        qT = self.sb("ml_qT", [128, 2, 2, 2, 128], SDT)
        P.op("dve", lambda e: e.memset(qT[:], 0.0), writes=["ml_qT"])
        kT = self.sb("ml_kT", [128, 2, 2, 128], SDT)
        vT = self.sb("ml_vT", [128, 2, 2, 128], SDT)
        gT = self.sb("ml_gT", [8, 2, 128], F32)
        ktm = self.sb("ml_ktm", [64, 8, 64], SDT)
        vaug = self.sb("ml_vaug", [64, 8, 65], SDT)
        pm = self.sb("ml_pm", [64, 8, 64], SDT)
        osb = self.sb("ml_osb", [64, 2, 2, 256], F32)
        gtm = self.sb("ml_gtm", [64, 2, 8], F32)
        li = self.sb("ml_li", [64, 8], F32)
        sp = self.sb("ml_sp", [64, 8], F32)
        lib = self.sb("ml_lib", [64, 8], F32)
        e1 = self.sb("ml_e1", [64, 8], F32)
        eb = self.sb("ml_eb", [64, 8], F32)
        wk = self.sb("ml_wk", [64, 8], F32)
        nbl = self.sb("ml_nbl", [64, 8], F32)
        ebls = self.sb("ml_ebls", [128, 4], F32)
        dn = self.sb("ml_dn", [64, 8], F32)
        for t in range(NT):
            tiles = (t, NT - 1 - t)
            for hp in range(2):
                self.proj_fm(wb, wkeys, 128 * hp, 128, tiles, 0)
                for hq in range(2):
                    rows = slice(64 * hq, 64 * hq + 64)
                    self.evac_fm(lambda z, hp=hp, hq=hq, rows=rows: qT[rows, hp, hq, z, :], 0, 128, "ml_qT", rows=rows, eng="dve" if hq else "act")
                self.proj_fm(wb, wkeys, 256 + 128 * hp, 128, tiles, 1)
                self.evac_fm(lambda z, hp=hp: kT[:, hp, z, :], 1, 128, "ml_kT", scale=0.125)
                self.proj_fm(wb, wkeys, 512 + 128 * hp, 128, tiles, 0)
                self.evac_fm(lambda z, hp=hp: vT[:, hp, z, :], 0, 128, "ml_vT", eng="dve")
            for z in range(2):
                tok0 = 2 + 128 * tiles[z]
                for k in range(8):
                    P.op("pe", lambda e, k=k, z=z, tok0=tok0: e.matmul(ps[0:8, 1, 128 * z:128 * z + 128], wb[:, k, 768 + 8 * z:768 + 8 * z + 8],
                                                                    self.uT[:, k, tok0:tok0 + 128], start=(k == 0), stop=(k == 7)),
                         reads=wkeys + [("uT", tiles[z])], writes=["ps1"])
            self.evac_fm(lambda z: gT[:, z, :], 1, 8, "ml_gT", eng="dve")
            for c in range(2):
                cs = slice(64 * c, 64 * c + 64)
                for z in range(2):
                    P.op("pe", lambda e, z=z: e.transpose(ps[0:64, 7, 8 * z:8 * z + 8], gT[:, z, cs], self.ident[0:8, 0:8]), reads=["ml_gT", "ident"], writes=["ps7"])
                P.op("dve", lambda e: e.tensor_copy(gtm[:], ps[0:64, 7, 0:16].rearrange("p (z g) -> p z g", g=8)), writes=["ps7", "ml_gtm"])
                P.op("dve", lambda e: e.tensor_tensor(li[:].rearrange("p (z h) -> p z h", h=4), gtm[:, :, 0:4],
                                                      ib[0:64, :].rearrange("p (z h) -> p z h", h=4), ALU.add), reads=["ml_gtm", "ml_ib"], writes=["ml_li"])
                P.op("dve", lambda e: e.tensor_tensor(sp[:].rearrange("p (z h) -> p z h", h=4), gtm[:, :, 4:8],
                                                      fb[0:64, :].rearrange("p (z h) -> p z h", h=4), ALU.add), reads=["ml_gtm", "ml_fb"], writes=["ml_sp"])
                P.op("act", lambda e: e.activation(sp[:], sp[:], AF.Exp, scale=-1.0), reads=["ml_sp"], writes=["ml_sp"])
                P.op("act", lambda e: e.activation(sp[:], sp[:], AF.Ln, bias=1.0, scale=1.0), reads=["ml_sp"], writes=["ml_sp"])
                P.op("pe", lambda e: e.matmul(ps[0:64, 7, 16:24], incl, sp[:], start=True, stop=True), reads=["consts", "ml_sp"], writes=["ps7"])
                P.op("pe", lambda e: e.matmul(ps[:, 7, 24:32], ones, sp[:], start=True, stop=True), reads=["consts", "ml_sp"], writes=["ps7"])
                P.op("dve", lambda e: e.tensor_tensor(lib[:], ps[0:64, 7, 16:24], li[:], ALU.add), reads=["ml_li"], writes=["ps7", "ml_lib"])
                P.op("act", lambda e: e.activation(eb[:], ps[0:64, 7, 16:24], AF.Exp, scale=-1.0), writes=["ps7", "ml_eb"])
                P.op("dve", lambda e: e.tensor_copy(nbl[:], ps[0:64, 7, 24:32]), writes=["ps7", "ml_nbl"])
                for hq in range(2):
                    rows = slice(64 * hq, 64 * hq + 64)
                    P.op("act", lambda e, rows=rows, hq=hq: e.activation(ebls[rows, :], ps[rows, 7, 24 + hq:32:2], AF.Exp, scale=-1.0), writes=["ps7", "ml_ebls"])
                P.op("act", lambda e: e.activation(e1[:], lib[:], AF.Exp), reads=["ml_lib"], writes=["ml_e1"])
                P.op("dve", lambda e: e.tensor_tensor(wk[:], lib[:], nbl[:], ALU.subtract), reads=["ml_lib", "ml_nbl"], writes=["ml_wk"])
                ktv = self.tm_transposes(kT, "ml_kT", cs, 2, 0)
                vtv = self.tm_transposes(vT, "ml_vT", cs, 2, 512)
                P.op("act", lambda e: e.activation(ktm[:], ktv, AF.Copy), writes=["ps2", "ps3", "ml_ktm"])
                P.op("dve", lambda e: e.tensor_tensor(vaug[:, :, 0:64], vtv, e1[:].unsqueeze(2).to_broadcast([64, 8, 64]), ALU.mult),
                     reads=["ml_e1"], writes=["ps2", "ps3", "ml_vaug"])
                P.op("dve", lambda e: e.tensor_copy(vaug[:, :, 64:65], e1[:].unsqueeze(2)), reads=["ml_e1"], writes=["ml_vaug"])
                for z in range(2):
                    for h in range(4):
                        hp, hq = h // 2, h % 2
                        u = 4 * z + h
                        P.op("pe", lambda e, z=z, hp=hp, hq=hq, u=u: e.matmul(ps[0:64, 4, 64 * u:64 * u + 64], kT[:, hp, z, cs], qT[:, hp, hq, z, cs], start=True, stop=True),
                             reads=["ml_kT", "ml_qT"], writes=["ps4"])
                P.op("dve", lambda e: e.tensor_tensor(pm[:], ps[0:64, 4, :].rearrange("p (u l) -> p u l", l=64), incl.unsqueeze(1).to_broadcast([64, 8, 64]), ALU.mult),
                     reads=["consts"], writes=["ps4", "ml_pm"])
                for z in range(2):
                    bank = 5 + z
                    for h in range(4):
                        hp, hq = h // 2, h % 2
                        u = 4 * z + h
                        P.op("pe", lambda e, u=u, h=h, bank=bank: e.matmul(ps[0:64, bank, 65 * h:65 * h + 65], pm[:, u, :], vaug[:, u, :], start=True, stop=False),
                             reads=["ml_pm", "ml_vaug"], writes=["ps%d" % bank])
                        P.op("pe", lambda e, z=z, hp=hp, hq=hq, h=h, bank=bank: e.matmul(ps[0:64, bank, 65 * h:65 * h + 65], qT[:, hp, hq, z, cs], Cb[:, 2 * z + hp, :], start=False, stop=True),
                             reads=["ml_qT", "ml_Cb"], writes=["ps%d" % bank])
                for z in range(2):
                    bank = 5 + z
                    o3 = ps[0:64, bank, 0:260].rearrange("p (h e) -> p h e", e=65)
                    P.op("dve", lambda e, z=z, o3=o3: e.tensor_tensor(dn[:, 4 * z:4 * z + 4], o3[:, :, 64], eb[:, 4 * z:4 * z + 4], ALU.mult),
                         reads=["ml_eb"], writes=["ps%d" % bank, "ml_dn"])
                P.op("act", lambda e: e.activation(dn[:], dn[:], AF.Abs), reads=["ml_dn"], writes=["ml_dn"])
                P.op("dve", lambda e: e.tensor_scalar(dn[:], dn[:], 1.0, None, ALU.max), reads=["ml_dn"], writes=["ml_dn"])
                P.op("dve", lambda e: e.reciprocal(dn[:], dn[:]), reads=["ml_dn"], writes=["ml_dn"])
                P.op("dve", lambda e: e.tensor_tensor(dn[:], dn[:], eb[:], ALU.mult), reads=["ml_dn", "ml_eb"], writes=["ml_dn"])
                for z in range(2):
                    bank = 5 + z
                    o3 = ps[0:64, bank, 0:260].rearrange("p (h e) -> p h e", e=65)
                    P.op("dve", lambda e, z=z, o3=o3, c=c: e.tensor_tensor(
                        osb[:, c, z, :].rearrange("p (h e) -> p h e", e=64), o3[:, :, 0:64],
                        dn[:, 4 * z:4 * z + 4].unsqueeze(2).to_broadcast([64, 4, 64]), ALU.mult),
                        reads=["ml_dn"], writes=["ps%d" % bank, ("ml_osb", c)])
                for z in range(2):
                    for h in range(4):
                        hp, hq = h // 2, h % 2
                        rows = slice(64 * hq, 64 * hq + 64)
                        u = 4 * z + h
                        col = (2 * z + hp) * 65
                        P.op("pe", lambda e, rows=rows, u=u, col=col: e.matmul(ps[rows, 3, col:col + 65], ktm[:, u, :], vaug[:, u, :], start=True, stop=True),
                             reads=["ml_ktm", "ml_vaug"], writes=["ps3"])
                P.op("dve", lambda e: e.tensor_tensor(tmpC[:], ps[:, 3, 0:260].rearrange("p (a e) -> p a e", e=65), Cg[:], ALU.add),
                     reads=["ml_C"], writes=["ps3", "ml_tmpC"])
                P.op("dve", lambda e: e.tensor_tensor(Cg[:], tmpC[:], ebls[:].unsqueeze(2).to_broadcast([128, 4, 65]), ALU.mult),
                     reads=["ml_tmpC", "ml_ebls"], writes=["ml_C"])
                if not (t % 2 == 1 and c == 1):
                    P.op("act", lambda e: e.activation(Cb[:], Cg[:], AF.Copy), reads=["ml_C"], writes=["ml_Cb"])
                P.op("pe", lambda e: e.transpose(ps[0:8, 7, 64:128], wk[:], self.ident[0:64, 0:64]), reads=["ml_wk", "ident"], writes=["ps7"])
                P.op("pe", lambda e: e.transpose(ps[0:8, 7, 128:192], nbl[:], self.ident[0:64, 0:64]), reads=["ml_nbl", "ident"], writes=["ps7"])
                P.op("dve", lambda e: e.tensor_reduce(msc[:, 0:1], ps[0:8, 7, 64:128], AX.X, ALU.max), writes=["ps7", "ml_msc"])
                P.op("dve", lambda e: e.tensor_tensor(msc[:, 1:2], mst[:], ps[0:8, 7, 128:129], ALU.subtract), reads=["ml_m"], writes=["ps7", "ml_msc"])
                P.op("dve", lambda e: e.tensor_tensor(mst[:], msc[:, 0:1], msc[:, 1:2], ALU.max), reads=["ml_msc"], writes=["ml_m"])
            self.emit_out_tile(t, osb, [("ml_osb", 0), ("ml_osb", 1)], 7)
            if t % 2 == 1:
                bcast_units(mst[:, 0:1], -1.0, None)
                P.op("dve", lambda e: e.tensor_tensor(Cout[:], Cg[:], esl[:].unsqueeze(2).to_broadcast([128, 4, 65]), ALU.mult),
                     reads=["ml_C", "ml_esl"], writes=["ml_Cout"])
                for z in range(2):
                    seg = (t - 1) // 2 if z == 0 else (NT - 1 - t) // 2
                    P.dma("sp", lambda e, z=z, seg=seg: e.dma_start(
                        out=self.out_mC[seg, l, z].rearrange("(hp hq) d e -> (hq d) hp e", hq=2), in_=Cout[:, 2 * z:2 * z + 2, 0:64]), reads=["ml_Cout"])
                    P.dma("sp", lambda e, z=z, seg=seg: e.dma_start(
                        out=self.out_mn[seg, l, z].rearrange("(hp hq) (d o) -> (hq d) hp o", hq=2, o=1), in_=Cout[:, 2 * z:2 * z + 2, 64:65],
                        allow_slow_non_contiguous=True), reads=["ml_Cout"])
                    P.dma("sp", lambda e, z=z, seg=seg: e.dma_start(
                        out=self.out_mm[seg, l, z].rearrange("(h o) -> h o", o=1), in_=mst[4 * z:4 * z + 4, :], allow_slow_non_contiguous=True), reads=["ml_m"])
                P.op("dve", lambda e: e.tensor_scalar(Cg[:], Cg[:], self.keep[:, 0:1], None, ALU.mult), reads=["ml_C", "keep"], writes=["ml_C"])
                P.op("dve", lambda e: e.tensor_scalar(mst[:], mst[:], self.keep[0:8, 0:1], None, ALU.mult), reads=["ml_m", "keep"], writes=["ml_m"])
                P.op("act", lambda e: e.activation(Cb[:], Cg[:], AF.Copy), reads=["ml_C"], writes=["ml_Cb"])
        if self.debug.get("yacc") == "mlstm":
            P.dma("sp", lambda e: e.dma_start(out=self.dbg_yacc, in_=self.yacc[:]), reads=[("yacc", i) for i in range(NT)])
        if self.debug.get("post", True):
            self.post_simple(l, "ml", wb, wkeys, 784, AF.Sigmoid, self.ml_norm, True, 0, first)
        self.release(m0)

    def neumann_inverse(self, pfx, Pm, Qm, Rm, banks, keys=None):
        P, ps = self.P, self.psum
        bP, bQ, bR = banks
        kP, kQ, kR = keys if keys is not None else (pfx + "P", pfx + "Q", pfx + "R")
        ident64 = self.ident[0:64, 0:64]
        for u in range(8):
            P.op("pe", lambda e, u=u: e.transpose(ps[0:64, bQ, 64 * u:64 * u + 64], Pm[:, u, :], ident64), reads=[kP, "ident"], writes=["ps%d" % bQ])
        P.op("act", lambda e: e.activation(Qm[:], ps[0:64, bQ, :].rearrange("p (u l) -> p u l", l=64), AF.Copy), writes=["ps%d" % bQ, kQ])
        P.op("dve", lambda e: e.tensor_tensor(Rm[:], Pm[:], ident64.unsqueeze(1).to_broadcast([64, 8, 64]), ALU.add), reads=[kP, "ident"], writes=[kR])
        for lvl in range(5):
            last = lvl == 4
            if not last:
                for u in range(8):
                    P.op("pe", lambda e, u=u: e.matmul(ps[0:64, bP, 64 * u:64 * u + 64], Qm[:, u, :], Pm[:, u, :], start=True, stop=True),
                         reads=[kP, kQ], writes=["ps%d" % bP])
            for u in range(8):
                P.op("pe", lambda e, u=u: e.matmul(ps[0:64, bQ, 64 * u:64 * u + 64], Pm[:, u, :], Qm[:, u, :], start=True, stop=True),
                     reads=[kP, kQ], writes=["ps%d" % bQ])
            if not last:
                P.op("dve", lambda e: e.tensor_copy(Pm[:], ps[0:64, bP, :].rearrange("p (u l) -> p u l", l=64)), writes=["ps%d" % bP, kP])
            P.op("act", lambda e: e.activation(Qm[:], ps[0:64, bQ, :].rearrange("p (u l) -> p u l", l=64), AF.Copy), writes=["ps%d" % bQ, kQ])
            for u in range(8):
                P.op("pe", lambda e, u=u: e.matmul(ps[0:64, bR, 64 * u:64 * u + 64], Qm[:, u, :], Rm[:, u, :], start=True, stop=True),
                     reads=[kQ, kR], writes=["ps%d" % bR])
            P.op("dve", lambda e: e.tensor_tensor(Rm[:], ps[0:64, bR, :].rearrange("p (u l) -> p u l", l=64), Rm[:], ALU.add),
                 reads=[kR], writes=["ps%d" % bR, kR])

    def mixer_delta(self, l, first):
        P, ps = self.P, self.psum
        m0 = self.mark()
        wb, wkeys = self.load_w("wB", self.wB[l], 1040)
        incl = self.cst("INCL")
        strict = self.cst("STRICT")
        ones64 = self.consts[0:64, CO["ONES"][0]:CO["ONES"][0] + 64]
        ones128 = self.consts[0:64, CO["ONES"][0]:CO["ONES"][0] + 128]
        blk = self.consts[:, CO["BLK"][0]:CO["BLK"][0] + 128]
        ident64 = self.ident[0:64, 0:64]
        cw = self.sb("dl_cw", [128, 6, 5], F32)
        P.dma("sp", lambda e: e.dma_start(out=cw[:], in_=self.dl_conv[l]), writes=["dl_cw"])
        Au = self.sb("dl_A", [128, 8], F32)
        dtb = self.sb("dl_dtb", [128, 8], F32)
        P.dma("sp", lambda e: e.dma_start(out=Au[:], in_=self.dl_alog[l:l + 1, :].partition_broadcast(128)), writes=["dl_A"])
        P.dma("sp", lambda e: e.dma_start(out=dtb[:], in_=self.dl_dtb[l:l + 1, :].partition_broadcast(128)), writes=["dl_dtb"])
        P.op("act", lambda e: e.activation(Au[:], Au[:], AF.Exp), reads=["dl_A"], writes=["dl_A"])
        S = self.sb("dl_S", [128, 4, 64], F32)
        Sb = self.sb("dl_Sb", [128, 4, 64], BF16)
        Sout = self.sb("dl_Sout", [128, 4, 64], F32)
        tmpS = self.sb("dl_tmpS", [128, 4, 64], F32)
        P.dma("sp", lambda e: e.dma_start(out=S[:], in_=self.init_delta[l].rearrange("z (hp hq) d e -> (hq d) (z hp) e", hq=2)), writes=["dl_S"])
        P.op("act", lambda e: e.activation(Sb[:], S[:], AF.Copy), reads=["dl_S"], writes=["dl_Sb"])
        pre = self.sb("dl_pre", [128, 2, 132], F32)
        acc = self.sb("dl_acc", [128, 2, 128], F32)
        sq = self.sb("dl_sq", [128, 2, 128], F32)
        rn = self.sb("dl_rn", [128, 2, 128], F32)
        qTp = self.sb("dl_qTp", [128, 2, 2, 2, 128], BF16)
        kTp = self.sb("dl_kTp", [128, 2, 2, 2, 128], BF16)
        P.op("dve", lambda e: e.memset(qTp[:], 0.0), writes=["dl_qTp"])
        P.op("dve", lambda e: e.memset(kTp[:], 0.0), writes=["dl_kTp"])
        kT = self.sb("dl_kT", [128, 2, 2, 128], BF16)
        vT = self.sb("dl_vT", [128, 2, 2, 128], BF16)
        gT = self.sb("dl_gT", [8, 2, 128], F32)
        gtm = self.sb("dl_gtm", [64, 4, 8], F32)
        beta2 = self.sb("dl_beta", [64, 2, 8], F32)
        nbeta2 = self.sb("dl_nbeta", [64, 2, 8], F32)
        ng2 = self.sb("dl_ng", [64, 2, 8], F32)
        ngc2 = self.sb("dl_ngc", [64, 2, 8], F32)
        eg2 = self.sb("dl_eg", [64, 2, 8], F32)
        eglt2 = self.sb("dl_eglt", [64, 2, 8], F32)
        egls2 = self.sb("dl_egls", [128, 2, 4], F32)
        dgm = self.sb("dl_dgm", [64, 8, 64], F32)
        dT = self.sb("dl_dT", [64, 8, 64], F32)
        dTs = self.sb("dl_dTs", [64, 8, 64], F32)
        Pm = self.sb("dl_P", [64, 8, 64], F32)
        Qm = self.sb("dl_Q", [64, 8, 64], F32)
        Rm = self.sb("dl_R", [64, 8, 64], F32)
        qkd = self.sb("dl_qkd", [64, 8, 64], BF16)
        ktm = self.sb("dl_ktm", [64, 8, 64], BF16)
        vtm = self.sb("dl_vtm", [64, 8, 64], F32)
        kd = self.sb("dl_kd", [64, 8, 64], BF16)
        rr = self.sb("dl_r", [64, 8, 64], F32)
        vnew = self.sb("dl_vnew", [64, 8, 64], BF16)
        vnf = self.sb("dl_vnf", [64, 8, 64], F32)
        t1 = self.sb("dl_t1", [64, 8, 64], F32)
        osb = self.sb("dl_osb", [64, 2, 2, 256], F32)
        for t in range(NT):
            tiles = (t, NT - 1 - t)
            edge = slice(0, 2) if t % 2 == 0 else slice(130, 132)
            for j in range(6):
                bank = j % 2
                self.proj_fm(wb, wkeys, 128 * j, 128, tiles, bank, halo=2)
                self.evac_fm(lambda z: pre[:, z, :], bank, 128, "dl_pre", W=132, eng="act")
                P.op("dve", lambda e: e.tensor_scalar(pre[:, :, edge], pre[:, :, edge], self.keep[:, 0:1], None, ALU.mult), reads=["dl_pre", "keep"], writes=["dl_pre"])
                for z in range(2):
                    eng = "dve"
                    for k in range(5):
                        wk_ = cw[:, j, k:k + 1] if z == 0 else cw[:, j, 4 - k:5 - k]
                        if k == 0:
                            P.op(eng, lambda e, z=z, wk_=wk_: e.tensor_scalar(acc[:, z, :], pre[:, z, 0:128], wk_, None, ALU.mult),
                                 reads=["dl_pre", "dl_cw"], writes=[("dl_acc", z)])
                        else:
                            P.op(eng, lambda e, z=z, k=k, wk_=wk_: e.scalar_tensor_tensor(acc[:, z, :], pre[:, z, k:k + 128], wk_, acc[:, z, :], ALU.mult, ALU.add),
                                 reads=["dl_pre", "dl_cw", ("dl_acc", z)], writes=[("dl_acc", z)])
                akeys = [("dl_acc", 0), ("dl_acc", 1)]
                self.act_sigmoid(sq[:], acc[:], akeys, ["dl_sq"])
                if j >= 4:
                    P.op("dve", lambda e, j=j: e.tensor_tensor(vT[:, j - 4, :, :], acc[:], sq[:], ALU.mult), reads=akeys + ["dl_sq"], writes=["dl_vT"])
                    continue
                P.op("dve", lambda e: e.tensor_tensor(acc[:], acc[:], sq[:], ALU.mult), reads=akeys + ["dl_sq"], writes=akeys)
                P.op("act", lambda e: e.activation(sq[:], acc[:], AF.Square), reads=akeys, writes=["dl_sq"])
                P.op("pe", lambda e: e.matmul(ps[:, 2, 0:256], blk, sq[:].rearrange("p z n -> p (z n)"), start=True, stop=True), reads=["consts", "dl_sq"], writes=["ps2"])
                P.op("act", lambda e: e.activation(rn[:], ps[:, 2, 0:256].rearrange("p (z n) -> p z n", z=2), AF.Ln, bias=1e-6, scale=1.0), writes=["ps2", "dl_rn"])
                P.op("act", lambda e: e.activation(rn[:], rn[:], AF.Exp, scale=-0.5), reads=["dl_rn"], writes=["dl_rn"])
                hp = j % 2
                if j < 2:
                    for hq in range(2):
                        rows = slice(64 * hq, 64 * hq + 64)
                        P.op("dve", lambda e, rows=rows, hp=hp, hq=hq: e.scalar_tensor_tensor(qTp[rows, hp, hq, :, :], acc[rows, :, :], 0.125, rn[rows, :, :], ALU.mult, ALU.mult),
                             reads=akeys + ["dl_rn"], writes=["dl_qTp"])
                else:
                    P.op("dve", lambda e, hp=hp: e.tensor_tensor(kT[:, hp, :, :], acc[:], rn[:], ALU.mult), reads=akeys + ["dl_rn"], writes=["dl_kT"])
                    for hq in range(2):
                        rows = slice(64 * hq, 64 * hq + 64)
                        P.op("pool", lambda e, rows=rows, hp=hp, hq=hq: e.tensor_copy(kTp[rows, hp, hq, :, :], kT[rows, hp, :, :]), reads=["dl_kT"], writes=["dl_kTp"])
            for z in range(2):
                tok0 = 2 + 128 * tiles[z]
                for k in range(8):
                    P.op("pe", lambda e, k=k, z=z, tok0=tok0: e.matmul(ps[0:8, 1, 128 * z:128 * z + 128], wb[:, k, 768 + 8 * z:768 + 8 * z + 8],
                                                                    self.uT[:, k, tok0:tok0 + 128], start=(k == 0), stop=(k == 7)),
                         reads=wkeys + [("uT", tiles[z])], writes=["ps1"])
            self.evac_fm(lambda z: gT[:, z, :], 1, 8, "dl_gT", eng="dve")
            for c in range(2):
                for z in range(2):
                    q_ = 2 * c + z
                    P.op("pe", lambda e, z=z, c=c, q_=q_: e.transpose(ps[0:64, 7, 8 * q_:8 * q_ + 8], gT[:, z, 64 * c:64 * c + 64], self.ident[0:8, 0:8]),
                         reads=["dl_gT", "ident"], writes=["ps7"])
            P.op("dve", lambda e: e.tensor_copy(gtm[:], ps[0:64, 7, 0:32].rearrange("p (q g) -> p q g", g=8)), writes=["ps7", "dl_gtm"])
            b4 = beta2[:].rearrange("p c (z h) -> p (c z) h", h=4)
            P.op("act", lambda e: e.activation(b4, gtm[:, :, 0:4], AF.Exp, scale=-1.0), reads=["dl_gtm"], writes=["dl_beta"])
            P.op("act", lambda e: e.activation(beta2[:], beta2[:], AF.Ln, bias=1.0, scale=1.0), reads=["dl_beta"], writes=["dl_beta"])
            P.op("act", lambda e: e.activation(beta2[:], beta2[:], AF.Exp, scale=-1.0), reads=["dl_beta"], writes=["dl_beta"])
            P.op("dve", lambda e: e.tensor_scalar(nbeta2[:], beta2[:], -1.0, None, ALU.mult), reads=["dl_beta"], writes=["dl_nbeta"])
            for c in range(2):
                P.op("dve", lambda e, c=c: e.tensor_tensor(ng2[:, c, :].rearrange("p (z h) -> p z h", h=4), gtm[:, 2 * c:2 * c + 2, 4:8],
                                                           dtb[0:64, :].rearrange("p (z h) -> p z h", h=4), ALU.add),
                     reads=["dl_gtm", "dl_dtb"], writes=["dl_ng"])
            P.op("act", lambda e: e.activation(ng2[:], ng2[:], AF.Exp), reads=["dl_ng"], writes=["dl_ng"])
            P.op("act", lambda e: e.activation(ng2[:], ng2[:], AF.Ln, bias=1.0, scale=1.0), reads=["dl_ng"], writes=["dl_ng"])
            P.op("dve", lambda e: e.tensor_tensor(ng2[:], ng2[:], Au[0:64, :].unsqueeze(1).to_broadcast([64, 2, 8]), ALU.mult), reads=["dl_ng", "dl_A"], writes=["dl_ng"])
            for c in range(2):
                P.op("pe", lambda e, c=c: e.matmul(ps[0:64, 7, 32 + 8 * c:40 + 8 * c], incl, ng2[:, c, :], start=True, stop=True), reads=["consts", "dl_ng"], writes=["ps7"])
                P.op("pe", lambda e, c=c: e.matmul(ps[:, 7, 48 + 8 * c:56 + 8 * c], ones128, ng2[:, c, :], start=True, stop=True), reads=["consts", "dl_ng"], writes=["ps7"])
            ngcp = ps[0:64, 7, 32:48].rearrange("p (c u) -> p c u", u=8)
            nglp = ps[0:64, 7, 48:64].rearrange("p (c u) -> p c u", u=8)
            P.op("dve", lambda e: e.tensor_copy(ngc2[:], ngcp), writes=["ps7", "dl_ngc"])
            P.op("act", lambda e: e.activation(eg2[:], ngcp, AF.Exp, scale=-1.0), writes=["ps7", "dl_eg"])
            P.op("dve", lambda e: e.tensor_tensor(eglt2[:], ngc2[:], nglp, ALU.subtract), reads=["dl_ngc"], writes=["ps7", "dl_eglt"])
            P.op("act", lambda e: e.activation(eglt2[:], eglt2[:], AF.Exp), reads=["dl_eglt"], writes=["dl_eglt"])
            for hq in range(2):
                rows = slice(64 * hq, 64 * hq + 64)
                P.op("act", lambda e, rows=rows, hq=hq: e.activation(egls2[rows, :, :], ps[rows, 7, 48:64].rearrange("p (c u) -> p c u", u=8)[:, :, hq:8:2], AF.Exp, scale=-1.0),
                     writes=["ps7", "dl_egls"])
            for c in range(2):
                cs = slice(64 * c, 64 * c + 64)
                beta, nbeta, ngc, eg, eglt, egls = beta2[:, c, :], nbeta2[:, c, :], ngc2[:, c, :], eg2[:, c, :], eglt2[:, c, :], egls2[:, c, :]
                P.op("dve", lambda e: e.tensor_tensor(dgm[:], ident64.unsqueeze(1).to_broadcast([64, 8, 64]), ngc[:].unsqueeze(2).to_broadcast([64, 8, 64]), ALU.mult),
                     reads=["ident", "dl_ngc"], writes=["dl_dgm"])
                P.op("pe", lambda e: e.matmul(ps[0:64, 3, :], ones64, dgm[:].rearrange("p u l -> p (u l)"), start=True, stop=True), reads=["consts", "dl_dgm"], writes=["ps3"])
                P.op("dve", lambda e: e.tensor_tensor(dT[:], ngc[:].unsqueeze(2).to_broadcast([64, 8, 64]), ps[0:64, 3, :].rearrange("p (u l) -> p u l", l=64), ALU.subtract),
                     reads=["dl_ngc"], writes=["ps3", "dl_dT"])
                P.op("dve", lambda e: e.tensor_scalar(dT[:], dT[:], 0.0, None, ALU.min), reads=["dl_dT"], writes=["dl_dT"])
                P.op("act", lambda e: e.activation(dT[:], dT[:], AF.Exp), reads=["dl_dT"], writes=["dl_dT"])
                P.op("pool", lambda e: e.tensor_tensor(dTs[:], dT[:], strict.unsqueeze(1).to_broadcast([64, 8, 64]), ALU.mult), reads=["dl_dT", "consts"], writes=["dl_dTs"])
                P.op("pool", lambda e: e.tensor_tensor(dT[:], dT[:], incl.unsqueeze(1).to_broadcast([64, 8, 64]), ALU.mult), reads=["dl_dT", "consts"], writes=["dl_dT"])
                ktv = self.tm_transposes(kT, "dl_kT", cs, 2, 0)
                vtv = self.tm_transposes(vT, "dl_vT", cs, 2, 512)
                P.op("act", lambda e: e.activation(ktm[:], ktv, AF.Copy), writes=["ps2", "dl_ktm"])
                P.op("dve", lambda e: e.tensor_copy(vtm[:], vtv), writes=["ps2", "dl_vtm"])
                P.op("dve", lambda e: e.tensor_tensor(kd[:], ktm[:], eglt[:].unsqueeze(2).to_broadcast([64, 8, 64]), ALU.mult), reads=["dl_ktm", "dl_eglt"], writes=["dl_kd"])
                for z in range(2):
                    for h in range(4):
                        hp, hq = h // 2, h % 2
                        u = 4 * z + h
                        P.op("pe", lambda e, z=z, hp=hp, hq=hq, u=u: e.matmul(ps[0:64, 4, 64 * u:64 * u + 64], kT[:, hp, z, cs], kTp[:, hp, hq, z, cs], start=True, stop=True),
                             reads=["dl_kT", "dl_kTp"], writes=["ps4"])
                        P.op("pe", lambda e, z=z, hp=hp, hq=hq, u=u: e.matmul(ps[0:64, 5, 64 * u:64 * u + 64], kT[:, hp, z, cs], qTp[:, hp, hq, z, cs], start=True, stop=True),
                             reads=["dl_kT", "dl_qTp"], writes=["ps5"])
                P.op("dve", lambda e: e.tensor_tensor(Pm[:], ps[0:64, 4, :].rearrange("p (u l) -> p u l", l=64), dTs[:], ALU.mult), reads=["dl_dTs"], writes=["ps4", "dl_P"])
                P.op("dve", lambda e: e.tensor_tensor(Pm[:], Pm[:], nbeta[:].unsqueeze(2).to_broadcast([64, 8, 64]), ALU.mult), reads=["dl_P", "dl_nbeta"], writes=["dl_P"])
                P.op("dve", lambda e: e.tensor_tensor(qkd[:], ps[0:64, 5, :].rearrange("p (u l) -> p u l", l=64), dT[:], ALU.mult), reads=["dl_dT"], writes=["ps5", "dl_qkd"])
                self.neumann_inverse("dl_", Pm, Qm, Rm, (4, 5, 6))
                for z in range(2):
                    bank = 3 + z
                    for h in range(4):
                        hp, hq = h // 2, h % 2
                        P.op("pe", lambda e, z=z, hp=hp, hq=hq, h=h, bank=bank: e.matmul(ps[0:64, bank, 128 * h:128 * h + 64], kTp[:, hp, hq, z, cs], Sb[:, 2 * z + hp, :], start=True, stop=True),
                             reads=["dl_kTp", "dl_Sb"], writes=["ps%d" % bank])
                        P.op("pe", lambda e, z=z, hp=hp, hq=hq, h=h, bank=bank: e.matmul(ps[0:64, bank, 128 * h + 64:128 * h + 128], qTp[:, hp, hq, z, cs], Sb[:, 2 * z + hp, :], start=True, stop=True),
                             reads=["dl_qTp", "dl_Sb"], writes=["ps%d" % bank])
                for z in range(2):
                    bank = 3 + z
                    ks4 = ps[0:64, bank, :].rearrange("p (h two e) -> p h two e", two=2, e=64)
                    us = slice(4 * z, 4 * z + 4)
                    P.op("dve", lambda e, ks4=ks4, us=us: e.tensor_tensor(t1[:, us, :], ks4[:, :, 0, :], eg[:, us].unsqueeze(2).to_broadcast([64, 4, 64]), ALU.mult),
                         reads=["dl_eg"], writes=["ps%d" % bank, ("dl_t1", z)])
                    P.op("dve", lambda e, us=us, z=z: e.tensor_tensor(rr[:, us, :], vtm[:, us, :], t1[:, us, :], ALU.subtract), reads=["dl_vtm", ("dl_t1", z)], writes=[("dl_r", z)])
                    P.op("dve", lambda e, ks4=ks4, us=us: e.tensor_tensor(t1[:, us, :], ks4[:, :, 1, :], eg[:, us].unsqueeze(2).to_broadcast([64, 4, 64]), ALU.mult),
                         reads=["dl_eg", ("dl_r", z)], writes=["ps%d" % bank, ("dl_t1", z)])
                rkeys = [("dl_r", 0), ("dl_r", 1)]
                for u in range(8):
                    P.op("pe", lambda e, u=u: e.matmul(ps[0:64, 5, 64 * u:64 * u + 64], Rm[:, u, :], rr[:, u, :], start=True, stop=True), reads=["dl_R"] + rkeys, writes=["ps5"])
                P.op("dve", lambda e: e.tensor_tensor(vnew[:], ps[0:64, 5, :].rearrange("p (u l) -> p u l", l=64), beta[:].unsqueeze(2).to_broadcast([64, 8, 64]), ALU.mult),
                     reads=["dl_beta"], writes=["ps5", "dl_vnew"])
                for u in range(8):
                    P.op("pe", lambda e, u=u: e.matmul(ps[0:64, 6, 64 * u:64 * u + 64], qkd[:, u, :], vnew[:, u, :], start=True, stop=True), reads=["dl_qkd", "dl_vnew"], writes=["ps6"])
                P.op("dve", lambda e, c=c: e.tensor_tensor(osb[:, c, :, :].rearrange("p z (h e) -> p (z h) e", e=64), ps[0:64, 6, :].rearrange("p (u e) -> p u e", e=64), t1[:], ALU.add),
                     reads=[("dl_t1", 0), ("dl_t1", 1)], writes=["ps6", ("dl_osb", c)])
                for z in range(2):
                    for h in range(4):
                        hp, hq = h // 2, h % 2
                        rows = slice(64 * hq, 64 * hq + 64)
                        u = 4 * z + h
                        col = (2 * z + hp) * 64
                        P.op("pe", lambda e, rows=rows, u=u, col=col: e.matmul(ps[rows, 3, col:col + 64], kd[:, u, :], vnew[:, u, :], start=True, stop=True),
                             reads=["dl_kd", "dl_vnew"], writes=["ps3"])
                P.op("dve", lambda e: e.tensor_tensor(tmpS[:], S[:], egls[:].unsqueeze(2).to_broadcast([128, 4, 64]), ALU.mult), reads=["dl_S", "dl_egls"], writes=["dl_tmpS"])
                P.op("dve", lambda e: e.tensor_tensor(S[:], ps[:, 3, 0:256].rearrange("p (a e) -> p a e", e=64), tmpS[:], ALU.add), reads=["dl_tmpS"], writes=["ps3", "dl_S"])
                if not (t % 2 == 1 and c == 1):
                    P.op("act", lambda e: e.activation(Sb[:], S[:], AF.Copy), reads=["dl_S"], writes=["dl_Sb"])
            self.emit_out_tile(t, osb, [("dl_osb", 0), ("dl_osb", 1)], 7)
            if t % 2 == 1:
                P.op("dve", lambda e: e.tensor_copy(Sout[:], S[:]), reads=["dl_S"], writes=["dl_Sout"])
                for z in range(2):
                    seg = (t - 1) // 2 if z == 0 else (NT - 1 - t) // 2
                    P.dma("sp", lambda e, z=z, seg=seg: e.dma_start(
                        out=self.out_delta[seg, l, z].rearrange("(hp hq) d e -> (hq d) hp e", hq=2), in_=Sout[:, 2 * z:2 * z + 2, :]), reads=["dl_Sout"])
                P.op("dve", lambda e: e.tensor_scalar(S[:], S[:], self.keep[:, 0:1], None, ALU.mult), reads=["dl_S", "keep"], writes=["dl_S"])
                P.op("act", lambda e: e.activation(Sb[:], S[:], AF.Copy), reads=["dl_S"], writes=["dl_Sb"])
        if self.debug.get("yacc") == "delta":
            P.dma("sp", lambda e: e.dma_start(out=self.dbg_yacc, in_=self.yacc[:]), reads=[("yacc", i) for i in range(NT)])
        if self.debug.get("post", True):
            self.post_simple(l, "dl", wb, wkeys, 784, AF.Silu, self.dl_norm, False, 256, first)
        self.release(m0)

    def rwkv_mix_fm(self, pre, mu_col, dst, keys_r, keys_w, eng="dve"):
        P = self.P
        n = dst.shape[-1]
        P.op("dve", lambda e: e.tensor_tensor(dst, pre[:, :, 0:n], pre[:, :, 2:n + 2], ALU.add), reads=keys_r, writes=keys_w)
        P.op("dve", lambda e: e.scalar_tensor_tensor(dst, dst, 0.5, pre[:, :, 1:n + 1], ALU.mult, ALU.subtract), reads=keys_r + keys_w, writes=keys_w)
        P.op("dve", lambda e: e.scalar_tensor_tensor(dst, dst, mu_col, pre[:, :, 1:n + 1], ALU.mult, ALU.add), reads=keys_r + keys_w + ["rw_mu"], writes=keys_w)

    def mixer_rwkv(self, l, first):
        P, ps = self.P, self.psum
        m0 = self.mark()
        wb, wkeys = self.load_w("wD", self.wD[l], 1152)
        incl = self.cst("INCL")
        strict = self.cst("STRICT")
        ones64 = self.consts[0:64, CO["ONES"][0]:CO["ONES"][0] + 64]
        ident64 = self.ident[0:64, 0:64]
        bacc = self.sb("rw_bacc", [128, NT, 256], BF16)
        mu = self.sb("rw_mu", [128, 9], F32)
        P.dma("sp", lambda e: e.dma_start(out=mu[:], in_=self.rw_mu[l]), writes=["rw_mu"])
        m1 = self.mark()
        w2p = self.sb("rw_w2p", [128, 2, 256], F32)
        a2p = self.sb("rw_a2p", [128, 2, 256], F32)
        P.dma("sp", lambda e: e.dma_start(out=w2p[:], in_=self.rw_w2p[l]), writes=["rw_w2p"])
        P.dma("sp", lambda e: e.dma_start(out=a2p[:], in_=self.rw_a2p[l]), writes=["rw_a2p"])
        bcs = {}
        for nm, src, width in (("w0", self.rw_w0, 512), ("a0", self.rw_a0, 512), ("kk", self.rw_kk, 256), ("ka", self.rw_ka, 256), ("rk", self.rw_rk, 256)):
            bcs[nm] = self.sb("rw_bc_" + nm, [64, width], F32)
            P.dma("sp", lambda e, nm=nm, src=src: e.dma_start(out=bcs[nm][:], in_=src[l:l + 1, :].partition_broadcast(64)), writes=["rw_bc"])
        omka = self.sb("rw_omka", [64, 256], F32)
        P.op("dve", lambda e: e.tensor_scalar(omka[:], bcs["ka"][:], -1.0, 1.0, ALU.mult, ALU.add), reads=["rw_bc"], writes=["rw_omka"])
        M = self.sb("rw_M", [64, 8, 64], F32)
        pre = self.sb("rw_pre", [128, 2, 130], F32)
        rT = self.sb("rw_rT", [128, 2, 2, 128], BF16)
        kT = self.sb("rw_kT", [128, 2, 2, 128], BF16)
        vT = self.sb("rw_vT", [128, 2, 2, 128], BF16)
        twT = self.sb("rw_twT", [128, 2, 128], F32)
        daT = self.sb("rw_daT", [128, 2, 128], F32)
        rtm = self.sb("rw_rtm", [64, 2, 256], F32)
        ktm = self.sb("rw_ktm", [64, 2, 256], F32)
        vtm = self.sb("rw_vtm", [64, 2, 256], F32)
        lw = self.sb("rw_lw", [64, 2, 256], F32)
        av = self.sb("rw_a", [64, 2, 256], F32)
        ecl = self.sb("rw_ecl", [64, 2, 256], F32)
        encl = self.sb("rw_encl", [64, 2, 256], F32)
        ecw = self.sb("rw_ecw", [64, 2, 256], F32)
        kh = self.sb("rw_kh", [64, 2, 256], F32)
        kx = self.sb("rw_kx", [64, 2, 256], F32)
        ss8 = self.sb("rw_ss8", [64, 8], F32)
        bs8 = self.sb("rw_bs8", [64, 8], F32)
        ktz = self.sb("rw_ktz", [64, 2, 256], F32)
        al = self.sb("rw_al", [64, 2, 256], F32)
        be = self.sb("rw_be", [64, 2, 256], F32)
        kti = self.sb("rw_kti", [64, 2, 256], F32)
        rti = self.sb("rw_rti", [64, 2, 256], F32)
        beT = self.sb("rw_beT", [64, 8, 64], F32)
        Pm = lw[:].rearrange("p z (h j) -> p (z h) j", j=64)
        Qm = av[:].rearrange("p z (h j) -> p (z h) j", j=64)
        Aak = ecl[:].rearrange("p z (h j) -> p (z h) j", j=64)
        Ara = encl[:].rearrange("p z (h j) -> p (z h) j", j=64)
        Ark = ecw[:].rearrange("p z (h j) -> p (z h) j", j=64)
        X1 = kh[:].rearrange("p z (h j) -> p (z h) j", j=64)
        Uu = kx[:].rearrange("p z (h j) -> p (z h) j", j=64)
        tmpM = ktz[:].rearrange("p z (h j) -> p (z h) j", j=64)
        alT = rtm[:].rearrange("p z (h j) -> p (z h) j", j=64)
        ktT = ktm[:].rearrange("p z (h j) -> p (z h) j", j=64)
        rtT = be[:].rearrange("p z (h j) -> p (z h) j", j=64)
        Mt = kh[:].rearrange("p z (h j) -> p (z h) j", j=64)
        Rm = rti[:].rearrange("p z (h j) -> p (z h) j", j=64)
        WL = self.sb("rw_WL", [64, 8], F32)
        P.dma("sp", lambda e: e.dma_start(out=Mt[:], in_=self.init_rwkv[l].rearrange("z h i j -> i (z h) j")), writes=["rw_kh"])
        for u in range(8):
            P.op("pe", lambda e, u=u: e.transpose(ps[0:64, 2, 64 * u:64 * u + 64], Mt[:, u, :], ident64), reads=["rw_kh", "ident"], writes=["ps2"])
        P.op("dve", lambda e: e.tensor_copy(M[:], ps[0:64, 2, :].rearrange("p (u e) -> p u e", e=64)), writes=["ps2", "rw_M"])
        osb = self.sb("rw_osb", [64, 2, 512], F32)
        v3 = lambda x_: x_[:].rearrange("p z (h j) -> p (z h) j", j=64)
        for t in range(NT):
            tiles = (t, NT - 1 - t)
            edge = slice(0, 1) if t % 2 == 0 else slice(129, 130)
            for j in range(9):
                bank = j % 2
                self.proj_fm(wb, wkeys, 128 * j, 128, tiles, bank, halo=1)
                self.evac_fm(lambda z: pre[:, z, :], bank, 128, "rw_pre", W=130, eng="act")
                P.op("dve", lambda e: e.tensor_scalar(pre[:, :, edge], pre[:, :, edge], self.keep[:, 0:1], None, ALU.mult), reads=["rw_pre", "keep"], writes=["rw_pre"])
                if j < 6:
                    dst, dk = ((rT, "rw_rT"), (kT, "rw_kT"), (vT, "rw_vT"))[j // 2]
                    dst = dst[:, j % 2, :, :]
                elif j == 6:
                    dst, dk = twT[:], "rw_twT"
                elif j == 7:
                    dst, dk = daT[:], "rw_daT"
                else:
                    continue
                self.rwkv_mix_fm(pre, mu[:, j:j + 1], dst, ["rw_pre"], [dk])
                if j == 6:
                    P.op("act", lambda e: e.activation(twT[:], twT[:], AF.Tanh), reads=["rw_twT"], writes=["rw_twT"])
            for c in range(2):
                cs = slice(64 * c, 64 * c + 64)
                for z in range(2):
                    P.op("pe", lambda e, z=z: e.matmul(ps[0:64, 2, 256 * z:256 * z + 256], twT[:, z, cs], w2p[:, z, :], start=True, stop=True), reads=["rw_twT", "rw_w2p"], writes=["ps2"])
                    P.op("pe", lambda e, z=z: e.matmul(ps[0:64, 3, 256 * z:256 * z + 256], daT[:, z, cs], a2p[:, z, :], start=True, stop=True), reads=["rw_daT", "rw_a2p"], writes=["ps3"])
                lwf = lw[:].rearrange("p z n -> p (z n)")
                avf = av[:].rearrange("p z n -> p (z n)")
                P.op("dve", lambda e: e.tensor_tensor(lwf, ps[0:64, 2, :], bcs["w0"][:], ALU.add), reads=["rw_bc"], writes=["ps2", "rw_lw"])
                self.act_sigmoid(lwf, lwf, ["rw_lw"], ["rw_lw"])
                P.op("dve", lambda e: e.tensor_scalar(lwf, lwf, -float(np.exp(-0.5)), None, ALU.mult), reads=["rw_lw"], writes=["rw_lw"])
                P.op("dve", lambda e: e.tensor_tensor(avf, ps[0:64, 3, :], bcs["a0"][:], ALU.add), reads=["rw_bc"], writes=["ps3", "rw_a"])
                self.act_sigmoid(avf, avf, ["rw_a"], ["rw_a"])
                P.op("pe", lambda e: e.matmul(ps[0:64, 4, :], incl, lwf, start=True, stop=True), reads=["consts", "rw_lw"], writes=["ps4"])
                for u in range(8):
                    z, h = u // 4, u % 4
                    P.op("pe", lambda e, u=u, z=z, h=h: e.matmul(ps[0:64, 7, 64 + u:65 + u], lw[:, z, 64 * h:64 * h + 64], ones64[:, 0:1], start=True, stop=True),
                         reads=["rw_lw", "consts"], writes=["ps7"])
                P.op("act", lambda e: e.activation(WL[:], ps[0:64, 7, 64:72], AF.Exp), writes=["ps7", "rw_WL"])
                eclf = ecl[:].rearrange("p z n -> p (z n)")
                P.op("act", lambda e: e.activation(eclf, ps[0:64, 4, :], AF.Exp), writes=["ps4", "rw_ecl"])
                P.op("act", lambda e: e.activation(encl[:].rearrange("p z n -> p (z n)"), ps[0:64, 4, :], AF.Exp, scale=-1.0), writes=["ps4", "rw_encl"])
                P.op("dve", lambda e: e.tensor_tensor(ecw[:].rearrange("p z n -> p (z n)"), ps[0:64, 4, :], lwf, ALU.subtract), reads=["rw_lw"], writes=["ps4", "rw_ecw"])
                P.op("act", lambda e: e.activation(ecw[:], ecw[:], AF.Exp), reads=["rw_ecw"], writes=["rw_ecw"])
                for (src, skey, dstt, dkey, bank) in ((rT, "rw_rT", rtm, "rw_rtm", 5), (kT, "rw_kT", ktm, "rw_ktm", 6), (vT, "rw_vT", vtm, "rw_vtm", 5)):
                    bv = ps[:, bank, :].bitcast(BF16)
                    for z in range(2):
                        for hp in range(2):
                            col = (4 * z + 2 * hp) * 64
                            P.op("pe", lambda e, src=src, z=z, hp=hp, col=col, bv=bv: e.transpose(bv[0:64, col:col + 128], src[:, hp, z, cs], self.ident_bf[:]),
                                 reads=[skey, "ident_bf"], writes=["ps%d" % bank])
                    P.op("act", lambda e, dstt=dstt, bv=bv: e.activation(dstt[:].rearrange("p z n -> p (z n)"), bv[0:64, 0:512], AF.Copy), writes=["ps%d" % bank, dkey])
                kk2 = bcs["kk"][:].unsqueeze(1).to_broadcast([64, 2, 256])
                ka2 = bcs["ka"][:].unsqueeze(1).to_broadcast([64, 2, 256])
                P.op("dve", lambda e: e.tensor_tensor(kx[:], ktm[:], kk2, ALU.mult), reads=["rw_ktm", "rw_bc"], writes=["rw_kx"])
                P.op("act", lambda e: e.activation(kh[:], kx[:], AF.Square), reads=["rw_kx"], writes=["rw_kh"])
                P.op("dve", lambda e: e.tensor_reduce(ss8[:], v3(kh), AX.X, ALU.add), reads=["rw_kh"], writes=["rw_ss8"])
                P.op("act", lambda e: e.activation(ss8[:], ss8[:], AF.Ln, bias=1e-6, scale=1.0), reads=["rw_ss8"], writes=["rw_ss8"])
                P.op("act", lambda e: e.activation(ss8[:], ss8[:], AF.Exp, scale=-0.5), reads=["rw_ss8"], writes=["rw_ss8"])
                P.op("dve", lambda e: e.tensor_tensor(v3(kh), v3(kx), ss8[:].unsqueeze(2).to_broadcast([64, 8, 64]), ALU.mult), reads=["rw_kx", "rw_ss8"], writes=["rw_kh"])
                P.op("dve", lambda e: e.tensor_tensor(ktz[:], av[:], ka2, ALU.mult), reads=["rw_a", "rw_bc"], writes=["rw_ktz"])
                P.op("dve", lambda e: e.tensor_tensor(ktz[:], ktz[:], omka[:].unsqueeze(1).to_broadcast([64, 2, 256]), ALU.add), reads=["rw_ktz", "rw_omka"], writes=["rw_ktz"])
                P.op("dve", lambda e: e.tensor_tensor(ktz[:], ktz[:], ktm[:], ALU.mult), reads=["rw_ktz", "rw_ktm"], writes=["rw_ktz"])
                P.op("dve", lambda e: e.tensor_tensor(al[:], av[:], kh[:], ALU.mult), reads=["rw_a", "rw_kh"], writes=["rw_al"])
                P.op("dve", lambda e: e.scalar_tensor_tensor(al[:], al[:], -1.0, encl[:], ALU.mult, ALU.mult), reads=["rw_al", "rw_encl"], writes=["rw_al"])
                P.op("dve", lambda e: e.tensor_tensor(be[:], kh[:], ecw[:], ALU.mult), reads=["rw_kh", "rw_ecw"], writes=["rw_be"])
                P.op("dve", lambda e: e.tensor_tensor(kti[:], ktz[:], encl[:], ALU.mult), reads=["rw_ktz", "rw_encl"], writes=["rw_kti"])
                P.op("dve", lambda e: e.tensor_tensor(rti[:], rtm[:], ecl[:], ALU.mult), reads=["rw_rtm", "rw_ecl"], writes=["rw_rti"])
                P.op("dve", lambda e: e.tensor_tensor(kx[:], rtm[:], ktz[:], ALU.mult), reads=["rw_rtm", "rw_ktz"], writes=["rw_kx"])
                P.op("dve", lambda e: e.tensor_tensor(kx[:], kx[:], bcs["rk"][:].unsqueeze(1).to_broadcast([64, 2, 256]), ALU.mult), reads=["rw_kx", "rw_bc"], writes=["rw_kx"])
                P.op("dve", lambda e: e.tensor_reduce(bs8[:], v3(kx), AX.X, ALU.add), reads=["rw_kx"], writes=["rw_bs8"])
                for z in range(2):
                    P.op("dve", lambda e, z=z, c=c: e.tensor_tensor(osb[:, z, 256:512].rearrange("p (h j) -> p h j", j=64), vtm[:, z, :].rearrange("p (h j) -> p h j", j=64),
                                                               bs8[:, 4 * z:4 * z + 4].unsqueeze(2).to_broadcast([64, 4, 64]), ALU.mult),
                         reads=["rw_vtm", "rw_bs8"], writes=["rw_osb"])
                for (src, skey, dstT, dkey, bank) in ((al, "rw_al", alT, "rw_rtm", 2), (be, "rw_be", beT, "rw_beT", 3), (kti, "rw_kti", ktT, "rw_ktm", 4), (rti, "rw_rti", rtT, "rw_be", 5)):
                    for u in range(8):
                        z, h = u // 4, u % 4
                        P.op("pe", lambda e, src=src, u=u, z=z, h=h, bank=bank: e.transpose(ps[0:64, bank, 64 * u:64 * u + 64], src[:, z, 64 * h:64 * h + 64], ident64),
                             reads=[skey, "ident"], writes=["ps%d" % bank])
                    P.op("act", lambda e, dstT=dstT, bank=bank: e.activation(dstT[:], ps[0:64, bank, :].rearrange("p (u l) -> p u l", l=64), AF.Copy), writes=["ps%d" % bank, dkey])
                for (lhs, lkey, rhs, rkey, bank, msk, dst, dkey) in ((alT, "rw_rtm", beT, "rw_beT", 2, strict, Pm, "rw_lw"), (ktT, "rw_ktm", beT, "rw_beT", 3, strict, Aak, "rw_ecl"),
                                                                      (alT, "rw_rtm", rtT, "rw_be", 4, incl, Ara, "rw_encl"), (ktT, "rw_ktm", rtT, "rw_be", 5, incl, Ark, "rw_ecw")):
                    for u in range(8):
                        P.op("pe", lambda e, lhs=lhs, rhs=rhs, u=u, bank=bank: e.matmul(ps[0:64, bank, 64 * u:64 * u + 64], lhs[:, u, :], rhs[:, u, :], start=True, stop=True),
                             reads=[lkey, rkey], writes=["ps%d" % bank])
                    P.op("dve", lambda e, bank=bank, msk=msk, dst=dst: e.tensor_tensor(dst[:], ps[0:64, bank, :].rearrange("p (u l) -> p u l", l=64),
                                                                                     msk.unsqueeze(1).to_broadcast([64, 8, 64]), ALU.mult),
                         reads=["consts"], writes=["ps%d" % bank, dkey])
                self.neumann_inverse("rw_", Pm, Qm, Rm, (2, 3, 4), keys=("rw_lw", "rw_a", "rw_rti"))
                for u in range(8):
                    z, h = u // 4, u % 4
                    P.op("pe", lambda e, u=u: e.matmul(ps[0:64, 5, 64 * u:64 * u + 64], beT[:, u, :], M[:, u, :], start=True, stop=False), reads=["rw_beT", "rw_M"], writes=["ps5"])
                    P.op("pe", lambda e, u=u, z=z, h=h: e.matmul(ps[0:64, 5, 64 * u:64 * u + 64], Aak[:, u, :], vtm[:, z, 64 * h:64 * h + 64], start=False, stop=True),
                         reads=["rw_ecl", "rw_vtm"], writes=["ps5"])
                P.op("act", lambda e: e.activation(X1[:], ps[0:64, 5, :].rearrange("p (u e) -> p u e", e=64), AF.Copy), writes=["ps5", "rw_kh"])
                for u in range(8):
                    P.op("pe", lambda e, u=u: e.matmul(ps[0:64, 6, 64 * u:64 * u + 64], Rm[:, u, :], X1[:, u, :], start=True, stop=True), reads=["rw_rti", "rw_kh"], writes=["ps6"])
                P.op("act", lambda e: e.activation(Uu[:], ps[0:64, 6, :].rearrange("p (u e) -> p u e", e=64), AF.Copy), writes=["ps6", "rw_kx"])
                for u in range(8):
                    z, h = u // 4, u % 4
                    vu = vtm[:, z, 64 * h:64 * h + 64]
                    P.op("pe", lambda e, u=u: e.matmul(ps[0:64, 5, 64 * u:64 * u + 64], rtT[:, u, :], M[:, u, :], start=True, stop=False), reads=["rw_be", "rw_M"], writes=["ps5"])
                    P.op("pe", lambda e, u=u: e.matmul(ps[0:64, 5, 64 * u:64 * u + 64], Ara[:, u, :], Uu[:, u, :], start=False, stop=False), reads=["rw_encl", "rw_kx"], writes=["ps5"])
                    P.op("pe", lambda e, u=u, vu=vu: e.matmul(ps[0:64, 5, 64 * u:64 * u + 64], Ark[:, u, :], vu, start=False, stop=True), reads=["rw_ecw", "rw_vtm"], writes=["ps5"])
                for z in range(2):
                    P.op("act", lambda e, z=z, c=c: e.activation(osb[:, z, 0:256], ps[0:64, 5, 256 * z:256 * z + 256], AF.Copy), writes=["ps5", "rw_osb"])
                for u in range(8):
                    z, h = u // 4, u % 4
                    vu = vtm[:, z, 64 * h:64 * h + 64]
                    P.op("pe", lambda e, u=u, z=z, h=h: e.matmul(ps[0:64, 6, 64 * u:64 * u + 64], al[:, z, 64 * h:64 * h + 64], Uu[:, u, :], start=True, stop=False), reads=["rw_al", "rw_kx"], writes=["ps6"])
                    P.op("pe", lambda e, u=u, z=z, h=h, vu=vu: e.matmul(ps[0:64, 6, 64 * u:64 * u + 64], kti[:, z, 64 * h:64 * h + 64], vu, start=False, stop=True), reads=["rw_kti", "rw_vtm"], writes=["ps6"])
                P.op("dve", lambda e: e.tensor_tensor(tmpM[:], ps[0:64, 6, :].rearrange("p (u e) -> p u e", e=64), M[:], ALU.add), reads=["rw_M"], writes=["ps6", "rw_ktz"])
                P.op("dve", lambda e: e.tensor_tensor(M[:], tmpM[:], WL[:].unsqueeze(2).to_broadcast([64, 8, 64]), ALU.mult), reads=["rw_ktz", "rw_WL"], writes=["rw_M"])
                self.emit_out_tile2(t, c, osb, ["rw_osb"], bacc)
            if t % 2 == 1:
                for u in range(8):
                    P.op("pe", lambda e, u=u: e.transpose(ps[0:64, 2, 64 * u:64 * u + 64], M[:, u, :], ident64), reads=["rw_M", "ident"], writes=["ps2"])
                P.op("dve", lambda e: e.tensor_copy(Mt[:], ps[0:64, 2, :].rearrange("p (u e) -> p u e", e=64)), writes=["ps2", "rw_kh"])
                for z in range(2):
                    seg = (t - 1) // 2 if z == 0 else (NT - 1 - t) // 2
                    P.dma("sp", lambda e, z=z, seg=seg: e.dma_start(out=self.out_rwkv[seg, l, z].rearrange("h i j -> i h j"), in_=Mt[:, 4 * z:4 * z + 4, :]), reads=["rw_kh"])
                P.op("dve", lambda e: e.tensor_scalar(M[:], M[:], self.keep[0:64, 0:1], None, ALU.mult), reads=["rw_M", "keep"], writes=["rw_M"])
        if self.debug.get("yacc") == "rwkv":
            P.dma("sp", lambda e: e.dma_start(out=self.dbg_yacc, in_=self.yacc[:]), reads=[("yacc", i) for i in range(NT)])
        if self.debug.get("yacc") == "rwkv_bonus":
            P.dma("sp", lambda e: e.dma_start(out=self.dbg_yacc, in_=bacc[:]), reads=[("bacc", i) for i in range(NT)])
        self.release(m1)
        if self.debug.get("post", True):
            self.post_rwkv(l, wb, wkeys, bacc, mu, first)
        self.release(m0)

    def emit_out_tile2(self, t, c, osb, osb_keys, bacc):
        P, ps = self.P, self.psum
        sel = self.cst("SEL").rearrange("p (q n) -> p q n", n=128)
        for z in range(2):
            bank = z
            pk = "ps%d" % bank
            tile = t if z == 0 else NT - 1 - t
            P.op("pe", lambda e, z=z, bank=bank: e.matmul(ps[:, bank, :], sel[:, 2 * z + c, :], osb[:, z, :], start=(c == 0), stop=(c == 1)),
                 reads=["consts"] + osb_keys, writes=[pk])
            if c == 0:
                continue
            yk = ("yacc", tile)
            bk = ("bacc", tile)
            if t < NT // 2:
                P.op("act", lambda e, tile=tile, bank=bank: e.activation(self.yacc[:, tile, :], ps[:, bank, 0:256], AF.Copy), writes=[pk, yk])
                P.op("act", lambda e, tile=tile, bank=bank: e.activation(bacc[:, tile, :], ps[:, bank, 256:512], AF.Copy), writes=[pk, bk])
            else:
                P.op("dve", lambda e, tile=tile, bank=bank: e.tensor_tensor(self.yacc[:, tile, :], ps[:, bank, 0:256], self.yacc[:, tile, :], ALU.add), writes=[pk, yk])
                P.op("dve", lambda e, tile=tile, bank=bank: e.tensor_tensor(bacc[:, tile, :], ps[:, bank, 256:512], bacc[:, tile, :], ALU.add), writes=[pk, bk])

    def post_rwkv(self, l, wb, wkeys, bacc, mu, first):
        P, ps = self.P, self.psum
        m0 = self.mark()
        wo, wokeys = self.load_w("wo_rw", self.w_out[l][768:1024, :], D, kchunks=2)
        gain_bc = self.sb("hn_gain", [128, 256], F32)
        P.dma("sp", lambda e: e.dma_start(out=gain_bc[:], in_=self.rw_norm[l:l + 1, :].partition_broadcast(128)), writes=["hn_gain"])
        g2 = self.sb("rw_g2", [128, 256], F32)
        P.dma("sp", lambda e: e.dma_start(out=g2[:], in_=self.rw_g2[l]), writes=["rw_g2"])
        tmp = (self.sb("hn_yc", [128, 256], F32), self.sb("hn_sq", [128, 256], F32), self.sb("hn_st", [128, 2, 4], F32),
               self.sb("yact", [128, 256], F32))
        gate = self.sb("hn_gate", [128, 256], F32)
        otmp = (self.sb("yTm", [128, 2, 128], BF16), self.sb("gtmp", [128, D], F32))
        pre1 = self.sb("rwp_pre", [128, 1, 130], F32)
        sg = self.sb("rwp_sg", [128, 1, 128], F32)
        for i in range(NT):
            tok0 = 2 + 128 * i - 1
            for k in range(8):
                P.op("pe", lambda e, k=k, tok0=tok0: e.matmul(ps[:, 0, 0:130], wb[:, k, 1024:1152], self.uT[:, k, tok0:tok0 + 130], start=(k == 0), stop=(k == 7)),
                     reads=wkeys + [("uT", i)] + ([("uT", i - 1)] if i > 0 else []) + ([("uT", i + 1)] if i < NT - 1 else []), writes=["ps0"])
            P.op("act", lambda e: e.activation(pre1[:, 0, :], ps[:, 0, 0:130], AF.Copy), writes=["ps0", "rwp_pre"])
            edge = slice(0, 1) if i % 2 == 0 else slice(129, 130)
            P.op("dve", lambda e, edge=edge: e.tensor_scalar(pre1[:, :, edge], pre1[:, :, edge], self.keep[:, 0:1], None, ALU.mult), reads=["rwp_pre", "keep"], writes=["rwp_pre"])
            self.rwkv_mix_fm(pre1, mu[:, 8:9], sg[:], ["rwp_pre"], ["rwp_sg"])
            self.act_sigmoid(sg[:], sg[:], ["rwp_sg"], ["rwp_sg"])
            P.op("pe", lambda e: e.matmul(ps[:, 1, 0:256], sg[:, 0, :], g2[:], start=True, stop=True), reads=["rwp_sg", "rw_g2"], writes=["ps1"])
            P.op("act", lambda e: e.activation(gate[:], ps[:, 1, 0:256], AF.Copy), writes=["ps1", "hn_gate"])
            self.head_norm_tile(i, True, gain_bc, gate[:], ["hn_gate"], tmp, extra=(bacc[:, i, :], [("bacc", i)]))
            self.out_proj_tile(i, tmp[3], wo, wokeys, first, otmp)
        self.release(m0)

    def head_norm_tile(self, i, center, gain_bc, gate_ap, gate_keys, tmp, extra=None):
        P = self.P
        yc, sq, st4, yact = tmp
        y3 = self.yacc[:, i, :].rearrange("p (h e) -> p h e", e=64)
        yk = ("yacc", i)
        yc3 = yc[:].rearrange("p (h e) -> p h e", e=64)
        if center:
            P.op("dve", lambda e: e.tensor_reduce(st4[:, 0, :], y3, AX.X, ALU.add), reads=[yk], writes=["hn_st"])
            P.op("dve", lambda e: e.tensor_scalar(st4[:, 0, :], st4[:, 0, :], -1.0 / 64, None, ALU.mult), reads=["hn_st"], writes=["hn_st"])
            P.op("dve", lambda e: e.tensor_tensor(yc3, y3, st4[:, 0, :].unsqueeze(2).to_broadcast([128, 4, 64]), ALU.add),
                 reads=[yk, "hn_st"], writes=["hn_yc"])
        else:
            P.op("dve", lambda e: e.tensor_copy(yc[:], self.yacc[:, i, :]), reads=[yk], writes=["hn_yc"])
        P.op("act", lambda e: e.activation(sq[:], yc[:], AF.Square), reads=["hn_yc"], writes=["hn_sq"])
        P.op("dve", lambda e: e.tensor_reduce(st4[:, 1, :], sq[:].rearrange("p (h e) -> p h e", e=64), AX.X, ALU.add), reads=["hn_sq"], writes=["hn_st1"])
        P.op("act", lambda e: e.activation(st4[:, 1, :], st4[:, 1, :], AF.Ln, bias=LN_EPS, scale=1.0 / 64), reads=["hn_st1"], writes=["hn_st1"])
        P.op("act", lambda e: e.activation(st4[:, 1, :], st4[:, 1, :], AF.Exp, scale=-0.5), reads=["hn_st1"], writes=["hn_st1"])
        P.op("dve", lambda e: e.tensor_tensor(yc3, yc3, st4[:, 1, :].unsqueeze(2).to_broadcast([128, 4, 64]), ALU.mult),
             reads=["hn_yc", "hn_st1"], writes=["hn_yc"])
        P.op("dve", lambda e: e.tensor_tensor(yc[:], yc[:], gain_bc[:], ALU.mult), reads=["hn_yc", "hn_gain"], writes=["hn_yc"])
        if extra is not None:
            P.op("dve", lambda e: e.tensor_tensor(yc[:], yc[:], extra[0], ALU.add), reads=["hn_yc"] + extra[1], writes=["hn_yc"])
        P.op("dve", lambda e: e.tensor_tensor(yact[:], yc[:], gate_ap, ALU.mult), reads=["hn_yc"] + gate_keys, writes=["yact"])

    def out_proj_tile(self, i, yact, wo, wokeys, first, tmp):
        P, ps = self.P, self.psum
        yTm, gtmp = tmp
        for kk in range(2):
            P.op("pe", lambda e, kk=kk: e.transpose(ps[:, 2, 128 * kk:128 * kk + 128], yact[:, 128 * kk:128 * kk + 128], self.ident[:]),
                 reads=["yact", "ident"], writes=["ps2"])
        P.op("act", lambda e: e.activation(yTm[:], ps[:, 2, 0:256].rearrange("p (k n) -> p k n", n=128), AF.Copy), writes=["ps2", "yTm"])
        for n in range(2):
            for kk in range(2):
                P.op("pe", lambda e, n=n, kk=kk: e.matmul(ps[:, 4 + n, :], yTm[:, kk, :], wo[:, kk, 512 * n:512 * n + 512],
                                                           start=(kk == 0), stop=(kk == 1)),
                     reads=["yTm"] + wokeys, writes=["ps%d" % (4 + n)])
        xk = ("xres", i)
        for n in range(2):
            cols = slice(512 * n, 512 * n + 512)
            P.op("dve", lambda e, n=n, cols=cols: e.tensor_tensor(gtmp[:, cols], ps[:, 4 + n, :], self.g_bc["g1"][:, cols], ALU.mult),
                 reads=["g1_bc"], writes=["ps%d" % (4 + n), "gtmp"])
        P.op("dve", lambda e: e.scalar_tensor_tensor(self.xres[:, i, :], self.xres[:, i, :], ALPHA if first else 1.0, gtmp[:], ALU.mult, ALU.add),
             reads=["gtmp"], writes=[xk])

    def post_simple(self, l, name, wb, wkeys, gcol0, gate_func, norm_dram, center, wo_row0, first):
        P, ps = self.P, self.psum
        m0 = self.mark()
        wo, wokeys = self.load_w("wo_" + name, self.w_out[l][wo_row0:wo_row0 + 256, :], D, kchunks=2)
        gain_bc = self.sb("hn_gain", [128, 256], F32)
        P.dma("sp", lambda e: e.dma_start(out=gain_bc[:], in_=norm_dram[l:l + 1, :].partition_broadcast(128)), writes=["hn_gain"])
        tmp = (self.sb("hn_yc", [128, 256], F32), self.sb("hn_sq", [128, 256], F32), self.sb("hn_st", [128, 2, 4], F32),
               self.sb("yact", [128, 256], F32))
        gate = self.sb("hn_gate", [128, 256], F32)
        otmp = (self.sb("yTm", [128, 2, 128], BF16), self.sb("gtmp", [128, D], F32))
        for i in range(NT):
            for k in range(8):
                P.op("pe", lambda e, k=k, i=i: e.matmul(ps[:, 0, 0:256], self.uT[:, k, 2 + 128 * i:2 + 128 * i + 128], wb[:, k, gcol0:gcol0 + 256],
                                                         start=(k == 0), stop=(k == 7)),
                     reads=wkeys + [("uT", i)], writes=["ps0"])
            self.act_sigmoid(gate[:], ps[:, 0, 0:256], [], ["ps0", "hn_gate"])
            if gate_func == AF.Silu:
                P.op("dve", lambda e: e.tensor_tensor(gate[:], ps[:, 0, 0:256], gate[:], ALU.mult), writes=["ps0", "hn_gate"])
            self.head_norm_tile(i, center, gain_bc, gate[:], ["hn_gate"], tmp)
            self.out_proj_tile(i, tmp[3], wo, wokeys, first, otmp)
        self.release(m0)

    def ln_affine_tile(self, i, g_bc, b_bc):
        P = self.P
        stats, mv, rstd, xhat = self.lntmp
        xk = ("xres", i)
        src = self.xres[:, i, :]
        for hh in range(2):
            P.op("dve", lambda e, hh=hh: e.bn_stats(stats[:, hh, :], src[:, 512 * hh:512 * hh + 512]), reads=[xk], writes=["lnstats"])
        P.op("dve", lambda e: e.bn_aggr(mv[:], stats[:]), reads=["lnstats"], writes=["lnmv"])
        P.op("act", lambda e: e.activation(rstd[:], mv[:, 1:2], AF.Ln, bias=LN_EPS, scale=1.0), reads=["lnmv"], writes=["lnrstd"])
        P.op("act", lambda e: e.activation(rstd[:], rstd[:], AF.Exp, scale=-0.5), reads=["lnrstd"], writes=["lnrstd"])
        P.op("dve", lambda e: e.tensor_scalar(xhat[:], src, mv[:, 0:1], rstd[:, 0:1], ALU.subtract, ALU.mult),
             reads=[xk, "lnmv", "lnrstd"], writes=["xhat"])
        P.op("pool", lambda e: e.tensor_tensor(xhat[:], xhat[:], g_bc[:], ALU.mult), reads=["xhat", "lnbc"], writes=["xhat"])
        P.op("pool", lambda e: e.tensor_tensor(src, xhat[:], b_bc[:], ALU.add), reads=["xhat", "lnbc"], writes=[xk])

    def phase_c(self, l):
        P, ps = self.P, self.psum
        m0 = self.mark()
        self.lntmp = (self.sb("lnstats", [128, 2, 6], F32), self.sb("lnmv", [128, 2], F32),
                      self.sb("lnrstd", [128, 1], F32), self.sb("xhat", [128, D], F32))
        bc = {}
        for nm, src in (("ln1_g", self.ln1_g), ("ln1_b", self.ln1_b), ("ln2_g", self.ln2_g), ("ln2_b", self.ln2_b)):
            bc[nm] = self.sb("bc_" + nm, [128, D], F32)
            P.dma("sp", lambda e, nm=nm, src=src: e.dma_start(out=bc[nm][:], in_=src[l:l + 1, :].partition_broadcast(128)), writes=["lnbc"])
        u2T = self.sb("u2T", [128, 8, TOK], BF16)
        hT = [self.sb("hT%d" % i, [128, 4, 512], BF16) for i in range(2)]
        rtmp = [self.sb("ffn_rtmp%d" % i, [128, 512], F32) for i in range(2)]
        gtmp = [self.sb("ffn_gtmp%d" % i, [128, 512], F32) for i in range(2)]
        w1b = [self.sb("w1b%d" % i, [128, 8, 512], BF16) for i in range(2)]
        w2b = [self.sb("w2b%d" % i, [128, 4, 1024], BF16) for i in range(2)]
        w1v = self.w_ff1[l].rearrange("(k p) n -> p k n", p=128)
        w2v = self.w_ff2[l].rearrange("(c p) n -> p c n", p=128)
        for i in range(NT):
            self.ln_affine_tile(i, bc["ln1_g"], bc["ln1_b"])
            self.ln_mod_T(i, lambda k, i=i: u2T[:, k, 128 * i:128 * i + 128], self.sc2p, 24, self.lntmp, [("u2T", i)])
        nh = 0
        ng_ = 0
        for sl in range(8):
            wa = w1b[sl % 2]
            wbk = w2b[sl % 2]
            ka = "w1b%d" % (sl % 2)
            kb = "w2b%d" % (sl % 2)
            for kh in range(2):
                P.dma("pool", lambda e, wa=wa, sl=sl, kh=kh: e.dma_start(out=wa[:, 4 * kh:4 * kh + 4, :], in_=w1v[:, 4 * kh:4 * kh + 4, 512 * sl:512 * sl + 512]), writes=[(ka, kh)])
            for kh in range(2):
                P.dma("pool", lambda e, wbk=wbk, sl=sl, kh=kh: e.dma_start(out=wbk[:, 2 * kh:2 * kh + 2, :], in_=w2v[:, 4 * sl + 2 * kh:4 * sl + 2 * kh + 2, :]), writes=[(kb, kh)])
            wakeys = [(ka, 0), (ka, 1)]
            wbkeys = [(kb, 0), (kb, 1)]
            for blk in range(4):
                hb = hT[nh % 2]
                hk = "hT%d" % (nh % 2)
                nh += 1
                u2keys = [("u2T", i) for i in range(4 * blk, 4 * blk + 4)]
                for cc in range(4):
                    bank = cc % 2
                    for k in range(8):
                        P.op("pe", lambda e, wa=wa, cc=cc, k=k, bank=bank, blk=blk: e.matmul(
                            ps[:, bank, :], wa[:, k, 128 * cc:128 * cc + 128], u2T[:, k, 512 * blk:512 * blk + 512], start=(k == 0), stop=(k == 7)),
                            reads=wakeys + u2keys, writes=["ps%d" % bank])
                    rt = rtmp[cc % 2]
                    rk = "ffn_rtmp%d" % (cc % 2)
                    P.op("act", lambda e, rt=rt, bank=bank: e.activation(rt[:], ps[:, bank, :], AF.Relu), writes=["ps%d" % bank, rk])
                    P.op("pool", lambda e, rt=rt, cc=cc, hb=hb: e.tensor_tensor(hb[:, cc, :], rt[:], rt[:], ALU.mult), reads=[rk], writes=[(hk, cc)])
                hkeys = [(hk, cc) for cc in range(4)]
                for j in range(4):
                    i = 4 * blk + j
                    for n in range(2):
                        bank = 2 + (ng_ % 4)
                        gt = gtmp[ng_ % 2]
                        gk = "ffn_gtmp%d" % (ng_ % 2)
                        ng_ += 1
                        cols = slice(512 * n, 512 * n + 512)
                        for hc in range(4):
                            P.op("pe", lambda e, wbk=wbk, hc=hc, j=j, bank=bank, hb=hb, cols=cols: e.matmul(
                                ps[:, bank, :], hb[:, hc, 128 * j:128 * j + 128], wbk[:, hc, cols], start=(hc == 0), stop=(hc == 3)),
                                reads=wbkeys + hkeys, writes=["ps%d" % bank])
                        P.op("dve", lambda e, bank=bank, cols=cols, gt=gt: e.tensor_tensor(gt[:], ps[:, bank, :], self.g_bc["g2"][:, cols], ALU.mult),
                             reads=["g2_bc"], writes=["ps%d" % bank, gk])
                        P.op("dve", lambda e, i=i, cols=cols, gt=gt, sl=sl: e.scalar_tensor_tensor(
                            self.xres[:, i, cols], self.xres[:, i, cols], ALPHA if sl == 0 else 1.0, gt[:], ALU.mult, ALU.add),
                            reads=[gk], writes=[("xres", i)])
        for i in range(NT):
            self.ln_affine_tile(i, bc["ln2_g"], bc["ln2_b"])
        self.release(m0)

    def alloc_lntmp(self):
        self.lntmp = (self.sb("lnstats", [128, 2, 6], F32), self.sb("lnmv", [128, 2], F32),
                      self.sb("lnrstd", [128, 1], F32), self.sb("xhat", [128, D], F32))

    def layer(self, l):
        P = self.P
        self.compute_mod(l)
        mL = self.mark()
        self.uT = self.sb("uT", [128, 8, TOK + 4], BF16)
        self.yacc = self.sb("yacc", [128, NT, 256], F32)
        P.op("dve", lambda e: e.memset(self.uT[:, :, 0:2], 0.0), writes=["uTpadL"])
        P.op("dve", lambda e: e.memset(self.uT[:, :, TOK + 2:TOK + 4], 0.0), writes=["uTpadR"])
        mA = self.mark()
        self.alloc_lntmp()
        for i in range(NT):
            self.ln_mod_T(i, lambda k, i=i: self.uT[:, k, 2 + 128 * i:2 + 128 * i + 128], self.sc1p, 0, self.lntmp, [("uT", i)])
        self.release(mA)
        if "uT" in self.debug and l == self.debug["uT"]:
            mm_ = self.mark()
            dbgf = self.sb("dbgf", [128, 8, 512], F32)
            for q in range(4):
                P.op("dve", lambda e, q=q: e.tensor_copy(dbgf[:], self.uT[:, :, 2 + 512 * q:2 + 512 * q + 512]),
                     reads=[("uT", i) for i in range(4 * q, 4 * q + 4)], writes=["dbgf"])
                P.dma("sp", lambda e, q=q: e.dma_start(out=self.dbg_uT[:, :, 512 * q:512 * q + 512], in_=dbgf[:]), reads=["dbgf"])
            self.release(mm_)
        mixers = self.debug.get("mixers", ["mlstm", "delta", "ret", "rwkv"])
        first = True
        for mx in mixers:
            getattr(self, "mixer_" + mx)(l, first)
            first = False
        self.release(mL)
        if self.debug.get("phase_c", True):
            self.phase_c(l)

    def finish(self):
        P = self.P
        yv = self.y_out.rearrange("(i p) d -> p i d", p=128)
        for q in range(4):
            P.dma("sp", lambda e, q=q: e.dma_start(out=yv[:, 4 * q:4 * q + 4, :], in_=self.xres[:, 4 * q:4 * q + 4, :]),
                  reads=[("xres", i) for i in range(4 * q, 4 * q + 4)])


PROMPT_ASSIGN = [[0, 1, 2], [3, 4, 5], [6, 7, 8], [9, 10, 11], [12, 13], [14, 15]]

OFF_A, OFF_B, OFF_C, OFF_D = 0, 1040, 2080, 3104


def rope_tables(is_sample):
    tab = np.zeros((TOK, 64, 2), np.float32)
    tab[:, :, 0] = 1.0
    if is_sample:
        n = np.arange(TOK)
        posv = (n // 64, n % 64)
        inv = 10000.0 ** (-np.arange(16, dtype=np.float32) / 16)
        for half in range(2):
            ang = posv[half].astype(np.float32)[:, None] * inv[None, :]
            cos, sin = np.cos(ang), np.sin(ang)
            base = 32 * half
            tab[:, base:base + 16, 0] = cos
            tab[:, base + 16:base + 32, 0] = cos
            tab[:, base:base + 16, 1] = -sin
            tab[:, base + 16:base + 32, 1] = sin
    t = tab.reshape(NT, 128, 64, 2).transpose(0, 2, 3, 1)
    t = np.concatenate([t, t], axis=1)
    return np.ascontiguousarray(t.astype(np.float32))


def swap_perm():
    idx = []
    for h in range(4):
        for half in range(2):
            b = 64 * h + 32 * half
            idx += list(range(b + 16, b + 32)) + list(range(b, b + 16))
    return np.array(idx)


def make_in_maps(inp, kern):
    f32 = np.float32
    g = lambda k: np.asarray(inp[k], f32)
    maps = []
    ident = np.eye(128, dtype=f32)
    consts = build_consts()
    b_mod = np.ascontiguousarray(g("b_mod").reshape(DEPTH, 48, 128))
    w_mod = np.ascontiguousarray(g("w_mod"))
    w_in = g("w_in")
    sw = swap_perm()
    cq = w_in[:, :, OFF_C:OFF_C + 256]
    ck = w_in[:, :, OFF_C + 256:OFF_C + 512]
    cv = w_in[:, :, OFF_C + 512:OFF_C + 768]
    cg = w_in[:, :, OFF_C + 768:OFF_C + 1024]
    aq = w_in[:, :, OFF_A:OFF_A + 256]
    ak = w_in[:, :, OFF_A + 256:OFF_A + 512]
    av = w_in[:, :, OFF_A + 512:OFF_A + 768]
    ao = w_in[:, :, OFF_A + 768:OFF_A + 1024]
    ai = w_in[:, :, OFF_A + 1024:OFF_A + 1032]
    af = w_in[:, :, OFF_A + 1032:OFF_A + 1040]
    agate = np.concatenate([ai[:, :, 0:4], af[:, :, 0:4], ai[:, :, 4:8], af[:, :, 4:8]], axis=2)
    bqkv = w_in[:, :, OFF_B:OFF_B + 768]
    bz = w_in[:, :, OFF_B + 768:OFF_B + 1024]
    bbeta = w_in[:, :, OFF_B + 1024:OFF_B + 1032]
    balpha = w_in[:, :, OFF_B + 1032:OFF_B + 1040]
    bgate = np.concatenate([bbeta[:, :, 0:4], balpha[:, :, 0:4], bbeta[:, :, 4:8], balpha[:, :, 4:8]], axis=2)
    dconv = g("delta_conv")
    dconv = np.ascontiguousarray(dconv.reshape(DEPTH, 5, 6, 128).transpose(0, 3, 2, 1))
    w2 = g("rwkv_w2")
    a2 = g("rwkv_a2")
    w2p = np.zeros((DEPTH, 128, 2, 256), f32)
    a2p = np.zeros((DEPTH, 128, 2, 256), f32)
    for z_ in range(2):
        w2p[:, 64 * z_:64 * z_ + 64, z_, :] = w2[:, z_]
        a2p[:, 64 * z_:64 * z_ + 64, z_, :] = a2[:, z_]
    shared = {
        "wD": np.ascontiguousarray(w_in[:, :, OFF_D:OFF_D + 1152]),
        "rw_mu": np.ascontiguousarray(g("rwkv_mu").reshape(DEPTH, 9, 128).transpose(0, 2, 1)),
        "rw_w2p": w2p, "rw_a2p": a2p,
        "rw_w0": np.ascontiguousarray(g("rwkv_w0").reshape(DEPTH, 512)),
        "rw_a0": np.ascontiguousarray(g("rwkv_a0").reshape(DEPTH, 512)),
        "rw_kk": g("rwkv_kk"), "rw_ka": g("rwkv_ka"),
        "rw_rk": np.ascontiguousarray(g("rwkv_rk").reshape(DEPTH, 256)),
        "rw_norm": g("rwkv_norm"), "rw_g2": np.ascontiguousarray(g("rwkv_g2")),
        "wB": np.ascontiguousarray(np.concatenate([bqkv, bgate, bz], axis=2)),
        "dl_conv": dconv,
        "dl_alog": np.ascontiguousarray(g("delta_a_log").reshape(DEPTH, 8)),
        "dl_dtb": np.ascontiguousarray(g("delta_dt_bias").reshape(DEPTH, 8)),
        "dl_norm": np.ascontiguousarray(g("delta_norm")),
        "wA": np.ascontiguousarray(np.concatenate([aq, ak, av, agate, ao], axis=2)),
        "ml_ib": np.ascontiguousarray(g("mlstm_i_bias").reshape(DEPTH, 8)),
        "ml_fb": np.ascontiguousarray(g("mlstm_f_bias").reshape(DEPTH, 8)),
        "ml_norm": np.ascontiguousarray(g("mlstm_norm")),
        "ident": ident, "consts": consts, "w_mod": w_mod, "b_mod": b_mod,
        "w_out": np.ascontiguousarray(g("w_out")),
        "wC": np.ascontiguousarray(np.concatenate([cq, ck, cv, cq[:, :, sw], ck[:, :, sw], cg], axis=2)),
        "ret_decay": np.ascontiguousarray(g("ret_decay").reshape(DEPTH, 8)),
        "ret_norm": np.ascontiguousarray(g("ret_norm")),
        "ln1_g": g("ln1_g"), "ln1_b": g("ln1_b"), "ln2_g": g("ln2_g"), "ln2_b": g("ln2_b"),
        "w_ff1": np.ascontiguousarray(g("w_ff1")), "w_ff2": np.ascontiguousarray(g("w_ff2")),
    }
    ropes = {True: rope_tables(True), False: rope_tables(False)}
    zeros_mat = np.zeros((DEPTH, 2, H, HD, HD), f32)
    for c in range(N_CORES):
        m = dict(shared)
        if c < 2:
            x = g("x_sample")[c]
            cond = g("c")[c]
            m["init_ret"] = np.ascontiguousarray(g("state_ret")[c])
            m["init_mC"] = np.ascontiguousarray(g("state_mlstm_C")[c])
            m["init_delta"] = np.ascontiguousarray(g("state_delta")[c])
            m["init_rwkv"] = np.ascontiguousarray(g("state_rwkv")[c])
            m["init_mn"] = np.ascontiguousarray(g("state_mlstm_n")[c])
            m["init_mm"] = np.ascontiguousarray(g("state_mlstm_m")[c])
        else:
            x = np.zeros((TOK, D), f32)
            mine = PROMPT_ASSIGN[c - 2]
            for s_ in range(8):
                x[256 * s_:256 * s_ + 256] = inp["x_prompt"][mine[s_ % len(mine)]]
            cond = g("c_ctx")
            m["init_ret"] = zeros_mat
            m["init_mC"] = zeros_mat
            m["init_delta"] = zeros_mat
            m["init_rwkv"] = zeros_mat
            m["init_mn"] = np.zeros((DEPTH, 2, H, HD), f32)
            m["init_mm"] = np.zeros((DEPTH, 2, H), f32)
        m["x"] = np.ascontiguousarray(x)
        m["cond"] = np.ascontiguousarray(cond.reshape(8, 128))
        m["keep"] = np.full((1, 1), 1.0 if c < 2 else 0.0, f32)
        m["rope"] = ropes[c < 2]
        missing = [k for k in kern.ins if k not in m]
        assert not missing, missing
        maps.append({k: m[k] for k in kern.ins})
    return maps


def run(inp, debug=None, trace=False):
    kern = K(debug)
    nc = kern.build()
    maps = make_in_maps(inp, kern)
    res = run_bass_kernel_spmd(nc, maps, core_ids=list(range(N_CORES)), trace=trace)
    return kern, res


def gather_states(r, name, shape_tail):
    out = np.zeros((16, DEPTH, 2) + shape_tail, np.float32)
    for c in range(2, N_CORES):
        for s_, b in enumerate(PROMPT_ASSIGN[c - 2]):
            out[b] = r[c][name][s_]
    return out


def kernel(**inp):
    kern, res = run(inp)
    r = res.results
    BATCH, SEQ = 16, 256
    y_prompt = np.zeros((BATCH, SEQ, D), np.float32)
    y_sample = np.zeros((2, TOK, D), np.float32)
    for c in range(2):
        y_sample[c] = r[c]["y"]
    for c in range(2, N_CORES):
        for s_, b in enumerate(PROMPT_ASSIGN[c - 2]):
            y_prompt[b] = r[c]["y"][256 * s_:256 * s_ + 256]
    new_ret = gather_states(r, "out_ret", (H, HD, HD))
    new_mC = gather_states(r, "out_mC", (H, HD, HD))
    new_mn = gather_states(r, "out_mn", (H, HD))
    new_mm = gather_states(r, "out_mm", (H,))
    new_delta = gather_states(r, "out_delta", (H, HD, HD)) if "out_delta" in r[0] else np.zeros_like(new_ret)
    new_rwkv = gather_states(r, "out_rwkv", (H, HD, HD)) if "out_rwkv" in r[0] else np.zeros_like(new_ret)
    return (y_prompt, y_sample, new_mC, new_mn, new_mm, new_delta, new_ret, new_rwkv)
```

```python
import contextlib
import numpy as np
import concourse.bass as bass
import concourse.mybir as mybir
from concourse.bass_utils import run_bass_kernel_spmd

F32 = mybir.dt.float32
BF16 = mybir.dt.bfloat16
AF = mybir.ActivationFunctionType
ALU = mybir.AluOpType
AX = mybir.AxisListType

D = 1024
NT = 16
TOK = 2048
DEPTH = 2
H = 4
HD = 64
ALPHA = (2 * DEPTH) ** 0.25
LN_EPS = 1e-5
N_CORES = 8

ENGS = ("pe", "act", "dve", "pool", "sp")
NSLOT = 6

CO = {}
_off = 0
for _n, _w in (("INCL", 64), ("STRICT", 64), ("ONES", 128), ("SEL", 512), ("PIDX", 1), ("NPIDX", 1), ("BLK", 128), ("PP1", 1)):
    CO[_n] = (_off, _w)
    _off += _w
NCONST = _off
ARENA_WORDS = 53200


def build_consts():
    c = np.zeros((128, NCONST), np.float32)
    s = np.arange(64)
    incl = (s[:, None] <= s[None, :]).astype(np.float32)
    strict = (s[:, None] < s[None, :]).astype(np.float32)
    for half in range(2):
        c[64 * half:64 * half + 64, CO["INCL"][0]:CO["INCL"][0] + 64] = incl
        c[64 * half:64 * half + 64, CO["STRICT"][0]:CO["STRICT"][0] + 64] = strict
    c[:, CO["ONES"][0]:CO["ONES"][0] + 128] = 1.0
    sel = np.zeros((64, 4, 128), np.float32)
    for z in range(2):
        for ch in range(2):
            for lp in range(64):
                n = 64 * ch + lp if z == 0 else 127 - 64 * ch - lp
                sel[lp, 2 * z + ch, n] = 1.0
    c[0:64, CO["SEL"][0]:CO["SEL"][0] + 512] = sel.reshape(64, 512)
    p = np.arange(128) % 64
    c[:, CO["PIDX"][0]] = p
    c[:, CO["NPIDX"][0]] = -p
    c[:, CO["PP1"][0]] = p + 1
    blk = np.zeros((128, 128), np.float32)
    blk[0:64, 0:64] = 1.0
    blk[64:128, 64:128] = 1.0
    c[:, CO["BLK"][0]:CO["BLK"][0] + 128] = blk
    return c


class Op:
    __slots__ = ("eng", "fn", "reads", "writes", "chan", "val", "is_dma", "idx", "signal")

    def __init__(self, eng, fn, reads, writes, is_dma):
        self.eng = eng
        self.fn = fn
        self.reads = reads
        self.writes = writes
        self.is_dma = is_dma
        self.chan = None
        self.val = 0
        self.signal = False


class _Rec:
    def __getattr__(self, name):
        def f(*a, **kw):
            self.call = (name, a, kw)
            return self
        return f


class Prog:
    def __init__(self):
        self.ops = []

    def op(self, eng, fn, reads=(), writes=()):
        r = _Rec()
        fn(r)
        o = Op(eng, r.call, tuple(reads), tuple(writes), False)
        self.ops.append(o)
        return o

    def dma(self, eng, fn, reads=(), writes=()):
        r = _Rec()
        fn(r)
        o = Op(eng, r.call, tuple(reads), tuple(writes), True)
        self.ops.append(o)
        return o

    def barrier(self):
        self.ops.append(None)

    def emit(self, nc, stack):
        raw = self.ops
        ops = []
        barrier_at = set()
        for o in raw:
            if o is None:
                barrier_at.add(len(ops))
            else:
                ops.append(o)
        dma_n = {e: 0 for e in ENGS}
        slot_prev = {}
        for i, o in enumerate(ops):
            o.idx = i
            if o.is_dma:
                j = dma_n[o.eng] % NSLOT
                dma_n[o.eng] += 1
                o.chan = ("dma", o.eng, j)
            else:
                o.chan = o.eng
        lastw = {}
        readers = {}
        deps = []
        last_chan = {}
        bar_deps = set()
        for o in ops:
            if o.idx in barrier_at:
                bar_deps = set(last_chan.values())
            last_chan[o.chan] = o.idx
            d = set(bar_deps)
            for k in o.reads:
                w = lastw.get(k)
                if w is not None:
                    d.add(w)
            for k in o.writes:
                w = lastw.get(k)
                if w is not None:
                    d.add(w)
                for r in readers.get(k, ()):
                    d.add(r)
            if o.is_dma:
                p = slot_prev.get(o.chan)
                if p is not None:
                    d.add(p)
                slot_prev[o.chan] = o.idx
            d.discard(o.idx)
            deps.append(d)
            for k in o.reads:
                readers.setdefault(k, []).append(o.idx)
            for k in o.writes:
                lastw[k] = o.idx
                readers[k] = []
        pos = {}
        cnt = {}
        for o in ops:
            c = o.chan
            cnt[c] = cnt.get(c, 0) + 1
            pos[o.idx] = cnt[c]
        know_stream = {e: {} for e in ENGS}
        know_op = [None] * len(ops)
        needed = [None] * len(ops)
        for o in ops:
            ks = know_stream[o.eng]
            need = []
            for p in sorted(deps[o.idx], reverse=True):
                po = ops[p]
                if po.eng == "pe" and o.eng == "pe" and not po.is_dma and not o.is_dma:
                    continue
                if ks.get(po.chan, 0) >= pos[p]:
                    continue
                need.append(p)
                for c, v in know_op[p].items():
                    if ks.get(c, 0) < v:
                        ks[c] = v
                if ks.get(po.chan, 0) < pos[p]:
                    ks[po.chan] = pos[p]
            needed[o.idx] = need
            know_op[o.idx] = dict(ks)
            for p in need:
                ops[p].signal = True
        for o in ops:
            if o.is_dma:
                o.signal = True
        cnt = {}
        for o in ops:
            if o.signal:
                cnt[o.chan] = cnt.get(o.chan, 0) + 1
                o.val = cnt[o.chan] * (16 if o.is_dma else 1)
        sems = {}
        for c in cnt:
            name = "s_" + ("_".join(str(x) for x in c) if isinstance(c, tuple) else c)
            sems[c] = stack.enter_context(nc.semaphore(name))
        self.maxval = dict(cnt)
        block = stack.enter_context(nc.Block())
        streams = {e: [] for e in ENGS}
        for o in ops:
            streams[o.eng].append(o)

        def run_stream(engname, engobj):
            for o in streams[engname]:
                for p in needed[o.idx]:
                    po = ops[p]
                    engobj.wait_ge(sems[po.chan], po.val)
                name, a, kw = o.fn
                inst = getattr(engobj, name)(*a, **kw)
                if o.signal:
                    inst.then_inc(sems[o.chan], 16 if o.is_dma else 1)

        @block.tensor
        def _(e):
            run_stream("pe", e)

        @block.scalar
        def _(e):
            run_stream("act", e)

        @block.vector
        def _(e):
            run_stream("dve", e)

        @block.gpsimd
        def _(e):
            run_stream("pool", e)

        @block.sync
        def _(e):
            run_stream("sp", e)
            for c, n in cnt.items():
                if isinstance(c, tuple):
                    e.wait_ge(sems[c], n * 16)
        return len(ops)


class K:
    def __init__(self, debug=None):
        self.debug = debug or {}
        self.nc = bass.Bass("TRN2", target_bir_lowering=False)
        self.P = Prog()
        self.ins = {}
        self.outs = {}
        self.uid = 0

    def din(self, name, shape):
        t = self.nc.dram_tensor(name, list(shape), F32, kind="ExternalInput").ap()
        self.ins[name] = t
        return t

    def dout(self, name, shape):
        t = self.nc.dram_tensor(name, list(shape), F32, kind="ExternalOutput").ap()
        self.outs[name] = t
        return t

    def sb(self, name, shape, dt=F32):
        shape = list(shape)
        nelem = 1
        for d in shape[1:]:
            nelem *= d
        words = (nelem * (2 if dt == BF16 else 4) + 3) // 4
        words = (words + 7) // 8 * 8
        off = self.arena_off
        assert off + words <= ARENA_WORDS, ("SBUF arena overflow", name, off, words)
        self.arena_off = off + words
        self.arena_peak = max(self.arena_peak, self.arena_off)
        ap = self.arena[0:shape[0], off:off + words]
        if dt == BF16:
            ap = ap.bitcast(BF16)
        ap = ap[:, 0:nelem]
        if len(shape) == 3:
            ap = ap.rearrange("p (a b) -> p a b", b=shape[2])
        elif len(shape) == 4:
            ap = ap.rearrange("p (a b c) -> p a b c", b=shape[2], c=shape[3])
        elif len(shape) == 5:
            ap = ap.rearrange("p (a b c d) -> p a b c d", b=shape[2], c=shape[3], d=shape[4])
        return ap

    def mark(self):
        return self.arena_off

    def release(self, m):
        self.arena_off = m
        self.P.barrier()

    def build(self):
        nc, P = self.nc, self.P
        with contextlib.ExitStack() as st:
            self.stack = st
            self.arena = st.enter_context(nc.sbuf_tensor("arena", [128, ARENA_WORDS], F32))
            self.arena_off = 0
            self.arena_peak = 0
            self.declare_io()
            self.alloc_global()
            self.load_consts()
            for l in range(self.debug.get("layers", DEPTH)):
                self.layer(l)
            self.finish()
            n = P.emit(nc, st)
            self.n_ops = n
        return nc

    def declare_io(self):
        self.x_in = self.din("x", [TOK, D])
        self.cond = self.din("cond", [8, 128])
        self.ident_in = self.din("ident", [128, 128])
        self.w_mod = self.din("w_mod", [DEPTH, D, 6 * D])
        self.b_mod = self.din("b_mod", [DEPTH, 48, 128])
        self.y_out = self.dout("y", [TOK, D])
        self.consts_in = self.din("consts", [128, NCONST])
        self.keep_in = self.din("keep", [1, 1])
        self.rope_in = self.din("rope", [NT, 128, 2, 128])
        self.w_out = self.din("w_out", [DEPTH, D, D])
        self.wC = self.din("wC", [DEPTH, D, 1536])
        self.wA = self.din("wA", [DEPTH, D, 1040])
        self.ml_ib = self.din("ml_ib", [DEPTH, 8])
        self.ml_fb = self.din("ml_fb", [DEPTH, 8])
        self.ml_norm = self.din("ml_norm", [DEPTH, 256])
        self.init_mC = self.din("init_mC", [DEPTH, 2, H, HD, HD])
        self.init_mn = self.din("init_mn", [DEPTH, 2, H, HD])
        self.init_mm = self.din("init_mm", [DEPTH, 2, H])
        self.out_mC = self.dout("out_mC", [8, DEPTH, 2, H, HD, HD])
        self.out_mn = self.dout("out_mn", [8, DEPTH, 2, H, HD])
        self.out_mm = self.dout("out_mm", [8, DEPTH, 2, H])
        self.wB = self.din("wB", [DEPTH, D, 1040])
        self.dl_conv = self.din("dl_conv", [DEPTH, 128, 6, 5])
        self.dl_alog = self.din("dl_alog", [DEPTH, 8])
        self.dl_dtb = self.din("dl_dtb", [DEPTH, 8])
        self.dl_norm = self.din("dl_norm", [DEPTH, 256])
        self.init_delta = self.din("init_delta", [DEPTH, 2, H, HD, HD])
        self.out_delta = self.dout("out_delta", [8, DEPTH, 2, H, HD, HD])
        self.wD = self.din("wD", [DEPTH, D, 1152])
        self.rw_mu = self.din("rw_mu", [DEPTH, 128, 9])
        self.rw_w2p = self.din("rw_w2p", [DEPTH, 128, 2, 256])
        self.rw_a2p = self.din("rw_a2p", [DEPTH, 128, 2, 256])
        self.rw_w0 = self.din("rw_w0", [DEPTH, 512])
        self.rw_a0 = self.din("rw_a0", [DEPTH, 512])
        self.rw_kk = self.din("rw_kk", [DEPTH, 256])
        self.rw_ka = self.din("rw_ka", [DEPTH, 256])
        self.rw_rk = self.din("rw_rk", [DEPTH, 256])
        self.rw_norm = self.din("rw_norm", [DEPTH, 256])
        self.rw_g2 = self.din("rw_g2", [DEPTH, 128, 256])
        self.init_rwkv = self.din("init_rwkv", [DEPTH, 2, H, HD, HD])
        self.out_rwkv = self.dout("out_rwkv", [8, DEPTH, 2, H, HD, HD])
        self.ln1_g = self.din("ln1_g", [DEPTH, D])
        self.ln1_b = self.din("ln1_b", [DEPTH, D])
        self.ln2_g = self.din("ln2_g", [DEPTH, D])
        self.ln2_b = self.din("ln2_b", [DEPTH, D])
        self.w_ff1 = self.din("w_ff1", [DEPTH, D, 4 * D])
        self.w_ff2 = self.din("w_ff2", [DEPTH, 4 * D, D])
        self.ret_decay = self.din("ret_decay", [DEPTH, 8])
        self.ret_norm = self.din("ret_norm", [DEPTH, 256])
        self.init_ret = self.din("init_ret", [DEPTH, 2, H, HD, HD])
        self.out_ret = self.dout("out_ret", [8, DEPTH, 2, H, HD, HD])
        if "yacc" in self.debug:
            self.dbg_yacc = self.dout("dbg_yacc", [128, NT, 256])
        if "uT" in self.debug:
            self.dbg_uT = self.dout("dbg_uT", [128, 8, TOK])

    def alloc_global(self):
        nc = self.nc
        self.xres = self.sb("xres", [128, NT, D], F32)
        self.ident = self.sb("ident", [128, 128], F32)
        self.psum = self.stack.enter_context(nc.psum_tensor("psum", [128, 8, 512], F32))
        self.modT = self.sb("modT", [128, 48], F32)
        self.sc1p = self.sb("sc1p", [128, 8], F32)
        self.sc2p = self.sb("sc2p", [128, 8], F32)
        self.scT = self.sb("scT", [128, 8], F32)
        self.bmodT = self.sb("bmodT", [128, 48], F32)
        self.consts = self.sb("consts", [128, NCONST], F32)
        self.ident_bf = self.sb("ident_bf", [128, 128], BF16)
        self.keep = self.sb("keep", [128, 1], F32)
        self.g_bc = {"g1": self.sb("g1_bc", [128, D], F32), "g2": self.sb("g2_bc", [128, D], F32)}

    def load_consts(self):
        P = self.P
        xv = self.x_in.rearrange("(i p) d -> p i d", p=128)
        for q in range(4):
            P.dma("sp", lambda e, q=q: e.dma_start(out=self.xres[:, 4 * q:4 * q + 4, :], in_=xv[:, 4 * q:4 * q + 4, :]),
                  writes=[("xres", i) for i in range(4 * q, 4 * q + 4)])
        P.dma("sp", lambda e: e.dma_start(out=self.ident[:], in_=self.ident_in), writes=["ident"])
        P.dma("sp", lambda e: e.dma_start(out=self.consts[:], in_=self.consts_in), writes=["consts"])
        P.dma("sp", lambda e: e.dma_start(out=self.keep[:], in_=self.keep_in.partition_broadcast(128)), writes=["keep"])
        P.op("dve", lambda e: e.tensor_copy(self.ident_bf[:], self.ident[:]), reads=["ident"], writes=["ident_bf"])
        c8 = self.sb("c8", [8, 128], F32)
        P.dma("sp", lambda e: e.dma_start(out=c8[:], in_=self.cond), writes=["c8"])
        ps = self.psum
        P.op("pe", lambda e: e.transpose(ps[:, 0, 0:8], c8[:], self.ident[0:8, 0:8]), reads=["c8", "ident"], writes=["ps0"])
        P.op("act", lambda e: e.activation(self.scT[:], ps[:, 0, 0:8], AF.Silu), writes=["ps0", "scT"])

    def compute_mod(self, l):
        P, ps = self.P, self.psum
        m = self.mark()
        self.scbc = self.sb("scbc", [128, 8, 128], F32)
        P.op("dve", lambda e: e.tensor_copy(self.scbc[:], self.scT[:].unsqueeze(2).to_broadcast([128, 8, 128])),
             reads=["scT"], writes=["scbc"])
        b48 = self.sb("b48", [48, 128], F32)
        P.dma("sp", lambda e: e.dma_start(out=b48[:], in_=self.b_mod[l]), writes=["b48"])
        P.op("pe", lambda e: e.transpose(ps[:, 1, 0:48], b48[:], self.ident[0:48, 0:48]), reads=["b48", "ident"], writes=["ps1"])
        P.op("dve", lambda e: e.tensor_copy(self.bmodT[:], ps[:, 1, 0:48]), writes=["ps1", "bmodT"])
        wv = self.w_mod[l].rearrange("(k p) n -> p k n", p=128)
        wblk = [self.sb("wmodblk%d" % i, [128, 8, 512], F32) for i in range(2)]
        bb = self.sb("g_bb", [128, D], F32)
        for b in range(12):
            wb = wblk[b % 2]
            key = "wmodblk%d" % (b % 2)
            P.dma("sp", lambda e, b=b, wb=wb: e.dma_start(out=wb[:], in_=wv[:, :, 512 * b:512 * b + 512]), writes=[key])
            for jj in range(4):
                j = 4 * b + jj
                for k in range(8):
                    P.op("pe", lambda e, wb=wb, jj=jj, j=j, k=k: e.matmul(
                        ps[:, 2, j:j + 1], wb[:, k, 128 * jj:128 * jj + 128], self.scT[:, k:k + 1],
                        start=(k == 0), stop=(k == 7)), reads=[key, "scT"], writes=["ps2"])
            if b in (4, 5, 10, 11):
                which = "g1" if b < 6 else "g2"
                half = b % 2
                if half == 0:
                    off = 2048 if which == "g1" else 5120
                    bm = self.b_mod[l].rearrange("a b -> (a b)")[off:off + 1024].unsqueeze(0)
                    P.dma("sp", lambda e, bm=bm: e.dma_start(out=bb[:], in_=bm.partition_broadcast(128)), writes=["g_bb"])
                gt = self.g_bc[which]
                for k in range(8):
                    P.op("pe", lambda e, wb=wb, k=k: e.matmul(ps[:, 3, :], self.scbc[:, k, :], wb[:, k, :], start=(k == 0), stop=(k == 7)),
                         reads=[key, "scbc"], writes=["ps3"])
                P.op("dve", lambda e, gt=gt, half=half: e.tensor_tensor(
                    gt[:, 512 * half:512 * half + 512], ps[:, 3, :], bb[:, 512 * half:512 * half + 512], ALU.add),
                    reads=["g_bb"], writes=["ps3", which + "_bc"])
        P.op("dve", lambda e: e.tensor_tensor(self.modT[:], ps[:, 2, 0:48], self.bmodT[:], ALU.add), reads=["bmodT"], writes=["ps2", "modT"])
        P.op("dve", lambda e: e.tensor_scalar(self.sc1p[:], self.modT[:, 8:16], 1.0, None, ALU.add), reads=["modT"], writes=["sc1p"])
        P.op("dve", lambda e: e.tensor_scalar(self.sc2p[:], self.modT[:, 32:40], 1.0, None, ALU.add), reads=["modT"], writes=["sc2p"])
        self.release(m)

    def ln_mod_T(self, tile, dst_fn, scp, sh_off, tmp, dst_keys):
        P, ps = self.P, self.psum
        stats, mv, rstd, xhat = tmp
        xk = ("xres", tile)
        src = self.xres[:, tile, :]
        for hh in range(2):
            P.op("dve", lambda e, hh=hh: e.bn_stats(stats[:, hh, :], src[:, 512 * hh:512 * hh + 512]), reads=[xk], writes=["lnstats"])
        P.op("dve", lambda e: e.bn_aggr(mv[:], stats[:]), reads=["lnstats"], writes=["lnmv"])
        P.op("act", lambda e: e.activation(rstd[:], mv[:, 1:2], AF.Ln, bias=LN_EPS, scale=1.0), reads=["lnmv"], writes=["lnrstd"])
        P.op("act", lambda e: e.activation(rstd[:], rstd[:], AF.Exp, scale=-0.5), reads=["lnrstd"], writes=["lnrstd"])
        P.op("dve", lambda e: e.tensor_scalar(xhat[:], src, mv[:, 0:1], rstd[:, 0:1], ALU.subtract, ALU.mult),
             reads=[xk, "lnmv", "lnrstd"], writes=["xhat"])
        for k in range(8):
            b = 4 + (k // 4)
            P.op("pe", lambda e, k=k, b=b: e.transpose(ps[:, b, 128 * (k % 4):128 * (k % 4) + 128], xhat[:, 128 * k:128 * k + 128], self.ident[:]),
                 reads=["xhat", "ident"], writes=["ps%d" % b])
        for k in range(8):
            b = 4 + (k // 4)
            P.op("act", lambda e, k=k, b=b: e.activation(dst_fn(k), ps[:, b, 128 * (k % 4):128 * (k % 4) + 128], AF.Identity,
                                                       bias=self.modT[:, sh_off + k:sh_off + k + 1], scale=scp[:, k:k + 1]),
                 reads=["modT", "sc1p", "sc2p"], writes=["ps%d" % b] + dst_keys)

    def dump(self, name, ap, keys, shape):
        want = self.debug.get("dump", ())
        if name not in want or ("dmp_" + name) in self.outs:
            return
        P = self.P
        out = self.dout("dmp_" + name, shape)
        m = self.mark()
        tmp = self.sb("dmp_" + name, shape, F32)
        P.op("dve", lambda e: e.tensor_copy(tmp[:], ap), reads=keys, writes=["dmp_" + name])
        P.dma("sp", lambda e: e.dma_start(out=out, in_=tmp[:]), reads=["dmp_" + name])
        self.release(m)

    def act_sigmoid(self, out, in_, reads, writes):
        P = self.P
        P.op("act", lambda e: e.activation(out, in_, AF.Exp, scale=-1.0), reads=reads, writes=writes)
        P.op("act", lambda e: e.activation(out, out, AF.Ln, bias=1.0, scale=1.0), reads=writes, writes=writes)
        P.op("act", lambda e: e.activation(out, out, AF.Exp, scale=-1.0), reads=writes, writes=writes)

    def cst(self, name, rows=64):
        o, w = CO[name]
        return self.consts[0:rows, o:o + w]

    def load_w(self, name, src, ncols, kchunks=8):
        P = self.P
        wb = self.sb(name, [128, kchunks, ncols], BF16)
        wv = src.rearrange("(k p) n -> p k n", p=128)
        step = 2 if kchunks % 2 == 0 else 1
        for k0 in range(0, kchunks, step):
            P.dma("pool", lambda e, k0=k0: e.dma_start(out=wb[:, k0:k0 + step, :], in_=wv[:, k0:k0 + step, :]),
                  writes=[(name, k0 // step)])
        return wb, [(name, i) for i in range(kchunks // step)]

    def emit_out_tile(self, t, osb, osb_keys, bank):
        P, ps = self.P, self.psum
        sel = self.cst("SEL").rearrange("p (q n) -> p q n", n=128)
        pk = "ps%d" % bank
        for z in range(2):
            tile = t if z == 0 else NT - 1 - t
            for c in range(2):
                P.op("pe", lambda e, z=z, c=c: e.matmul(ps[:, bank, 256 * z:256 * z + 256], sel[:, 2 * z + c, :], osb[:, c, z, :],
                                                         start=(c == 0), stop=(c == 1)),
                     reads=["consts"] + osb_keys, writes=[pk])
            yk = ("yacc", tile)
            if t < NT // 2:
                P.op("act", lambda e, z=z, tile=tile: e.activation(self.yacc[:, tile, :], ps[:, bank, 256 * z:256 * z + 256], AF.Copy),
                     writes=[pk, yk])
            else:
                P.op("dve", lambda e, z=z, tile=tile: e.tensor_tensor(self.yacc[:, tile, :], ps[:, bank, 256 * z:256 * z + 256],
                                                                     self.yacc[:, tile, :], ALU.add),
                     writes=[pk, yk])

    def mixer_ret(self, l, first):
        P, ps = self.P, self.psum
        m0 = self.mark()
        wb, wkeys = self.load_w("wC", self.wC[l], 1536)
        m1 = self.mark()
        lg = self.sb("ret_lg", [128, 8], F32)
        P.dma("sp", lambda e: e.dma_start(out=lg[:], in_=self.ret_decay[l:l + 1, :].partition_broadcast(128)), writes=["ret_lg"])
        P.op("act", lambda e: e.activation(lg[:], lg[:], AF.Exp), reads=["ret_lg"], writes=["ret_lg"])
        P.op("dve", lambda e: e.tensor_scalar(lg[:], lg[:], -1.0, None, ALU.mult), reads=["ret_lg"], writes=["ret_lg"])
        gs = self.sb("ret_gs", [128, 8], F32)
        ga = self.sb("ret_ga", [128, 8], F32)
        pidx = self.consts[:, CO["PIDX"][0]:CO["PIDX"][0] + 1]
        npidx = self.consts[:, CO["NPIDX"][0]:CO["NPIDX"][0] + 1]
        P.op("act", lambda e: e.activation(gs[:], lg[:], AF.Exp, scale=npidx), reads=["ret_lg", "consts"], writes=["ret_gs"])
        P.op("act", lambda e: e.activation(ga[:], lg[:], AF.Exp, scale=pidx), reads=["ret_lg", "consts"], writes=["ret_ga"])
        G64 = self.sb("ret_g64", [128, 4], F32)
        GAM = self.sb("ret_gam", [128, 4], F32)
        GAMI = self.sb("ret_gami", [128, 4], F32)
        for hq in range(2):
            rows = slice(64 * hq, 64 * hq + 64)
            P.op("act", lambda e, rows=rows, hq=hq: e.activation(G64[rows, :], lg[rows, hq::2], AF.Exp, scale=64.0), reads=["ret_lg"], writes=["ret_g64"])
            P.op("act", lambda e, rows=rows, hq=hq: e.activation(GAM[rows, :], lg[rows, hq::2], AF.Exp, scale=1.0), reads=["ret_lg"], writes=["ret_gam"])
            P.op("act", lambda e, rows=rows, hq=hq: e.activation(GAMI[rows, :], lg[rows, hq::2], AF.Exp, scale=-1.0), reads=["ret_lg"], writes=["ret_gami"])
        S = self.sb("ret_S", [128, 4, 64], F32)
        Sb = self.sb("ret_Sb", [128, 4, 64], BF16)
        Sout = self.sb("ret_Sout", [128, 4, 64], F32)
        P.dma("sp", lambda e: e.dma_start(out=S[:], in_=self.init_ret[l].rearrange("z (hp hq) d e -> (hq d) (z hp) e", hq=2)), writes=["ret_S"])
        P.op("dve", lambda e: e.tensor_tensor(S[:], S[:], GAM[:].unsqueeze(2).to_broadcast([128, 4, 64]), ALU.mult), reads=["ret_gam"], writes=["ret_S"])
        P.op("act", lambda e: e.activation(Sb[:], S[:], AF.Copy), reads=["ret_S"], writes=["ret_Sb"])
        rc = [self.sb("ret_rc%d" % i, [128, 2, 2, 128], F32) for i in range(2)]
        ta = self.sb("ret_ta", [128, 2, 128], F32)
        tb = self.sb("ret_tb", [128, 2, 128], F32)
        qT = self.sb("ret_qT", [128, 2, 2, 2, 128], BF16)
        P.op("dve", lambda e: e.memset(qT[:], 0.0), writes=["ret_qT"])
        kT = self.sb("ret_kT", [128, 2, 2, 128], BF16)
        vT = self.sb("ret_vT", [128, 2, 2, 128], BF16)
        ktm = self.sb("ret_ktm", [64, 8, 64], BF16)
        vtm = self.sb("ret_vtm", [64, 8, 64], BF16)
        pm = self.sb("ret_pm", [64, 8, 64], BF16)
        osb = self.sb("ret_osb", [64, 2, 2, 256], F32)
        tmpS = self.sb("ret_tmpS", [128, 4, 64], F32)
        incl = self.cst("INCL")
        psb = ps.bitcast(BF16) if False else None

        def bfview(bank):
            return ps[:, bank, :].bitcast(BF16)

        stop = self.debug.get("stop", 99)
        for t in range(NT if stop > 1 else 0):
            tiles = (t, NT - 1 - t)
            r = rc[t % 2]
            rk = "ret_rc%d" % (t % 2)
            for z in range(2):
                P.dma("sp", lambda e, z=z, r=r: e.dma_start(out=r[:, z, :, :], in_=self.rope_in[tiles[z]]), writes=[rk])
            def proj(j, bank, z):
                tok0 = 2 + 128 * tiles[z]
                for k in range(8):
                    P.op("pe", lambda e, j=j, k=k, z=z, tok0=tok0, bank=bank: e.matmul(
                        ps[:, bank, 128 * z:128 * z + 128], wb[:, k, 128 * j:128 * j + 128], self.uT[:, k, tok0:tok0 + 128],
                        start=(k == 0), stop=(k == 7)),
                        reads=wkeys + [("uT", tiles[z])], writes=["ps%d" % bank])
            for which, dst, dkey in ((0, qT, "ret_qT"), (1, kT, "ret_kT")):
                for hp in range(2):
                    j = 2 * which + hp
                    for z in range(2):
                        proj(j, 0, z)
                        proj(6 + j, 1, z)
                    scale = 0.125 if which == 0 else 1.0
                    P.op("dve", lambda e, scale=scale, r=r: e.scalar_tensor_tensor(
                        ta[:], ps[:, 0, 0:256].rearrange("p (z n) -> p z n", z=2), scale, r[:, :, 0, :], ALU.mult, ALU.mult),
                        reads=[rk], writes=["ps0", "ret_ta"])
                    P.op("dve", lambda e, scale=scale, r=r: e.scalar_tensor_tensor(
                        tb[:], ps[:, 1, 0:256].rearrange("p (z n) -> p z n", z=2), scale, r[:, :, 1, :], ALU.mult, ALU.mult),
                        reads=[rk], writes=["ps1", "ret_tb"])
                    if which == 1:
                        P.op("dve", lambda e, dst=dst, hp=hp: e.tensor_tensor(dst[:, hp, 0, :], ta[:, 0, :], tb[:, 0, :], ALU.add),
                             reads=["ret_ta", "ret_tb"], writes=[dkey])
                        P.op("dve", lambda e, dst=dst, hp=hp: e.tensor_tensor(dst[:, hp, 1, ::-1], ta[:, 1, :], tb[:, 1, :], ALU.add),
                             reads=["ret_ta", "ret_tb"], writes=[dkey])
                    else:
                        for hq in range(2):
                            rws = slice(64 * hq, 64 * hq + 64)
                            P.op("dve", lambda e, dst=dst, hp=hp, hq=hq, rws=rws: e.tensor_tensor(
                                dst[rws, hp, hq, 0, :], ta[rws, 0, :], tb[rws, 0, :], ALU.add),
                                reads=["ret_ta", "ret_tb"], writes=[dkey])
                            P.op("dve", lambda e, dst=dst, hp=hp, hq=hq, rws=rws: e.tensor_tensor(
                                dst[rws, hp, hq, 1, ::-1], ta[rws, 1, :], tb[rws, 1, :], ALU.add),
                                reads=["ret_ta", "ret_tb"], writes=[dkey])
            for hp in range(2):
                bank = hp
                for z in range(2):
                    proj(4 + hp, bank, z)
                P.op("act", lambda e, hp=hp, bank=bank: e.activation(vT[:, hp, 0, :], ps[:, bank, 0:128], AF.Copy), writes=["ps%d" % bank, "ret_vT"])
                P.op("act", lambda e, hp=hp, bank=bank: e.activation(vT[:, hp, 1, ::-1], ps[:, bank, 128:256], AF.Copy), writes=["ps%d" % bank, "ret_vT"])
            self.dump("ret_qT", qT[:], ["ret_qT"], [128, 2, 2, 2, 128])
            self.dump("ret_kT", kT[:], ["ret_kT"], [128, 2, 2, 128])
            self.dump("ret_vT", vT[:], ["ret_vT"], [128, 2, 2, 128])
            for c in range(2 if stop > 2 else 0):
                cs = slice(64 * c, 64 * c + 64)
                for (src, skey, bank) in ((kT, "ret_kT", 2), (vT, "ret_vT", 3)):
                    bv = bfview(bank)
                    for z in range(2):
                        for hp in range(2):
                            col = (4 * z + 2 * hp) * 64
                            P.op("pe", lambda e, src=src, z=z, hp=hp, col=col, bv=bv: e.transpose(
                                bv[0:64, col:col + 128], src[:, hp, z, cs], self.ident_bf[:]),
                                reads=[skey, "ident_bf"], writes=["ps%d" % bank])
                P.op("act", lambda e: e.activation(ktm[:], bfview(2)[0:64, 0:512].rearrange("p (u d) -> p u d", d=64), AF.Copy),
                     writes=["ps2", "ret_ktm"])
                P.op("dve", lambda e: e.tensor_tensor(vtm[:], bfview(3)[0:64, 0:512].rearrange("p (u d) -> p u d", d=64),
                                                      gs[0:64, :].unsqueeze(2).to_broadcast([64, 8, 64]), ALU.mult),
                     reads=["ret_gs"], writes=["ps3", "ret_vtm"])
                self.dump("ret_ktm", ktm[:], ["ret_ktm"], [64, 8, 64])
                self.dump("ret_vtm", vtm[:], ["ret_vtm"], [64, 8, 64])
                if stop <= 3:
                    continue
                for z in range(2):
                    for h in range(4):
                        hp, hq = h // 2, h % 2
                        rows = slice(64 * hq, 64 * hq + 64)
                        u = 4 * z + h
                        P.op("pe", lambda e, z=z, hp=hp, hq=hq, u=u: e.matmul(
                            ps[0:64, 4, 64 * u:64 * u + 64], kT[:, hp, z, cs], qT[:, hp, hq, z, cs], start=True, stop=True),
                            reads=["ret_kT", "ret_qT"], writes=["ps4"])
                if self.debug.get("sub") == "a":
                    continue
                P.op("dve", lambda e: e.tensor_tensor(pm[:], ps[0:64, 4, :].rearrange("p (u l) -> p u l", l=64),
                                                      incl.unsqueeze(1).to_broadcast([64, 8, 64]), ALU.mult),
                     reads=["consts"], writes=["ps4", "ret_pm"])
                self.dump("ret_pm", pm[:], ["ret_pm"], [64, 8, 64])
                if stop <= 4:
                    continue
                for z in range(2):
                    for h in range(4):
                        hp, hq = h // 2, h % 2
                        rows = slice(64 * hq, 64 * hq + 64)
                        u = 4 * z + h
                        P.op("pe", lambda e, u=u: e.matmul(ps[0:64, 5, 64 * u:64 * u + 64], pm[:, u, :], vtm[:, u, :], start=True, stop=False),
                             reads=["ret_pm", "ret_vtm"], writes=["ps5"])
                        P.op("pe", lambda e, z=z, hp=hp, hq=hq, u=u: e.matmul(
                            ps[0:64, 5, 64 * u:64 * u + 64], qT[:, hp, hq, z, cs], Sb[:, 2 * z + hp, :], start=False, stop=True),
                            reads=["ret_qT", "ret_Sb"], writes=["ps5"])
                P.op("dve", lambda e, c=c: e.tensor_tensor(
                    osb[:, c, :, :].rearrange("p z (h e) -> p (z h) e", e=64), ps[0:64, 5, :].rearrange("p (u e) -> p u e", e=64),
                    ga[0:64, :].unsqueeze(2).to_broadcast([64, 8, 64]), ALU.mult),
                    reads=["ret_ga"], writes=["ps5", ("ret_osb", c)])
                self.dump("ret_osb", osb[:, 0, :, :], [("ret_osb", 0)], [64, 2, 256])
                if stop <= 5:
                    continue
                for z in range(2):
                    for h in range(4):
                        hp, hq = h // 2, h % 2
                        rows = slice(64 * hq, 64 * hq + 64)
                        u = 4 * z + h
                        col = (2 * z + hp) * 64
                        P.op("pe", lambda e, rows=rows, u=u, col=col: e.matmul(ps[rows, 6, col:col + 64], ktm[:, u, :], vtm[:, u, :], start=True, stop=True),
                             reads=["ret_ktm", "ret_vtm"], writes=["ps6"])
                P.op("dve", lambda e: e.tensor_tensor(tmpS[:], ps[:, 6, 0:256].rearrange("p (a e) -> p a e", e=64), S[:], ALU.add),
                     reads=["ret_S"], writes=["ps6", "ret_tmpS"])
                P.op("dve", lambda e: e.tensor_tensor(S[:], tmpS[:], G64[:].unsqueeze(2).to_broadcast([128, 4, 64]), ALU.mult),
                     reads=["ret_tmpS", "ret_g64"], writes=["ret_S"])
                self.dump("ret_S1", S[:], ["ret_S"], [128, 4, 64])
                if not (t % 2 == 1 and c == 1):
                    P.op("act", lambda e: e.activation(Sb[:], S[:], AF.Copy), reads=["ret_S"], writes=["ret_Sb"])
            if stop > 6:
                self.emit_out_tile(t, osb, [("ret_osb", 0), ("ret_osb", 1)], 7)
            if t % 2 == 1 and stop > 7:
                P.op("dve", lambda e: e.tensor_tensor(Sout[:], S[:], GAMI[:].unsqueeze(2).to_broadcast([128, 4, 64]), ALU.mult),
                     reads=["ret_S", "ret_gami"], writes=["ret_Sout"])
                for z in range(2):
                    seg = (t - 1) // 2 if z == 0 else (NT - 1 - t) // 2
                    P.dma("sp", lambda e, z=z, seg=seg: e.dma_start(
                        out=self.out_ret[seg, l, z].rearrange("(hp hq) d e -> (hq d) hp e", hq=2), in_=Sout[:, 2 * z:2 * z + 2, :]),
                        reads=["ret_Sout"])
                P.op("dve", lambda e: e.tensor_scalar(S[:], S[:], self.keep[:, 0:1], None, ALU.mult), reads=["ret_S", "keep"], writes=["ret_S"])
                P.op("act", lambda e: e.activation(Sb[:], S[:], AF.Copy), reads=["ret_S"], writes=["ret_Sb"])
        if self.debug.get("yacc") == "ret":
            P.dma("sp", lambda e: e.dma_start(out=self.dbg_yacc, in_=self.yacc[:]), reads=[("yacc", i) for i in range(NT)])
        self.release(m1)
        if self.debug.get("post", True):
            self.post_simple(l, "ret", wb, wkeys, 1280, AF.Silu, self.ret_norm, True, 512, first)
        self.release(m0)

    def proj_fm(self, wb, wkeys, col, M, tiles, bank, halo=0):
        P, ps = self.P, self.psum
        W = 128 + 2 * halo
        for z in range(2):
            tok0 = 2 + 128 * tiles[z] - halo
            for k in range(8):
                P.op("pe", lambda e, k=k, z=z, tok0=tok0: e.matmul(
                    ps[0:M, bank, W * z:W * z + W], wb[:, k, col:col + M], self.uT[:, k, tok0:tok0 + W],
                    start=(k == 0), stop=(k == 7)),
                    reads=wkeys + [("uT", tiles[z])], writes=["ps%d" % bank])

    def evac_fm(self, dst_fn, bank, M, dkey, scale=1.0, W=128, eng="act", rows=None):
        P, ps = self.P, self.psum
        rs = slice(0, M) if rows is None else rows
        for z in range(2):
            src = ps[rs, bank, W * z:W * z + W]
            dst = dst_fn(z)
            if z == 1:
                dst = dst[:, ::-1]
            if eng == "act":
                P.op("act", lambda e, dst=dst, src=src: e.activation(dst, src, AF.Copy, scale=scale), writes=["ps%d" % bank, dkey])
            else:
                P.op("dve", lambda e, dst=dst, src=src: e.tensor_scalar(dst, src, scale, None, ALU.mult), writes=["ps%d" % bank, dkey])

    def tm_transposes(self, srcT, skey, cs, bank, col0):
        P, ps = self.P, self.psum
        if srcT.dtype == BF16:
            bv = ps[:, bank, :].bitcast(BF16)
            idn = self.ident_bf
        else:
            bv = ps[:, bank:bank + 2, :].rearrange("p a b -> p (a b)")
            idn = self.ident
        for z in range(2):
            for hp in range(2):
                col = col0 + (4 * z + 2 * hp) * 64
                P.op("pe", lambda e, z=z, hp=hp, col=col: e.transpose(bv[0:64, col:col + 128], srcT[:, hp, z, cs], idn[:]),
                     reads=[skey, "ident_bf", "ident"], writes=["ps%d" % bank, "ps%d" % (bank + (0 if srcT.dtype == BF16 else 1))])
        return bv[0:64, col0:col0 + 512].rearrange("p (u d) -> p u d", d=64)

    def mixer_mlstm(self, l, first):
        P, ps = self.P, self.psum
        SDT = F32 if self.debug.get("mlf32") else BF16
        m0 = self.mark()
        wb, wkeys = self.load_w("wA", self.wA[l], 1040)
        m1 = self.mark()
        incl = self.cst("INCL")
        ones = self.consts[0:64, CO["ONES"][0]:CO["ONES"][0] + 128]
        ib = self.sb("ml_ib", [128, 8], F32)
        fb = self.sb("ml_fb", [128, 8], F32)
        P.dma("sp", lambda e: e.dma_start(out=ib[:], in_=self.ml_ib[l:l + 1, :].partition_broadcast(128)), writes=["ml_ib"])
        P.dma("sp", lambda e: e.dma_start(out=fb[:], in_=self.ml_fb[l:l + 1, :].partition_broadcast(128)), writes=["ml_fb"])
        Cg = self.sb("ml_C", [128, 4, 65], F32)
        Cb = self.sb("ml_Cb", [128, 4, 65], SDT)
        Cout = self.sb("ml_Cout", [128, 4, 65], F32)
        tmpC = self.sb("ml_tmpC", [128, 4, 65], F32)
        mst = self.sb("ml_m", [8, 1], F32)
        msc = self.sb("ml_msc", [8, 4], F32)
        dg = self.sb("ml_dg", [8, 8], F32)
        esl = self.sb("ml_esl", [128, 4], F32)
        P.dma("sp", lambda e: e.dma_start(out=Cg[:, :, 0:64], in_=self.init_mC[l].rearrange("z (hp hq) d e -> (hq d) (z hp) e", hq=2)), writes=["ml_C"])
        P.dma("sp", lambda e: e.dma_start(out=Cg[:, :, 64:65], in_=self.init_mn[l].rearrange("z (hp hq) (d o) -> (hq d) (z hp) o", hq=2, o=1),
                                          allow_slow_non_contiguous=True), writes=["ml_C"])
        P.dma("sp", lambda e: e.dma_start(out=mst[:], in_=self.init_mm[l].rearrange("z (h o) -> (z h) o", o=1), allow_slow_non_contiguous=True), writes=["ml_m"])

        def bcast_units(src81, sign, dst_keys):
            P.op("dve", lambda e: e.tensor_scalar(dg[:], self.ident[0:8, 0:8], src81, None, ALU.mult), reads=["ident", "ml_m"], writes=["ml_dg"])
            P.op("pe", lambda e: e.matmul(ps[:, 7, 0:8], self.consts[0:8, CO["ONES"][0]:CO["ONES"][0] + 128], dg[:], start=True, stop=True),
                 reads=["consts", "ml_dg"], writes=["ps7"])
            for hq in range(2):
                rows = slice(64 * hq, 64 * hq + 64)
                P.op("act", lambda e, rows=rows, hq=hq: e.activation(esl[rows, :], ps[rows, 7, hq:8:2], AF.Exp, scale=sign), writes=["ps7", "ml_esl"])

        bcast_units(mst[:, 0:1], 1.0, None)
        P.op("dve", lambda e: e.tensor_tensor(Cg[:], Cg[:], esl[:].unsqueeze(2).to_broadcast([128, 4, 65]), ALU.mult), reads=["ml_esl", "ml_C"], writes=["ml_C"])
        P.op("act", lambda e: e.activation(Cb[:], Cg[:], AF.Copy), reads=["ml_C"], writes=["ml_Cb"])
        qT = self.sb("ml_qT", [128, 2, 2, 2, 128], SDT)
        P.op("dve", lambda e: e.memset(qT[:], 0.0), writes=["ml_qT"])
        kT = self.sb("ml_kT", [128, 2, 2, 128], SDT)
        vT = self.sb("ml_vT", [128, 2, 2, 128], SDT)
        gT = self.sb("ml_gT", [8, 2, 128], F32)
        ktm = self.sb("ml_ktm", [64, 8, 64], SDT)
        vaug = self.sb("ml_vaug", [64, 8, 65], SDT)
        pm = self.sb("ml_pm", [64, 8, 64], SDT)
        osb = self.sb("ml_osb", [64, 2, 2, 256], F32)
        gtm = self.sb("ml_gtm", [64, 2, 8], F32)
        li = self.sb("ml_li", [64, 8], F32)
        sp = self.sb("ml_sp", [64, 8], F32)
        lib = self.sb("ml_lib", [64, 8], F32)
        e1 = self.sb("ml_e1", [64, 8], F32)
        eb = self.sb("ml_eb", [64, 8], F32)
        wk = self.sb("ml_wk", [64, 8], F32)
        nbl = self.sb("ml_nbl", [64, 8], F32)
        ebls = self.sb("ml_ebls", [128, 4], F32)
        dn = self.sb("ml_dn", [64, 8], F32)
        for t in range(NT):
            tiles = (t, NT - 1 - t)
            for hp in range(2):
                self.proj_fm(wb, wkeys, 128 * hp, 128, tiles, 0)
                for hq in range(2):
                    rows = slice(64 * hq, 64 * hq + 64)
                    self.evac_fm(lambda z, hp=hp, hq=hq, rows=rows: qT[rows, hp, hq, z, :], 0, 128, "ml_qT", rows=rows, eng="dve" if hq else "act")
                self.proj_fm(wb, wkeys, 256 + 128 * hp, 128, tiles, 1)
                self.evac_fm(lambda z, hp=hp: kT[:, hp, z, :], 1, 128, "ml_kT", scale=0.125)
                self.proj_fm(wb, wkeys, 512 + 128 * hp, 128, tiles, 0)
                self.evac_fm(lambda z, hp=hp: vT[:, hp, z, :], 0, 128, "ml_vT", eng="dve")
            for z in range(2):
                tok0 = 2 + 128 * tiles[z]
                for k in range(8):
                    P.op("pe", lambda e, k=k, z=z, tok0=tok0: e.matmul(ps[0:8, 1, 128 * z:128 * z + 128], wb[:, k, 768 + 8 * z:768 + 8 * z + 8],
                                                                    self.uT[:, k, tok0:tok0 + 128], start=(k == 0), stop=(k == 7)),
                         reads=wkeys + [("uT", tiles[z])], writes=["ps1"])
            self.evac_fm(lambda z: gT[:, z, :], 1, 8, "ml_gT", eng="dve")
            for c in range(2):
                cs = slice(64 * c, 64 * c + 64)
                for z in range(2):
                    P.op("pe", lambda e, z=z: e.transpose(ps[0:64, 7, 8 * z:8 * z + 8], gT[:, z, cs], self.ident[0:8, 0:8]), reads=["ml_gT", "ident"], writes=["ps7"])
                P.op("dve", lambda e: e.tensor_copy(gtm[:], ps[0:64, 7, 0:16].rearrange("p (z g) -> p z g", g=8)), writes=["ps7", "ml_gtm"])
                P.op("dve", lambda e: e.tensor_tensor(li[:].rearrange("p (z h) -> p z h", h=4), gtm[:, :, 0:4],
                                                      ib[0:64, :].rearrange("p (z h) -> p z h", h=4), ALU.add), reads=["ml_gtm", "ml_ib"], writes=["ml_li"])
                P.op("dve", lambda e: e.tensor_tensor(sp[:].rearrange("p (z h) -> p z h", h=4), gtm[:, :, 4:8],
                                                      fb[0:64, :].rearrange("p (z h) -> p z h", h=4), ALU.add), reads=["ml_gtm", "ml_fb"], writes=["ml_sp"])
                P.op("act", lambda e: e.activation(sp[:], sp[:], AF.Exp, scale=-1.0), reads=["ml_sp"], writes=["ml_sp"])
                P.op("act", lambda e: e.activation(sp[:], sp[:], AF.Ln, bias=1.0, scale=1.0), reads=["ml_sp"], writes=["ml_sp"])
                P.op("pe", lambda e: e.matmul(ps[0:64, 7, 16:24], incl, sp[:], start=True, stop=True), reads=["consts", "ml_sp"], writes=["ps7"])
                P.op("pe", lambda e: e.matmul(ps[:, 7, 24:32], ones, sp[:], start=True, stop=True), reads=["consts", "ml_sp"], writes=["ps7"])
                P.op("dve", lambda e: e.tensor_tensor(lib[:], ps[0:64, 7, 16:24], li[:], ALU.add), reads=["ml_li"], writes=["ps7", "ml_lib"])
                P.op("act", lambda e: e.activation(eb[:], ps[0:64, 7, 16:24], AF.Exp, scale=-1.0), writes=["ps7", "ml_eb"])
                P.op("dve", lambda e: e.tensor_copy(nbl[:], ps[0:64, 7, 24:32]), writes=["ps7", "ml_nbl"])
                for hq in range(2):
                    rows = slice(64 * hq, 64 * hq + 64)
                    P.op("act", lambda e, rows=rows, hq=hq: e.activation(ebls[rows, :], ps[rows, 7, 24 + hq:32:2], AF.Exp, scale=-1.0), writes=["ps7", "ml_ebls"])
                P.op("act", lambda e: e.activation(e1[:], lib[:], AF.Exp), reads=["ml_lib"], writes=["ml_e1"])
                P.op("dve", lambda e: e.tensor_tensor(wk[:], lib[:], nbl[:], ALU.subtract), reads=["ml_lib", "ml_nbl"], writes=["ml_wk"])
                ktv = self.tm_transposes(kT, "ml_kT", cs, 2, 0)
                vtv = self.tm_transposes(vT, "ml_vT", cs, 2, 512)
                P.op("act", lambda e: e.activation(ktm[:], ktv, AF.Copy), writes=["ps2", "ps3", "ml_ktm"])
                P.op("dve", lambda e: e.tensor_tensor(vaug[:, :, 0:64], vtv, e1[:].unsqueeze(2).to_broadcast([64, 8, 64]), ALU.mult),
                     reads=["ml_e1"], writes=["ps2", "ps3", "ml_vaug"])
                P.op("dve", lambda e: e.tensor_copy(vaug[:, :, 64:65], e1[:].unsqueeze(2)), reads=["ml_e1"], writes=["ml_vaug"])
                for z in range(2):
                    for h in range(4):
                        hp, hq = h // 2, h % 2
                        u = 4 * z + h
                        P.op("pe", lambda e, z=z, hp=hp, hq=hq, u=u: e.matmul(ps[0:64, 4, 64 * u:64 * u + 64], kT[:, hp, z, cs], qT[:, hp, hq, z, cs], start=True, stop=True),
                             reads=["ml_kT", "ml_qT"], writes=["ps4"])
                P.op("dve", lambda e: e.tensor_tensor(pm[:], ps[0:64, 4, :].rearrange("p (u l) -> p u l", l=64), incl.unsqueeze(1).to_broadcast([64, 8, 64]), ALU.mult),
                     reads=["consts"], writes=["ps4", "ml_pm"])
                for z in range(2):
                    bank = 5 + z
                    for h in range(4):
                        hp, hq = h // 2, h % 2
                        u = 4 * z + h
                        P.op("pe", lambda e, u=u, h=h, bank=bank: e.matmul(ps[0:64, bank, 65 * h:65 * h + 65], pm[:, u, :], vaug[:, u, :], start=True, stop=False),
                             reads=["ml_pm", "ml_vaug"], writes=["ps%d" % bank])
                        P.op("pe", lambda e, z=z, hp=hp, hq=hq, h=h, bank=bank: e.matmul(ps[0:64, bank, 65 * h:65 * h + 65], qT[:, hp, hq, z, cs], Cb[:, 2 * z + hp, :], start=False, stop=True),
                             reads=["ml_qT", "ml_Cb"], writes=["ps%d" % bank])
                for z in range(2):
                    bank = 5 + z
                    o3 = ps[0:64, bank, 0:260].rearrange("p (h e) -> p h e", e=65)
                    P.op("dve", lambda e, z=z, o3=o3: e.tensor_tensor(dn[:, 4 * z:4 * z + 4], o3[:, :, 64], eb[:, 4 * z:4 * z + 4], ALU.mult),
                         reads=["ml_eb"], writes=["ps%d" % bank, "ml_dn"])
                P.op("act", lambda e: e.activation(dn[:], dn[:], AF.Abs), reads=["ml_dn"], writes=["ml_dn"])
                P.op("dve", lambda e: e.tensor_scalar(dn[:], dn[:], 1.0, None, ALU.max), reads=["ml_dn"], writes=["ml_dn"])
                P.op("dve", lambda e: e.reciprocal(dn[:], dn[:]), reads=["ml_dn"], writes=["ml_dn"])
                P.op("dve", lambda e: e.tensor_tensor(dn[:], dn[:], eb[:], ALU.mult), reads=["ml_dn", "ml_eb"], writes=["ml_dn"])
                for z in range(2):
                    bank = 5 + z
                    o3 = ps[0:64, bank, 0:260].rearrange("p (h e) -> p h e", e=65)
                    P.op("dve", lambda e, z=z, o3=o3, c=c: e.tensor_tensor(
                        osb[:, c, z, :].rearrange("p (h e) -> p h e", e=64), o3[:, :, 0:64],
                        dn[:, 4 * z:4 * z + 4].unsqueeze(2).to_broadcast([64, 4, 64]), ALU.mult),
                        reads=["ml_dn"], writes=["ps%d" % bank, ("ml_osb", c)])
                for z in range(2):
                    for h in range(4):
                        hp, hq = h // 2, h % 2
                        rows = slice(64 * hq, 64 * hq + 64)
                        u = 4 * z + h
                        col = (2 * z + hp) * 65
                        P.op("pe", lambda e, rows=rows, u=u, col=col: e.matmul(ps[rows, 3, col:col + 65], ktm[:, u, :], vaug[:, u, :], start=True, stop=True),
                             reads=["ml_ktm", "ml_vaug"], writes=["ps3"])
                P.op("dve", lambda e: e.tensor_tensor(tmpC[:], ps[:, 3, 0:260].rearrange("p (a e) -> p a e", e=65), Cg[:], ALU.add),
                     reads=["ml_C"], writes=["ps3", "ml_tmpC"])
                P.op("dve", lambda e: e.tensor_tensor(Cg[:], tmpC[:], ebls[:].unsqueeze(2).to_broadcast([128, 4, 65]), ALU.mult),
                     reads=["ml_tmpC", "ml_ebls"], writes=["ml_C"])
                if not (t % 2 == 1 and c == 1):
                    P.op("act", lambda e: e.activation(Cb[:], Cg[:], AF.Copy), reads=["ml_C"], writes=["ml_Cb"])
                P.op("pe", lambda e: e.transpose(ps[0:8, 7, 64:128], wk[:], self.ident[0:64, 0:64]), reads=["ml_wk", "ident"], writes=["ps7"])
                P.op("pe", lambda e: e.transpose(ps[0:8, 7, 128:192], nbl[:], self.ident[0:64, 0:64]), reads=["ml_nbl", "ident"], writes=["ps7"])
                P.op("dve", lambda e: e.tensor_reduce(msc[:, 0:1], ps[0:8, 7, 64:128], AX.X, ALU.max), writes=["ps7", "ml_msc"])
                P.op("dve", lambda e: e.tensor_tensor(msc[:, 1:2], mst[:], ps[0:8, 7, 128:129], ALU.subtract), reads=["ml_m"], writes=["ps7", "ml_msc"])
                P.op("dve", lambda e: e.tensor_tensor(mst[:], msc[:, 0:1], msc[:, 1:2], ALU.max), reads=["ml_msc"], writes=["ml_m"])
            self.emit_out_tile(t, osb, [("ml_osb", 0), ("ml_osb", 1)], 7)
            if t % 2 == 1:
                bcast_units(mst[:, 0:1], -1.0, None)
                P.op("dve", lambda e: e.tensor_tensor(Cout[:], Cg[:], esl[:].unsqueeze(2).to_broadcast([128, 4, 65]), ALU.mult),
                     reads=["ml_C", "ml_esl"], writes=["ml_Cout"])
                for z in range(2):
                    seg = (t - 1) // 2 if z == 0 else (NT - 1 - t) // 2
                    P.dma("sp", lambda e, z=z, seg=seg: e.dma_start(
                        out=self.out_mC[seg, l, z].rearrange("(hp hq) d e -> (hq d) hp e", hq=2), in_=Cout[:, 2 * z:2 * z + 2, 0:64]), reads=["ml_Cout"])
                    P.dma("sp", lambda e, z=z, seg=seg: e.dma_start(
                        out=self.out_mn[seg, l, z].rearrange("(hp hq) (d o) -> (hq d) hp o", hq=2, o=1), in_=Cout[:, 2 * z:2 * z + 2, 64:65],
                        allow_slow_non_contiguous=True), reads=["ml_Cout"])
                    P.dma("sp", lambda e, z=z, seg=seg: e.dma_start(
                        out=self.out_mm[seg, l, z].rearrange("(h o) -> h o", o=1), in_=mst[4 * z:4 * z + 4, :], allow_slow_non_contiguous=True), reads=["ml_m"])
                P.op("dve", lambda e: e.tensor_scalar(Cg[:], Cg[:], self.keep[:, 0:1], None, ALU.mult), reads=["ml_C", "keep"], writes=["ml_C"])
                P.op("dve", lambda e: e.tensor_scalar(mst[:], mst[:], self.keep[0:8, 0:1], None, ALU.mult), reads=["ml_m", "keep"], writes=["ml_m"])
                P.op("act", lambda e: e.activation(Cb[:], Cg[:], AF.Copy), reads=["ml_C"], writes=["ml_Cb"])
        if self.debug.get("yacc") == "mlstm":
            P.dma("sp", lambda e: e.dma_start(out=self.dbg_yacc, in_=self.yacc[:]), reads=[("yacc", i) for i in range(NT)])
        self.release(m1)
        if self.debug.get("post", True):
            self.post_simple(l, "ml", wb, wkeys, 784, AF.Sigmoid, self.ml_norm, True, 0, first)
        self.release(m0)

    def neumann_inverse(self, pfx, Pm, Qm, Rm, banks, keys=None):
        P, ps = self.P, self.psum
        bP, bQ, bR = banks
        kP, kQ, kR = keys if keys is not None else (pfx + "P", pfx + "Q", pfx + "R")
        ident64 = self.ident[0:64, 0:64]
        for u in range(8):
            P.op("pe", lambda e, u=u: e.transpose(ps[0:64, bQ, 64 * u:64 * u + 64], Pm[:, u, :], ident64), reads=[kP, "ident"], writes=["ps%d" % bQ])
        P.op("act", lambda e: e.activation(Qm[:], ps[0:64, bQ, :].rearrange("p (u l) -> p u l", l=64), AF.Copy), writes=["ps%d" % bQ, kQ])
        P.op("dve", lambda e: e.tensor_tensor(Rm[:], Pm[:], ident64.unsqueeze(1).to_broadcast([64, 8, 64]), ALU.add), reads=[kP, "ident"], writes=[kR])
        for lvl in range(5):
            last = lvl == 4
            if not last:
                for u in range(8):
                    P.op("pe", lambda e, u=u: e.matmul(ps[0:64, bP, 64 * u:64 * u + 64], Qm[:, u, :], Pm[:, u, :], start=True, stop=True),
                         reads=[kP, kQ], writes=["ps%d" % bP])
            for u in range(8):
                P.op("pe", lambda e, u=u: e.matmul(ps[0:64, bQ, 64 * u:64 * u + 64], Pm[:, u, :], Qm[:, u, :], start=True, stop=True),
                     reads=[kP, kQ], writes=["ps%d" % bQ])
            if not last:
                P.op("dve", lambda e: e.tensor_copy(Pm[:], ps[0:64, bP, :].rearrange("p (u l) -> p u l", l=64)), writes=["ps%d" % bP, kP])
            P.op("act", lambda e: e.activation(Qm[:], ps[0:64, bQ, :].rearrange("p (u l) -> p u l", l=64), AF.Copy), writes=["ps%d" % bQ, kQ])
            for u in range(8):
                P.op("pe", lambda e, u=u: e.matmul(ps[0:64, bR, 64 * u:64 * u + 64], Qm[:, u, :], Rm[:, u, :], start=True, stop=True),
                     reads=[kQ, kR], writes=["ps%d" % bR])
            P.op("dve", lambda e: e.tensor_tensor(Rm[:], ps[0:64, bR, :].rearrange("p (u l) -> p u l", l=64), Rm[:], ALU.add),
                 reads=[kR], writes=["ps%d" % bR, kR])

    def mixer_delta(self, l, first):
        P, ps = self.P, self.psum
        m0 = self.mark()
        wb, wkeys = self.load_w("wB", self.wB[l], 1040)
        m1 = self.mark()
        incl = self.cst("INCL")
        strict = self.cst("STRICT")
        ones64 = self.consts[0:64, CO["ONES"][0]:CO["ONES"][0] + 64]
        ones128 = self.consts[0:64, CO["ONES"][0]:CO["ONES"][0] + 128]
        blk = self.consts[:, CO["BLK"][0]:CO["BLK"][0] + 128]
        ident64 = self.ident[0:64, 0:64]
        cw = self.sb("dl_cw", [128, 6, 5], F32)
        P.dma("sp", lambda e: e.dma_start(out=cw[:], in_=self.dl_conv[l]), writes=["dl_cw"])
        Au = self.sb("dl_A", [128, 8], F32)
        dtb = self.sb("dl_dtb", [128, 8], F32)
        P.dma("sp", lambda e: e.dma_start(out=Au[:], in_=self.dl_alog[l:l + 1, :].partition_broadcast(128)), writes=["dl_A"])
        P.dma("sp", lambda e: e.dma_start(out=dtb[:], in_=self.dl_dtb[l:l + 1, :].partition_broadcast(128)), writes=["dl_dtb"])
        P.op("act", lambda e: e.activation(Au[:], Au[:], AF.Exp), reads=["dl_A"], writes=["dl_A"])
        S = self.sb("dl_S", [128, 4, 64], F32)
        Sb = self.sb("dl_Sb", [128, 4, 64], BF16)
        Sout = self.sb("dl_Sout", [128, 4, 64], F32)
        tmpS = self.sb("dl_tmpS", [128, 4, 64], F32)
        P.dma("sp", lambda e: e.dma_start(out=S[:], in_=self.init_delta[l].rearrange("z (hp hq) d e -> (hq d) (z hp) e", hq=2)), writes=["dl_S"])
        P.op("act", lambda e: e.activation(Sb[:], S[:], AF.Copy), reads=["dl_S"], writes=["dl_Sb"])
        pre = self.sb("dl_pre", [128, 2, 132], F32)
        acc = self.sb("dl_acc", [128, 2, 128], F32)
        sq = self.sb("dl_sq", [128, 2, 128], F32)
        rn = self.sb("dl_rn", [128, 2, 128], F32)
        qTp = self.sb("dl_qTp", [128, 2, 2, 2, 128], BF16)
        kTp = self.sb("dl_kTp", [128, 2, 2, 2, 128], BF16)
        P.op("dve", lambda e: e.memset(qTp[:], 0.0), writes=["dl_qTp"])
        P.op("dve", lambda e: e.memset(kTp[:], 0.0), writes=["dl_kTp"])
        kT = self.sb("dl_kT", [128, 2, 2, 128], BF16)
        vT = self.sb("dl_vT", [128, 2, 2, 128], BF16)
        gT = self.sb("dl_gT", [8, 2, 128], F32)
        gtm = self.sb("dl_gtm", [64, 4, 8], F32)
        beta2 = self.sb("dl_beta", [64, 2, 8], F32)
        nbeta2 = self.sb("dl_nbeta", [64, 2, 8], F32)
        ng2 = self.sb("dl_ng", [64, 2, 8], F32)
        ngc2 = self.sb("dl_ngc", [64, 2, 8], F32)
        eg2 = self.sb("dl_eg", [64, 2, 8], F32)
        eglt2 = self.sb("dl_eglt", [64, 2, 8], F32)
        egls2 = self.sb("dl_egls", [128, 2, 4], F32)
        dgm = self.sb("dl_dgm", [64, 8, 64], F32)
        dT = self.sb("dl_dT", [64, 8, 64], F32)
        dTs = self.sb("dl_dTs", [64, 8, 64], F32)
        Pm = self.sb("dl_P", [64, 8, 64], F32)
        Qm = self.sb("dl_Q", [64, 8, 64], F32)
        Rm = self.sb("dl_R", [64, 8, 64], F32)
        qkd = self.sb("dl_qkd", [64, 8, 64], BF16)
        ktm = self.sb("dl_ktm", [64, 8, 64], BF16)
        vtm = self.sb("dl_vtm", [64, 8, 64], F32)
        kd = self.sb("dl_kd", [64, 8, 64], BF16)
        rr = self.sb("dl_r", [64, 8, 64], F32)
        vnew = self.sb("dl_vnew", [64, 8, 64], BF16)
        vnf = self.sb("dl_vnf", [64, 8, 64], F32)
        t1 = self.sb("dl_t1", [64, 8, 64], F32)
        osb = self.sb("dl_osb", [64, 2, 2, 256], F32)
        for t in range(NT):
            tiles = (t, NT - 1 - t)
            edge = slice(0, 2) if t % 2 == 0 else slice(130, 132)
            for j in range(6):
                bank = j % 2
                self.proj_fm(wb, wkeys, 128 * j, 128, tiles, bank, halo=2)
                self.evac_fm(lambda z: pre[:, z, :], bank, 128, "dl_pre", W=132, eng="act")
                P.op("dve", lambda e: e.tensor_scalar(pre[:, :, edge], pre[:, :, edge], self.keep[:, 0:1], None, ALU.mult), reads=["dl_pre", "keep"], writes=["dl_pre"])
                for z in range(2):
                    eng = "dve"
                    for k in range(5):
                        wk_ = cw[:, j, k:k + 1] if z == 0 else cw[:, j, 4 - k:5 - k]
                        if k == 0:
                            P.op(eng, lambda e, z=z, wk_=wk_: e.tensor_scalar(acc[:, z, :], pre[:, z, 0:128], wk_, None, ALU.mult),
                                 reads=["dl_pre", "dl_cw"], writes=[("dl_acc", z)])
                        else:
                            P.op(eng, lambda e, z=z, k=k, wk_=wk_: e.scalar_tensor_tensor(acc[:, z, :], pre[:, z, k:k + 128], wk_, acc[:, z, :], ALU.mult, ALU.add),
                                 reads=["dl_pre", "dl_cw", ("dl_acc", z)], writes=[("dl_acc", z)])
                akeys = [("dl_acc", 0), ("dl_acc", 1)]
                self.act_sigmoid(sq[:], acc[:], akeys, ["dl_sq"])
                if j >= 4:
                    P.op("dve", lambda e, j=j: e.tensor_tensor(vT[:, j - 4, :, :], acc[:], sq[:], ALU.mult), reads=akeys + ["dl_sq"], writes=["dl_vT"])
                    continue
                P.op("dve", lambda e: e.tensor_tensor(acc[:], acc[:], sq[:], ALU.mult), reads=akeys + ["dl_sq"], writes=akeys)
                P.op("act", lambda e: e.activation(sq[:], acc[:], AF.Square), reads=akeys, writes=["dl_sq"])
                P.op("pe", lambda e: e.matmul(ps[:, 2, 0:256], blk, sq[:].rearrange("p z n -> p (z n)"), start=True, stop=True), reads=["consts", "dl_sq"], writes=["ps2"])
                P.op("act", lambda e: e.activation(rn[:], ps[:, 2, 0:256].rearrange("p (z n) -> p z n", z=2), AF.Ln, bias=1e-6, scale=1.0), writes=["ps2", "dl_rn"])
                P.op("act", lambda e: e.activation(rn[:], rn[:], AF.Exp, scale=-0.5), reads=["dl_rn"], writes=["dl_rn"])
                hp = j % 2
                if j < 2:
                    for hq in range(2):
                        rows = slice(64 * hq, 64 * hq + 64)
                        P.op("dve", lambda e, rows=rows, hp=hp, hq=hq: e.scalar_tensor_tensor(qTp[rows, hp, hq, :, :], acc[rows, :, :], 0.125, rn[rows, :, :], ALU.mult, ALU.mult),
                             reads=akeys + ["dl_rn"], writes=["dl_qTp"])
                else:
                    P.op("dve", lambda e, hp=hp: e.tensor_tensor(kT[:, hp, :, :], acc[:], rn[:], ALU.mult), reads=akeys + ["dl_rn"], writes=["dl_kT"])
                    for hq in range(2):
                        rows = slice(64 * hq, 64 * hq + 64)
                        P.op("pool", lambda e, rows=rows, hp=hp, hq=hq: e.tensor_copy(kTp[rows, hp, hq, :, :], kT[rows, hp, :, :]), reads=["dl_kT"], writes=["dl_kTp"])
            for z in range(2):
                tok0 = 2 + 128 * tiles[z]
                for k in range(8):
                    P.op("pe", lambda e, k=k, z=z, tok0=tok0: e.matmul(ps[0:8, 1, 128 * z:128 * z + 128], wb[:, k, 768 + 8 * z:768 + 8 * z + 8],
                                                                    self.uT[:, k, tok0:tok0 + 128], start=(k == 0), stop=(k == 7)),
                         reads=wkeys + [("uT", tiles[z])], writes=["ps1"])
            self.evac_fm(lambda z: gT[:, z, :], 1, 8, "dl_gT", eng="dve")
            for c in range(2):
                for z in range(2):
                    q_ = 2 * c + z
                    P.op("pe", lambda e, z=z, c=c, q_=q_: e.transpose(ps[0:64, 7, 8 * q_:8 * q_ + 8], gT[:, z, 64 * c:64 * c + 64], self.ident[0:8, 0:8]),
                         reads=["dl_gT", "ident"], writes=["ps7"])
            P.op("dve", lambda e: e.tensor_copy(gtm[:], ps[0:64, 7, 0:32].rearrange("p (q g) -> p q g", g=8)), writes=["ps7", "dl_gtm"])
            b4 = beta2[:].rearrange("p c (z h) -> p (c z) h", h=4)
            P.op("act", lambda e: e.activation(b4, gtm[:, :, 0:4], AF.Exp, scale=-1.0), reads=["dl_gtm"], writes=["dl_beta"])
            P.op("act", lambda e: e.activation(beta2[:], beta2[:], AF.Ln, bias=1.0, scale=1.0), reads=["dl_beta"], writes=["dl_beta"])
            P.op("act", lambda e: e.activation(beta2[:], beta2[:], AF.Exp, scale=-1.0), reads=["dl_beta"], writes=["dl_beta"])
            P.op("dve", lambda e: e.tensor_scalar(nbeta2[:], beta2[:], -1.0, None, ALU.mult), reads=["dl_beta"], writes=["dl_nbeta"])
            for c in range(2):
                P.op("dve", lambda e, c=c: e.tensor_tensor(ng2[:, c, :].rearrange("p (z h) -> p z h", h=4), gtm[:, 2 * c:2 * c + 2, 4:8],
                                                           dtb[0:64, :].rearrange("p (z h) -> p z h", h=4), ALU.add),
                     reads=["dl_gtm", "dl_dtb"], writes=["dl_ng"])
            P.op("act", lambda e: e.activation(ng2[:], ng2[:], AF.Exp), reads=["dl_ng"], writes=["dl_ng"])
            P.op("act", lambda e: e.activation(ng2[:], ng2[:], AF.Ln, bias=1.0, scale=1.0), reads=["dl_ng"], writes=["dl_ng"])
            P.op("dve", lambda e: e.tensor_tensor(ng2[:], ng2[:], Au[0:64, :].unsqueeze(1).to_broadcast([64, 2, 8]), ALU.mult), reads=["dl_ng", "dl_A"], writes=["dl_ng"])
            for c in range(2):
                P.op("pe", lambda e, c=c: e.matmul(ps[0:64, 7, 32 + 8 * c:40 + 8 * c], incl, ng2[:, c, :], start=True, stop=True), reads=["consts", "dl_ng"], writes=["ps7"])
                P.op("pe", lambda e, c=c: e.matmul(ps[:, 7, 48 + 8 * c:56 + 8 * c], ones128, ng2[:, c, :], start=True, stop=True), reads=["consts", "dl_ng"], writes=["ps7"])
            ngcp = ps[0:64, 7, 32:48].rearrange("p (c u) -> p c u", u=8)
            nglp = ps[0:64, 7, 48:64].rearrange("p (c u) -> p c u", u=8)
            P.op("dve", lambda e: e.tensor_copy(ngc2[:], ngcp), writes=["ps7", "dl_ngc"])
            P.op("act", lambda e: e.activation(eg2[:], ngcp, AF.Exp, scale=-1.0), writes=["ps7", "dl_eg"])
            P.op("dve", lambda e: e.tensor_tensor(eglt2[:], ngc2[:], nglp, ALU.subtract), reads=["dl_ngc"], writes=["ps7", "dl_eglt"])
            P.op("act", lambda e: e.activation(eglt2[:], eglt2[:], AF.Exp), reads=["dl_eglt"], writes=["dl_eglt"])
            for hq in range(2):
                rows = slice(64 * hq, 64 * hq + 64)
                P.op("act", lambda e, rows=rows, hq=hq: e.activation(egls2[rows, :, :], ps[rows, 7, 48:64].rearrange("p (c u) -> p c u", u=8)[:, :, hq:8:2], AF.Exp, scale=-1.0),
                     writes=["ps7", "dl_egls"])
            for c in range(2):
                cs = slice(64 * c, 64 * c + 64)
                beta, nbeta, ngc, eg, eglt, egls = beta2[:, c, :], nbeta2[:, c, :], ngc2[:, c, :], eg2[:, c, :], eglt2[:, c, :], egls2[:, c, :]
                P.op("dve", lambda e: e.tensor_tensor(dgm[:], ident64.unsqueeze(1).to_broadcast([64, 8, 64]), ngc[:].unsqueeze(2).to_broadcast([64, 8, 64]), ALU.mult),
                     reads=["ident", "dl_ngc"], writes=["dl_dgm"])
                P.op("pe", lambda e: e.matmul(ps[0:64, 3, :], ones64, dgm[:].rearrange("p u l -> p (u l)"), start=True, stop=True), reads=["consts", "dl_dgm"], writes=["ps3"])
                P.op("dve", lambda e: e.tensor_tensor(dT[:], ngc[:].unsqueeze(2).to_broadcast([64, 8, 64]), ps[0:64, 3, :].rearrange("p (u l) -> p u l", l=64), ALU.subtract),
                     reads=["dl_ngc"], writes=["ps3", "dl_dT"])
                P.op("dve", lambda e: e.tensor_scalar(dT[:], dT[:], 0.0, None, ALU.min), reads=["dl_dT"], writes=["dl_dT"])
                P.op("act", lambda e: e.activation(dT[:], dT[:], AF.Exp), reads=["dl_dT"], writes=["dl_dT"])
                P.op("pool", lambda e: e.tensor_tensor(dTs[:], dT[:], strict.unsqueeze(1).to_broadcast([64, 8, 64]), ALU.mult), reads=["dl_dT", "consts"], writes=["dl_dTs"])
                P.op("pool", lambda e: e.tensor_tensor(dT[:], dT[:], incl.unsqueeze(1).to_broadcast([64, 8, 64]), ALU.mult), reads=["dl_dT", "consts"], writes=["dl_dT"])
                ktv = self.tm_transposes(kT, "dl_kT", cs, 2, 0)
                vtv = self.tm_transposes(vT, "dl_vT", cs, 2, 512)
                P.op("act", lambda e: e.activation(ktm[:], ktv, AF.Copy), writes=["ps2", "dl_ktm"])
                P.op("dve", lambda e: e.tensor_copy(vtm[:], vtv), writes=["ps2", "dl_vtm"])
                P.op("dve", lambda e: e.tensor_tensor(kd[:], ktm[:], eglt[:].unsqueeze(2).to_broadcast([64, 8, 64]), ALU.mult), reads=["dl_ktm", "dl_eglt"], writes=["dl_kd"])
                for z in range(2):
                    for h in range(4):
                        hp, hq = h // 2, h % 2
                        u = 4 * z + h
                        P.op("pe", lambda e, z=z, hp=hp, hq=hq, u=u: e.matmul(ps[0:64, 4, 64 * u:64 * u + 64], kT[:, hp, z, cs], kTp[:, hp, hq, z, cs], start=True, stop=True),
                             reads=["dl_kT", "dl_kTp"], writes=["ps4"])
                        P.op("pe", lambda e, z=z, hp=hp, hq=hq, u=u: e.matmul(ps[0:64, 5, 64 * u:64 * u + 64], kT[:, hp, z, cs], qTp[:, hp, hq, z, cs], start=True, stop=True),
                             reads=["dl_kT", "dl_qTp"], writes=["ps5"])
                P.op("dve", lambda e: e.tensor_tensor(Pm[:], ps[0:64, 4, :].rearrange("p (u l) -> p u l", l=64), dTs[:], ALU.mult), reads=["dl_dTs"], writes=["ps4", "dl_P"])
                P.op("dve", lambda e: e.tensor_tensor(Pm[:], Pm[:], nbeta[:].unsqueeze(2).to_broadcast([64, 8, 64]), ALU.mult), reads=["dl_P", "dl_nbeta"], writes=["dl_P"])
                P.op("dve", lambda e: e.tensor_tensor(qkd[:], ps[0:64, 5, :].rearrange("p (u l) -> p u l", l=64), dT[:], ALU.mult), reads=["dl_dT"], writes=["ps5", "dl_qkd"])
                self.neumann_inverse("dl_", Pm, Qm, Rm, (4, 5, 6))
                for z in range(2):
                    bank = 3 + z
                    for h in range(4):
                        hp, hq = h // 2, h % 2
                        P.op("pe", lambda e, z=z, hp=hp, hq=hq, h=h, bank=bank: e.matmul(ps[0:64, bank, 128 * h:128 * h + 64], kTp[:, hp, hq, z, cs], Sb[:, 2 * z + hp, :], start=True, stop=True),
                             reads=["dl_kTp", "dl_Sb"], writes=["ps%d" % bank])
                        P.op("pe", lambda e, z=z, hp=hp, hq=hq, h=h, bank=bank: e.matmul(ps[0:64, bank, 128 * h + 64:128 * h + 128], qTp[:, hp, hq, z, cs], Sb[:, 2 * z + hp, :], start=True, stop=True),
                             reads=["dl_qTp", "dl_Sb"], writes=["ps%d" % bank])
                for z in range(2):
                    bank = 3 + z
                    ks4 = ps[0:64, bank, :].rearrange("p (h two e) -> p h two e", two=2, e=64)
                    us = slice(4 * z, 4 * z + 4)
                    P.op("dve", lambda e, ks4=ks4, us=us: e.tensor_tensor(t1[:, us, :], ks4[:, :, 0, :], eg[:, us].unsqueeze(2).to_broadcast([64, 4, 64]), ALU.mult),
                         reads=["dl_eg"], writes=["ps%d" % bank, ("dl_t1", z)])
                    P.op("dve", lambda e, us=us, z=z: e.tensor_tensor(rr[:, us, :], vtm[:, us, :], t1[:, us, :], ALU.subtract), reads=["dl_vtm", ("dl_t1", z)], writes=[("dl_r", z)])
                    P.op("dve", lambda e, ks4=ks4, us=us: e.tensor_tensor(t1[:, us, :], ks4[:, :, 1, :], eg[:, us].unsqueeze(2).to_broadcast([64, 4, 64]), ALU.mult),
                         reads=["dl_eg", ("dl_r", z)], writes=["ps%d" % bank, ("dl_t1", z)])
                rkeys = [("dl_r", 0), ("dl_r", 1)]
                for u in range(8):
                    P.op("pe", lambda e, u=u: e.matmul(ps[0:64, 5, 64 * u:64 * u + 64], Rm[:, u, :], rr[:, u, :], start=True, stop=True), reads=["dl_R"] + rkeys, writes=["ps5"])
                P.op("dve", lambda e: e.tensor_tensor(vnew[:], ps[0:64, 5, :].rearrange("p (u l) -> p u l", l=64), beta[:].unsqueeze(2).to_broadcast([64, 8, 64]), ALU.mult),
                     reads=["dl_beta"], writes=["ps5", "dl_vnew"])
                for u in range(8):
                    P.op("pe", lambda e, u=u: e.matmul(ps[0:64, 6, 64 * u:64 * u + 64], qkd[:, u, :], vnew[:, u, :], start=True, stop=True), reads=["dl_qkd", "dl_vnew"], writes=["ps6"])
                P.op("dve", lambda e, c=c: e.tensor_tensor(osb[:, c, :, :].rearrange("p z (h e) -> p (z h) e", e=64), ps[0:64, 6, :].rearrange("p (u e) -> p u e", e=64), t1[:], ALU.add),
                     reads=[("dl_t1", 0), ("dl_t1", 1)], writes=["ps6", ("dl_osb", c)])
                for z in range(2):
                    for h in range(4):
                        hp, hq = h // 2, h % 2
                        rows = slice(64 * hq, 64 * hq + 64)
                        u = 4 * z + h
                        col = (2 * z + hp) * 64
                        P.op("pe", lambda e, rows=rows, u=u, col=col: e.matmul(ps[rows, 3, col:col + 64], kd[:, u, :], vnew[:, u, :], start=True, stop=True),
                             reads=["dl_kd", "dl_vnew"], writes=["ps3"])
                P.op("dve", lambda e: e.tensor_tensor(tmpS[:], S[:], egls[:].unsqueeze(2).to_broadcast([128, 4, 64]), ALU.mult), reads=["dl_S", "dl_egls"], writes=["dl_tmpS"])
                P.op("dve", lambda e: e.tensor_tensor(S[:], ps[:, 3, 0:256].rearrange("p (a e) -> p a e", e=64), tmpS[:], ALU.add), reads=["dl_tmpS"], writes=["ps3", "dl_S"])
                if not (t % 2 == 1 and c == 1):
                    P.op("act", lambda e: e.activation(Sb[:], S[:], AF.Copy), reads=["dl_S"], writes=["dl_Sb"])
            self.emit_out_tile(t, osb, [("dl_osb", 0), ("dl_osb", 1)], 7)
            if t % 2 == 1:
                P.op("dve", lambda e: e.tensor_copy(Sout[:], S[:]), reads=["dl_S"], writes=["dl_Sout"])
                for z in range(2):
                    seg = (t - 1) // 2 if z == 0 else (NT - 1 - t) // 2
                    P.dma("sp", lambda e, z=z, seg=seg: e.dma_start(
                        out=self.out_delta[seg, l, z].rearrange("(hp hq) d e -> (hq d) hp e", hq=2), in_=Sout[:, 2 * z:2 * z + 2, :]), reads=["dl_Sout"])
                P.op("dve", lambda e: e.tensor_scalar(S[:], S[:], self.keep[:, 0:1], None, ALU.mult), reads=["dl_S", "keep"], writes=["dl_S"])
                P.op("act", lambda e: e.activation(Sb[:], S[:], AF.Copy), reads=["dl_S"], writes=["dl_Sb"])
        if self.debug.get("yacc") == "delta":
            P.dma("sp", lambda e: e.dma_start(out=self.dbg_yacc, in_=self.yacc[:]), reads=[("yacc", i) for i in range(NT)])
        self.release(m1)
        if self.debug.get("post", True):
            self.post_simple(l, "dl", wb, wkeys, 784, AF.Silu, self.dl_norm, False, 256, first)
        self.release(m0)

    def rwkv_mix_fm(self, pre, mu_col, dst, keys_r, keys_w, eng="dve"):
        P = self.P
        n = dst.shape[-1]
        P.op("dve", lambda e: e.tensor_tensor(dst, pre[:, :, 0:n], pre[:, :, 2:n + 2], ALU.add), reads=keys_r, writes=keys_w)
        P.op("dve", lambda e: e.scalar_tensor_tensor(dst, dst, 0.5, pre[:, :, 1:n + 1], ALU.mult, ALU.subtract), reads=keys_r + keys_w, writes=keys_w)
        P.op("dve", lambda e: e.scalar_tensor_tensor(dst, dst, mu_col, pre[:, :, 1:n + 1], ALU.mult, ALU.add), reads=keys_r + keys_w + ["rw_mu"], writes=keys_w)

    def mixer_rwkv(self, l, first):
        P, ps = self.P, self.psum
        m0 = self.mark()
        wb, wkeys = self.load_w("wD", self.wD[l], 1152)
        incl = self.cst("INCL")
        strict = self.cst("STRICT")
        ones64 = self.consts[0:64, CO["ONES"][0]:CO["ONES"][0] + 64]
        ident64 = self.ident[0:64, 0:64]
        bacc = self.sb("rw_bacc", [128, NT, 256], BF16)
        mu = self.sb("rw_mu", [128, 9], F32)
        P.dma("sp", lambda e: e.dma_start(out=mu[:], in_=self.rw_mu[l]), writes=["rw_mu"])
        m1 = self.mark()
        w2p = self.sb("rw_w2p", [128, 2, 256], F32)
        a2p = self.sb("rw_a2p", [128, 2, 256], F32)
        P.dma("sp", lambda e: e.dma_start(out=w2p[:], in_=self.rw_w2p[l]), writes=["rw_w2p"])
        P.dma("sp", lambda e: e.dma_start(out=a2p[:], in_=self.rw_a2p[l]), writes=["rw_a2p"])
        bcs = {}
        for nm, src, width in (("w0", self.rw_w0, 512), ("a0", self.rw_a0, 512), ("kk", self.rw_kk, 256), ("ka", self.rw_ka, 256), ("rk", self.rw_rk, 256)):
            bcs[nm] = self.sb("rw_bc_" + nm, [64, width], F32)
            P.dma("sp", lambda e, nm=nm, src=src: e.dma_start(out=bcs[nm][:], in_=src[l:l + 1, :].partition_broadcast(64)), writes=["rw_bc"])
        omka = self.sb("rw_omka", [64, 256], F32)
        P.op("dve", lambda e: e.tensor_scalar(omka[:], bcs["ka"][:], -1.0, 1.0, ALU.mult, ALU.add), reads=["rw_bc"], writes=["rw_omka"])
        M = self.sb("rw_M", [64, 8, 64], F32)
        pre = self.sb("rw_pre", [128, 2, 130], F32)
        rT = self.sb("rw_rT", [128, 2, 2, 128], BF16)
        kT = self.sb("rw_kT", [128, 2, 2, 128], BF16)
        vT = self.sb("rw_vT", [128, 2, 2, 128], BF16)
        twT = self.sb("rw_twT", [128, 2, 128], F32)
        daT = self.sb("rw_daT", [128, 2, 128], F32)
        rtm = self.sb("rw_rtm", [64, 2, 256], F32)
        ktm = self.sb("rw_ktm", [64, 2, 256], F32)
        vtm = self.sb("rw_vtm", [64, 2, 256], F32)
        lw = self.sb("rw_lw", [64, 2, 256], F32)
        av = self.sb("rw_a", [64, 2, 256], F32)
        ecl = self.sb("rw_ecl", [64, 2, 256], F32)
        encl = self.sb("rw_encl", [64, 2, 256], F32)
        ecw = self.sb("rw_ecw", [64, 2, 256], F32)
        kh = self.sb("rw_kh", [64, 2, 256], F32)
        kx = self.sb("rw_kx", [64, 2, 256], F32)
        ss8 = self.sb("rw_ss8", [64, 8], F32)
        bs8 = self.sb("rw_bs8", [64, 8], F32)
        ktz = self.sb("rw_ktz", [64, 2, 256], F32)
        al = self.sb("rw_al", [64, 2, 256], F32)
        be = self.sb("rw_be", [64, 2, 256], F32)
        kti = self.sb("rw_kti", [64, 2, 256], F32)
        rti = self.sb("rw_rti", [64, 2, 256], F32)
        beT = self.sb("rw_beT", [64, 8, 64], F32)
        Pm = lw[:].rearrange("p z (h j) -> p (z h) j", j=64)
        Qm = av[:].rearrange("p z (h j) -> p (z h) j", j=64)
        Aak = ecl[:].rearrange("p z (h j) -> p (z h) j", j=64)
        Ara = encl[:].rearrange("p z (h j) -> p (z h) j", j=64)
        Ark = ecw[:].rearrange("p z (h j) -> p (z h) j", j=64)
        X1 = kh[:].rearrange("p z (h j) -> p (z h) j", j=64)
        Uu = kx[:].rearrange("p z (h j) -> p (z h) j", j=64)
        tmpM = ktz[:].rearrange("p z (h j) -> p (z h) j", j=64)
        alT = rtm[:].rearrange("p z (h j) -> p (z h) j", j=64)
        ktT = ktm[:].rearrange("p z (h j) -> p (z h) j", j=64)
        rtT = be[:].rearrange("p z (h j) -> p (z h) j", j=64)
        Mt = kh[:].rearrange("p z (h j) -> p (z h) j", j=64)
        Rm = rti[:].rearrange("p z (h j) -> p (z h) j", j=64)
        WL = self.sb("rw_WL", [64, 8], F32)
        P.dma("sp", lambda e: e.dma_start(out=Mt[:], in_=self.init_rwkv[l].rearrange("z h i j -> i (z h) j")), writes=["rw_kh"])
        for u in range(8):
            P.op("pe", lambda e, u=u: e.transpose(ps[0:64, 2, 64 * u:64 * u + 64], Mt[:, u, :], ident64), reads=["rw_kh", "ident"], writes=["ps2"])
        P.op("dve", lambda e: e.tensor_copy(M[:], ps[0:64, 2, :].rearrange("p (u e) -> p u e", e=64)), writes=["ps2", "rw_M"])
        osb = self.sb("rw_osb", [64, 2, 512], F32)
        v3 = lambda x_: x_[:].rearrange("p z (h j) -> p (z h) j", j=64)
        for t in range(NT):
            tiles = (t, NT - 1 - t)
            edge = slice(0, 1) if t % 2 == 0 else slice(129, 130)
            for j in range(9):
                bank = j % 2
                self.proj_fm(wb, wkeys, 128 * j, 128, tiles, bank, halo=1)
                self.evac_fm(lambda z: pre[:, z, :], bank, 128, "rw_pre", W=130, eng="act")
                P.op("dve", lambda e: e.tensor_scalar(pre[:, :, edge], pre[:, :, edge], self.keep[:, 0:1], None, ALU.mult), reads=["rw_pre", "keep"], writes=["rw_pre"])
                if j < 6:
                    dst, dk = ((rT, "rw_rT"), (kT, "rw_kT"), (vT, "rw_vT"))[j // 2]
                    dst = dst[:, j % 2, :, :]
                elif j == 6:
                    dst, dk = twT[:], "rw_twT"
                elif j == 7:
                    dst, dk = daT[:], "rw_daT"
                else:
                    continue
                self.rwkv_mix_fm(pre, mu[:, j:j + 1], dst, ["rw_pre"], [dk])
                if j == 6:
                    P.op("act", lambda e: e.activation(twT[:], twT[:], AF.Tanh), reads=["rw_twT"], writes=["rw_twT"])
            for c in range(2):
                cs = slice(64 * c, 64 * c + 64)
                for z in range(2):
                    P.op("pe", lambda e, z=z: e.matmul(ps[0:64, 2, 256 * z:256 * z + 256], twT[:, z, cs], w2p[:, z, :], start=True, stop=True), reads=["rw_twT", "rw_w2p"], writes=["ps2"])
                    P.op("pe", lambda e, z=z: e.matmul(ps[0:64, 3, 256 * z:256 * z + 256], daT[:, z, cs], a2p[:, z, :], start=True, stop=True), reads=["rw_daT", "rw_a2p"], writes=["ps3"])
                lwf = lw[:].rearrange("p z n -> p (z n)")
                avf = av[:].rearrange("p z n -> p (z n)")
                P.op("dve", lambda e: e.tensor_tensor(lwf, ps[0:64, 2, :], bcs["w0"][:], ALU.add), reads=["rw_bc"], writes=["ps2", "rw_lw"])
                self.act_sigmoid(lwf, lwf, ["rw_lw"], ["rw_lw"])
                P.op("dve", lambda e: e.tensor_scalar(lwf, lwf, -float(np.exp(-0.5)), None, ALU.mult), reads=["rw_lw"], writes=["rw_lw"])
                P.op("dve", lambda e: e.tensor_tensor(avf, ps[0:64, 3, :], bcs["a0"][:], ALU.add), reads=["rw_bc"], writes=["ps3", "rw_a"])
                self.act_sigmoid(avf, avf, ["rw_a"], ["rw_a"])
                P.op("pe", lambda e: e.matmul(ps[0:64, 4, :], incl, lwf, start=True, stop=True), reads=["consts", "rw_lw"], writes=["ps4"])
                for u in range(8):
                    z, h = u // 4, u % 4
                    P.op("pe", lambda e, u=u, z=z, h=h: e.matmul(ps[0:64, 7, 64 + u:65 + u], lw[:, z, 64 * h:64 * h + 64], ones64[:, 0:1], start=True, stop=True),
                         reads=["rw_lw", "consts"], writes=["ps7"])
                P.op("act", lambda e: e.activation(WL[:], ps[0:64, 7, 64:72], AF.Exp), writes=["ps7", "rw_WL"])
                eclf = ecl[:].rearrange("p z n -> p (z n)")
                P.op("act", lambda e: e.activation(eclf, ps[0:64, 4, :], AF.Exp), writes=["ps4", "rw_ecl"])
                P.op("act", lambda e: e.activation(encl[:].rearrange("p z n -> p (z n)"), ps[0:64, 4, :], AF.Exp, scale=-1.0), writes=["ps4", "rw_encl"])
                P.op("dve", lambda e: e.tensor_tensor(ecw[:].rearrange("p z n -> p (z n)"), ps[0:64, 4, :], lwf, ALU.subtract), reads=["rw_lw"], writes=["ps4", "rw_ecw"])
                P.op("act", lambda e: e.activation(ecw[:], ecw[:], AF.Exp), reads=["rw_ecw"], writes=["rw_ecw"])
                for (src, skey, dstt, dkey, bank) in ((rT, "rw_rT", rtm, "rw_rtm", 5), (kT, "rw_kT", ktm, "rw_ktm", 6), (vT, "rw_vT", vtm, "rw_vtm", 5)):
                    bv = ps[:, bank, :].bitcast(BF16)
                    for z in range(2):
                        for hp in range(2):
                            col = (4 * z + 2 * hp) * 64
                            P.op("pe", lambda e, src=src, z=z, hp=hp, col=col, bv=bv: e.transpose(bv[0:64, col:col + 128], src[:, hp, z, cs], self.ident_bf[:]),
                                 reads=[skey, "ident_bf"], writes=["ps%d" % bank])
                    P.op("act", lambda e, dstt=dstt, bv=bv: e.activation(dstt[:].rearrange("p z n -> p (z n)"), bv[0:64, 0:512], AF.Copy), writes=["ps%d" % bank, dkey])
                kk2 = bcs["kk"][:].unsqueeze(1).to_broadcast([64, 2, 256])
                ka2 = bcs["ka"][:].unsqueeze(1).to_broadcast([64, 2, 256])
                P.op("dve", lambda e: e.tensor_tensor(kx[:], ktm[:], kk2, ALU.mult), reads=["rw_ktm", "rw_bc"], writes=["rw_kx"])
                P.op("act", lambda e: e.activation(kh[:], kx[:], AF.Square), reads=["rw_kx"], writes=["rw_kh"])
                P.op("dve", lambda e: e.tensor_reduce(ss8[:], v3(kh), AX.X, ALU.add), reads=["rw_kh"], writes=["rw_ss8"])
                P.op("act", lambda e: e.activation(ss8[:], ss8[:], AF.Ln, bias=1e-6, scale=1.0), reads=["rw_ss8"], writes=["rw_ss8"])
                P.op("act", lambda e: e.activation(ss8[:], ss8[:], AF.Exp, scale=-0.5), reads=["rw_ss8"], writes=["rw_ss8"])
                P.op("dve", lambda e: e.tensor_tensor(v3(kh), v3(kx), ss8[:].unsqueeze(2).to_broadcast([64, 8, 64]), ALU.mult), reads=["rw_kx", "rw_ss8"], writes=["rw_kh"])
                P.op("dve", lambda e: e.tensor_tensor(ktz[:], av[:], ka2, ALU.mult), reads=["rw_a", "rw_bc"], writes=["rw_ktz"])
                P.op("dve", lambda e: e.tensor_tensor(ktz[:], ktz[:], omka[:].unsqueeze(1).to_broadcast([64, 2, 256]), ALU.add), reads=["rw_ktz", "rw_omka"], writes=["rw_ktz"])
                P.op("dve", lambda e: e.tensor_tensor(ktz[:], ktz[:], ktm[:], ALU.mult), reads=["rw_ktz", "rw_ktm"], writes=["rw_ktz"])
                P.op("dve", lambda e: e.tensor_tensor(al[:], av[:], kh[:], ALU.mult), reads=["rw_a", "rw_kh"], writes=["rw_al"])
                P.op("dve", lambda e: e.scalar_tensor_tensor(al[:], al[:], -1.0, encl[:], ALU.mult, ALU.mult), reads=["rw_al", "rw_encl"], writes=["rw_al"])
                P.op("dve", lambda e: e.tensor_tensor(be[:], kh[:], ecw[:], ALU.mult), reads=["rw_kh", "rw_ecw"], writes=["rw_be"])
                P.op("dve", lambda e: e.tensor_tensor(kti[:], ktz[:], encl[:], ALU.mult), reads=["rw_ktz", "rw_encl"], writes=["rw_kti"])
                P.op("dve", lambda e: e.tensor_tensor(rti[:], rtm[:], ecl[:], ALU.mult), reads=["rw_rtm", "rw_ecl"], writes=["rw_rti"])
                P.op("dve", lambda e: e.tensor_tensor(kx[:], rtm[:], ktz[:], ALU.mult), reads=["rw_rtm", "rw_ktz"], writes=["rw_kx"])
                P.op("dve", lambda e: e.tensor_tensor(kx[:], kx[:], bcs["rk"][:].unsqueeze(1).to_broadcast([64, 2, 256]), ALU.mult), reads=["rw_kx", "rw_bc"], writes=["rw_kx"])
                P.op("dve", lambda e: e.tensor_reduce(bs8[:], v3(kx), AX.X, ALU.add), reads=["rw_kx"], writes=["rw_bs8"])
                for z in range(2):
                    P.op("dve", lambda e, z=z, c=c: e.tensor_tensor(osb[:, z, 256:512].rearrange("p (h j) -> p h j", j=64), vtm[:, z, :].rearrange("p (h j) -> p h j", j=64),
                                                               bs8[:, 4 * z:4 * z + 4].unsqueeze(2).to_broadcast([64, 4, 64]), ALU.mult),
                         reads=["rw_vtm", "rw_bs8"], writes=["rw_osb"])
                for (src, skey, dstT, dkey, bank) in ((al, "rw_al", alT, "rw_rtm", 2), (be, "rw_be", beT, "rw_beT", 3), (kti, "rw_kti", ktT, "rw_ktm", 4), (rti, "rw_rti", rtT, "rw_be", 5)):
                    for u in range(8):
                        z, h = u // 4, u % 4
                        P.op("pe", lambda e, src=src, u=u, z=z, h=h, bank=bank: e.transpose(ps[0:64, bank, 64 * u:64 * u + 64], src[:, z, 64 * h:64 * h + 64], ident64),
                             reads=[skey, "ident"], writes=["ps%d" % bank])
                    P.op("act", lambda e, dstT=dstT, bank=bank: e.activation(dstT[:], ps[0:64, bank, :].rearrange("p (u l) -> p u l", l=64), AF.Copy), writes=["ps%d" % bank, dkey])
                for (lhs, lkey, rhs, rkey, bank, msk, dst, dkey) in ((alT, "rw_rtm", beT, "rw_beT", 2, strict, Pm, "rw_lw"), (ktT, "rw_ktm", beT, "rw_beT", 3, strict, Aak, "rw_ecl"),
                                                                      (alT, "rw_rtm", rtT, "rw_be", 4, incl, Ara, "rw_encl"), (ktT, "rw_ktm", rtT, "rw_be", 5, incl, Ark, "rw_ecw")):
                    for u in range(8):
                        P.op("pe", lambda e, lhs=lhs, rhs=rhs, u=u, bank=bank: e.matmul(ps[0:64, bank, 64 * u:64 * u + 64], lhs[:, u, :], rhs[:, u, :], start=True, stop=True),
                             reads=[lkey, rkey], writes=["ps%d" % bank])
                    P.op("dve", lambda e, bank=bank, msk=msk, dst=dst: e.tensor_tensor(dst[:], ps[0:64, bank, :].rearrange("p (u l) -> p u l", l=64),
                                                                                     msk.unsqueeze(1).to_broadcast([64, 8, 64]), ALU.mult),
                         reads=["consts"], writes=["ps%d" % bank, dkey])
                self.neumann_inverse("rw_", Pm, Qm, Rm, (2, 3, 4), keys=("rw_lw", "rw_a", "rw_rti"))
                for u in range(8):
                    z, h = u // 4, u % 4
                    P.op("pe", lambda e, u=u: e.matmul(ps[0:64, 5, 64 * u:64 * u + 64], beT[:, u, :], M[:, u, :], start=True, stop=False), reads=["rw_beT", "rw_M"], writes=["ps5"])
                    P.op("pe", lambda e, u=u, z=z, h=h: e.matmul(ps[0:64, 5, 64 * u:64 * u + 64], Aak[:, u, :], vtm[:, z, 64 * h:64 * h + 64], start=False, stop=True),
                         reads=["rw_ecl", "rw_vtm"], writes=["ps5"])
                P.op("act", lambda e: e.activation(X1[:], ps[0:64, 5, :].rearrange("p (u e) -> p u e", e=64), AF.Copy), writes=["ps5", "rw_kh"])
                for u in range(8):
                    P.op("pe", lambda e, u=u: e.matmul(ps[0:64, 6, 64 * u:64 * u + 64], Rm[:, u, :], X1[:, u, :], start=True, stop=True), reads=["rw_rti", "rw_kh"], writes=["ps6"])
                P.op("act", lambda e: e.activation(Uu[:], ps[0:64, 6, :].rearrange("p (u e) -> p u e", e=64), AF.Copy), writes=["ps6", "rw_kx"])
                for u in range(8):
                    z, h = u // 4, u % 4
                    vu = vtm[:, z, 64 * h:64 * h + 64]
                    P.op("pe", lambda e, u=u: e.matmul(ps[0:64, 5, 64 * u:64 * u + 64], rtT[:, u, :], M[:, u, :], start=True, stop=False), reads=["rw_be", "rw_M"], writes=["ps5"])
                    P.op("pe", lambda e, u=u: e.matmul(ps[0:64, 5, 64 * u:64 * u + 64], Ara[:, u, :], Uu[:, u, :], start=False, stop=False), reads=["rw_encl", "rw_kx"], writes=["ps5"])
                    P.op("pe", lambda e, u=u, vu=vu: e.matmul(ps[0:64, 5, 64 * u:64 * u + 64], Ark[:, u, :], vu, start=False, stop=True), reads=["rw_ecw", "rw_vtm"], writes=["ps5"])
                for z in range(2):
                    P.op("act", lambda e, z=z, c=c: e.activation(osb[:, z, 0:256], ps[0:64, 5, 256 * z:256 * z + 256], AF.Copy), writes=["ps5", "rw_osb"])
                for u in range(8):
                    z, h = u // 4, u % 4
                    vu = vtm[:, z, 64 * h:64 * h + 64]
                    P.op("pe", lambda e, u=u, z=z, h=h: e.matmul(ps[0:64, 6, 64 * u:64 * u + 64], al[:, z, 64 * h:64 * h + 64], Uu[:, u, :], start=True, stop=False), reads=["rw_al", "rw_kx"], writes=["ps6"])
                    P.op("pe", lambda e, u=u, z=z, h=h, vu=vu: e.matmul(ps[0:64, 6, 64 * u:64 * u + 64], kti[:, z, 64 * h:64 * h + 64], vu, start=False, stop=True), reads=["rw_kti", "rw_vtm"], writes=["ps6"])
                P.op("dve", lambda e: e.tensor_tensor(tmpM[:], ps[0:64, 6, :].rearrange("p (u e) -> p u e", e=64), M[:], ALU.add), reads=["rw_M"], writes=["ps6", "rw_ktz"])
                P.op("dve", lambda e: e.tensor_tensor(M[:], tmpM[:], WL[:].unsqueeze(2).to_broadcast([64, 8, 64]), ALU.mult), reads=["rw_ktz", "rw_WL"], writes=["rw_M"])
                self.emit_out_tile2(t, c, osb, ["rw_osb"], bacc)
            if t % 2 == 1:
                for u in range(8):
                    P.op("pe", lambda e, u=u: e.transpose(ps[0:64, 2, 64 * u:64 * u + 64], M[:, u, :], ident64), reads=["rw_M", "ident"], writes=["ps2"])
                P.op("dve", lambda e: e.tensor_copy(Mt[:], ps[0:64, 2, :].rearrange("p (u e) -> p u e", e=64)), writes=["ps2", "rw_kh"])
                for z in range(2):
                    seg = (t - 1) // 2 if z == 0 else (NT - 1 - t) // 2
                    P.dma("sp", lambda e, z=z, seg=seg: e.dma_start(out=self.out_rwkv[seg, l, z].rearrange("h i j -> i h j"), in_=Mt[:, 4 * z:4 * z + 4, :]), reads=["rw_kh"])
                P.op("dve", lambda e: e.tensor_scalar(M[:], M[:], self.keep[0:64, 0:1], None, ALU.mult), reads=["rw_M", "keep"], writes=["rw_M"])
        if self.debug.get("yacc") == "rwkv":
            P.dma("sp", lambda e: e.dma_start(out=self.dbg_yacc, in_=self.yacc[:]), reads=[("yacc", i) for i in range(NT)])
        if self.debug.get("yacc") == "rwkv_bonus":
            P.dma("sp", lambda e: e.dma_start(out=self.dbg_yacc, in_=bacc[:]), reads=[("bacc", i) for i in range(NT)])
        self.release(m1)
        if self.debug.get("post", True):
            self.post_rwkv(l, wb, wkeys, bacc, mu, first)
        self.release(m0)

    def emit_out_tile2(self, t, c, osb, osb_keys, bacc):
        P, ps = self.P, self.psum
        sel = self.cst("SEL").rearrange("p (q n) -> p q n", n=128)
        for z in range(2):
            bank = z
            pk = "ps%d" % bank
            tile = t if z == 0 else NT - 1 - t
            P.op("pe", lambda e, z=z, bank=bank: e.matmul(ps[:, bank, :], sel[:, 2 * z + c, :], osb[:, z, :], start=(c == 0), stop=(c == 1)),
                 reads=["consts"] + osb_keys, writes=[pk])
            if c == 0:
                continue
            yk = ("yacc", tile)
            bk = ("bacc", tile)
            if t < NT // 2:
                P.op("act", lambda e, tile=tile, bank=bank: e.activation(self.yacc[:, tile, :], ps[:, bank, 0:256], AF.Copy), writes=[pk, yk])
                P.op("act", lambda e, tile=tile, bank=bank: e.activation(bacc[:, tile, :], ps[:, bank, 256:512], AF.Copy), writes=[pk, bk])
            else:
                P.op("dve", lambda e, tile=tile, bank=bank: e.tensor_tensor(self.yacc[:, tile, :], ps[:, bank, 0:256], self.yacc[:, tile, :], ALU.add), writes=[pk, yk])
                P.op("dve", lambda e, tile=tile, bank=bank: e.tensor_tensor(bacc[:, tile, :], ps[:, bank, 256:512], bacc[:, tile, :], ALU.add), writes=[pk, bk])

    def post_rwkv(self, l, wb, wkeys, bacc, mu, first):
        P, ps = self.P, self.psum
        m0 = self.mark()
        wo, wokeys = self.load_w("wo_rw", self.w_out[l][768:1024, :], D, kchunks=2)
        gain_bc = self.sb("hn_gain", [128, 256], F32)
        P.dma("sp", lambda e: e.dma_start(out=gain_bc[:], in_=self.rw_norm[l:l + 1, :].partition_broadcast(128)), writes=["hn_gain"])
        g2 = self.sb("rw_g2", [128, 256], F32)
        P.dma("sp", lambda e: e.dma_start(out=g2[:], in_=self.rw_g2[l]), writes=["rw_g2"])
        tmp = (self.sb("hn_yc", [128, 256], F32), self.sb("hn_sq", [128, 256], F32), self.sb("hn_st", [128, 2, 4], F32),
               self.sb("yact", [128, 256], F32))
        gate = self.sb("hn_gate", [128, 256], F32)
        otmp = (self.sb("yTm", [128, 2, 128], BF16), self.sb("gtmp", [128, D], F32))
        pre1 = self.sb("rwp_pre", [128, 1, 130], F32)
        sg = self.sb("rwp_sg", [128, 1, 128], F32)
        for i in range(NT):
            tok0 = 2 + 128 * i - 1
            for k in range(8):
                P.op("pe", lambda e, k=k, tok0=tok0: e.matmul(ps[:, 0, 0:130], wb[:, k, 1024:1152], self.uT[:, k, tok0:tok0 + 130], start=(k == 0), stop=(k == 7)),
                     reads=wkeys + [("uT", i)] + ([("uT", i - 1)] if i > 0 else []) + ([("uT", i + 1)] if i < NT - 1 else []), writes=["ps0"])
            P.op("act", lambda e: e.activation(pre1[:, 0, :], ps[:, 0, 0:130], AF.Copy), writes=["ps0", "rwp_pre"])
            edge = slice(0, 1) if i % 2 == 0 else slice(129, 130)
            P.op("dve", lambda e, edge=edge: e.tensor_scalar(pre1[:, :, edge], pre1[:, :, edge], self.keep[:, 0:1], None, ALU.mult), reads=["rwp_pre", "keep"], writes=["rwp_pre"])
            self.rwkv_mix_fm(pre1, mu[:, 8:9], sg[:], ["rwp_pre"], ["rwp_sg"])
            self.act_sigmoid(sg[:], sg[:], ["rwp_sg"], ["rwp_sg"])
            P.op("pe", lambda e: e.matmul(ps[:, 1, 0:256], sg[:, 0, :], g2[:], start=True, stop=True), reads=["rwp_sg", "rw_g2"], writes=["ps1"])
            P.op("act", lambda e: e.activation(gate[:], ps[:, 1, 0:256], AF.Copy), writes=["ps1", "hn_gate"])
            self.head_norm_tile(i, True, gain_bc, gate[:], ["hn_gate"], tmp, extra=(bacc[:, i, :], [("bacc", i)]))
            self.out_proj_tile(i, tmp[3], wo, wokeys, first, otmp)
        self.release(m0)

    def head_norm_tile(self, i, center, gain_bc, gate_ap, gate_keys, tmp, extra=None, sfx=""):
        P = self.P
        yc, sq, st4, yact = tmp
        kst, kst1, kyc, ksq, kya = "hn_st" + sfx, "hn_st1" + sfx, "hn_yc" + sfx, "hn_sq" + sfx, "yact" + sfx
        y3 = self.yacc[:, i, :].rearrange("p (h e) -> p h e", e=64)
        yk = ("yacc", i)
        yc3 = yc[:].rearrange("p (h e) -> p h e", e=64)
        if center:
            P.op("dve", lambda e: e.tensor_reduce(st4[:, 0, :], y3, AX.X, ALU.add), reads=[yk], writes=[kst])
            P.op("dve", lambda e: e.tensor_scalar(st4[:, 0, :], st4[:, 0, :], -1.0 / 64, None, ALU.mult), reads=[kst], writes=[kst])
            P.op("dve", lambda e: e.tensor_tensor(yc3, y3, st4[:, 0, :].unsqueeze(2).to_broadcast([128, 4, 64]), ALU.add),
                 reads=[yk, kst], writes=[kyc])
        else:
            P.op("dve", lambda e: e.tensor_copy(yc[:], self.yacc[:, i, :]), reads=[yk], writes=[kyc])
        P.op("act", lambda e: e.activation(sq[:], yc[:], AF.Square), reads=[kyc], writes=[ksq])
        P.op("dve", lambda e: e.tensor_reduce(st4[:, 1, :], sq[:].rearrange("p (h e) -> p h e", e=64), AX.X, ALU.add), reads=[ksq], writes=[kst1])
        P.op("act", lambda e: e.activation(st4[:, 1, :], st4[:, 1, :], AF.Ln, bias=LN_EPS, scale=1.0 / 64), reads=[kst1], writes=[kst1])
        P.op("act", lambda e: e.activation(st4[:, 1, :], st4[:, 1, :], AF.Exp, scale=-0.5), reads=[kst1], writes=[kst1])
        P.op("dve", lambda e: e.tensor_tensor(yc3, yc3, st4[:, 1, :].unsqueeze(2).to_broadcast([128, 4, 64]), ALU.mult),
             reads=[kyc, kst1], writes=[kyc])
        P.op("dve", lambda e: e.tensor_tensor(yc[:], yc[:], gain_bc[:], ALU.mult), reads=[kyc, "hn_gain"], writes=[kyc])
        if extra is not None:
            P.op("dve", lambda e: e.tensor_tensor(yc[:], yc[:], extra[0], ALU.add), reads=[kyc] + extra[1], writes=[kyc])
        P.op("dve", lambda e: e.tensor_tensor(yact[:], yc[:], gate_ap, ALU.mult), reads=[kyc] + gate_keys, writes=[kya])

    def out_proj_tile(self, i, yact, wo, wokeys, first, tmp, sfx="", banks=(2, 4, 5)):
        P, ps = self.P, self.psum
        yTm, gtmp = tmp
        bT, b0, b1 = banks
        kya, kyT, kgt = "yact" + sfx, "yTm" + sfx, "gtmp" + sfx
        for kk in range(2):
            P.op("pe", lambda e, kk=kk: e.transpose(ps[:, bT, 128 * kk:128 * kk + 128], yact[:, 128 * kk:128 * kk + 128], self.ident[:]),
                 reads=[kya, "ident"], writes=["ps%d" % bT])
        P.op("act", lambda e: e.activation(yTm[:], ps[:, bT, 0:256].rearrange("p (k n) -> p k n", n=128), AF.Copy), writes=["ps%d" % bT, kyT])
        for n, bn in enumerate((b0, b1)):
            for kk in range(2):
                P.op("pe", lambda e, n=n, kk=kk, bn=bn: e.matmul(ps[:, bn, :], yTm[:, kk, :], wo[:, kk, 512 * n:512 * n + 512],
                                                                 start=(kk == 0), stop=(kk == 1)),
                     reads=[kyT] + wokeys, writes=["ps%d" % bn])
        xk = ("xres", i)
        for n, bn in enumerate((b0, b1)):
            cols = slice(512 * n, 512 * n + 512)
            P.op("dve", lambda e, bn=bn, cols=cols: e.tensor_tensor(gtmp[:, cols], ps[:, bn, :], self.g_bc["g1"][:, cols], ALU.mult),
                 reads=["g1_bc"], writes=["ps%d" % bn, kgt])
        P.op("dve", lambda e: e.scalar_tensor_tensor(self.xres[:, i, :], self.xres[:, i, :], ALPHA if first else 1.0, gtmp[:], ALU.mult, ALU.add),
             reads=[kgt], writes=[xk])

    def post_simple(self, l, name, wb, wkeys, gcol0, gate_func, norm_dram, center, wo_row0, first):
        P, ps = self.P, self.psum
        m0 = self.mark()
        wo, wokeys = self.load_w("wo_" + name, self.w_out[l][wo_row0:wo_row0 + 256, :], D, kchunks=2)
        gain_bc = self.sb("hn_gain", [128, 256], F32)
        P.dma("sp", lambda e: e.dma_start(out=gain_bc[:], in_=norm_dram[l:l + 1, :].partition_broadcast(128)), writes=["hn_gain"])
        tmps = [(self.sb("hn_yc%d" % q, [128, 256], F32), self.sb("hn_sq%d" % q, [128, 256], F32), self.sb("hn_st%d" % q, [128, 2, 4], F32),
                 self.sb("yact%d" % q, [128, 256], F32)) for q in range(2)]
        gates = [self.sb("hn_gate%d" % q, [128, 256], F32) for q in range(3)]
        otmps = [(self.sb("yTm%d" % q, [128, 2, 128], BF16), self.sb("gtmp%d" % q, [128, D], F32)) for q in range(2)]

        def stage_a(i):
            gate = gates[i % 3]
            gk = "hn_gate%d" % (i % 3)
            bank = i % 2
            for k in range(8):
                P.op("pe", lambda e, k=k: e.matmul(ps[:, bank, 0:256], self.uT[:, k, 2 + 128 * i:2 + 128 * i + 128], wb[:, k, gcol0:gcol0 + 256],
                                                   start=(k == 0), stop=(k == 7)),
                     reads=wkeys + [("uT", i)], writes=["ps%d" % bank])
            self.act_sigmoid(gate[:], ps[:, bank, 0:256], [], ["ps%d" % bank, gk])
            if gate_func == AF.Silu:
                P.op("dve", lambda e: e.tensor_tensor(gate[:], ps[:, bank, 0:256], gate[:], ALU.mult), writes=["ps%d" % bank, gk])

        def stage_b(i):
            self.head_norm_tile(i, center, gain_bc, gates[i % 3][:], ["hn_gate%d" % (i % 3)], tmps[i % 2], sfx=str(i % 2))

        def stage_c(i):
            self.out_proj_tile(i, tmps[i % 2][3], wo, wokeys, first, otmps[i % 2], sfx=str(i % 2), banks=((2, 4, 5) if i % 2 == 0 else (3, 6, 7)))

        for step in range(NT + 2):
            if step < NT:
                stage_a(step)
            if 1 <= step <= NT:
                stage_b(step - 1)
            if step >= 2:
                stage_c(step - 2)
        self.release(m0)

    def ln_affine_tile(self, i, g_bc, b_bc):
        P = self.P
        stats, mv, rstd, xhat = self.lntmp
        xk = ("xres", i)
        src = self.xres[:, i, :]
        for hh in range(2):
            P.op("dve", lambda e, hh=hh: e.bn_stats(stats[:, hh, :], src[:, 512 * hh:512 * hh + 512]), reads=[xk], writes=["lnstats"])
        P.op("dve", lambda e: e.bn_aggr(mv[:], stats[:]), reads=["lnstats"], writes=["lnmv"])
        P.op("act", lambda e: e.activation(rstd[:], mv[:, 1:2], AF.Ln, bias=LN_EPS, scale=1.0), reads=["lnmv"], writes=["lnrstd"])
        P.op("act", lambda e: e.activation(rstd[:], rstd[:], AF.Exp, scale=-0.5), reads=["lnrstd"], writes=["lnrstd"])
        P.op("dve", lambda e: e.tensor_scalar(xhat[:], src, mv[:, 0:1], rstd[:, 0:1], ALU.subtract, ALU.mult),
             reads=[xk, "lnmv", "lnrstd"], writes=["xhat"])
        P.op("pool", lambda e: e.tensor_tensor(xhat[:], xhat[:], g_bc[:], ALU.mult), reads=["xhat", "lnbc"], writes=["xhat"])
        P.op("pool", lambda e: e.tensor_tensor(src, xhat[:], b_bc[:], ALU.add), reads=["xhat", "lnbc"], writes=[xk])

    def phase_c(self, l):
        P, ps = self.P, self.psum
        m0 = self.mark()
        self.lntmp = (self.sb("lnstats", [128, 2, 6], F32), self.sb("lnmv", [128, 2], F32),
                      self.sb("lnrstd", [128, 1], F32), self.sb("xhat", [128, D], F32))
        bc = {}
        for nm, src in (("ln1_g", self.ln1_g), ("ln1_b", self.ln1_b), ("ln2_g", self.ln2_g), ("ln2_b", self.ln2_b)):
            bc[nm] = self.sb("bc_" + nm, [128, D], F32)
            P.dma("sp", lambda e, nm=nm, src=src: e.dma_start(out=bc[nm][:], in_=src[l:l + 1, :].partition_broadcast(128)), writes=["lnbc"])
        u2T = self.sb("u2T", [128, 8, TOK], BF16)
        hT = [self.sb("hT%d" % i, [128, 4, 512], BF16) for i in range(2)]
        rtmp = [self.sb("ffn_rtmp%d" % i, [128, 512], F32) for i in range(2)]
        gtmp = [self.sb("ffn_gtmp%d" % i, [128, 512], F32) for i in range(2)]
        w1b = [self.sb("w1b%d" % i, [128, 8, 512], BF16) for i in range(2)]
        w2b = [self.sb("w2b%d" % i, [128, 4, 1024], BF16) for i in range(2)]
        w1v = self.w_ff1[l].rearrange("(k p) n -> p k n", p=128)
        w2v = self.w_ff2[l].rearrange("(c p) n -> p c n", p=128)
        for i in range(NT):
            self.ln_affine_tile(i, bc["ln1_g"], bc["ln1_b"])
            self.ln_mod_T(i, lambda k, i=i: u2T[:, k, 128 * i:128 * i + 128], self.sc2p, 24, self.lntmp, [("u2T", i)])
        nh = 0
        ng_ = 0
        for sl in range(8):
            wa = w1b[sl % 2]
            wbk = w2b[sl % 2]
            ka = "w1b%d" % (sl % 2)
            kb = "w2b%d" % (sl % 2)
            for kh in range(2):
                P.dma("pool", lambda e, wa=wa, sl=sl, kh=kh: e.dma_start(out=wa[:, 4 * kh:4 * kh + 4, :], in_=w1v[:, 4 * kh:4 * kh + 4, 512 * sl:512 * sl + 512]), writes=[(ka, kh)])
            for kh in range(2):
                P.dma("pool", lambda e, wbk=wbk, sl=sl, kh=kh: e.dma_start(out=wbk[:, 2 * kh:2 * kh + 2, :], in_=w2v[:, 4 * sl + 2 * kh:4 * sl + 2 * kh + 2, :]), writes=[(kb, kh)])
            wakeys = [(ka, 0), (ka, 1)]
            wbkeys = [(kb, 0), (kb, 1)]
            for blk in range(4):
                hb = hT[nh % 2]
                hk = "hT%d" % (nh % 2)
                nh += 1
                u2keys = [("u2T", i) for i in range(4 * blk, 4 * blk + 4)]
                for cc in range(4):
                    bank = cc % 2
                    for k in range(8):
                        P.op("pe", lambda e, wa=wa, cc=cc, k=k, bank=bank, blk=blk: e.matmul(
                            ps[:, bank, :], wa[:, k, 128 * cc:128 * cc + 128], u2T[:, k, 512 * blk:512 * blk + 512], start=(k == 0), stop=(k == 7)),
                            reads=wakeys + u2keys, writes=["ps%d" % bank])
                    rt = rtmp[cc % 2]
                    rk = "ffn_rtmp%d" % (cc % 2)
                    P.op("act", lambda e, rt=rt, bank=bank: e.activation(rt[:], ps[:, bank, :], AF.Relu), writes=["ps%d" % bank, rk])
                    P.op("pool", lambda e, rt=rt, cc=cc, hb=hb: e.tensor_tensor(hb[:, cc, :], rt[:], rt[:], ALU.mult), reads=[rk], writes=[(hk, cc)])
                hkeys = [(hk, cc) for cc in range(4)]
                for j in range(4):
                    i = 4 * blk + j
                    for n in range(2):
                        bank = 2 + (ng_ % 4)
                        gt = gtmp[ng_ % 2]
                        gk = "ffn_gtmp%d" % (ng_ % 2)
                        ng_ += 1
                        cols = slice(512 * n, 512 * n + 512)
                        for hc in range(4):
                            P.op("pe", lambda e, wbk=wbk, hc=hc, j=j, bank=bank, hb=hb, cols=cols: e.matmul(
                                ps[:, bank, :], hb[:, hc, 128 * j:128 * j + 128], wbk[:, hc, cols], start=(hc == 0), stop=(hc == 3)),
                                reads=wbkeys + hkeys, writes=["ps%d" % bank])
                        P.op("dve", lambda e, bank=bank, cols=cols, gt=gt: e.tensor_tensor(gt[:], ps[:, bank, :], self.g_bc["g2"][:, cols], ALU.mult),
                             reads=["g2_bc"], writes=["ps%d" % bank, gk])
                        P.op("dve", lambda e, i=i, cols=cols, gt=gt, sl=sl: e.scalar_tensor_tensor(
                            self.xres[:, i, cols], self.xres[:, i, cols], ALPHA if sl == 0 else 1.0, gt[:], ALU.mult, ALU.add),
                            reads=[gk], writes=[("xres", i)])
        for i in range(NT):
            self.ln_affine_tile(i, bc["ln2_g"], bc["ln2_b"])
        self.release(m0)

    def alloc_lntmp(self):
        self.lntmp = (self.sb("lnstats", [128, 2, 6], F32), self.sb("lnmv", [128, 2], F32),
                      self.sb("lnrstd", [128, 1], F32), self.sb("xhat", [128, D], F32))

    def layer(self, l):
        P = self.P
        self.compute_mod(l)
        mL = self.mark()
        self.uT = self.sb("uT", [128, 8, TOK + 4], BF16)
        self.yacc = self.sb("yacc", [128, NT, 256], F32)
        P.op("dve", lambda e: e.memset(self.uT[:, :, 0:2], 0.0), writes=["uTpadL"])
        P.op("dve", lambda e: e.memset(self.uT[:, :, TOK + 2:TOK + 4], 0.0), writes=["uTpadR"])
        mA = self.mark()
        self.alloc_lntmp()
        for i in range(NT):
            self.ln_mod_T(i, lambda k, i=i: self.uT[:, k, 2 + 128 * i:2 + 128 * i + 128], self.sc1p, 0, self.lntmp, [("uT", i)])
        self.release(mA)
        if "uT" in self.debug and l == self.debug["uT"]:
            mm_ = self.mark()
            dbgf = self.sb("dbgf", [128, 8, 512], F32)
            for q in range(4):
                P.op("dve", lambda e, q=q: e.tensor_copy(dbgf[:], self.uT[:, :, 2 + 512 * q:2 + 512 * q + 512]),
                     reads=[("uT", i) for i in range(4 * q, 4 * q + 4)], writes=["dbgf"])
                P.dma("sp", lambda e, q=q: e.dma_start(out=self.dbg_uT[:, :, 512 * q:512 * q + 512], in_=dbgf[:]), reads=["dbgf"])
            self.release(mm_)
        mixers = self.debug.get("mixers", ["mlstm", "delta", "ret", "rwkv"])
        first = True
        for mx in mixers:
            getattr(self, "mixer_" + mx)(l, first)
            first = False
        self.release(mL)
        if self.debug.get("phase_c", True):
            self.phase_c(l)

    def finish(self):
        P = self.P
        yv = self.y_out.rearrange("(i p) d -> p i d", p=128)
        for q in range(4):
            P.dma("sp", lambda e, q=q: e.dma_start(out=yv[:, 4 * q:4 * q + 4, :], in_=self.xres[:, 4 * q:4 * q + 4, :]),
                  reads=[("xres", i) for i in range(4 * q, 4 * q + 4)])


PROMPT_ASSIGN = [[0, 1, 2], [3, 4, 5], [6, 7, 8], [9, 10, 11], [12, 13], [14, 15]]

OFF_A, OFF_B, OFF_C, OFF_D = 0, 1040, 2080, 3104


def rope_tables(is_sample):
    tab = np.zeros((TOK, 64, 2), np.float32)
    tab[:, :, 0] = 1.0
    if is_sample:
        n = np.arange(TOK)
        posv = (n // 64, n % 64)
        inv = 10000.0 ** (-np.arange(16, dtype=np.float32) / 16)
        for half in range(2):
            ang = posv[half].astype(np.float32)[:, None] * inv[None, :]
            cos, sin = np.cos(ang), np.sin(ang)
            base = 32 * half
            tab[:, base:base + 16, 0] = cos
            tab[:, base + 16:base + 32, 0] = cos
            tab[:, base:base + 16, 1] = -sin
            tab[:, base + 16:base + 32, 1] = sin
    t = tab.reshape(NT, 128, 64, 2).transpose(0, 2, 3, 1)
    t = np.concatenate([t, t], axis=1)
    return np.ascontiguousarray(t.astype(np.float32))


def swap_perm():
    idx = []
    for h in range(4):
        for half in range(2):
            b = 64 * h + 32 * half
            idx += list(range(b + 16, b + 32)) + list(range(b, b + 16))
    return np.array(idx)


def make_in_maps(inp, kern):
    f32 = np.float32
    g = lambda k: np.asarray(inp[k], f32)
    maps = []
    ident = np.eye(128, dtype=f32)
    consts = build_consts()
    b_mod = np.ascontiguousarray(g("b_mod").reshape(DEPTH, 48, 128))
    w_mod = np.ascontiguousarray(g("w_mod"))
    w_in = g("w_in")
    sw = swap_perm()
    cq = w_in[:, :, OFF_C:OFF_C + 256]
    ck = w_in[:, :, OFF_C + 256:OFF_C + 512]
    cv = w_in[:, :, OFF_C + 512:OFF_C + 768]
    cg = w_in[:, :, OFF_C + 768:OFF_C + 1024]
    aq = w_in[:, :, OFF_A:OFF_A + 256]
    ak = w_in[:, :, OFF_A + 256:OFF_A + 512]
    av = w_in[:, :, OFF_A + 512:OFF_A + 768]
    ao = w_in[:, :, OFF_A + 768:OFF_A + 1024]
    ai = w_in[:, :, OFF_A + 1024:OFF_A + 1032]
    af = w_in[:, :, OFF_A + 1032:OFF_A + 1040]
    agate = np.concatenate([ai[:, :, 0:4], af[:, :, 0:4], ai[:, :, 4:8], af[:, :, 4:8]], axis=2)
    bqkv = w_in[:, :, OFF_B:OFF_B + 768]
    bz = w_in[:, :, OFF_B + 768:OFF_B + 1024]
    bbeta = w_in[:, :, OFF_B + 1024:OFF_B + 1032]
    balpha = w_in[:, :, OFF_B + 1032:OFF_B + 1040]
    bgate = np.concatenate([bbeta[:, :, 0:4], balpha[:, :, 0:4], bbeta[:, :, 4:8], balpha[:, :, 4:8]], axis=2)
    dconv = g("delta_conv")
    dconv = np.ascontiguousarray(dconv.reshape(DEPTH, 5, 6, 128).transpose(0, 3, 2, 1))
    w2 = g("rwkv_w2")
    a2 = g("rwkv_a2")
    w2p = np.zeros((DEPTH, 128, 2, 256), f32)
    a2p = np.zeros((DEPTH, 128, 2, 256), f32)
    for z_ in range(2):
        w2p[:, 64 * z_:64 * z_ + 64, z_, :] = w2[:, z_]
        a2p[:, 64 * z_:64 * z_ + 64, z_, :] = a2[:, z_]
    shared = {
        "wD": np.ascontiguousarray(w_in[:, :, OFF_D:OFF_D + 1152]),
        "rw_mu": np.ascontiguousarray(g("rwkv_mu").reshape(DEPTH, 9, 128).transpose(0, 2, 1)),
        "rw_w2p": w2p, "rw_a2p": a2p,
        "rw_w0": np.ascontiguousarray(g("rwkv_w0").reshape(DEPTH, 512)),
        "rw_a0": np.ascontiguousarray(g("rwkv_a0").reshape(DEPTH, 512)),
        "rw_kk": g("rwkv_kk"), "rw_ka": g("rwkv_ka"),
        "rw_rk": np.ascontiguousarray(g("rwkv_rk").reshape(DEPTH, 256)),
        "rw_norm": g("rwkv_norm"), "rw_g2": np.ascontiguousarray(g("rwkv_g2")),
        "wB": np.ascontiguousarray(np.concatenate([bqkv, bgate, bz], axis=2)),
        "dl_conv": dconv,
        "dl_alog": np.ascontiguousarray(g("delta_a_log").reshape(DEPTH, 8)),
        "dl_dtb": np.ascontiguousarray(g("delta_dt_bias").reshape(DEPTH, 8)),
        "dl_norm": np.ascontiguousarray(g("delta_norm")),
        "wA": np.ascontiguousarray(np.concatenate([aq, ak, av, agate, ao], axis=2)),
        "ml_ib": np.ascontiguousarray(g("mlstm_i_bias").reshape(DEPTH, 8)),
        "ml_fb": np.ascontiguousarray(g("mlstm_f_bias").reshape(DEPTH, 8)),
        "ml_norm": np.ascontiguousarray(g("mlstm_norm")),
        "ident": ident, "consts": consts, "w_mod": w_mod, "b_mod": b_mod,
        "w_out": np.ascontiguousarray(g("w_out")),
        "wC": np.ascontiguousarray(np.concatenate([cq, ck, cv, cq[:, :, sw], ck[:, :, sw], cg], axis=2)),
        "ret_decay": np.ascontiguousarray(g("ret_decay").reshape(DEPTH, 8)),
        "ret_norm": np.ascontiguousarray(g("ret_norm")),
        "ln1_g": g("ln1_g"), "ln1_b": g("ln1_b"), "ln2_g": g("ln2_g"), "ln2_b": g("ln2_b"),
        "w_ff1": np.ascontiguousarray(g("w_ff1")), "w_ff2": np.ascontiguousarray(g("w_ff2")),
    }
    ropes = {True: rope_tables(True), False: rope_tables(False)}
    zeros_mat = np.zeros((DEPTH, 2, H, HD, HD), f32)
    for c in range(N_CORES):
        m = dict(shared)
        if c < 2:
            x = g("x_sample")[c]
            cond = g("c")[c]
            m["init_ret"] = np.ascontiguousarray(g("state_ret")[c])
            m["init_mC"] = np.ascontiguousarray(g("state_mlstm_C")[c])
            m["init_delta"] = np.ascontiguousarray(g("state_delta")[c])
            m["init_rwkv"] = np.ascontiguousarray(g("state_rwkv")[c])
            m["init_mn"] = np.ascontiguousarray(g("state_mlstm_n")[c])
            m["init_mm"] = np.ascontiguousarray(g("state_mlstm_m")[c])
        else:
            x = np.zeros((TOK, D), f32)
            mine = PROMPT_ASSIGN[c - 2]
            for s_ in range(8):
                x[256 * s_:256 * s_ + 256] = inp["x_prompt"][mine[s_ % len(mine)]]
            cond = g("c_ctx")
            m["init_ret"] = zeros_mat
            m["init_mC"] = zeros_mat
            m["init_delta"] = zeros_mat
            m["init_rwkv"] = zeros_mat
            m["init_mn"] = np.zeros((DEPTH, 2, H, HD), f32)
            m["init_mm"] = np.zeros((DEPTH, 2, H), f32)
        m["x"] = np.ascontiguousarray(x)
        m["cond"] = np.ascontiguousarray(cond.reshape(8, 128))
        m["keep"] = np.full((1, 1), 1.0 if c < 2 else 0.0, f32)
        m["rope"] = ropes[c < 2]
        missing = [k for k in kern.ins if k not in m]
        assert not missing, missing
        maps.append({k: m[k] for k in kern.ins})
    return maps


def run(inp, debug=None, trace=False):
    kern = K(debug)
    nc = kern.build()
    maps = make_in_maps(inp, kern)
    res = run_bass_kernel_spmd(nc, maps, core_ids=list(range(N_CORES)), trace=trace)
    return kern, res


def gather_states(r, name, shape_tail):
    out = np.zeros((16, DEPTH, 2) + shape_tail, np.float32)
    for c in range(2, N_CORES):
        for s_, b in enumerate(PROMPT_ASSIGN[c - 2]):
            out[b] = r[c][name][s_]
    return out


def kernel(**inp):
    kern, res = run(inp)
    r = res.results
    BATCH, SEQ = 16, 256
    y_prompt = np.zeros((BATCH, SEQ, D), np.float32)
    y_sample = np.zeros((2, TOK, D), np.float32)
    for c in range(2):
        y_sample[c] = r[c]["y"]
    for c in range(2, N_CORES):
        for s_, b in enumerate(PROMPT_ASSIGN[c - 2]):
            y_prompt[b] = r[c]["y"][256 * s_:256 * s_ + 256]
    new_ret = gather_states(r, "out_ret", (H, HD, HD))
    new_mC = gather_states(r, "out_mC", (H, HD, HD))
    new_mn = gather_states(r, "out_mn", (H, HD))
    new_mm = gather_states(r, "out_mm", (H,))
    new_delta = gather_states(r, "out_delta", (H, HD, HD)) if "out_delta" in r[0] else np.zeros_like(new_ret)
    new_rwkv = gather_states(r, "out_rwkv", (H, HD, HD)) if "out_rwkv" in r[0] else np.zeros_like(new_ret)
    return (y_prompt, y_sample, new_mC, new_mn, new_mm, new_delta, new_ret, new_rwkv)
```

```python
import contextlib
import numpy as np
import concourse.bass as bass
import concourse.mybir as mybir
from concourse.bass_utils import run_bass_kernel_spmd

F32 = mybir.dt.float32
BF16 = mybir.dt.bfloat16
AF = mybir.ActivationFunctionType
ALU = mybir.AluOpType
AX = mybir.AxisListType

D = 1024
NT = 16
TOK = 2048
DEPTH = 2
H = 4
HD = 64
ALPHA = (2 * DEPTH) ** 0.25
LN_EPS = 1e-5
N_CORES = 8

ENGS = ("pe", "act", "dve", "pool", "sp")
NSLOT = 6

CO = {}
_off = 0
for _n, _w in (("INCL", 64), ("STRICT", 64), ("ONES", 128), ("SEL", 512), ("PIDX", 1), ("NPIDX", 1), ("BLK", 128), ("PP1", 1)):
    CO[_n] = (_off, _w)
    _off += _w
NCONST = _off
ARENA_WORDS = 53200


def build_consts():
    c = np.zeros((128, NCONST), np.float32)
    s = np.arange(64)
    incl = (s[:, None] <= s[None, :]).astype(np.float32)
    strict = (s[:, None] < s[None, :]).astype(np.float32)
    for half in range(2):
        c[64 * half:64 * half + 64, CO["INCL"][0]:CO["INCL"][0] + 64] = incl
        c[64 * half:64 * half + 64, CO["STRICT"][0]:CO["STRICT"][0] + 64] = strict
    c[:, CO["ONES"][0]:CO["ONES"][0] + 128] = 1.0
    sel = np.zeros((64, 4, 128), np.float32)
    for z in range(2):
        for ch in range(2):
            for lp in range(64):
                n = 64 * ch + lp if z == 0 else 127 - 64 * ch - lp
                sel[lp, 2 * z + ch, n] = 1.0
    c[0:64, CO["SEL"][0]:CO["SEL"][0] + 512] = sel.reshape(64, 512)
    p = np.arange(128) % 64
    c[:, CO["PIDX"][0]] = p
    c[:, CO["NPIDX"][0]] = -p
    c[:, CO["PP1"][0]] = p + 1
    blk = np.zeros((128, 128), np.float32)
    blk[0:64, 0:64] = 1.0
    blk[64:128, 64:128] = 1.0
    c[:, CO["BLK"][0]:CO["BLK"][0] + 128] = blk
    return c


class Op:
    __slots__ = ("eng", "fn", "reads", "writes", "chan", "val", "is_dma", "idx", "signal")

    def __init__(self, eng, fn, reads, writes, is_dma):
        self.eng = eng
        self.fn = fn
        self.reads = reads
        self.writes = writes
        self.is_dma = is_dma
        self.chan = None
        self.val = 0
        self.signal = False


class _Rec:
    def __getattr__(self, name):
        def f(*a, **kw):
            self.call = (name, a, kw)
            return self
        return f


class Prog:
    def __init__(self):
        self.ops = []

    def op(self, eng, fn, reads=(), writes=()):
        r = _Rec()
        fn(r)
        o = Op(eng, r.call, tuple(reads), tuple(writes), False)
        self.ops.append(o)
        return o

    def dma(self, eng, fn, reads=(), writes=()):
        r = _Rec()
        fn(r)
        o = Op(eng, r.call, tuple(reads), tuple(writes), True)
        self.ops.append(o)
        return o

    def barrier(self):
        self.ops.append(None)

    def emit(self, nc, stack):
        raw = self.ops
        ops = []
        barrier_at = set()
        for o in raw:
            if o is None:
                barrier_at.add(len(ops))
            else:
                ops.append(o)
        dma_n = {e: 0 for e in ENGS}
        slot_prev = {}
        for i, o in enumerate(ops):
            o.idx = i
            if o.is_dma:
                j = dma_n[o.eng] % NSLOT
                dma_n[o.eng] += 1
                o.chan = ("dma", o.eng, j)
            else:
                o.chan = o.eng
        lastw = {}
        readers = {}
        deps = []
        last_chan = {}
        bar_deps = set()
        for o in ops:
            if o.idx in barrier_at:
                bar_deps = set(last_chan.values())
            last_chan[o.chan] = o.idx
            d = set(bar_deps)
            for k in o.reads:
                w = lastw.get(k)
                if w is not None:
                    d.add(w)
            for k in o.writes:
                w = lastw.get(k)
                if w is not None:
                    d.add(w)
                for r in readers.get(k, ()):
                    d.add(r)
            if o.is_dma:
                p = slot_prev.get(o.chan)
                if p is not None:
                    d.add(p)
                slot_prev[o.chan] = o.idx
            d.discard(o.idx)
            deps.append(d)
            for k in o.reads:
                readers.setdefault(k, []).append(o.idx)
            for k in o.writes:
                lastw[k] = o.idx
                readers[k] = []
        pos = {}
        cnt = {}
        for o in ops:
            c = o.chan
            cnt[c] = cnt.get(c, 0) + 1
            pos[o.idx] = cnt[c]
        know_stream = {e: {} for e in ENGS}
        know_op = [None] * len(ops)
        needed = [None] * len(ops)
        for o in ops:
            ks = know_stream[o.eng]
            need = []
            for p in sorted(deps[o.idx], reverse=True):
                po = ops[p]
                if po.eng == "pe" and o.eng == "pe" and not po.is_dma and not o.is_dma:
                    continue
                if ks.get(po.chan, 0) >= pos[p]:
                    continue
                need.append(p)
                for c, v in know_op[p].items():
                    if ks.get(c, 0) < v:
                        ks[c] = v
                if ks.get(po.chan, 0) < pos[p]:
                    ks[po.chan] = pos[p]
            needed[o.idx] = need
            know_op[o.idx] = dict(ks)
            for p in need:
                ops[p].signal = True
        for o in ops:
            if o.is_dma:
                o.signal = True
        cnt = {}
        for o in ops:
            if o.signal:
                cnt[o.chan] = cnt.get(o.chan, 0) + 1
                o.val = cnt[o.chan] * (16 if o.is_dma else 1)
        sems = {}
        for c in cnt:
            name = "s_" + ("_".join(str(x) for x in c) if isinstance(c, tuple) else c)
            sems[c] = stack.enter_context(nc.semaphore(name))
        self.maxval = dict(cnt)
        block = stack.enter_context(nc.Block())
        streams = {e: [] for e in ENGS}
        for o in ops:
            streams[o.eng].append(o)

        def run_stream(engname, engobj):
            for o in streams[engname]:
                for p in needed[o.idx]:
                    po = ops[p]
                    engobj.wait_ge(sems[po.chan], po.val)
                name, a, kw = o.fn
                inst = getattr(engobj, name)(*a, **kw)
                if o.signal:
                    inst.then_inc(sems[o.chan], 16 if o.is_dma else 1)

        @block.tensor
        def _(e):
            run_stream("pe", e)

        @block.scalar
        def _(e):
            run_stream("act", e)

        @block.vector
        def _(e):
            run_stream("dve", e)

        @block.gpsimd
        def _(e):
            run_stream("pool", e)

        @block.sync
        def _(e):
            run_stream("sp", e)
            for c, n in cnt.items():
                if isinstance(c, tuple):
                    e.wait_ge(sems[c], n * 16)
        return len(ops)


class K:
    def __init__(self, debug=None):
        self.debug = debug or {}
        self.nc = bass.Bass("TRN2", target_bir_lowering=False)
        self.P = Prog()
        self.ins = {}
        self.outs = {}
        self.uid = 0

    def din(self, name, shape):
        t = self.nc.dram_tensor(name, list(shape), F32, kind="ExternalInput").ap()
        self.ins[name] = t
        return t

    def dout(self, name, shape):
        t = self.nc.dram_tensor(name, list(shape), F32, kind="ExternalOutput").ap()
        self.outs[name] = t
        return t

    def sb(self, name, shape, dt=F32):
        shape = list(shape)
        nelem = 1
        for d in shape[1:]:
            nelem *= d
        words = (nelem * (2 if dt == BF16 else 4) + 3) // 4
        words = (words + 7) // 8 * 8
        off = self.arena_off
        assert off + words <= ARENA_WORDS, ("SBUF arena overflow", name, off, words)
        self.arena_off = off + words
        self.arena_peak = max(self.arena_peak, self.arena_off)
        ap = self.arena[0:shape[0], off:off + words]
        if dt == BF16:
            ap = ap.bitcast(BF16)
        ap = ap[:, 0:nelem]
        if len(shape) == 3:
            ap = ap.rearrange("p (a b) -> p a b", b=shape[2])
        elif len(shape) == 4:
            ap = ap.rearrange("p (a b c) -> p a b c", b=shape[2], c=shape[3])
        elif len(shape) == 5:
            ap = ap.rearrange("p (a b c d) -> p a b c d", b=shape[2], c=shape[3], d=shape[4])
        return ap

    def mark(self):
        return self.arena_off

    def release(self, m):
        self.arena_off = m
        self.P.barrier()

    def build(self):
        nc, P = self.nc, self.P
        with contextlib.ExitStack() as st:
            self.stack = st
            self.arena = st.enter_context(nc.sbuf_tensor("arena", [128, ARENA_WORDS], F32))
            self.arena_off = 0
            self.arena_peak = 0
            self.declare_io()
            self.alloc_global()
            self.load_consts()
            for l in range(self.debug.get("layers", DEPTH)):
                self.layer(l)
            self.finish()
            n = P.emit(nc, st)
            self.n_ops = n
        return nc

    def declare_io(self):
        self.x_in = self.din("x", [TOK, D])
        self.cond = self.din("cond", [8, 128])
        self.ident_in = self.din("ident", [128, 128])
        self.w_mod = self.din("w_mod", [DEPTH, D, 6 * D])
        self.b_mod = self.din("b_mod", [DEPTH, 48, 128])
        self.y_out = self.dout("y", [TOK, D])
        self.consts_in = self.din("consts", [128, NCONST])
        self.keep_in = self.din("keep", [1, 1])
        self.rope_in = self.din("rope", [NT, 128, 2, 128])
        self.w_out = self.din("w_out", [DEPTH, D, D])
        self.wC = self.din("wC", [DEPTH, D, 1536])
        self.wA = self.din("wA", [DEPTH, D, 1040])
        self.ml_ib = self.din("ml_ib", [DEPTH, 8])
        self.ml_fb = self.din("ml_fb", [DEPTH, 8])
        self.ml_norm = self.din("ml_norm", [DEPTH, 256])
        self.init_mC = self.din("init_mC", [DEPTH, 2, H, HD, HD])
        self.init_mn = self.din("init_mn", [DEPTH, 2, H, HD])
        self.init_mm = self.din("init_mm", [DEPTH, 2, H])
        self.out_mC = self.dout("out_mC", [8, DEPTH, 2, H, HD, HD])
        self.out_mn = self.dout("out_mn", [8, DEPTH, 2, H, HD])
        self.out_mm = self.dout("out_mm", [8, DEPTH, 2, H])
        self.wB = self.din("wB", [DEPTH, D, 1040])
        self.dl_conv = self.din("dl_conv", [DEPTH, 128, 6, 5])
        self.dl_alog = self.din("dl_alog", [DEPTH, 8])
        self.dl_dtb = self.din("dl_dtb", [DEPTH, 8])
        self.dl_norm = self.din("dl_norm", [DEPTH, 256])
        self.init_delta = self.din("init_delta", [DEPTH, 2, H, HD, HD])
        self.out_delta = self.dout("out_delta", [8, DEPTH, 2, H, HD, HD])
        self.wD = self.din("wD", [DEPTH, D, 1152])
        self.rw_mu = self.din("rw_mu", [DEPTH, 128, 9])
        self.rw_w2p = self.din("rw_w2p", [DEPTH, 128, 2, 256])
        self.rw_a2p = self.din("rw_a2p", [DEPTH, 128, 2, 256])
        self.rw_w0 = self.din("rw_w0", [DEPTH, 512])
        self.rw_a0 = self.din("rw_a0", [DEPTH, 512])
        self.rw_kk = self.din("rw_kk", [DEPTH, 256])
        self.rw_ka = self.din("rw_ka", [DEPTH, 256])
        self.rw_rk = self.din("rw_rk", [DEPTH, 256])
        self.rw_norm = self.din("rw_norm", [DEPTH, 256])
        self.rw_g2 = self.din("rw_g2", [DEPTH, 128, 256])
        self.init_rwkv = self.din("init_rwkv", [DEPTH, 2, H, HD, HD])
        self.out_rwkv = self.dout("out_rwkv", [8, DEPTH, 2, H, HD, HD])
        self.ln1_g = self.din("ln1_g", [DEPTH, D])
        self.ln1_b = self.din("ln1_b", [DEPTH, D])
        self.ln2_g = self.din("ln2_g", [DEPTH, D])
        self.ln2_b = self.din("ln2_b", [DEPTH, D])
        self.w_ff1 = self.din("w_ff1", [DEPTH, D, 4 * D])
        self.w_ff2 = self.din("w_ff2", [DEPTH, 4 * D, D])
        self.ret_decay = self.din("ret_decay", [DEPTH, 8])
        self.ret_norm = self.din("ret_norm", [DEPTH, 256])
        self.init_ret = self.din("init_ret", [DEPTH, 2, H, HD, HD])
        self.out_ret = self.dout("out_ret", [8, DEPTH, 2, H, HD, HD])
        if "yacc" in self.debug:
            self.dbg_yacc = self.dout("dbg_yacc", [128, NT, 256])
        if "uT" in self.debug:
            self.dbg_uT = self.dout("dbg_uT", [128, 8, TOK])

    def alloc_global(self):
        nc = self.nc
        self.xres = self.sb("xres", [128, NT, D], F32)
        self.ident = self.sb("ident", [128, 128], F32)
        self.psum = self.stack.enter_context(nc.psum_tensor("psum", [128, 8, 512], F32))
        self.modT = self.sb("modT", [128, 48], F32)
        self.sc1p = self.sb("sc1p", [128, 8], F32)
        self.sc2p = self.sb("sc2p", [128, 8], F32)
        self.scT = self.sb("scT", [128, 8], F32)
        self.bmodT = self.sb("bmodT", [128, 48], F32)
        self.consts = self.sb("consts", [128, NCONST], F32)
        self.ident_bf = self.sb("ident_bf", [128, 128], BF16)
        self.keep = self.sb("keep", [128, 1], F32)
        self.g_bc = {"g1": self.sb("g1_bc", [128, D], F32), "g2": self.sb("g2_bc", [128, D], F32)}

    def load_consts(self):
        P = self.P
        xv = self.x_in.rearrange("(i p) d -> p i d", p=128)
        for q in range(4):
            P.dma("sp", lambda e, q=q: e.dma_start(out=self.xres[:, 4 * q:4 * q + 4, :], in_=xv[:, 4 * q:4 * q + 4, :]),
                  writes=[("xres", i) for i in range(4 * q, 4 * q + 4)])
        P.dma("sp", lambda e: e.dma_start(out=self.ident[:], in_=self.ident_in), writes=["ident"])
        P.dma("sp", lambda e: e.dma_start(out=self.consts[:], in_=self.consts_in), writes=["consts"])
        P.dma("sp", lambda e: e.dma_start(out=self.keep[:], in_=self.keep_in.partition_broadcast(128)), writes=["keep"])
        P.op("dve", lambda e: e.tensor_copy(self.ident_bf[:], self.ident[:]), reads=["ident"], writes=["ident_bf"])
        c8 = self.sb("c8", [8, 128], F32)
        P.dma("sp", lambda e: e.dma_start(out=c8[:], in_=self.cond), writes=["c8"])
        ps = self.psum
        P.op("pe", lambda e: e.transpose(ps[:, 0, 0:8], c8[:], self.ident[0:8, 0:8]), reads=["c8", "ident"], writes=["ps0"])
        P.op("act", lambda e: e.activation(self.scT[:], ps[:, 0, 0:8], AF.Silu), writes=["ps0", "scT"])

    def compute_mod(self, l):
        P, ps = self.P, self.psum
        m = self.mark()
        self.scbc = self.sb("scbc", [128, 8, 128], F32)
        P.op("dve", lambda e: e.tensor_copy(self.scbc[:], self.scT[:].unsqueeze(2).to_broadcast([128, 8, 128])),
             reads=["scT"], writes=["scbc"])
        b48 = self.sb("b48", [48, 128], F32)
        P.dma("sp", lambda e: e.dma_start(out=b48[:], in_=self.b_mod[l]), writes=["b48"])
        P.op("pe", lambda e: e.transpose(ps[:, 1, 0:48], b48[:], self.ident[0:48, 0:48]), reads=["b48", "ident"], writes=["ps1"])
        P.op("dve", lambda e: e.tensor_copy(self.bmodT[:], ps[:, 1, 0:48]), writes=["ps1", "bmodT"])
        wv = self.w_mod[l].rearrange("(k p) n -> p k n", p=128)
        wblk = [self.sb("wmodblk%d" % i, [128, 8, 512], F32) for i in range(2)]
        bb = self.sb("g_bb", [128, D], F32)
        for b in range(12):
            wb = wblk[b % 2]
            key = "wmodblk%d" % (b % 2)
            P.dma("sp", lambda e, b=b, wb=wb: e.dma_start(out=wb[:], in_=wv[:, :, 512 * b:512 * b + 512]), writes=[key])
            for jj in range(4):
                j = 4 * b + jj
                for k in range(8):
                    P.op("pe", lambda e, wb=wb, jj=jj, j=j, k=k: e.matmul(
                        ps[:, 2, j:j + 1], wb[:, k, 128 * jj:128 * jj + 128], self.scT[:, k:k + 1],
                        start=(k == 0), stop=(k == 7)), reads=[key, "scT"], writes=["ps2"])
            if b in (4, 5, 10, 11):
                which = "g1" if b < 6 else "g2"
                half = b % 2
                if half == 0:
                    off = 2048 if which == "g1" else 5120
                    bm = self.b_mod[l].rearrange("a b -> (a b)")[off:off + 1024].unsqueeze(0)
                    P.dma("sp", lambda e, bm=bm: e.dma_start(out=bb[:], in_=bm.partition_broadcast(128)), writes=["g_bb"])
                gt = self.g_bc[which]
                for k in range(8):
                    P.op("pe", lambda e, wb=wb, k=k: e.matmul(ps[:, 3, :], self.scbc[:, k, :], wb[:, k, :], start=(k == 0), stop=(k == 7)),
                         reads=[key, "scbc"], writes=["ps3"])
                P.op("dve", lambda e, gt=gt, half=half: e.tensor_tensor(
                    gt[:, 512 * half:512 * half + 512], ps[:, 3, :], bb[:, 512 * half:512 * half + 512], ALU.add),
                    reads=["g_bb"], writes=["ps3", which + "_bc"])
        P.op("dve", lambda e: e.tensor_tensor(self.modT[:], ps[:, 2, 0:48], self.bmodT[:], ALU.add), reads=["bmodT"], writes=["ps2", "modT"])
        P.op("dve", lambda e: e.tensor_scalar(self.sc1p[:], self.modT[:, 8:16], 1.0, None, ALU.add), reads=["modT"], writes=["sc1p"])
        P.op("dve", lambda e: e.tensor_scalar(self.sc2p[:], self.modT[:, 32:40], 1.0, None, ALU.add), reads=["modT"], writes=["sc2p"])
        self.release(m)

    def ln_mod_T(self, tile, dst_fn, scp, sh_off, tmp, dst_keys, par=0):
        P, ps = self.P, self.psum
        stats, mv, rstd, xhat = tmp
        sf = str(par)
        b0 = 4 + 2 * par
        xk = ("xres", tile)
        src = self.xres[:, tile, :]
        for hh in range(2):
            P.op("dve", lambda e, hh=hh: e.bn_stats(stats[:, hh, :], src[:, 512 * hh:512 * hh + 512]), reads=[xk], writes=["lnstats" + sf])
        P.op("dve", lambda e: e.bn_aggr(mv[:], stats[:]), reads=["lnstats" + sf], writes=["lnmv" + sf])
        P.op("act", lambda e: e.activation(rstd[:], mv[:, 1:2], AF.Ln, bias=LN_EPS, scale=1.0), reads=["lnmv" + sf], writes=["lnrstd" + sf])
        P.op("act", lambda e: e.activation(rstd[:], rstd[:], AF.Exp, scale=-0.5), reads=["lnrstd" + sf], writes=["lnrstd" + sf])
        P.op("dve", lambda e: e.tensor_scalar(xhat[:], src, mv[:, 0:1], rstd[:, 0:1], ALU.subtract, ALU.mult),
             reads=[xk, "lnmv" + sf, "lnrstd" + sf], writes=["xhat" + sf])
        for k in range(8):
            b = b0 + (k // 4)
            P.op("pe", lambda e, k=k, b=b: e.transpose(ps[:, b, 128 * (k % 4):128 * (k % 4) + 128], xhat[:, 128 * k:128 * k + 128], self.ident[:]),
                 reads=["xhat" + sf, "ident"], writes=["ps%d" % b])
        for k in range(8):
            b = b0 + (k // 4)
            P.op("act", lambda e, k=k, b=b: e.activation(dst_fn(k), ps[:, b, 128 * (k % 4):128 * (k % 4) + 128], AF.Identity,
                                                       bias=self.modT[:, sh_off + k:sh_off + k + 1], scale=scp[:, k:k + 1]),
                 reads=["modT", "sc1p", "sc2p"], writes=["ps%d" % b] + dst_keys)

    def dump(self, name, ap, keys, shape):
        want = self.debug.get("dump", ())
        if name not in want or ("dmp_" + name) in self.outs:
            return
        P = self.P
        out = self.dout("dmp_" + name, shape)
        m = self.mark()
        tmp = self.sb("dmp_" + name, shape, F32)
        P.op("dve", lambda e: e.tensor_copy(tmp[:], ap), reads=keys, writes=["dmp_" + name])
        P.dma("sp", lambda e: e.dma_start(out=out, in_=tmp[:]), reads=["dmp_" + name])
        self.release(m)

    def act_sigmoid(self, out, in_, reads, writes):
        P = self.P
        P.op("act", lambda e: e.activation(out, in_, AF.Exp, scale=-1.0), reads=reads, writes=writes)
        P.op("act", lambda e: e.activation(out, out, AF.Ln, bias=1.0, scale=1.0), reads=writes, writes=writes)
        P.op("act", lambda e: e.activation(out, out, AF.Exp, scale=-1.0), reads=writes, writes=writes)

    def cst(self, name, rows=64):
        o, w = CO[name]
        return self.consts[0:rows, o:o + w]

    def load_w(self, name, src, ncols, kchunks=8):
        P = self.P
        wb = self.sb(name, [128, kchunks, ncols], BF16)
        wv = src.rearrange("(k p) n -> p k n", p=128)
        step = 2 if kchunks % 2 == 0 else 1
        for k0 in range(0, kchunks, step):
            P.dma("pool", lambda e, k0=k0: e.dma_start(out=wb[:, k0:k0 + step, :], in_=wv[:, k0:k0 + step, :]),
                  writes=[(name, k0 // step)])
        return wb, [(name, i) for i in range(kchunks // step)]

    def emit_out_tile(self, t, osb, osb_keys, bank):
        P, ps = self.P, self.psum
        sel = self.cst("SEL").rearrange("p (q n) -> p q n", n=128)
        pk = "ps%d" % bank
        for z in range(2):
            tile = t if z == 0 else NT - 1 - t
            for c in range(2):
                P.op("pe", lambda e, z=z, c=c: e.matmul(ps[:, bank, 256 * z:256 * z + 256], sel[:, 2 * z + c, :], osb[:, c, z, :],
                                                         start=(c == 0), stop=(c == 1)),
                     reads=["consts"] + osb_keys, writes=[pk])
            yk = ("yacc", tile)
            if t < NT // 2:
                P.op("act", lambda e, z=z, tile=tile: e.activation(self.yacc[:, tile, :], ps[:, bank, 256 * z:256 * z + 256], AF.Copy),
                     writes=[pk, yk])
            else:
                P.op("dve", lambda e, z=z, tile=tile: e.tensor_tensor(self.yacc[:, tile, :], ps[:, bank, 256 * z:256 * z + 256],
                                                                     self.yacc[:, tile, :], ALU.add),
                     writes=[pk, yk])

    def mixer_ret(self, l, first):
        P, ps = self.P, self.psum
        m0 = self.mark()
        wb, wkeys = self.load_w("wC", self.wC[l], 1536)
        m1 = self.mark()
        lg = self.sb("ret_lg", [128, 8], F32)
        P.dma("sp", lambda e: e.dma_start(out=lg[:], in_=self.ret_decay[l:l + 1, :].partition_broadcast(128)), writes=["ret_lg"])
        P.op("act", lambda e: e.activation(lg[:], lg[:], AF.Exp), reads=["ret_lg"], writes=["ret_lg"])
        P.op("dve", lambda e: e.tensor_scalar(lg[:], lg[:], -1.0, None, ALU.mult), reads=["ret_lg"], writes=["ret_lg"])
        gs = self.sb("ret_gs", [128, 8], F32)
        ga = self.sb("ret_ga", [128, 8], F32)
        pidx = self.consts[:, CO["PIDX"][0]:CO["PIDX"][0] + 1]
        npidx = self.consts[:, CO["NPIDX"][0]:CO["NPIDX"][0] + 1]
        P.op("act", lambda e: e.activation(gs[:], lg[:], AF.Exp, scale=npidx), reads=["ret_lg", "consts"], writes=["ret_gs"])
        P.op("act", lambda e: e.activation(ga[:], lg[:], AF.Exp, scale=pidx), reads=["ret_lg", "consts"], writes=["ret_ga"])
        G64 = self.sb("ret_g64", [128, 4], F32)
        GAM = self.sb("ret_gam", [128, 4], F32)
        GAMI = self.sb("ret_gami", [128, 4], F32)
        for hq in range(2):
            rows = slice(64 * hq, 64 * hq + 64)
            P.op("act", lambda e, rows=rows, hq=hq: e.activation(G64[rows, :], lg[rows, hq::2], AF.Exp, scale=64.0), reads=["ret_lg"], writes=["ret_g64"])
            P.op("act", lambda e, rows=rows, hq=hq: e.activation(GAM[rows, :], lg[rows, hq::2], AF.Exp, scale=1.0), reads=["ret_lg"], writes=["ret_gam"])
            P.op("act", lambda e, rows=rows, hq=hq: e.activation(GAMI[rows, :], lg[rows, hq::2], AF.Exp, scale=-1.0), reads=["ret_lg"], writes=["ret_gami"])
        S = self.sb("ret_S", [128, 4, 64], F32)
        Sb = self.sb("ret_Sb", [128, 4, 64], BF16)
        Sout = self.sb("ret_Sout", [128, 4, 64], F32)
        P.dma("sp", lambda e: e.dma_start(out=S[:], in_=self.init_ret[l].rearrange("z (hp hq) d e -> (hq d) (z hp) e", hq=2)), writes=["ret_S"])
        P.op("dve", lambda e: e.tensor_tensor(S[:], S[:], GAM[:].unsqueeze(2).to_broadcast([128, 4, 64]), ALU.mult), reads=["ret_gam"], writes=["ret_S"])
        P.op("act", lambda e: e.activation(Sb[:], S[:], AF.Copy), reads=["ret_S"], writes=["ret_Sb"])
        rc = [self.sb("ret_rc%d" % i, [128, 2, 2, 128], F32) for i in range(2)]
        ta = self.sb("ret_ta", [128, 2, 128], F32)
        tb = self.sb("ret_tb", [128, 2, 128], F32)
        qT = self.sb("ret_qT", [128, 2, 2, 2, 128], BF16)
        P.op("dve", lambda e: e.memset(qT[:], 0.0), writes=["ret_qT"])
        kT = self.sb("ret_kT", [128, 2, 2, 128], BF16)
        vT = self.sb("ret_vT", [128, 2, 2, 128], BF16)
        ktm = self.sb("ret_ktm", [64, 8, 64], BF16)
        vtm = self.sb("ret_vtm", [64, 8, 64], BF16)
        pm = self.sb("ret_pm", [64, 8, 64], BF16)
        osb = self.sb("ret_osb", [64, 2, 2, 256], F32)
        tmpS = self.sb("ret_tmpS", [128, 4, 64], F32)
        incl = self.cst("INCL")
        psb = ps.bitcast(BF16) if False else None

        def bfview(bank):
            return ps[:, bank, :].bitcast(BF16)

        stop = self.debug.get("stop", 99)
        for t in range(NT if stop > 1 else 0):
            tiles = (t, NT - 1 - t)
            r = rc[t % 2]
            rk = "ret_rc%d" % (t % 2)
            for z in range(2):
                P.dma("sp", lambda e, z=z, r=r: e.dma_start(out=r[:, z, :, :], in_=self.rope_in[tiles[z]]), writes=[rk])
            def proj(j, bank, z):
                tok0 = 2 + 128 * tiles[z]
                for k in range(8):
                    P.op("pe", lambda e, j=j, k=k, z=z, tok0=tok0, bank=bank: e.matmul(
                        ps[:, bank, 128 * z:128 * z + 128], wb[:, k, 128 * j:128 * j + 128], self.uT[:, k, tok0:tok0 + 128],
                        start=(k == 0), stop=(k == 7)),
                        reads=wkeys + [("uT", tiles[z])], writes=["ps%d" % bank])
            for which, dst, dkey in ((0, qT, "ret_qT"), (1, kT, "ret_kT")):
                for hp in range(2):
                    j = 2 * which + hp
                    for z in range(2):
                        proj(j, 0, z)
                        proj(6 + j, 1, z)
                    scale = 0.125 if which == 0 else 1.0
                    P.op("dve", lambda e, scale=scale, r=r: e.scalar_tensor_tensor(
                        ta[:], ps[:, 0, 0:256].rearrange("p (z n) -> p z n", z=2), scale, r[:, :, 0, :], ALU.mult, ALU.mult),
                        reads=[rk], writes=["ps0", "ret_ta"])
                    P.op("dve", lambda e, scale=scale, r=r: e.scalar_tensor_tensor(
                        tb[:], ps[:, 1, 0:256].rearrange("p (z n) -> p z n", z=2), scale, r[:, :, 1, :], ALU.mult, ALU.mult),
                        reads=[rk], writes=["ps1", "ret_tb"])
                    if which == 1:
                        P.op("dve", lambda e, dst=dst, hp=hp: e.tensor_tensor(dst[:, hp, 0, :], ta[:, 0, :], tb[:, 0, :], ALU.add),
                             reads=["ret_ta", "ret_tb"], writes=[dkey])
                        P.op("dve", lambda e, dst=dst, hp=hp: e.tensor_tensor(dst[:, hp, 1, ::-1], ta[:, 1, :], tb[:, 1, :], ALU.add),
                             reads=["ret_ta", "ret_tb"], writes=[dkey])
                    else:
                        for hq in range(2):
                            rws = slice(64 * hq, 64 * hq + 64)
                            P.op("dve", lambda e, dst=dst, hp=hp, hq=hq, rws=rws: e.tensor_tensor(
                                dst[rws, hp, hq, 0, :], ta[rws, 0, :], tb[rws, 0, :], ALU.add),
                                reads=["ret_ta", "ret_tb"], writes=[dkey])
                            P.op("dve", lambda e, dst=dst, hp=hp, hq=hq, rws=rws: e.tensor_tensor(
                                dst[rws, hp, hq, 1, ::-1], ta[rws, 1, :], tb[rws, 1, :], ALU.add),
                                reads=["ret_ta", "ret_tb"], writes=[dkey])
            for hp in range(2):
                bank = hp
                for z in range(2):
                    proj(4 + hp, bank, z)
                P.op("act", lambda e, hp=hp, bank=bank: e.activation(vT[:, hp, 0, :], ps[:, bank, 0:128], AF.Copy), writes=["ps%d" % bank, "ret_vT"])
                P.op("act", lambda e, hp=hp, bank=bank: e.activation(vT[:, hp, 1, ::-1], ps[:, bank, 128:256], AF.Copy), writes=["ps%d" % bank, "ret_vT"])
            self.dump("ret_qT", qT[:], ["ret_qT"], [128, 2, 2, 2, 128])
            self.dump("ret_kT", kT[:], ["ret_kT"], [128, 2, 2, 128])
            self.dump("ret_vT", vT[:], ["ret_vT"], [128, 2, 2, 128])
            for c in range(2 if stop > 2 else 0):
                cs = slice(64 * c, 64 * c + 64)
                for (src, skey, bank) in ((kT, "ret_kT", 2), (vT, "ret_vT", 3)):
                    bv = bfview(bank)
                    for z in range(2):
                        for hp in range(2):
                            col = (4 * z + 2 * hp) * 64
                            P.op("pe", lambda e, src=src, z=z, hp=hp, col=col, bv=bv: e.transpose(
                                bv[0:64, col:col + 128], src[:, hp, z, cs], self.ident_bf[:]),
                                reads=[skey, "ident_bf"], writes=["ps%d" % bank])
                P.op("act", lambda e: e.activation(ktm[:], bfview(2)[0:64, 0:512].rearrange("p (u d) -> p u d", d=64), AF.Copy),
                     writes=["ps2", "ret_ktm"])
                P.op("dve", lambda e: e.tensor_tensor(vtm[:], bfview(3)[0:64, 0:512].rearrange("p (u d) -> p u d", d=64),
                                                      gs[0:64, :].unsqueeze(2).to_broadcast([64, 8, 64]), ALU.mult),
                     reads=["ret_gs"], writes=["ps3", "ret_vtm"])
                self.dump("ret_ktm", ktm[:], ["ret_ktm"], [64, 8, 64])
                self.dump("ret_vtm", vtm[:], ["ret_vtm"], [64, 8, 64])
                if stop <= 3:
                    continue
                for z in range(2):
                    for h in range(4):
                        hp, hq = h // 2, h % 2
                        rows = slice(64 * hq, 64 * hq + 64)
                        u = 4 * z + h
                        P.op("pe", lambda e, z=z, hp=hp, hq=hq, u=u: e.matmul(
                            ps[0:64, 4, 64 * u:64 * u + 64], kT[:, hp, z, cs], qT[:, hp, hq, z, cs], start=True, stop=True),
                            reads=["ret_kT", "ret_qT"], writes=["ps4"])
                if self.debug.get("sub") == "a":
                    continue
                P.op("dve", lambda e: e.tensor_tensor(pm[:], ps[0:64, 4, :].rearrange("p (u l) -> p u l", l=64),
                                                      incl.unsqueeze(1).to_broadcast([64, 8, 64]), ALU.mult),
                     reads=["consts"], writes=["ps4", "ret_pm"])
                self.dump("ret_pm", pm[:], ["ret_pm"], [64, 8, 64])
                if stop <= 4:
                    continue
                for z in range(2):
                    for h in range(4):
                        hp, hq = h // 2, h % 2
                        rows = slice(64 * hq, 64 * hq + 64)
                        u = 4 * z + h
                        P.op("pe", lambda e, u=u: e.matmul(ps[0:64, 5, 64 * u:64 * u + 64], pm[:, u, :], vtm[:, u, :], start=True, stop=False),
                             reads=["ret_pm", "ret_vtm"], writes=["ps5"])
                        P.op("pe", lambda e, z=z, hp=hp, hq=hq, u=u: e.matmul(
                            ps[0:64, 5, 64 * u:64 * u + 64], qT[:, hp, hq, z, cs], Sb[:, 2 * z + hp, :], start=False, stop=True),
                            reads=["ret_qT", "ret_Sb"], writes=["ps5"])
                P.op("dve", lambda e, c=c: e.tensor_tensor(
                    osb[:, c, :, :].rearrange("p z (h e) -> p (z h) e", e=64), ps[0:64, 5, :].rearrange("p (u e) -> p u e", e=64),
                    ga[0:64, :].unsqueeze(2).to_broadcast([64, 8, 64]), ALU.mult),
                    reads=["ret_ga"], writes=["ps5", ("ret_osb", c)])
                self.dump("ret_osb", osb[:, 0, :, :], [("ret_osb", 0)], [64, 2, 256])
                if stop <= 5:
                    continue
                for z in range(2):
                    for h in range(4):
                        hp, hq = h // 2, h % 2
                        rows = slice(64 * hq, 64 * hq + 64)
                        u = 4 * z + h
                        col = (2 * z + hp) * 64
                        P.op("pe", lambda e, rows=rows, u=u, col=col: e.matmul(ps[rows, 6, col:col + 64], ktm[:, u, :], vtm[:, u, :], start=True, stop=True),
                             reads=["ret_ktm", "ret_vtm"], writes=["ps6"])
                P.op("dve", lambda e: e.tensor_tensor(tmpS[:], ps[:, 6, 0:256].rearrange("p (a e) -> p a e", e=64), S[:], ALU.add),
                     reads=["ret_S"], writes=["ps6", "ret_tmpS"])
                P.op("dve", lambda e: e.tensor_tensor(S[:], tmpS[:], G64[:].unsqueeze(2).to_broadcast([128, 4, 64]), ALU.mult),
                     reads=["ret_tmpS", "ret_g64"], writes=["ret_S"])
                self.dump("ret_S1", S[:], ["ret_S"], [128, 4, 64])
                if not (t % 2 == 1 and c == 1):
                    P.op("act", lambda e: e.activation(Sb[:], S[:], AF.Copy), reads=["ret_S"], writes=["ret_Sb"])
            if stop > 6:
                self.emit_out_tile(t, osb, [("ret_osb", 0), ("ret_osb", 1)], 7)
            if t % 2 == 1 and stop > 7:
                P.op("dve", lambda e: e.tensor_tensor(Sout[:], S[:], GAMI[:].unsqueeze(2).to_broadcast([128, 4, 64]), ALU.mult),
                     reads=["ret_S", "ret_gami"], writes=["ret_Sout"])
                for z in range(2):
                    seg = (t - 1) // 2 if z == 0 else (NT - 1 - t) // 2
                    P.dma("sp", lambda e, z=z, seg=seg: e.dma_start(
                        out=self.out_ret[seg, l, z].rearrange("(hp hq) d e -> (hq d) hp e", hq=2), in_=Sout[:, 2 * z:2 * z + 2, :]),
                        reads=["ret_Sout"])
                P.op("dve", lambda e: e.tensor_scalar(S[:], S[:], self.keep[:, 0:1], None, ALU.mult), reads=["ret_S", "keep"], writes=["ret_S"])
                P.op("act", lambda e: e.activation(Sb[:], S[:], AF.Copy), reads=["ret_S"], writes=["ret_Sb"])
        if self.debug.get("yacc") == "ret":
            P.dma("sp", lambda e: e.dma_start(out=self.dbg_yacc, in_=self.yacc[:]), reads=[("yacc", i) for i in range(NT)])
        self.release(m1)
        if self.debug.get("post", True):
            self.post_simple(l, "ret", wb, wkeys, 1280, AF.Silu, self.ret_norm, True, 512, first)
        self.release(m0)

    def proj_fm(self, wb, wkeys, col, M, tiles, bank, halo=0):
        P, ps = self.P, self.psum
        W = 128 + 2 * halo
        for z in range(2):
            tok0 = 2 + 128 * tiles[z] - halo
            for k in range(8):
                P.op("pe", lambda e, k=k, z=z, tok0=tok0: e.matmul(
                    ps[0:M, bank, W * z:W * z + W], wb[:, k, col:col + M], self.uT[:, k, tok0:tok0 + W],
                    start=(k == 0), stop=(k == 7)),
                    reads=wkeys + [("uT", tiles[z])], writes=["ps%d" % bank])

    def evac_fm(self, dst_fn, bank, M, dkey, scale=1.0, W=128, eng="act", rows=None):
        P, ps = self.P, self.psum
        rs = slice(0, M) if rows is None else rows
        for z in range(2):
            src = ps[rs, bank, W * z:W * z + W]
            dst = dst_fn(z)
            if z == 1:
                dst = dst[:, ::-1]
            if eng == "act":
                P.op("act", lambda e, dst=dst, src=src: e.activation(dst, src, AF.Copy, scale=scale), writes=["ps%d" % bank, dkey])
            else:
                P.op("dve", lambda e, dst=dst, src=src: e.tensor_scalar(dst, src, scale, None, ALU.mult), writes=["ps%d" % bank, dkey])

    def tm_transposes(self, srcT, skey, cs, bank, col0):
        P, ps = self.P, self.psum
        if srcT.dtype == BF16:
            bv = ps[:, bank, :].bitcast(BF16)
            idn = self.ident_bf
        else:
            bv = ps[:, bank:bank + 2, :].rearrange("p a b -> p (a b)")
            idn = self.ident
        for z in range(2):
            for hp in range(2):
                col = col0 + (4 * z + 2 * hp) * 64
                P.op("pe", lambda e, z=z, hp=hp, col=col: e.transpose(bv[0:64, col:col + 128], srcT[:, hp, z, cs], idn[:]),
                     reads=[skey, "ident_bf", "ident"], writes=["ps%d" % bank, "ps%d" % (bank + (0 if srcT.dtype == BF16 else 1))])
        return bv[0:64, col0:col0 + 512].rearrange("p (u d) -> p u d", d=64)

    def mixer_mlstm(self, l, first):
        P, ps = self.P, self.psum
        SDT = F32 if self.debug.get("mlf32") else BF16
        m0 = self.mark()
        wb, wkeys = self.load_w("wA", self.wA[l], 1040)
        m1 = self.mark()
        incl = self.cst("INCL")
        ones = self.consts[0:64, CO["ONES"][0]:CO["ONES"][0] + 128]
        ib = self.sb("ml_ib", [128, 8], F32)
        fb = self.sb("ml_fb", [128, 8], F32)
        P.dma("sp", lambda e: e.dma_start(out=ib[:], in_=self.ml_ib[l:l + 1, :].partition_broadcast(128)), writes=["ml_ib"])
        P.dma("sp", lambda e: e.dma_start(out=fb[:], in_=self.ml_fb[l:l + 1, :].partition_broadcast(128)), writes=["ml_fb"])
        Cg = self.sb("ml_C", [128, 4, 65], F32)
        Cb = self.sb("ml_Cb", [128, 4, 65], SDT)
        Cout = self.sb("ml_Cout", [128, 4, 65], F32)
        tmpC = self.sb("ml_tmpC", [128, 4, 65], F32)
        mst = self.sb("ml_m", [8, 1], F32)
        msc = self.sb("ml_msc", [8, 4], F32)
        dg = self.sb("ml_dg", [8, 8], F32)
        esl = self.sb("ml_esl", [128, 4], F32)
        P.dma("sp", lambda e: e.dma_start(out=Cg[:, :, 0:64], in_=self.init_mC[l].rearrange("z (hp hq) d e -> (hq d) (z hp) e", hq=2)), writes=["ml_C"])
        P.dma("sp", lambda e: e.dma_start(out=Cg[:, :, 64:65], in_=self.init_mn[l].rearrange("z (hp hq) (d o) -> (hq d) (z hp) o", hq=2, o=1),
                                          allow_slow_non_contiguous=True), writes=["ml_C"])
        P.dma("sp", lambda e: e.dma_start(out=mst[:], in_=self.init_mm[l].rearrange("z (h o) -> (z h) o", o=1), allow_slow_non_contiguous=True), writes=["ml_m"])

        def bcast_units(src81, sign, dst_keys):
            P.op("dve", lambda e: e.tensor_scalar(dg[:], self.ident[0:8, 0:8], src81, None, ALU.mult), reads=["ident", "ml_m"], writes=["ml_dg"])
            P.op("pe", lambda e: e.matmul(ps[:, 7, 0:8], self.consts[0:8, CO["ONES"][0]:CO["ONES"][0] + 128], dg[:], start=True, stop=True),
                 reads=["consts", "ml_dg"], writes=["ps7"])
            for hq in range(2):
                rows = slice(64 * hq, 64 * hq + 64)
                P.op("act", lambda e, rows=rows, hq=hq: e.activation(esl[rows, :], ps[rows, 7, hq:8:2], AF.Exp, scale=sign), writes=["ps7", "ml_esl"])

        bcast_units(mst[:, 0:1], 1.0, None)
        P.op("dve", lambda e: e.tensor_tensor(Cg[:], Cg[:], esl[:].unsqueeze(2).to_broadcast([128, 4, 65]), ALU.mult), reads=["ml_esl", "ml_C"], writes=["ml_C"])
        P.op("act", lambda e: e.activation(Cb[:], Cg[:], AF.Copy), reads=["ml_C"], writes=["ml_Cb"])
        qT = self.sb("ml_qT", [128, 2, 2, 2, 128], SDT)
        P.op("dve", lambda e: e.memset(qT[:], 0.0), writes=["ml_qT"])
        kT = self.sb("ml_kT", [128, 2, 2, 128], SDT)
        vT = self.sb("ml_vT", [128, 2, 2, 128], SDT)
        gT = self.sb("ml_gT", [8, 2, 128], F32)
        ktm = self.sb("ml_ktm", [64, 8, 64], SDT)
        vaug = self.sb("ml_vaug", [64, 8, 65], SDT)
        pm = self.sb("ml_pm", [64, 8, 64], SDT)
        osb = self.sb("ml_osb", [64, 2, 2, 256], F32)
        gtm = self.sb("ml_gtm", [64, 2, 8], F32)
        li = self.sb("ml_li", [64, 8], F32)
        sp = self.sb("ml_sp", [64, 8], F32)
        lib = self.sb("ml_lib", [64, 8], F32)
        e1 = self.sb("ml_e1", [64, 8], F32)
        eb = self.sb("ml_eb", [64, 8], F32)
        wk = self.sb("ml_wk", [64, 8], F32)
        nbl = self.sb("ml_nbl", [64, 8], F32)
        ebls = self.sb("ml_ebls", [128, 4], F32)
        dn = self.sb("ml_dn", [64, 8], F32)
        for t in range(NT):
            tiles = (t, NT - 1 - t)
            for hp in range(2):
                self.proj_fm(wb, wkeys, 128 * hp, 128, tiles, 0)
                for hq in range(2):
                    rows = slice(64 * hq, 64 * hq + 64)
                    self.evac_fm(lambda z, hp=hp, hq=hq, rows=rows: qT[rows, hp, hq, z, :], 0, 128, "ml_qT", rows=rows, eng="dve" if hq else "act")
                self.proj_fm(wb, wkeys, 256 + 128 * hp, 128, tiles, 1)
                self.evac_fm(lambda z, hp=hp: kT[:, hp, z, :], 1, 128, "ml_kT", scale=0.125)
                self.proj_fm(wb, wkeys, 512 + 128 * hp, 128, tiles, 0)
                self.evac_fm(lambda z, hp=hp: vT[:, hp, z, :], 0, 128, "ml_vT", eng="dve")
            for z in range(2):
                tok0 = 2 + 128 * tiles[z]
                for k in range(8):
                    P.op("pe", lambda e, k=k, z=z, tok0=tok0: e.matmul(ps[0:8, 1, 128 * z:128 * z + 128], wb[:, k, 768 + 8 * z:768 + 8 * z + 8],
                                                                    self.uT[:, k, tok0:tok0 + 128], start=(k == 0), stop=(k == 7)),
                         reads=wkeys + [("uT", tiles[z])], writes=["ps1"])
            self.evac_fm(lambda z: gT[:, z, :], 1, 8, "ml_gT", eng="dve")
            for c in range(2):
                cs = slice(64 * c, 64 * c + 64)
                for z in range(2):
                    P.op("pe", lambda e, z=z: e.transpose(ps[0:64, 7, 8 * z:8 * z + 8], gT[:, z, cs], self.ident[0:8, 0:8]), reads=["ml_gT", "ident"], writes=["ps7"])
                P.op("dve", lambda e: e.tensor_copy(gtm[:], ps[0:64, 7, 0:16].rearrange("p (z g) -> p z g", g=8)), writes=["ps7", "ml_gtm"])
                P.op("dve", lambda e: e.tensor_tensor(li[:].rearrange("p (z h) -> p z h", h=4), gtm[:, :, 0:4],
                                                      ib[0:64, :].rearrange("p (z h) -> p z h", h=4), ALU.add), reads=["ml_gtm", "ml_ib"], writes=["ml_li"])
                P.op("dve", lambda e: e.tensor_tensor(sp[:].rearrange("p (z h) -> p z h", h=4), gtm[:, :, 4:8],
                                                      fb[0:64, :].rearrange("p (z h) -> p z h", h=4), ALU.add), reads=["ml_gtm", "ml_fb"], writes=["ml_sp"])
                P.op("act", lambda e: e.activation(sp[:], sp[:], AF.Exp, scale=-1.0), reads=["ml_sp"], writes=["ml_sp"])
                P.op("act", lambda e: e.activation(sp[:], sp[:], AF.Ln, bias=1.0, scale=1.0), reads=["ml_sp"], writes=["ml_sp"])
                P.op("pe", lambda e: e.matmul(ps[0:64, 7, 16:24], incl, sp[:], start=True, stop=True), reads=["consts", "ml_sp"], writes=["ps7"])
                P.op("pe", lambda e: e.matmul(ps[:, 7, 24:32], ones, sp[:], start=True, stop=True), reads=["consts", "ml_sp"], writes=["ps7"])
                P.op("dve", lambda e: e.tensor_tensor(lib[:], ps[0:64, 7, 16:24], li[:], ALU.add), reads=["ml_li"], writes=["ps7", "ml_lib"])
                P.op("act", lambda e: e.activation(eb[:], ps[0:64, 7, 16:24], AF.Exp, scale=-1.0), writes=["ps7", "ml_eb"])
                P.op("dve", lambda e: e.tensor_copy(nbl[:], ps[0:64, 7, 24:32]), writes=["ps7", "ml_nbl"])
                for hq in range(2):
                    rows = slice(64 * hq, 64 * hq + 64)
                    P.op("act", lambda e, rows=rows, hq=hq: e.activation(ebls[rows, :], ps[rows, 7, 24 + hq:32:2], AF.Exp, scale=-1.0), writes=["ps7", "ml_ebls"])
                P.op("act", lambda e: e.activation(e1[:], lib[:], AF.Exp), reads=["ml_lib"], writes=["ml_e1"])
                P.op("dve", lambda e: e.tensor_tensor(wk[:], lib[:], nbl[:], ALU.subtract), reads=["ml_lib", "ml_nbl"], writes=["ml_wk"])
                ktv = self.tm_transposes(kT, "ml_kT", cs, 2, 0)
                vtv = self.tm_transposes(vT, "ml_vT", cs, 2, 512)
                P.op("act", lambda e: e.activation(ktm[:], ktv, AF.Copy), writes=["ps2", "ps3", "ml_ktm"])
                P.op("dve", lambda e: e.tensor_tensor(vaug[:, :, 0:64], vtv, e1[:].unsqueeze(2).to_broadcast([64, 8, 64]), ALU.mult),
                     reads=["ml_e1"], writes=["ps2", "ps3", "ml_vaug"])
                P.op("dve", lambda e: e.tensor_copy(vaug[:, :, 64:65], e1[:].unsqueeze(2)), reads=["ml_e1"], writes=["ml_vaug"])
                for z in range(2):
                    for h in range(4):
                        hp, hq = h // 2, h % 2
                        u = 4 * z + h
                        P.op("pe", lambda e, z=z, hp=hp, hq=hq, u=u: e.matmul(ps[0:64, 4, 64 * u:64 * u + 64], kT[:, hp, z, cs], qT[:, hp, hq, z, cs], start=True, stop=True),
                             reads=["ml_kT", "ml_qT"], writes=["ps4"])
                P.op("dve", lambda e: e.tensor_tensor(pm[:], ps[0:64, 4, :].rearrange("p (u l) -> p u l", l=64), incl.unsqueeze(1).to_broadcast([64, 8, 64]), ALU.mult),
                     reads=["consts"], writes=["ps4", "ml_pm"])
                for z in range(2):
                    bank = 5 + z
                    for h in range(4):
                        hp, hq = h // 2, h % 2
                        u = 4 * z + h
                        P.op("pe", lambda e, u=u, h=h, bank=bank: e.matmul(ps[0:64, bank, 65 * h:65 * h + 65], pm[:, u, :], vaug[:, u, :], start=True, stop=False),
                             reads=["ml_pm", "ml_vaug"], writes=["ps%d" % bank])
                        P.op("pe", lambda e, z=z, hp=hp, hq=hq, h=h, bank=bank: e.matmul(ps[0:64, bank, 65 * h:65 * h + 65], qT[:, hp, hq, z, cs], Cb[:, 2 * z + hp, :], start=False, stop=True),
                             reads=["ml_qT", "ml_Cb"], writes=["ps%d" % bank])
                for z in range(2):
                    bank = 5 + z
                    o3 = ps[0:64, bank, 0:260].rearrange("p (h e) -> p h e", e=65)
                    P.op("dve", lambda e, z=z, o3=o3: e.tensor_tensor(dn[:, 4 * z:4 * z + 4], o3[:, :, 64], eb[:, 4 * z:4 * z + 4], ALU.mult),
                         reads=["ml_eb"], writes=["ps%d" % bank, "ml_dn"])
                P.op("act", lambda e: e.activation(dn[:], dn[:], AF.Abs), reads=["ml_dn"], writes=["ml_dn"])
                P.op("dve", lambda e: e.tensor_scalar(dn[:], dn[:], 1.0, None, ALU.max), reads=["ml_dn"], writes=["ml_dn"])
                P.op("dve", lambda e: e.reciprocal(dn[:], dn[:]), reads=["ml_dn"], writes=["ml_dn"])
                P.op("dve", lambda e: e.tensor_tensor(dn[:], dn[:], eb[:], ALU.mult), reads=["ml_dn", "ml_eb"], writes=["ml_dn"])
                for z in range(2):
                    bank = 5 + z
                    o3 = ps[0:64, bank, 0:260].rearrange("p (h e) -> p h e", e=65)
                    P.op("dve", lambda e, z=z, o3=o3, c=c: e.tensor_tensor(
                        osb[:, c, z, :].rearrange("p (h e) -> p h e", e=64), o3[:, :, 0:64],
                        dn[:, 4 * z:4 * z + 4].unsqueeze(2).to_broadcast([64, 4, 64]), ALU.mult),
                        reads=["ml_dn"], writes=["ps%d" % bank, ("ml_osb", c)])
                for z in range(2):
                    for h in range(4):
                        hp, hq = h // 2, h % 2
                        rows = slice(64 * hq, 64 * hq + 64)
                        u = 4 * z + h
                        col = (2 * z + hp) * 65
                        P.op("pe", lambda e, rows=rows, u=u, col=col: e.matmul(ps[rows, 3, col:col + 65], ktm[:, u, :], vaug[:, u, :], start=True, stop=True),
                             reads=["ml_ktm", "ml_vaug"], writes=["ps3"])
                P.op("dve", lambda e: e.tensor_tensor(tmpC[:], ps[:, 3, 0:260].rearrange("p (a e) -> p a e", e=65), Cg[:], ALU.add),
                     reads=["ml_C"], writes=["ps3", "ml_tmpC"])
                P.op("dve", lambda e: e.tensor_tensor(Cg[:], tmpC[:], ebls[:].unsqueeze(2).to_broadcast([128, 4, 65]), ALU.mult),
                     reads=["ml_tmpC", "ml_ebls"], writes=["ml_C"])
                if not (t % 2 == 1 and c == 1):
                    P.op("act", lambda e: e.activation(Cb[:], Cg[:], AF.Copy), reads=["ml_C"], writes=["ml_Cb"])
                P.op("pe", lambda e: e.transpose(ps[0:8, 7, 64:128], wk[:], self.ident[0:64, 0:64]), reads=["ml_wk", "ident"], writes=["ps7"])
                P.op("pe", lambda e: e.transpose(ps[0:8, 7, 128:192], nbl[:], self.ident[0:64, 0:64]), reads=["ml_nbl", "ident"], writes=["ps7"])
                P.op("dve", lambda e: e.tensor_reduce(msc[:, 0:1], ps[0:8, 7, 64:128], AX.X, ALU.max), writes=["ps7", "ml_msc"])
                P.op("dve", lambda e: e.tensor_tensor(msc[:, 1:2], mst[:], ps[0:8, 7, 128:129], ALU.subtract), reads=["ml_m"], writes=["ps7", "ml_msc"])
                P.op("dve", lambda e: e.tensor_tensor(mst[:], msc[:, 0:1], msc[:, 1:2], ALU.max), reads=["ml_msc"], writes=["ml_m"])
            self.emit_out_tile(t, osb, [("ml_osb", 0), ("ml_osb", 1)], 7)
            if t % 2 == 1:
                bcast_units(mst[:, 0:1], -1.0, None)
                P.op("dve", lambda e: e.tensor_tensor(Cout[:], Cg[:], esl[:].unsqueeze(2).to_broadcast([128, 4, 65]), ALU.mult),
                     reads=["ml_C", "ml_esl"], writes=["ml_Cout"])
                for z in range(2):
                    seg = (t - 1) // 2 if z == 0 else (NT - 1 - t) // 2
                    P.dma("sp", lambda e, z=z, seg=seg: e.dma_start(
                        out=self.out_mC[seg, l, z].rearrange("(hp hq) d e -> (hq d) hp e", hq=2), in_=Cout[:, 2 * z:2 * z + 2, 0:64]), reads=["ml_Cout"])
                    P.dma("sp", lambda e, z=z, seg=seg: e.dma_start(
                        out=self.out_mn[seg, l, z].rearrange("(hp hq) (d o) -> (hq d) hp o", hq=2, o=1), in_=Cout[:, 2 * z:2 * z + 2, 64:65],
                        allow_slow_non_contiguous=True), reads=["ml_Cout"])
                    P.dma("sp", lambda e, z=z, seg=seg: e.dma_start(
                        out=self.out_mm[seg, l, z].rearrange("(h o) -> h o", o=1), in_=mst[4 * z:4 * z + 4, :], allow_slow_non_contiguous=True), reads=["ml_m"])
                P.op("dve", lambda e: e.tensor_scalar(Cg[:], Cg[:], self.keep[:, 0:1], None, ALU.mult), reads=["ml_C", "keep"], writes=["ml_C"])
                P.op("dve", lambda e: e.tensor_scalar(mst[:], mst[:], self.keep[0:8, 0:1], None, ALU.mult), reads=["ml_m", "keep"], writes=["ml_m"])
                P.op("act", lambda e: e.activation(Cb[:], Cg[:], AF.Copy), reads=["ml_C"], writes=["ml_Cb"])
        if self.debug.get("yacc") == "mlstm":
            P.dma("sp", lambda e: e.dma_start(out=self.dbg_yacc, in_=self.yacc[:]), reads=[("yacc", i) for i in range(NT)])
        self.release(m1)
        if self.debug.get("post", True):
            self.post_simple(l, "ml", wb, wkeys, 784, AF.Sigmoid, self.ml_norm, True, 0, first)
        self.release(m0)

    def neumann_inverse(self, pfx, Pm, Qm, Rm, banks, keys=None):
        P, ps = self.P, self.psum
        bP, bQ, bR = banks
        kP, kQ, kR = keys if keys is not None else (pfx + "P", pfx + "Q", pfx + "R")
        ident64 = self.ident[0:64, 0:64]
        for u in range(8):
            P.op("pe", lambda e, u=u: e.transpose(ps[0:64, bQ, 64 * u:64 * u + 64], Pm[:, u, :], ident64), reads=[kP, "ident"], writes=["ps%d" % bQ])
        P.op("act", lambda e: e.activation(Qm[:], ps[0:64, bQ, :].rearrange("p (u l) -> p u l", l=64), AF.Copy), writes=["ps%d" % bQ, kQ])
        P.op("dve", lambda e: e.tensor_tensor(Rm[:], Pm[:], ident64.unsqueeze(1).to_broadcast([64, 8, 64]), ALU.add), reads=[kP, "ident"], writes=[kR])
        for lvl in range(5):
            last = lvl == 4
            if not last:
                for u in range(8):
                    P.op("pe", lambda e, u=u: e.matmul(ps[0:64, bP, 64 * u:64 * u + 64], Qm[:, u, :], Pm[:, u, :], start=True, stop=True),
                         reads=[kP, kQ], writes=["ps%d" % bP])
            for u in range(8):
                P.op("pe", lambda e, u=u: e.matmul(ps[0:64, bQ, 64 * u:64 * u + 64], Pm[:, u, :], Qm[:, u, :], start=True, stop=True),
                     reads=[kP, kQ], writes=["ps%d" % bQ])
            if not last:
                P.op("dve", lambda e: e.tensor_copy(Pm[:], ps[0:64, bP, :].rearrange("p (u l) -> p u l", l=64)), writes=["ps%d" % bP, kP])
            P.op("act", lambda e: e.activation(Qm[:], ps[0:64, bQ, :].rearrange("p (u l) -> p u l", l=64), AF.Copy), writes=["ps%d" % bQ, kQ])
            for u in range(8):
                P.op("pe", lambda e, u=u: e.matmul(ps[0:64, bR, 64 * u:64 * u + 64], Qm[:, u, :], Rm[:, u, :], start=True, stop=True),
                     reads=[kQ, kR], writes=["ps%d" % bR])
            P.op("dve", lambda e: e.tensor_tensor(Rm[:], ps[0:64, bR, :].rearrange("p (u l) -> p u l", l=64), Rm[:], ALU.add),
                 reads=[kR], writes=["ps%d" % bR, kR])

    def mixer_delta(self, l, first):
        P, ps = self.P, self.psum
        m0 = self.mark()
        wb, wkeys = self.load_w("wB", self.wB[l], 1040)
        m1 = self.mark()
        incl = self.cst("INCL")
        strict = self.cst("STRICT")
        ones64 = self.consts[0:64, CO["ONES"][0]:CO["ONES"][0] + 64]
        ones128 = self.consts[0:64, CO["ONES"][0]:CO["ONES"][0] + 128]
        blk = self.consts[:, CO["BLK"][0]:CO["BLK"][0] + 128]
        ident64 = self.ident[0:64, 0:64]
        cw = self.sb("dl_cw", [128, 6, 5], F32)
        P.dma("sp", lambda e: e.dma_start(out=cw[:], in_=self.dl_conv[l]), writes=["dl_cw"])
        Au = self.sb("dl_A", [128, 8], F32)
        dtb = self.sb("dl_dtb", [128, 8], F32)
        P.dma("sp", lambda e: e.dma_start(out=Au[:], in_=self.dl_alog[l:l + 1, :].partition_broadcast(128)), writes=["dl_A"])
        P.dma("sp", lambda e: e.dma_start(out=dtb[:], in_=self.dl_dtb[l:l + 1, :].partition_broadcast(128)), writes=["dl_dtb"])
        P.op("act", lambda e: e.activation(Au[:], Au[:], AF.Exp), reads=["dl_A"], writes=["dl_A"])
        S = self.sb("dl_S", [128, 4, 64], F32)
        Sb = self.sb("dl_Sb", [128, 4, 64], BF16)
        Sout = self.sb("dl_Sout", [128, 4, 64], F32)
        tmpS = self.sb("dl_tmpS", [128, 4, 64], F32)
        P.dma("sp", lambda e: e.dma_start(out=S[:], in_=self.init_delta[l].rearrange("z (hp hq) d e -> (hq d) (z hp) e", hq=2)), writes=["dl_S"])
        P.op("act", lambda e: e.activation(Sb[:], S[:], AF.Copy), reads=["dl_S"], writes=["dl_Sb"])
        pre = self.sb("dl_pre", [128, 2, 132], F32)
        acc = self.sb("dl_acc", [128, 2, 128], F32)
        sq = self.sb("dl_sq", [128, 2, 128], F32)
        rn = self.sb("dl_rn", [128, 2, 128], F32)
        qTp = self.sb("dl_qTp", [128, 2, 2, 2, 128], BF16)
        kTp = self.sb("dl_kTp", [128, 2, 2, 2, 128], BF16)
        P.op("dve", lambda e: e.memset(qTp[:], 0.0), writes=["dl_qTp"])
        P.op("dve", lambda e: e.memset(kTp[:], 0.0), writes=["dl_kTp"])
        kT = self.sb("dl_kT", [128, 2, 2, 128], BF16)
        vT = self.sb("dl_vT", [128, 2, 2, 128], BF16)
        gT = self.sb("dl_gT", [8, 2, 128], F32)
        gtm = self.sb("dl_gtm", [64, 4, 8], F32)
        beta2 = self.sb("dl_beta", [64, 2, 8], F32)
        nbeta2 = self.sb("dl_nbeta", [64, 2, 8], F32)
        ng2 = self.sb("dl_ng", [64, 2, 8], F32)
        ngc2 = self.sb("dl_ngc", [64, 2, 8], F32)
        eg2 = self.sb("dl_eg", [64, 2, 8], F32)
        eglt2 = self.sb("dl_eglt", [64, 2, 8], F32)
        egls2 = self.sb("dl_egls", [128, 2, 4], F32)
        dgm = self.sb("dl_dgm", [64, 8, 64], F32)
        dT = self.sb("dl_dT", [64, 8, 64], F32)
        dTs = self.sb("dl_dTs", [64, 8, 64], F32)
        Pm = self.sb("dl_P", [64, 8, 64], F32)
        Qm = self.sb("dl_Q", [64, 8, 64], F32)
        Rm = self.sb("dl_R", [64, 8, 64], F32)
        qkd = self.sb("dl_qkd", [64, 8, 64], BF16)
        ktm = self.sb("dl_ktm", [64, 8, 64], BF16)
        vtm = self.sb("dl_vtm", [64, 8, 64], F32)
        kd = self.sb("dl_kd", [64, 8, 64], BF16)
        rr = self.sb("dl_r", [64, 8, 64], F32)
        vnew = self.sb("dl_vnew", [64, 8, 64], BF16)
        vnf = self.sb("dl_vnf", [64, 8, 64], F32)
        t1 = self.sb("dl_t1", [64, 8, 64], F32)
        osb = self.sb("dl_osb", [64, 2, 2, 256], F32)
        for t in range(NT):
            tiles = (t, NT - 1 - t)
            edge = slice(0, 2) if t % 2 == 0 else slice(130, 132)
            for j in range(6):
                bank = j % 2
                self.proj_fm(wb, wkeys, 128 * j, 128, tiles, bank, halo=2)
                self.evac_fm(lambda z: pre[:, z, :], bank, 128, "dl_pre", W=132, eng="act")
                P.op("dve", lambda e: e.tensor_scalar(pre[:, :, edge], pre[:, :, edge], self.keep[:, 0:1], None, ALU.mult), reads=["dl_pre", "keep"], writes=["dl_pre"])
                for z in range(2):
                    eng = "dve"
                    for k in range(5):
                        wk_ = cw[:, j, k:k + 1] if z == 0 else cw[:, j, 4 - k:5 - k]
                        if k == 0:
                            P.op(eng, lambda e, z=z, wk_=wk_: e.tensor_scalar(acc[:, z, :], pre[:, z, 0:128], wk_, None, ALU.mult),
                                 reads=["dl_pre", "dl_cw"], writes=[("dl_acc", z)])
                        else:
                            P.op(eng, lambda e, z=z, k=k, wk_=wk_: e.scalar_tensor_tensor(acc[:, z, :], pre[:, z, k:k + 128], wk_, acc[:, z, :], ALU.mult, ALU.add),
                                 reads=["dl_pre", "dl_cw", ("dl_acc", z)], writes=[("dl_acc", z)])
                akeys = [("dl_acc", 0), ("dl_acc", 1)]
                self.act_sigmoid(sq[:], acc[:], akeys, ["dl_sq"])
                if j >= 4:
                    P.op("dve", lambda e, j=j: e.tensor_tensor(vT[:, j - 4, :, :], acc[:], sq[:], ALU.mult), reads=akeys + ["dl_sq"], writes=["dl_vT"])
                    continue
                P.op("dve", lambda e: e.tensor_tensor(acc[:], acc[:], sq[:], ALU.mult), reads=akeys + ["dl_sq"], writes=akeys)
                P.op("act", lambda e: e.activation(sq[:], acc[:], AF.Square), reads=akeys, writes=["dl_sq"])
                P.op("pe", lambda e: e.matmul(ps[:, 2, 0:256], blk, sq[:].rearrange("p z n -> p (z n)"), start=True, stop=True), reads=["consts", "dl_sq"], writes=["ps2"])
                P.op("act", lambda e: e.activation(rn[:], ps[:, 2, 0:256].rearrange("p (z n) -> p z n", z=2), AF.Ln, bias=1e-6, scale=1.0), writes=["ps2", "dl_rn"])
                P.op("act", lambda e: e.activation(rn[:], rn[:], AF.Exp, scale=-0.5), reads=["dl_rn"], writes=["dl_rn"])
                hp = j % 2
                if j < 2:
                    for hq in range(2):
                        rows = slice(64 * hq, 64 * hq + 64)
                        P.op("dve", lambda e, rows=rows, hp=hp, hq=hq: e.scalar_tensor_tensor(qTp[rows, hp, hq, :, :], acc[rows, :, :], 0.125, rn[rows, :, :], ALU.mult, ALU.mult),
                             reads=akeys + ["dl_rn"], writes=["dl_qTp"])
                else:
                    P.op("dve", lambda e, hp=hp: e.tensor_tensor(kT[:, hp, :, :], acc[:], rn[:], ALU.mult), reads=akeys + ["dl_rn"], writes=["dl_kT"])
                    for hq in range(2):
                        rows = slice(64 * hq, 64 * hq + 64)
                        P.op("pool", lambda e, rows=rows, hp=hp, hq=hq: e.tensor_copy(kTp[rows, hp, hq, :, :], kT[rows, hp, :, :]), reads=["dl_kT"], writes=["dl_kTp"])
            for z in range(2):
                tok0 = 2 + 128 * tiles[z]
                for k in range(8):
                    P.op("pe", lambda e, k=k, z=z, tok0=tok0: e.matmul(ps[0:8, 1, 128 * z:128 * z + 128], wb[:, k, 768 + 8 * z:768 + 8 * z + 8],
                                                                    self.uT[:, k, tok0:tok0 + 128], start=(k == 0), stop=(k == 7)),
                         reads=wkeys + [("uT", tiles[z])], writes=["ps1"])
            self.evac_fm(lambda z: gT[:, z, :], 1, 8, "dl_gT", eng="dve")
            for c in range(2):
                for z in range(2):
                    q_ = 2 * c + z
                    P.op("pe", lambda e, z=z, c=c, q_=q_: e.transpose(ps[0:64, 7, 8 * q_:8 * q_ + 8], gT[:, z, 64 * c:64 * c + 64], self.ident[0:8, 0:8]),
                         reads=["dl_gT", "ident"], writes=["ps7"])
            P.op("dve", lambda e: e.tensor_copy(gtm[:], ps[0:64, 7, 0:32].rearrange("p (q g) -> p q g", g=8)), writes=["ps7", "dl_gtm"])
            b4 = beta2[:].rearrange("p c (z h) -> p (c z) h", h=4)
            P.op("act", lambda e: e.activation(b4, gtm[:, :, 0:4], AF.Exp, scale=-1.0), reads=["dl_gtm"], writes=["dl_beta"])
            P.op("act", lambda e: e.activation(beta2[:], beta2[:], AF.Ln, bias=1.0, scale=1.0), reads=["dl_beta"], writes=["dl_beta"])
            P.op("act", lambda e: e.activation(beta2[:], beta2[:], AF.Exp, scale=-1.0), reads=["dl_beta"], writes=["dl_beta"])
            P.op("dve", lambda e: e.tensor_scalar(nbeta2[:], beta2[:], -1.0, None, ALU.mult), reads=["dl_beta"], writes=["dl_nbeta"])
            for c in range(2):
                P.op("dve", lambda e, c=c: e.tensor_tensor(ng2[:, c, :].rearrange("p (z h) -> p z h", h=4), gtm[:, 2 * c:2 * c + 2, 4:8],
                                                           dtb[0:64, :].rearrange("p (z h) -> p z h", h=4), ALU.add),
                     reads=["dl_gtm", "dl_dtb"], writes=["dl_ng"])
            P.op("act", lambda e: e.activation(ng2[:], ng2[:], AF.Exp), reads=["dl_ng"], writes=["dl_ng"])
            P.op("act", lambda e: e.activation(ng2[:], ng2[:], AF.Ln, bias=1.0, scale=1.0), reads=["dl_ng"], writes=["dl_ng"])
            P.op("dve", lambda e: e.tensor_tensor(ng2[:], ng2[:], Au[0:64, :].unsqueeze(1).to_broadcast([64, 2, 8]), ALU.mult), reads=["dl_ng", "dl_A"], writes=["dl_ng"])
            for c in range(2):
                P.op("pe", lambda e, c=c: e.matmul(ps[0:64, 7, 32 + 8 * c:40 + 8 * c], incl, ng2[:, c, :], start=True, stop=True), reads=["consts", "dl_ng"], writes=["ps7"])
                P.op("pe", lambda e, c=c: e.matmul(ps[:, 7, 48 + 8 * c:56 + 8 * c], ones128, ng2[:, c, :], start=True, stop=True), reads=["consts", "dl_ng"], writes=["ps7"])
            ngcp = ps[0:64, 7, 32:48].rearrange("p (c u) -> p c u", u=8)
            nglp = ps[0:64, 7, 48:64].rearrange("p (c u) -> p c u", u=8)
            P.op("dve", lambda e: e.tensor_copy(ngc2[:], ngcp), writes=["ps7", "dl_ngc"])
            P.op("act", lambda e: e.activation(eg2[:], ngcp, AF.Exp, scale=-1.0), writes=["ps7", "dl_eg"])
            P.op("dve", lambda e: e.tensor_tensor(eglt2[:], ngc2[:], nglp, ALU.subtract), reads=["dl_ngc"], writes=["ps7", "dl_eglt"])
            P.op("act", lambda e: e.activation(eglt2[:], eglt2[:], AF.Exp), reads=["dl_eglt"], writes=["dl_eglt"])
            for hq in range(2):
                rows = slice(64 * hq, 64 * hq + 64)
                P.op("act", lambda e, rows=rows, hq=hq: e.activation(egls2[rows, :, :], ps[rows, 7, 48:64].rearrange("p (c u) -> p c u", u=8)[:, :, hq:8:2], AF.Exp, scale=-1.0),
                     writes=["ps7", "dl_egls"])
            for c in range(2):
                cs = slice(64 * c, 64 * c + 64)
                beta, nbeta, ngc, eg, eglt, egls = beta2[:, c, :], nbeta2[:, c, :], ngc2[:, c, :], eg2[:, c, :], eglt2[:, c, :], egls2[:, c, :]
                P.op("dve", lambda e: e.tensor_tensor(dgm[:], ident64.unsqueeze(1).to_broadcast([64, 8, 64]), ngc[:].unsqueeze(2).to_broadcast([64, 8, 64]), ALU.mult),
                     reads=["ident", "dl_ngc"], writes=["dl_dgm"])
                P.op("pe", lambda e: e.matmul(ps[0:64, 3, :], ones64, dgm[:].rearrange("p u l -> p (u l)"), start=True, stop=True), reads=["consts", "dl_dgm"], writes=["ps3"])
                P.op("dve", lambda e: e.tensor_tensor(dT[:], ngc[:].unsqueeze(2).to_broadcast([64, 8, 64]), ps[0:64, 3, :].rearrange("p (u l) -> p u l", l=64), ALU.subtract),
                     reads=["dl_ngc"], writes=["ps3", "dl_dT"])
                P.op("dve", lambda e: e.tensor_scalar(dT[:], dT[:], 0.0, None, ALU.min), reads=["dl_dT"], writes=["dl_dT"])
                P.op("act", lambda e: e.activation(dT[:], dT[:], AF.Exp), reads=["dl_dT"], writes=["dl_dT"])
                P.op("pool", lambda e: e.tensor_tensor(dTs[:], dT[:], strict.unsqueeze(1).to_broadcast([64, 8, 64]), ALU.mult), reads=["dl_dT", "consts"], writes=["dl_dTs"])
                P.op("pool", lambda e: e.tensor_tensor(dT[:], dT[:], incl.unsqueeze(1).to_broadcast([64, 8, 64]), ALU.mult), reads=["dl_dT", "consts"], writes=["dl_dT"])
                ktv = self.tm_transposes(kT, "dl_kT", cs, 2, 0)
                vtv = self.tm_transposes(vT, "dl_vT", cs, 2, 512)
                P.op("act", lambda e: e.activation(ktm[:], ktv, AF.Copy), writes=["ps2", "dl_ktm"])
                P.op("dve", lambda e: e.tensor_copy(vtm[:], vtv), writes=["ps2", "dl_vtm"])
                P.op("dve", lambda e: e.tensor_tensor(kd[:], ktm[:], eglt[:].unsqueeze(2).to_broadcast([64, 8, 64]), ALU.mult), reads=["dl_ktm", "dl_eglt"], writes=["dl_kd"])
                for z in range(2):
                    for h in range(4):
                        hp, hq = h // 2, h % 2
                        u = 4 * z + h
                        P.op("pe", lambda e, z=z, hp=hp, hq=hq, u=u: e.matmul(ps[0:64, 4, 64 * u:64 * u + 64], kT[:, hp, z, cs], kTp[:, hp, hq, z, cs], start=True, stop=True),
                             reads=["dl_kT", "dl_kTp"], writes=["ps4"])
                        P.op("pe", lambda e, z=z, hp=hp, hq=hq, u=u: e.matmul(ps[0:64, 5, 64 * u:64 * u + 64], kT[:, hp, z, cs], qTp[:, hp, hq, z, cs], start=True, stop=True),
                             reads=["dl_kT", "dl_qTp"], writes=["ps5"])
                P.op("dve", lambda e: e.tensor_tensor(Pm[:], ps[0:64, 4, :].rearrange("p (u l) -> p u l", l=64), dTs[:], ALU.mult), reads=["dl_dTs"], writes=["ps4", "dl_P"])
                P.op("dve", lambda e: e.tensor_tensor(Pm[:], Pm[:], nbeta[:].unsqueeze(2).to_broadcast([64, 8, 64]), ALU.mult), reads=["dl_P", "dl_nbeta"], writes=["dl_P"])
                P.op("dve", lambda e: e.tensor_tensor(qkd[:], ps[0:64, 5, :].rearrange("p (u l) -> p u l", l=64), dT[:], ALU.mult), reads=["dl_dT"], writes=["ps5", "dl_qkd"])
                self.neumann_inverse("dl_", Pm, Qm, Rm, (4, 5, 6))
                for z in range(2):
                    bank = 3 + z
                    for h in range(4):
                        hp, hq = h // 2, h % 2
                        P.op("pe", lambda e, z=z, hp=hp, hq=hq, h=h, bank=bank: e.matmul(ps[0:64, bank, 128 * h:128 * h + 64], kTp[:, hp, hq, z, cs], Sb[:, 2 * z + hp, :], start=True, stop=True),
                             reads=["dl_kTp", "dl_Sb"], writes=["ps%d" % bank])
                        P.op("pe", lambda e, z=z, hp=hp, hq=hq, h=h, bank=bank: e.matmul(ps[0:64, bank, 128 * h + 64:128 * h + 128], qTp[:, hp, hq, z, cs], Sb[:, 2 * z + hp, :], start=True, stop=True),
                             reads=["dl_qTp", "dl_Sb"], writes=["ps%d" % bank])
                for z in range(2):
                    bank = 3 + z
                    ks4 = ps[0:64, bank, :].rearrange("p (h two e) -> p h two e", two=2, e=64)
                    us = slice(4 * z, 4 * z + 4)
                    P.op("dve", lambda e, ks4=ks4, us=us: e.tensor_tensor(t1[:, us, :], ks4[:, :, 0, :], eg[:, us].unsqueeze(2).to_broadcast([64, 4, 64]), ALU.mult),
                         reads=["dl_eg"], writes=["ps%d" % bank, ("dl_t1", z)])
                    P.op("dve", lambda e, us=us, z=z: e.tensor_tensor(rr[:, us, :], vtm[:, us, :], t1[:, us, :], ALU.subtract), reads=["dl_vtm", ("dl_t1", z)], writes=[("dl_r", z)])
                    P.op("dve", lambda e, ks4=ks4, us=us: e.tensor_tensor(t1[:, us, :], ks4[:, :, 1, :], eg[:, us].unsqueeze(2).to_broadcast([64, 4, 64]), ALU.mult),
                         reads=["dl_eg", ("dl_r", z)], writes=["ps%d" % bank, ("dl_t1", z)])
                rkeys = [("dl_r", 0), ("dl_r", 1)]
                for u in range(8):
                    P.op("pe", lambda e, u=u: e.matmul(ps[0:64, 5, 64 * u:64 * u + 64], Rm[:, u, :], rr[:, u, :], start=True, stop=True), reads=["dl_R"] + rkeys, writes=["ps5"])
                P.op("dve", lambda e: e.tensor_tensor(vnew[:], ps[0:64, 5, :].rearrange("p (u l) -> p u l", l=64), beta[:].unsqueeze(2).to_broadcast([64, 8, 64]), ALU.mult),
                     reads=["dl_beta"], writes=["ps5", "dl_vnew"])
                for u in range(8):
                    P.op("pe", lambda e, u=u: e.matmul(ps[0:64, 6, 64 * u:64 * u + 64], qkd[:, u, :], vnew[:, u, :], start=True, stop=True), reads=["dl_qkd", "dl_vnew"], writes=["ps6"])
                P.op("dve", lambda e, c=c: e.tensor_tensor(osb[:, c, :, :].rearrange("p z (h e) -> p (z h) e", e=64), ps[0:64, 6, :].rearrange("p (u e) -> p u e", e=64), t1[:], ALU.add),
                     reads=[("dl_t1", 0), ("dl_t1", 1)], writes=["ps6", ("dl_osb", c)])
                for z in range(2):
                    for h in range(4):
                        hp, hq = h // 2, h % 2
                        rows = slice(64 * hq, 64 * hq + 64)
                        u = 4 * z + h
                        col = (2 * z + hp) * 64
                        P.op("pe", lambda e, rows=rows, u=u, col=col: e.matmul(ps[rows, 3, col:col + 64], kd[:, u, :], vnew[:, u, :], start=True, stop=True),
                             reads=["dl_kd", "dl_vnew"], writes=["ps3"])
                P.op("dve", lambda e: e.tensor_tensor(tmpS[:], S[:], egls[:].unsqueeze(2).to_broadcast([128, 4, 64]), ALU.mult), reads=["dl_S", "dl_egls"], writes=["dl_tmpS"])
                P.op("dve", lambda e: e.tensor_tensor(S[:], ps[:, 3, 0:256].rearrange("p (a e) -> p a e", e=64), tmpS[:], ALU.add), reads=["dl_tmpS"], writes=["ps3", "dl_S"])
                if not (t % 2 == 1 and c == 1):
                    P.op("act", lambda e: e.activation(Sb[:], S[:], AF.Copy), reads=["dl_S"], writes=["dl_Sb"])
            self.emit_out_tile(t, osb, [("dl_osb", 0), ("dl_osb", 1)], 7)
            if t % 2 == 1:
                P.op("dve", lambda e: e.tensor_copy(Sout[:], S[:]), reads=["dl_S"], writes=["dl_Sout"])
                for z in range(2):
                    seg = (t - 1) // 2 if z == 0 else (NT - 1 - t) // 2
                    P.dma("sp", lambda e, z=z, seg=seg: e.dma_start(
                        out=self.out_delta[seg, l, z].rearrange("(hp hq) d e -> (hq d) hp e", hq=2), in_=Sout[:, 2 * z:2 * z + 2, :]), reads=["dl_Sout"])
                P.op("dve", lambda e: e.tensor_scalar(S[:], S[:], self.keep[:, 0:1], None, ALU.mult), reads=["dl_S", "keep"], writes=["dl_S"])
                P.op("act", lambda e: e.activation(Sb[:], S[:], AF.Copy), reads=["dl_S"], writes=["dl_Sb"])
        if self.debug.get("yacc") == "delta":
            P.dma("sp", lambda e: e.dma_start(out=self.dbg_yacc, in_=self.yacc[:]), reads=[("yacc", i) for i in range(NT)])
        self.release(m1)
        if self.debug.get("post", True):
            self.post_simple(l, "dl", wb, wkeys, 784, AF.Silu, self.dl_norm, False, 256, first)
        self.release(m0)

    def rwkv_mix_fm(self, pre, mu_col, dst, keys_r, keys_w, eng="dve"):
        P = self.P
        n = dst.shape[-1]
        P.op("dve", lambda e: e.tensor_tensor(dst, pre[:, :, 0:n], pre[:, :, 2:n + 2], ALU.add), reads=keys_r, writes=keys_w)
        P.op("dve", lambda e: e.scalar_tensor_tensor(dst, dst, 0.5, pre[:, :, 1:n + 1], ALU.mult, ALU.subtract), reads=keys_r + keys_w, writes=keys_w)
        P.op("dve", lambda e: e.scalar_tensor_tensor(dst, dst, mu_col, pre[:, :, 1:n + 1], ALU.mult, ALU.add), reads=keys_r + keys_w + ["rw_mu"], writes=keys_w)

    def mixer_rwkv(self, l, first):
        P, ps = self.P, self.psum
        m0 = self.mark()
        wb, wkeys = self.load_w("wD", self.wD[l], 1152)
        incl = self.cst("INCL")
        strict = self.cst("STRICT")
        ones64 = self.consts[0:64, CO["ONES"][0]:CO["ONES"][0] + 64]
        ident64 = self.ident[0:64, 0:64]
        bacc = self.sb("rw_bacc", [128, NT, 256], BF16)
        mu = self.sb("rw_mu", [128, 9], F32)
        P.dma("sp", lambda e: e.dma_start(out=mu[:], in_=self.rw_mu[l]), writes=["rw_mu"])
        m1 = self.mark()
        w2p = self.sb("rw_w2p", [128, 2, 256], F32)
        a2p = self.sb("rw_a2p", [128, 2, 256], F32)
        P.dma("sp", lambda e: e.dma_start(out=w2p[:], in_=self.rw_w2p[l]), writes=["rw_w2p"])
        P.dma("sp", lambda e: e.dma_start(out=a2p[:], in_=self.rw_a2p[l]), writes=["rw_a2p"])
        bcs = {}
        for nm, src, width in (("w0", self.rw_w0, 512), ("a0", self.rw_a0, 512), ("kk", self.rw_kk, 256), ("ka", self.rw_ka, 256), ("rk", self.rw_rk, 256)):
            bcs[nm] = self.sb("rw_bc_" + nm, [64, width], F32)
            P.dma("sp", lambda e, nm=nm, src=src: e.dma_start(out=bcs[nm][:], in_=src[l:l + 1, :].partition_broadcast(64)), writes=["rw_bc"])
        omka = self.sb("rw_omka", [64, 256], F32)
        P.op("dve", lambda e: e.tensor_scalar(omka[:], bcs["ka"][:], -1.0, 1.0, ALU.mult, ALU.add), reads=["rw_bc"], writes=["rw_omka"])
        M = self.sb("rw_M", [64, 8, 64], F32)
        pre = self.sb("rw_pre", [128, 2, 130], F32)
        rT = self.sb("rw_rT", [128, 2, 2, 128], BF16)
        kT = self.sb("rw_kT", [128, 2, 2, 128], BF16)
        vT = self.sb("rw_vT", [128, 2, 2, 128], BF16)
        twT = self.sb("rw_twT", [128, 2, 128], F32)
        daT = self.sb("rw_daT", [128, 2, 128], F32)
        rtm = self.sb("rw_rtm", [64, 2, 256], F32)
        ktm = self.sb("rw_ktm", [64, 2, 256], F32)
        vtm = self.sb("rw_vtm", [64, 2, 256], F32)
        lw = self.sb("rw_lw", [64, 2, 256], F32)
        av = self.sb("rw_a", [64, 2, 256], F32)
        ecl = self.sb("rw_ecl", [64, 2, 256], F32)
        encl = self.sb("rw_encl", [64, 2, 256], F32)
        ecw = self.sb("rw_ecw", [64, 2, 256], F32)
        kh = self.sb("rw_kh", [64, 2, 256], F32)
        kx = self.sb("rw_kx", [64, 2, 256], F32)
        ss8 = self.sb("rw_ss8", [64, 8], F32)
        bs8 = self.sb("rw_bs8", [64, 8], F32)
        ktz = self.sb("rw_ktz", [64, 2, 256], F32)
        al = self.sb("rw_al", [64, 2, 256], F32)
        be = self.sb("rw_be", [64, 2, 256], F32)
        kti = self.sb("rw_kti", [64, 2, 256], F32)
        rti = self.sb("rw_rti", [64, 2, 256], F32)
        beT = self.sb("rw_beT", [64, 8, 64], F32)
        Pm = lw[:].rearrange("p z (h j) -> p (z h) j", j=64)
        Qm = av[:].rearrange("p z (h j) -> p (z h) j", j=64)
        Aak = ecl[:].rearrange("p z (h j) -> p (z h) j", j=64)
        Ara = encl[:].rearrange("p z (h j) -> p (z h) j", j=64)
        Ark = ecw[:].rearrange("p z (h j) -> p (z h) j", j=64)
        X1 = kh[:].rearrange("p z (h j) -> p (z h) j", j=64)
        Uu = kx[:].rearrange("p z (h j) -> p (z h) j", j=64)
        tmpM = ktz[:].rearrange("p z (h j) -> p (z h) j", j=64)
        alT = rtm[:].rearrange("p z (h j) -> p (z h) j", j=64)
        ktT = ktm[:].rearrange("p z (h j) -> p (z h) j", j=64)
        rtT = be[:].rearrange("p z (h j) -> p (z h) j", j=64)
        Mt = kh[:].rearrange("p z (h j) -> p (z h) j", j=64)
        Rm = rti[:].rearrange("p z (h j) -> p (z h) j", j=64)
        WL = self.sb("rw_WL", [64, 8], F32)
        P.dma("sp", lambda e: e.dma_start(out=Mt[:], in_=self.init_rwkv[l].rearrange("z h i j -> i (z h) j")), writes=["rw_kh"])
        for u in range(8):
            P.op("pe", lambda e, u=u: e.transpose(ps[0:64, 2, 64 * u:64 * u + 64], Mt[:, u, :], ident64), reads=["rw_kh", "ident"], writes=["ps2"])
        P.op("dve", lambda e: e.tensor_copy(M[:], ps[0:64, 2, :].rearrange("p (u e) -> p u e", e=64)), writes=["ps2", "rw_M"])
        osb = self.sb("rw_osb", [64, 2, 512], F32)
        v3 = lambda x_: x_[:].rearrange("p z (h j) -> p (z h) j", j=64)
        for t in range(NT):
            tiles = (t, NT - 1 - t)
            edge = slice(0, 1) if t % 2 == 0 else slice(129, 130)
            for j in range(9):
                bank = j % 2
                self.proj_fm(wb, wkeys, 128 * j, 128, tiles, bank, halo=1)
                self.evac_fm(lambda z: pre[:, z, :], bank, 128, "rw_pre", W=130, eng="act")
                P.op("dve", lambda e: e.tensor_scalar(pre[:, :, edge], pre[:, :, edge], self.keep[:, 0:1], None, ALU.mult), reads=["rw_pre", "keep"], writes=["rw_pre"])
                if j < 6:
                    dst, dk = ((rT, "rw_rT"), (kT, "rw_kT"), (vT, "rw_vT"))[j // 2]
                    dst = dst[:, j % 2, :, :]
                elif j == 6:
                    dst, dk = twT[:], "rw_twT"
                elif j == 7:
                    dst, dk = daT[:], "rw_daT"
                else:
                    continue
                self.rwkv_mix_fm(pre, mu[:, j:j + 1], dst, ["rw_pre"], [dk])
                if j == 6:
                    P.op("act", lambda e: e.activation(twT[:], twT[:], AF.Tanh), reads=["rw_twT"], writes=["rw_twT"])
            for c in range(2):
                cs = slice(64 * c, 64 * c + 64)
                for z in range(2):
                    P.op("pe", lambda e, z=z: e.matmul(ps[0:64, 2, 256 * z:256 * z + 256], twT[:, z, cs], w2p[:, z, :], start=True, stop=True), reads=["rw_twT", "rw_w2p"], writes=["ps2"])
                    P.op("pe", lambda e, z=z: e.matmul(ps[0:64, 3, 256 * z:256 * z + 256], daT[:, z, cs], a2p[:, z, :], start=True, stop=True), reads=["rw_daT", "rw_a2p"], writes=["ps3"])
                lwf = lw[:].rearrange("p z n -> p (z n)")
                avf = av[:].rearrange("p z n -> p (z n)")
                P.op("dve", lambda e: e.tensor_tensor(lwf, ps[0:64, 2, :], bcs["w0"][:], ALU.add), reads=["rw_bc"], writes=["ps2", "rw_lw"])
                self.act_sigmoid(lwf, lwf, ["rw_lw"], ["rw_lw"])
                P.op("dve", lambda e: e.tensor_scalar(lwf, lwf, -float(np.exp(-0.5)), None, ALU.mult), reads=["rw_lw"], writes=["rw_lw"])
                P.op("dve", lambda e: e.tensor_tensor(avf, ps[0:64, 3, :], bcs["a0"][:], ALU.add), reads=["rw_bc"], writes=["ps3", "rw_a"])
                self.act_sigmoid(avf, avf, ["rw_a"], ["rw_a"])
                P.op("pe", lambda e: e.matmul(ps[0:64, 4, :], incl, lwf, start=True, stop=True), reads=["consts", "rw_lw"], writes=["ps4"])
                for u in range(8):
                    z, h = u // 4, u % 4
                    P.op("pe", lambda e, u=u, z=z, h=h: e.matmul(ps[0:64, 7, 64 + u:65 + u], lw[:, z, 64 * h:64 * h + 64], ones64[:, 0:1], start=True, stop=True),
                         reads=["rw_lw", "consts"], writes=["ps7"])
                P.op("act", lambda e: e.activation(WL[:], ps[0:64, 7, 64:72], AF.Exp), writes=["ps7", "rw_WL"])
                eclf = ecl[:].rearrange("p z n -> p (z n)")
                P.op("act", lambda e: e.activation(eclf, ps[0:64, 4, :], AF.Exp), writes=["ps4", "rw_ecl"])
                P.op("act", lambda e: e.activation(encl[:].rearrange("p z n -> p (z n)"), ps[0:64, 4, :], AF.Exp, scale=-1.0), writes=["ps4", "rw_encl"])
                P.op("dve", lambda e: e.tensor_tensor(ecw[:].rearrange("p z n -> p (z n)"), ps[0:64, 4, :], lwf, ALU.subtract), reads=["rw_lw"], writes=["ps4", "rw_ecw"])
                P.op("act", lambda e: e.activation(ecw[:], ecw[:], AF.Exp), reads=["rw_ecw"], writes=["rw_ecw"])
                for (src, skey, dstt, dkey, bank) in ((rT, "rw_rT", rtm, "rw_rtm", 5), (kT, "rw_kT", ktm, "rw_ktm", 6), (vT, "rw_vT", vtm, "rw_vtm", 5)):
                    bv = ps[:, bank, :].bitcast(BF16)
                    for z in range(2):
                        for hp in range(2):
                            col = (4 * z + 2 * hp) * 64
                            P.op("pe", lambda e, src=src, z=z, hp=hp, col=col, bv=bv: e.transpose(bv[0:64, col:col + 128], src[:, hp, z, cs], self.ident_bf[:]),
                                 reads=[skey, "ident_bf"], writes=["ps%d" % bank])
                    P.op("act", lambda e, dstt=dstt, bv=bv: e.activation(dstt[:].rearrange("p z n -> p (z n)"), bv[0:64, 0:512], AF.Copy), writes=["ps%d" % bank, dkey])
                kk2 = bcs["kk"][:].unsqueeze(1).to_broadcast([64, 2, 256])
                ka2 = bcs["ka"][:].unsqueeze(1).to_broadcast([64, 2, 256])
                P.op("dve", lambda e: e.tensor_tensor(kx[:], ktm[:], kk2, ALU.mult), reads=["rw_ktm", "rw_bc"], writes=["rw_kx"])
                P.op("act", lambda e: e.activation(kh[:], kx[:], AF.Square), reads=["rw_kx"], writes=["rw_kh"])
                P.op("dve", lambda e: e.tensor_reduce(ss8[:], v3(kh), AX.X, ALU.add), reads=["rw_kh"], writes=["rw_ss8"])
                P.op("act", lambda e: e.activation(ss8[:], ss8[:], AF.Ln, bias=1e-6, scale=1.0), reads=["rw_ss8"], writes=["rw_ss8"])
                P.op("act", lambda e: e.activation(ss8[:], ss8[:], AF.Exp, scale=-0.5), reads=["rw_ss8"], writes=["rw_ss8"])
                P.op("dve", lambda e: e.tensor_tensor(v3(kh), v3(kx), ss8[:].unsqueeze(2).to_broadcast([64, 8, 64]), ALU.mult), reads=["rw_kx", "rw_ss8"], writes=["rw_kh"])
                P.op("dve", lambda e: e.tensor_tensor(ktz[:], av[:], ka2, ALU.mult), reads=["rw_a", "rw_bc"], writes=["rw_ktz"])
                P.op("dve", lambda e: e.tensor_tensor(ktz[:], ktz[:], omka[:].unsqueeze(1).to_broadcast([64, 2, 256]), ALU.add), reads=["rw_ktz", "rw_omka"], writes=["rw_ktz"])
                P.op("dve", lambda e: e.tensor_tensor(ktz[:], ktz[:], ktm[:], ALU.mult), reads=["rw_ktz", "rw_ktm"], writes=["rw_ktz"])
                P.op("dve", lambda e: e.tensor_tensor(al[:], av[:], kh[:], ALU.mult), reads=["rw_a", "rw_kh"], writes=["rw_al"])
                P.op("dve", lambda e: e.scalar_tensor_tensor(al[:], al[:], -1.0, encl[:], ALU.mult, ALU.mult), reads=["rw_al", "rw_encl"], writes=["rw_al"])
                P.op("dve", lambda e: e.tensor_tensor(be[:], kh[:], ecw[:], ALU.mult), reads=["rw_kh", "rw_ecw"], writes=["rw_be"])
                P.op("dve", lambda e: e.tensor_tensor(kti[:], ktz[:], encl[:], ALU.mult), reads=["rw_ktz", "rw_encl"], writes=["rw_kti"])
                P.op("dve", lambda e: e.tensor_tensor(rti[:], rtm[:], ecl[:], ALU.mult), reads=["rw_rtm", "rw_ecl"], writes=["rw_rti"])
                P.op("dve", lambda e: e.tensor_tensor(kx[:], rtm[:], ktz[:], ALU.mult), reads=["rw_rtm", "rw_ktz"], writes=["rw_kx"])
                P.op("dve", lambda e: e.tensor_tensor(kx[:], kx[:], bcs["rk"][:].unsqueeze(1).to_broadcast([64, 2, 256]), ALU.mult), reads=["rw_kx", "rw_bc"], writes=["rw_kx"])
                P.op("dve", lambda e: e.tensor_reduce(bs8[:], v3(kx), AX.X, ALU.add), reads=["rw_kx"], writes=["rw_bs8"])
                for z in range(2):
                    P.op("dve", lambda e, z=z, c=c: e.tensor_tensor(osb[:, z, 256:512].rearrange("p (h j) -> p h j", j=64), vtm[:, z, :].rearrange("p (h j) -> p h j", j=64),
                                                               bs8[:, 4 * z:4 * z + 4].unsqueeze(2).to_broadcast([64, 4, 64]), ALU.mult),
                         reads=["rw_vtm", "rw_bs8"], writes=["rw_osb"])
                for (src, skey, dstT, dkey, bank) in ((al, "rw_al", alT, "rw_rtm", 2), (be, "rw_be", beT, "rw_beT", 3), (kti, "rw_kti", ktT, "rw_ktm", 4), (rti, "rw_rti", rtT, "rw_be", 5)):
                    for u in range(8):
                        z, h = u // 4, u % 4
                        P.op("pe", lambda e, src=src, u=u, z=z, h=h, bank=bank: e.transpose(ps[0:64, bank, 64 * u:64 * u + 64], src[:, z, 64 * h:64 * h + 64], ident64),
                             reads=[skey, "ident"], writes=["ps%d" % bank])
                    P.op("act", lambda e, dstT=dstT, bank=bank: e.activation(dstT[:], ps[0:64, bank, :].rearrange("p (u l) -> p u l", l=64), AF.Copy), writes=["ps%d" % bank, dkey])
                for (lhs, lkey, rhs, rkey, bank, msk, dst, dkey) in ((alT, "rw_rtm", beT, "rw_beT", 2, strict, Pm, "rw_lw"), (ktT, "rw_ktm", beT, "rw_beT", 3, strict, Aak, "rw_ecl"),
                                                                      (alT, "rw_rtm", rtT, "rw_be", 4, incl, Ara, "rw_encl"), (ktT, "rw_ktm", rtT, "rw_be", 5, incl, Ark, "rw_ecw")):
                    for u in range(8):
                        P.op("pe", lambda e, lhs=lhs, rhs=rhs, u=u, bank=bank: e.matmul(ps[0:64, bank, 64 * u:64 * u + 64], lhs[:, u, :], rhs[:, u, :], start=True, stop=True),
                             reads=[lkey, rkey], writes=["ps%d" % bank])
                    P.op("dve", lambda e, bank=bank, msk=msk, dst=dst: e.tensor_tensor(dst[:], ps[0:64, bank, :].rearrange("p (u l) -> p u l", l=64),
                                                                                     msk.unsqueeze(1).to_broadcast([64, 8, 64]), ALU.mult),
                         reads=["consts"], writes=["ps%d" % bank, dkey])
                self.neumann_inverse("rw_", Pm, Qm, Rm, (2, 3, 4), keys=("rw_lw", "rw_a", "rw_rti"))
                for u in range(8):
                    z, h = u // 4, u % 4
                    P.op("pe", lambda e, u=u: e.matmul(ps[0:64, 5, 64 * u:64 * u + 64], beT[:, u, :], M[:, u, :], start=True, stop=False), reads=["rw_beT", "rw_M"], writes=["ps5"])
                    P.op("pe", lambda e, u=u, z=z, h=h: e.matmul(ps[0:64, 5, 64 * u:64 * u + 64], Aak[:, u, :], vtm[:, z, 64 * h:64 * h + 64], start=False, stop=True),
                         reads=["rw_ecl", "rw_vtm"], writes=["ps5"])
                P.op("act", lambda e: e.activation(X1[:], ps[0:64, 5, :].rearrange("p (u e) -> p u e", e=64), AF.Copy), writes=["ps5", "rw_kh"])
                for u in range(8):
                    P.op("pe", lambda e, u=u: e.matmul(ps[0:64, 6, 64 * u:64 * u + 64], Rm[:, u, :], X1[:, u, :], start=True, stop=True), reads=["rw_rti", "rw_kh"], writes=["ps6"])
                P.op("act", lambda e: e.activation(Uu[:], ps[0:64, 6, :].rearrange("p (u e) -> p u e", e=64), AF.Copy), writes=["ps6", "rw_kx"])
                for u in range(8):
                    z, h = u // 4, u % 4
                    vu = vtm[:, z, 64 * h:64 * h + 64]
                    P.op("pe", lambda e, u=u: e.matmul(ps[0:64, 5, 64 * u:64 * u + 64], rtT[:, u, :], M[:, u, :], start=True, stop=False), reads=["rw_be", "rw_M"], writes=["ps5"])
                    P.op("pe", lambda e, u=u: e.matmul(ps[0:64, 5, 64 * u:64 * u + 64], Ara[:, u, :], Uu[:, u, :], start=False, stop=False), reads=["rw_encl", "rw_kx"], writes=["ps5"])
                    P.op("pe", lambda e, u=u, vu=vu: e.matmul(ps[0:64, 5, 64 * u:64 * u + 64], Ark[:, u, :], vu, start=False, stop=True), reads=["rw_ecw", "rw_vtm"], writes=["ps5"])
                for z in range(2):
                    P.op("act", lambda e, z=z, c=c: e.activation(osb[:, z, 0:256], ps[0:64, 5, 256 * z:256 * z + 256], AF.Copy), writes=["ps5", "rw_osb"])
                for u in range(8):
                    z, h = u // 4, u % 4
                    vu = vtm[:, z, 64 * h:64 * h + 64]
                    P.op("pe", lambda e, u=u, z=z, h=h: e.matmul(ps[0:64, 6, 64 * u:64 * u + 64], al[:, z, 64 * h:64 * h + 64], Uu[:, u, :], start=True, stop=False), reads=["rw_al", "rw_kx"], writes=["ps6"])
                    P.op("pe", lambda e, u=u, z=z, h=h, vu=vu: e.matmul(ps[0:64, 6, 64 * u:64 * u + 64], kti[:, z, 64 * h:64 * h + 64], vu, start=False, stop=True), reads=["rw_kti", "rw_vtm"], writes=["ps6"])
                P.op("dve", lambda e: e.tensor_tensor(tmpM[:], ps[0:64, 6, :].rearrange("p (u e) -> p u e", e=64), M[:], ALU.add), reads=["rw_M"], writes=["ps6", "rw_ktz"])
                P.op("dve", lambda e: e.tensor_tensor(M[:], tmpM[:], WL[:].unsqueeze(2).to_broadcast([64, 8, 64]), ALU.mult), reads=["rw_ktz", "rw_WL"], writes=["rw_M"])
                self.emit_out_tile2(t, c, osb, ["rw_osb"], bacc)
            if t % 2 == 1:
                for u in range(8):
                    P.op("pe", lambda e, u=u: e.transpose(ps[0:64, 2, 64 * u:64 * u + 64], M[:, u, :], ident64), reads=["rw_M", "ident"], writes=["ps2"])
                P.op("dve", lambda e: e.tensor_copy(Mt[:], ps[0:64, 2, :].rearrange("p (u e) -> p u e", e=64)), writes=["ps2", "rw_kh"])
                for z in range(2):
                    seg = (t - 1) // 2 if z == 0 else (NT - 1 - t) // 2
                    P.dma("sp", lambda e, z=z, seg=seg: e.dma_start(out=self.out_rwkv[seg, l, z].rearrange("h i j -> i h j"), in_=Mt[:, 4 * z:4 * z + 4, :]), reads=["rw_kh"])
                P.op("dve", lambda e: e.tensor_scalar(M[:], M[:], self.keep[0:64, 0:1], None, ALU.mult), reads=["rw_M", "keep"], writes=["rw_M"])
        if self.debug.get("yacc") == "rwkv":
            P.dma("sp", lambda e: e.dma_start(out=self.dbg_yacc, in_=self.yacc[:]), reads=[("yacc", i) for i in range(NT)])
        if self.debug.get("yacc") == "rwkv_bonus":
            P.dma("sp", lambda e: e.dma_start(out=self.dbg_yacc, in_=bacc[:]), reads=[("bacc", i) for i in range(NT)])
        self.release(m1)
        if self.debug.get("post", True):
            self.post_rwkv(l, wb, wkeys, bacc, mu, first)
        self.release(m0)

    def emit_out_tile2(self, t, c, osb, osb_keys, bacc):
        P, ps = self.P, self.psum
        sel = self.cst("SEL").rearrange("p (q n) -> p q n", n=128)
        for z in range(2):
            bank = z
            pk = "ps%d" % bank
            tile = t if z == 0 else NT - 1 - t
            P.op("pe", lambda e, z=z, bank=bank: e.matmul(ps[:, bank, :], sel[:, 2 * z + c, :], osb[:, z, :], start=(c == 0), stop=(c == 1)),
                 reads=["consts"] + osb_keys, writes=[pk])
            if c == 0:
                continue
            yk = ("yacc", tile)
            bk = ("bacc", tile)
            if t < NT // 2:
                P.op("act", lambda e, tile=tile, bank=bank: e.activation(self.yacc[:, tile, :], ps[:, bank, 0:256], AF.Copy), writes=[pk, yk])
                P.op("act", lambda e, tile=tile, bank=bank: e.activation(bacc[:, tile, :], ps[:, bank, 256:512], AF.Copy), writes=[pk, bk])
            else:
                P.op("dve", lambda e, tile=tile, bank=bank: e.tensor_tensor(self.yacc[:, tile, :], ps[:, bank, 0:256], self.yacc[:, tile, :], ALU.add), writes=[pk, yk])
                P.op("dve", lambda e, tile=tile, bank=bank: e.tensor_tensor(bacc[:, tile, :], ps[:, bank, 256:512], bacc[:, tile, :], ALU.add), writes=[pk, bk])

    def post_rwkv(self, l, wb, wkeys, bacc, mu, first):
        P, ps = self.P, self.psum
        m0 = self.mark()
        wo, wokeys = self.load_w("wo_rw", self.w_out[l][768:1024, :], D, kchunks=2)
        gain_bc = self.sb("hn_gain", [128, 256], F32)
        P.dma("sp", lambda e: e.dma_start(out=gain_bc[:], in_=self.rw_norm[l:l + 1, :].partition_broadcast(128)), writes=["hn_gain"])
        g2 = self.sb("rw_g2", [128, 256], F32)
        P.dma("sp", lambda e: e.dma_start(out=g2[:], in_=self.rw_g2[l]), writes=["rw_g2"])
        tmp = (self.sb("hn_yc", [128, 256], F32), self.sb("hn_sq", [128, 256], F32), self.sb("hn_st", [128, 2, 4], F32),
               self.sb("yact", [128, 256], F32))
        gate = self.sb("hn_gate", [128, 256], F32)
        otmp = (self.sb("yTm", [128, 2, 128], BF16), self.sb("gtmp", [128, D], F32))
        pre1 = self.sb("rwp_pre", [128, 1, 130], F32)
        sg = self.sb("rwp_sg", [128, 1, 128], F32)
        for i in range(NT):
            tok0 = 2 + 128 * i - 1
            for k in range(8):
                P.op("pe", lambda e, k=k, tok0=tok0: e.matmul(ps[:, 0, 0:130], wb[:, k, 1024:1152], self.uT[:, k, tok0:tok0 + 130], start=(k == 0), stop=(k == 7)),
                     reads=wkeys + [("uT", i)] + ([("uT", i - 1)] if i > 0 else []) + ([("uT", i + 1)] if i < NT - 1 else []), writes=["ps0"])
            P.op("act", lambda e: e.activation(pre1[:, 0, :], ps[:, 0, 0:130], AF.Copy), writes=["ps0", "rwp_pre"])
            edge = slice(0, 1) if i % 2 == 0 else slice(129, 130)
            P.op("dve", lambda e, edge=edge: e.tensor_scalar(pre1[:, :, edge], pre1[:, :, edge], self.keep[:, 0:1], None, ALU.mult), reads=["rwp_pre", "keep"], writes=["rwp_pre"])
            self.rwkv_mix_fm(pre1, mu[:, 8:9], sg[:], ["rwp_pre"], ["rwp_sg"])
            self.act_sigmoid(sg[:], sg[:], ["rwp_sg"], ["rwp_sg"])
            P.op("pe", lambda e: e.matmul(ps[:, 1, 0:256], sg[:, 0, :], g2[:], start=True, stop=True), reads=["rwp_sg", "rw_g2"], writes=["ps1"])
            P.op("act", lambda e: e.activation(gate[:], ps[:, 1, 0:256], AF.Copy), writes=["ps1", "hn_gate"])
            self.head_norm_tile(i, True, gain_bc, gate[:], ["hn_gate"], tmp, extra=(bacc[:, i, :], [("bacc", i)]))
            self.out_proj_tile(i, tmp[3], wo, wokeys, first, otmp)
        self.release(m0)

    def head_norm_tile(self, i, center, gain_bc, gate_ap, gate_keys, tmp, extra=None, sfx=""):
        P = self.P
        yc, sq, st4, yact = tmp
        kst, kst1, kyc, ksq, kya = "hn_st" + sfx, "hn_st1" + sfx, "hn_yc" + sfx, "hn_sq" + sfx, "yact" + sfx
        y3 = self.yacc[:, i, :].rearrange("p (h e) -> p h e", e=64)
        yk = ("yacc", i)
        yc3 = yc[:].rearrange("p (h e) -> p h e", e=64)
        if center:
            P.op("dve", lambda e: e.tensor_reduce(st4[:, 0, :], y3, AX.X, ALU.add), reads=[yk], writes=[kst])
            P.op("dve", lambda e: e.tensor_scalar(st4[:, 0, :], st4[:, 0, :], -1.0 / 64, None, ALU.mult), reads=[kst], writes=[kst])
            P.op("dve", lambda e: e.tensor_tensor(yc3, y3, st4[:, 0, :].unsqueeze(2).to_broadcast([128, 4, 64]), ALU.add),
                 reads=[yk, kst], writes=[kyc])
        else:
            P.op("dve", lambda e: e.tensor_copy(yc[:], self.yacc[:, i, :]), reads=[yk], writes=[kyc])
        P.op("act", lambda e: e.activation(sq[:], yc[:], AF.Square), reads=[kyc], writes=[ksq])
        P.op("dve", lambda e: e.tensor_reduce(st4[:, 1, :], sq[:].rearrange("p (h e) -> p h e", e=64), AX.X, ALU.add), reads=[ksq], writes=[kst1])
        P.op("act", lambda e: e.activation(st4[:, 1, :], st4[:, 1, :], AF.Ln, bias=LN_EPS, scale=1.0 / 64), reads=[kst1], writes=[kst1])
        P.op("act", lambda e: e.activation(st4[:, 1, :], st4[:, 1, :], AF.Exp, scale=-0.5), reads=[kst1], writes=[kst1])
        P.op("dve", lambda e: e.tensor_tensor(yc3, yc3, st4[:, 1, :].unsqueeze(2).to_broadcast([128, 4, 64]), ALU.mult),
             reads=[kyc, kst1], writes=[kyc])
        P.op("dve", lambda e: e.tensor_tensor(yc[:], yc[:], gain_bc[:], ALU.mult), reads=[kyc, "hn_gain"], writes=[kyc])
        if extra is not None:
            P.op("dve", lambda e: e.tensor_tensor(yc[:], yc[:], extra[0], ALU.add), reads=[kyc] + extra[1], writes=[kyc])
        P.op("dve", lambda e: e.tensor_tensor(yact[:], yc[:], gate_ap, ALU.mult), reads=[kyc] + gate_keys, writes=[kya])

    def out_proj_tile(self, i, yact, wo, wokeys, first, tmp, sfx="", banks=(2, 4, 5)):
        P, ps = self.P, self.psum
        yTm, gtmp = tmp
        bT, b0, b1 = banks
        kya, kyT, kgt = "yact" + sfx, "yTm" + sfx, "gtmp" + sfx
        for kk in range(2):
            P.op("pe", lambda e, kk=kk: e.transpose(ps[:, bT, 128 * kk:128 * kk + 128], yact[:, 128 * kk:128 * kk + 128], self.ident[:]),
                 reads=[kya, "ident"], writes=["ps%d" % bT])
        P.op("act", lambda e: e.activation(yTm[:], ps[:, bT, 0:256].rearrange("p (k n) -> p k n", n=128), AF.Copy), writes=["ps%d" % bT, kyT])
        for n, bn in enumerate((b0, b1)):
            for kk in range(2):
                P.op("pe", lambda e, n=n, kk=kk, bn=bn: e.matmul(ps[:, bn, :], yTm[:, kk, :], wo[:, kk, 512 * n:512 * n + 512],
                                                                 start=(kk == 0), stop=(kk == 1)),
                     reads=[kyT] + wokeys, writes=["ps%d" % bn])
        xk = ("xres", i)
        for n, bn in enumerate((b0, b1)):
            cols = slice(512 * n, 512 * n + 512)
            P.op("dve", lambda e, bn=bn, cols=cols: e.tensor_tensor(gtmp[:, cols], ps[:, bn, :], self.g_bc["g1"][:, cols], ALU.mult),
                 reads=["g1_bc"], writes=["ps%d" % bn, kgt])
        P.op("dve", lambda e: e.scalar_tensor_tensor(self.xres[:, i, :], self.xres[:, i, :], ALPHA if first else 1.0, gtmp[:], ALU.mult, ALU.add),
             reads=[kgt], writes=[xk])

    def post_simple(self, l, name, wb, wkeys, gcol0, gate_func, norm_dram, center, wo_row0, first):
        P, ps = self.P, self.psum
        m0 = self.mark()
        wo, wokeys = self.load_w("wo_" + name, self.w_out[l][wo_row0:wo_row0 + 256, :], D, kchunks=2)
        gain_bc = self.sb("hn_gain", [128, 256], F32)
        P.dma("sp", lambda e: e.dma_start(out=gain_bc[:], in_=norm_dram[l:l + 1, :].partition_broadcast(128)), writes=["hn_gain"])
        tmps = [(self.sb("hn_yc%d" % q, [128, 256], F32), self.sb("hn_sq%d" % q, [128, 256], F32), self.sb("hn_st%d" % q, [128, 2, 4], F32),
                 self.sb("yact%d" % q, [128, 256], F32)) for q in range(2)]
        gates = [self.sb("hn_gate%d" % q, [128, 256], F32) for q in range(3)]
        otmps = [(self.sb("yTm%d" % q, [128, 2, 128], BF16), self.sb("gtmp%d" % q, [128, D], F32)) for q in range(2)]

        def stage_a(i):
            gate = gates[i % 3]
            gk = "hn_gate%d" % (i % 3)
            bank = i % 2
            for k in range(8):
                P.op("pe", lambda e, k=k: e.matmul(ps[:, bank, 0:256], self.uT[:, k, 2 + 128 * i:2 + 128 * i + 128], wb[:, k, gcol0:gcol0 + 256],
                                                   start=(k == 0), stop=(k == 7)),
                     reads=wkeys + [("uT", i)], writes=["ps%d" % bank])
            self.act_sigmoid(gate[:], ps[:, bank, 0:256], [], ["ps%d" % bank, gk])
            if gate_func == AF.Silu:
                P.op("dve", lambda e: e.tensor_tensor(gate[:], ps[:, bank, 0:256], gate[:], ALU.mult), writes=["ps%d" % bank, gk])

        def stage_b(i):
            self.head_norm_tile(i, center, gain_bc, gates[i % 3][:], ["hn_gate%d" % (i % 3)], tmps[i % 2], sfx=str(i % 2))

        def stage_c(i):
            self.out_proj_tile(i, tmps[i % 2][3], wo, wokeys, first, otmps[i % 2], sfx=str(i % 2), banks=((2, 4, 5) if i % 2 == 0 else (3, 6, 7)))

        for step in range(NT + 2):
            if step < NT:
                stage_a(step)
            if 1 <= step <= NT:
                stage_b(step - 1)
            if step >= 2:
                stage_c(step - 2)
        self.release(m0)

    def ln_affine_tile(self, i, g_bc, b_bc, par=0):
        P = self.P
        stats, mv, rstd, xhat = self.lntmp2[par]
        sf = str(par)
        xk = ("xres", i)
        src = self.xres[:, i, :]
        for hh in range(2):
            P.op("dve", lambda e, hh=hh: e.bn_stats(stats[:, hh, :], src[:, 512 * hh:512 * hh + 512]), reads=[xk], writes=["lnstats" + sf])
        P.op("dve", lambda e: e.bn_aggr(mv[:], stats[:]), reads=["lnstats" + sf], writes=["lnmv" + sf])
        P.op("act", lambda e: e.activation(rstd[:], mv[:, 1:2], AF.Ln, bias=LN_EPS, scale=1.0), reads=["lnmv" + sf], writes=["lnrstd" + sf])
        P.op("act", lambda e: e.activation(rstd[:], rstd[:], AF.Exp, scale=-0.5), reads=["lnrstd" + sf], writes=["lnrstd" + sf])
        P.op("dve", lambda e: e.tensor_scalar(xhat[:], src, mv[:, 0:1], rstd[:, 0:1], ALU.subtract, ALU.mult),
             reads=[xk, "lnmv" + sf, "lnrstd" + sf], writes=["xhat" + sf])
        P.op("pool", lambda e: e.tensor_tensor(xhat[:], xhat[:], g_bc[:], ALU.mult), reads=["xhat" + sf, "lnbc"], writes=["xhat" + sf])
        P.op("pool", lambda e: e.tensor_tensor(src, xhat[:], b_bc[:], ALU.add), reads=["xhat" + sf, "lnbc"], writes=[xk])

    def phase_c(self, l):
        P, ps = self.P, self.psum
        m0 = self.mark()
        self.alloc_lntmp()
        bc = {}
        for nm, src in (("ln1_g", self.ln1_g), ("ln1_b", self.ln1_b), ("ln2_g", self.ln2_g), ("ln2_b", self.ln2_b)):
            bc[nm] = self.sb("bc_" + nm, [128, D], F32)
            P.dma("sp", lambda e, nm=nm, src=src: e.dma_start(out=bc[nm][:], in_=src[l:l + 1, :].partition_broadcast(128)), writes=["lnbc"])
        u2T = self.sb("u2T", [128, 8, TOK], BF16)
        hT = [self.sb("hT%d" % i, [128, 4, 512], BF16) for i in range(2)]
        rtmp = [self.sb("ffn_rtmp%d" % i, [128, 512], F32) for i in range(2)]
        gtmp = [self.sb("ffn_gtmp%d" % i, [128, 512], F32) for i in range(2)]
        w1b = [self.sb("w1b%d" % i, [128, 8, 512], BF16) for i in range(2)]
        w2b = [self.sb("w2b%d" % i, [128, 4, 1024], BF16) for i in range(2)]
        w1v = self.w_ff1[l].rearrange("(k p) n -> p k n", p=128)
        w2v = self.w_ff2[l].rearrange("(c p) n -> p c n", p=128)
        for i in range(NT):
            self.ln_affine_tile(i, bc["ln1_g"], bc["ln1_b"], par=i % 2)
            self.ln_mod_T(i, lambda k, i=i: u2T[:, k, 128 * i:128 * i + 128], self.sc2p, 24, self.lntmp2[i % 2], [("u2T", i)], par=i % 2)
        nh = 0
        ng_ = 0
        for sl in range(8):
            wa = w1b[sl % 2]
            wbk = w2b[sl % 2]
            ka = "w1b%d" % (sl % 2)
            kb = "w2b%d" % (sl % 2)
            for kh in range(2):
                P.dma("pool", lambda e, wa=wa, sl=sl, kh=kh: e.dma_start(out=wa[:, 4 * kh:4 * kh + 4, :], in_=w1v[:, 4 * kh:4 * kh + 4, 512 * sl:512 * sl + 512]), writes=[(ka, kh)])
            for kh in range(2):
                P.dma("pool", lambda e, wbk=wbk, sl=sl, kh=kh: e.dma_start(out=wbk[:, 2 * kh:2 * kh + 2, :], in_=w2v[:, 4 * sl + 2 * kh:4 * sl + 2 * kh + 2, :]), writes=[(kb, kh)])
            wakeys = [(ka, 0), (ka, 1)]
            wbkeys = [(kb, 0), (kb, 1)]
            for blk in range(4):
                hb = hT[nh % 2]
                hk = "hT%d" % (nh % 2)
                nh += 1
                u2keys = [("u2T", i) for i in range(4 * blk, 4 * blk + 4)]
                for cc in range(4):
                    bank = cc % 2
                    for k in range(8):
                        P.op("pe", lambda e, wa=wa, cc=cc, k=k, bank=bank, blk=blk: e.matmul(
                            ps[:, bank, :], wa[:, k, 128 * cc:128 * cc + 128], u2T[:, k, 512 * blk:512 * blk + 512], start=(k == 0), stop=(k == 7)),
                            reads=wakeys + u2keys, writes=["ps%d" % bank])
                    rt = rtmp[cc % 2]
                    rk = "ffn_rtmp%d" % (cc % 2)
                    P.op("act", lambda e, rt=rt, bank=bank: e.activation(rt[:], ps[:, bank, :], AF.Relu), writes=["ps%d" % bank, rk])
                    P.op("pool", lambda e, rt=rt, cc=cc, hb=hb: e.tensor_tensor(hb[:, cc, :], rt[:], rt[:], ALU.mult), reads=[rk], writes=[(hk, cc)])
                hkeys = [(hk, cc) for cc in range(4)]
                for j in range(4):
                    i = 4 * blk + j
                    for n in range(2):
                        bank = 2 + (ng_ % 4)
                        gt = gtmp[ng_ % 2]
                        gk = "ffn_gtmp%d" % (ng_ % 2)
                        ng_ += 1
                        cols = slice(512 * n, 512 * n + 512)
                        for hc in range(4):
                            P.op("pe", lambda e, wbk=wbk, hc=hc, j=j, bank=bank, hb=hb, cols=cols: e.matmul(
                                ps[:, bank, :], hb[:, hc, 128 * j:128 * j + 128], wbk[:, hc, cols], start=(hc == 0), stop=(hc == 3)),
                                reads=wbkeys + hkeys, writes=["ps%d" % bank])
                        P.op("dve", lambda e, bank=bank, cols=cols, gt=gt: e.tensor_tensor(gt[:], ps[:, bank, :], self.g_bc["g2"][:, cols], ALU.mult),
                             reads=["g2_bc"], writes=["ps%d" % bank, gk])
                        P.op("dve", lambda e, i=i, cols=cols, gt=gt, sl=sl: e.scalar_tensor_tensor(
                            self.xres[:, i, cols], self.xres[:, i, cols], ALPHA if sl == 0 else 1.0, gt[:], ALU.mult, ALU.add),
                            reads=[gk], writes=[("xres", i)])
        for i in range(NT):
            self.ln_affine_tile(i, bc["ln2_g"], bc["ln2_b"], par=i % 2)
        self.release(m0)

    def alloc_lntmp(self):
        self.lntmp2 = [(self.sb("lnstats%d" % q, [128, 2, 6], F32), self.sb("lnmv%d" % q, [128, 2], F32),
                        self.sb("lnrstd%d" % q, [128, 1], F32), self.sb("xhat%d" % q, [128, D], F32)) for q in range(2)]
        self.lntmp = self.lntmp2[0]

    def layer(self, l):
        P = self.P
        self.compute_mod(l)
        mL = self.mark()
        self.uT = self.sb("uT", [128, 8, TOK + 4], BF16)
        self.yacc = self.sb("yacc", [128, NT, 256], F32)
        P.op("dve", lambda e: e.memset(self.uT[:, :, 0:2], 0.0), writes=["uTpadL"])
        P.op("dve", lambda e: e.memset(self.uT[:, :, TOK + 2:TOK + 4], 0.0), writes=["uTpadR"])
        mA = self.mark()
        self.alloc_lntmp()
        for i in range(NT):
            self.ln_mod_T(i, lambda k, i=i: self.uT[:, k, 2 + 128 * i:2 + 128 * i + 128], self.sc1p, 0, self.lntmp2[i % 2], [("uT", i)], par=i % 2)
        self.release(mA)
        if "uT" in self.debug and l == self.debug["uT"]:
            mm_ = self.mark()
            dbgf = self.sb("dbgf", [128, 8, 512], F32)
            for q in range(4):
                P.op("dve", lambda e, q=q: e.tensor_copy(dbgf[:], self.uT[:, :, 2 + 512 * q:2 + 512 * q + 512]),
                     reads=[("uT", i) for i in range(4 * q, 4 * q + 4)], writes=["dbgf"])
                P.dma("sp", lambda e, q=q: e.dma_start(out=self.dbg_uT[:, :, 512 * q:512 * q + 512], in_=dbgf[:]), reads=["dbgf"])
            self.release(mm_)
        mixers = self.debug.get("mixers", ["mlstm", "delta", "ret", "rwkv"])
        first = True
        for mx in mixers:
            getattr(self, "mixer_" + mx)(l, first)
            first = False
        self.release(mL)
        if self.debug.get("phase_c", True):
            self.phase_c(l)

    def finish(self):
        P = self.P
        yv = self.y_out.rearrange("(i p) d -> p i d", p=128)
        for q in range(4):
            P.dma("sp", lambda e, q=q: e.dma_start(out=yv[:, 4 * q:4 * q + 4, :], in_=self.xres[:, 4 * q:4 * q + 4, :]),
                  reads=[("xres", i) for i in range(4 * q, 4 * q + 4)])


PROMPT_ASSIGN = [[0, 1, 2], [3, 4, 5], [6, 7, 8], [9, 10, 11], [12, 13], [14, 15]]

OFF_A, OFF_B, OFF_C, OFF_D = 0, 1040, 2080, 3104


def rope_tables(is_sample):
    tab = np.zeros((TOK, 64, 2), np.float32)
    tab[:, :, 0] = 1.0
    if is_sample:
        n = np.arange(TOK)
        posv = (n // 64, n % 64)
        inv = 10000.0 ** (-np.arange(16, dtype=np.float32) / 16)
        for half in range(2):
            ang = posv[half].astype(np.float32)[:, None] * inv[None, :]
            cos, sin = np.cos(ang), np.sin(ang)
            base = 32 * half
            tab[:, base:base + 16, 0] = cos
            tab[:, base + 16:base + 32, 0] = cos
            tab[:, base:base + 16, 1] = -sin
            tab[:, base + 16:base + 32, 1] = sin
    t = tab.reshape(NT, 128, 64, 2).transpose(0, 2, 3, 1)
    t = np.concatenate([t, t], axis=1)
    return np.ascontiguousarray(t.astype(np.float32))


def swap_perm():
    idx = []
    for h in range(4):
        for half in range(2):
            b = 64 * h + 32 * half
            idx += list(range(b + 16, b + 32)) + list(range(b, b + 16))
    return np.array(idx)


def make_in_maps(inp, kern):
    f32 = np.float32
    g = lambda k: np.asarray(inp[k], f32)
    maps = []
    ident = np.eye(128, dtype=f32)
    consts = build_consts()
    b_mod = np.ascontiguousarray(g("b_mod").reshape(DEPTH, 48, 128))
    w_mod = np.ascontiguousarray(g("w_mod"))
    w_in = g("w_in")
    sw = swap_perm()
    cq = w_in[:, :, OFF_C:OFF_C + 256]
    ck = w_in[:, :, OFF_C + 256:OFF_C + 512]
    cv = w_in[:, :, OFF_C + 512:OFF_C + 768]
    cg = w_in[:, :, OFF_C + 768:OFF_C + 1024]
    aq = w_in[:, :, OFF_A:OFF_A + 256]
    ak = w_in[:, :, OFF_A + 256:OFF_A + 512]
    av = w_in[:, :, OFF_A + 512:OFF_A + 768]
    ao = w_in[:, :, OFF_A + 768:OFF_A + 1024]
    ai = w_in[:, :, OFF_A + 1024:OFF_A + 1032]
    af = w_in[:, :, OFF_A + 1032:OFF_A + 1040]
    agate = np.concatenate([ai[:, :, 0:4], af[:, :, 0:4], ai[:, :, 4:8], af[:, :, 4:8]], axis=2)
    bqkv = w_in[:, :, OFF_B:OFF_B + 768]
    bz = w_in[:, :, OFF_B + 768:OFF_B + 1024]
    bbeta = w_in[:, :, OFF_B + 1024:OFF_B + 1032]
    balpha = w_in[:, :, OFF_B + 1032:OFF_B + 1040]
    bgate = np.concatenate([bbeta[:, :, 0:4], balpha[:, :, 0:4], bbeta[:, :, 4:8], balpha[:, :, 4:8]], axis=2)
    dconv = g("delta_conv")
    dconv = np.ascontiguousarray(dconv.reshape(DEPTH, 5, 6, 128).transpose(0, 3, 2, 1))
    w2 = g("rwkv_w2")
    a2 = g("rwkv_a2")
    w2p = np.zeros((DEPTH, 128, 2, 256), f32)
    a2p = np.zeros((DEPTH, 128, 2, 256), f32)
    for z_ in range(2):
        w2p[:, 64 * z_:64 * z_ + 64, z_, :] = w2[:, z_]
        a2p[:, 64 * z_:64 * z_ + 64, z_, :] = a2[:, z_]
    shared = {
        "wD": np.ascontiguousarray(w_in[:, :, OFF_D:OFF_D + 1152]),
        "rw_mu": np.ascontiguousarray(g("rwkv_mu").reshape(DEPTH, 9, 128).transpose(0, 2, 1)),
        "rw_w2p": w2p, "rw_a2p": a2p,
        "rw_w0": np.ascontiguousarray(g("rwkv_w0").reshape(DEPTH, 512)),
        "rw_a0": np.ascontiguousarray(g("rwkv_a0").reshape(DEPTH, 512)),
        "rw_kk": g("rwkv_kk"), "rw_ka": g("rwkv_ka"),
        "rw_rk": np.ascontiguousarray(g("rwkv_rk").reshape(DEPTH, 256)),
        "rw_norm": g("rwkv_norm"), "rw_g2": np.ascontiguousarray(g("rwkv_g2")),
        "wB": np.ascontiguousarray(np.concatenate([bqkv, bgate, bz], axis=2)),
        "dl_conv": dconv,
        "dl_alog": np.ascontiguousarray(g("delta_a_log").reshape(DEPTH, 8)),
        "dl_dtb": np.ascontiguousarray(g("delta_dt_bias").reshape(DEPTH, 8)),
        "dl_norm": np.ascontiguousarray(g("delta_norm")),
        "wA": np.ascontiguousarray(np.concatenate([aq, ak, av, agate, ao], axis=2)),
        "ml_ib": np.ascontiguousarray(g("mlstm_i_bias").reshape(DEPTH, 8)),
        "ml_fb": np.ascontiguousarray(g("mlstm_f_bias").reshape(DEPTH, 8)),
        "ml_norm": np.ascontiguousarray(g("mlstm_norm")),
        "ident": ident, "consts": consts, "w_mod": w_mod, "b_mod": b_mod,
        "w_out": np.ascontiguousarray(g("w_out")),
        "wC": np.ascontiguousarray(np.concatenate([cq, ck, cv, cq[:, :, sw], ck[:, :, sw], cg], axis=2)),
        "ret_decay": np.ascontiguousarray(g("ret_decay").reshape(DEPTH, 8)),
        "ret_norm": np.ascontiguousarray(g("ret_norm")),
        "ln1_g": g("ln1_g"), "ln1_b": g("ln1_b"), "ln2_g": g("ln2_g"), "ln2_b": g("ln2_b"),
        "w_ff1": np.ascontiguousarray(g("w_ff1")), "w_ff2": np.ascontiguousarray(g("w_ff2")),
    }
    ropes = {True: rope_tables(True), False: rope_tables(False)}
    zeros_mat = np.zeros((DEPTH, 2, H, HD, HD), f32)
    for c in range(N_CORES):
        m = dict(shared)
        if c < 2:
            x = g("x_sample")[c]
            cond = g("c")[c]
            m["init_ret"] = np.ascontiguousarray(g("state_ret")[c])
            m["init_mC"] = np.ascontiguousarray(g("state_mlstm_C")[c])
            m["init_delta"] = np.ascontiguousarray(g("state_delta")[c])
            m["init_rwkv"] = np.ascontiguousarray(g("state_rwkv")[c])
            m["init_mn"] = np.ascontiguousarray(g("state_mlstm_n")[c])
            m["init_mm"] = np.ascontiguousarray(g("state_mlstm_m")[c])
        else:
            x = np.zeros((TOK, D), f32)
            mine = PROMPT_ASSIGN[c - 2]
            for s_ in range(8):
                x[256 * s_:256 * s_ + 256] = inp["x_prompt"][mine[s_ % len(mine)]]
            cond = g("c_ctx")
            m["init_ret"] = zeros_mat
            m["init_mC"] = zeros_mat
            m["init_delta"] = zeros_mat
            m["init_rwkv"] = zeros_mat
            m["init_mn"] = np.zeros((DEPTH, 2, H, HD), f32)
            m["init_mm"] = np.zeros((DEPTH, 2, H), f32)
        m["x"] = np.ascontiguousarray(x)
        m["cond"] = np.ascontiguousarray(cond.reshape(8, 128))
        m["keep"] = np.full((1, 1), 1.0 if c < 2 else 0.0, f32)
        m["rope"] = ropes[c < 2]
        missing = [k for k in kern.ins if k not in m]
        assert not missing, missing
        maps.append({k: m[k] for k in kern.ins})
    return maps


def run(inp, debug=None, trace=False):
    kern = K(debug)
    nc = kern.build()
    maps = make_in_maps(inp, kern)
    res = run_bass_kernel_spmd(nc, maps, core_ids=list(range(N_CORES)), trace=trace)
    return kern, res


def gather_states(r, name, shape_tail):
    out = np.zeros((16, DEPTH, 2) + shape_tail, np.float32)
    for c in range(2, N_CORES):
        for s_, b in enumerate(PROMPT_ASSIGN[c - 2]):
            out[b] = r[c][name][s_]
    return out


def kernel(**inp):
    kern, res = run(inp)
    r = res.results
    BATCH, SEQ = 16, 256
    y_prompt = np.zeros((BATCH, SEQ, D), np.float32)
    y_sample = np.zeros((2, TOK, D), np.float32)
    for c in range(2):
        y_sample[c] = r[c]["y"]
    for c in range(2, N_CORES):
        for s_, b in enumerate(PROMPT_ASSIGN[c - 2]):
            y_prompt[b] = r[c]["y"][256 * s_:256 * s_ + 256]
    new_ret = gather_states(r, "out_ret", (H, HD, HD))
    new_mC = gather_states(r, "out_mC", (H, HD, HD))
    new_mn = gather_states(r, "out_mn", (H, HD))
    new_mm = gather_states(r, "out_mm", (H,))
    new_delta = gather_states(r, "out_delta", (H, HD, HD)) if "out_delta" in r[0] else np.zeros_like(new_ret)
    new_rwkv = gather_states(r, "out_rwkv", (H, HD, HD)) if "out_rwkv" in r[0] else np.zeros_like(new_ret)
    return (y_prompt, y_sample, new_mC, new_mn, new_mm, new_delta, new_ret, new_rwkv)
```

```python
import contextlib
import numpy as np
import concourse.bass as bass
import concourse.mybir as mybir
from concourse.bass_utils import run_bass_kernel_spmd

F32 = mybir.dt.float32
BF16 = mybir.dt.bfloat16
AF = mybir.ActivationFunctionType
ALU = mybir.AluOpType
AX = mybir.AxisListType

D = 1024
NT = 16
TOK = 2048
DEPTH = 2
H = 4
HD = 64
ALPHA = (2 * DEPTH) ** 0.25
LN_EPS = 1e-5
N_CORES = 8

ENGS = ("pe", "act", "dve", "pool", "sp")
NSLOT = 6

CO = {}
_off = 0
for _n, _w in (("INCL", 64), ("STRICT", 64), ("ONES", 128), ("SEL", 512), ("PIDX", 1), ("NPIDX", 1), ("BLK", 128), ("PP1", 1)):
    CO[_n] = (_off, _w)
    _off += _w
NCONST = _off
ARENA_WORDS = 53200


def build_consts():
    c = np.zeros((128, NCONST), np.float32)
    s = np.arange(64)
    incl = (s[:, None] <= s[None, :]).astype(np.float32)
    strict = (s[:, None] < s[None, :]).astype(np.float32)
    for half in range(2):
        c[64 * half:64 * half + 64, CO["INCL"][0]:CO["INCL"][0] + 64] = incl
        c[64 * half:64 * half + 64, CO["STRICT"][0]:CO["STRICT"][0] + 64] = strict
    c[:, CO["ONES"][0]:CO["ONES"][0] + 128] = 1.0
    sel = np.zeros((64, 4, 128), np.float32)
    for z in range(2):
        for ch in range(2):
            for lp in range(64):
                n = 64 * ch + lp if z == 0 else 127 - 64 * ch - lp
                sel[lp, 2 * z + ch, n] = 1.0
    c[0:64, CO["SEL"][0]:CO["SEL"][0] + 512] = sel.reshape(64, 512)
    p = np.arange(128) % 64
    c[:, CO["PIDX"][0]] = p
    c[:, CO["NPIDX"][0]] = -p
    c[:, CO["PP1"][0]] = p + 1
    blk = np.zeros((128, 128), np.float32)
    blk[0:64, 0:64] = 1.0
    blk[64:128, 64:128] = 1.0
    c[:, CO["BLK"][0]:CO["BLK"][0] + 128] = blk
    return c


class Op:
    __slots__ = ("eng", "fn", "reads", "writes", "chan", "val", "is_dma", "idx", "signal")

    def __init__(self, eng, fn, reads, writes, is_dma):
        self.eng = eng
        self.fn = fn
        self.reads = reads
        self.writes = writes
        self.is_dma = is_dma
        self.chan = None
        self.val = 0
        self.signal = False


class _Rec:
    def __getattr__(self, name):
        def f(*a, **kw):
            self.call = (name, a, kw)
            return self
        return f


class Prog:
    def __init__(self):
        self.ops = []

    def op(self, eng, fn, reads=(), writes=()):
        r = _Rec()
        fn(r)
        o = Op(eng, r.call, tuple(reads), tuple(writes), False)
        self.ops.append(o)
        return o

    def dma(self, eng, fn, reads=(), writes=()):
        r = _Rec()
        fn(r)
        o = Op(eng, r.call, tuple(reads), tuple(writes), True)
        self.ops.append(o)
        return o

    def barrier(self):
        self.ops.append(None)

    def emit(self, nc, stack):
        raw = self.ops
        ops = []
        barrier_at = set()
        for o in raw:
            if o is None:
                barrier_at.add(len(ops))
            else:
                ops.append(o)
        dma_n = {e: 0 for e in ENGS}
        slot_prev = {}
        for i, o in enumerate(ops):
            o.idx = i
            if o.is_dma:
                j = dma_n[o.eng] % NSLOT
                dma_n[o.eng] += 1
                o.chan = ("dma", o.eng, j)
            else:
                o.chan = o.eng
        lastw = {}
        readers = {}
        deps = []
        last_chan = {}
        bar_deps = set()
        for o in ops:
            if o.idx in barrier_at:
                bar_deps = set(last_chan.values())
            last_chan[o.chan] = o.idx
            d = set(bar_deps)
            for k in o.reads:
                w = lastw.get(k)
                if w is not None:
                    d.add(w)
            for k in o.writes:
                w = lastw.get(k)
                if w is not None:
                    d.add(w)
                for r in readers.get(k, ()):
                    d.add(r)
            if o.is_dma:
                p = slot_prev.get(o.chan)
                if p is not None:
                    d.add(p)
                slot_prev[o.chan] = o.idx
            d.discard(o.idx)
            deps.append(d)
            for k in o.reads:
                readers.setdefault(k, []).append(o.idx)
            for k in o.writes:
                lastw[k] = o.idx
                readers[k] = []
        pos = {}
        cnt = {}
        for o in ops:
            c = o.chan
            cnt[c] = cnt.get(c, 0) + 1
            pos[o.idx] = cnt[c]
        know_stream = {e: {} for e in ENGS}
        know_op = [None] * len(ops)
        needed = [None] * len(ops)
        for o in ops:
            ks = know_stream[o.eng]
            need = []
            for p in sorted(deps[o.idx], reverse=True):
                po = ops[p]
                if po.eng == "pe" and o.eng == "pe" and not po.is_dma and not o.is_dma:
                    continue
                if ks.get(po.chan, 0) >= pos[p]:
                    continue
                need.append(p)
                for c, v in know_op[p].items():
                    if ks.get(c, 0) < v:
                        ks[c] = v
                if ks.get(po.chan, 0) < pos[p]:
                    ks[po.chan] = pos[p]
            needed[o.idx] = need
            know_op[o.idx] = dict(ks)
            for p in need:
                ops[p].signal = True
        for o in ops:
            if o.is_dma:
                o.signal = True
        cnt = {}
        for o in ops:
            if o.signal:
                cnt[o.chan] = cnt.get(o.chan, 0) + 1
                o.val = cnt[o.chan] * (16 if o.is_dma else 1)
        sems = {}
        for c in cnt:
            name = "s_" + ("_".join(str(x) for x in c) if isinstance(c, tuple) else c)
            sems[c] = stack.enter_context(nc.semaphore(name))
        self.maxval = dict(cnt)
        block = stack.enter_context(nc.Block())
        streams = {e: [] for e in ENGS}
        for o in ops:
            streams[o.eng].append(o)

        def run_stream(engname, engobj):
            for o in streams[engname]:
                for p in needed[o.idx]:
                    po = ops[p]
                    engobj.wait_ge(sems[po.chan], po.val)
                name, a, kw = o.fn
                inst = getattr(engobj, name)(*a, **kw)
                if o.signal:
                    inst.then_inc(sems[o.chan], 16 if o.is_dma else 1)

        @block.tensor
        def _(e):
            run_stream("pe", e)

        @block.scalar
        def _(e):
            run_stream("act", e)

        @block.vector
        def _(e):
            run_stream("dve", e)

        @block.gpsimd
        def _(e):
            run_stream("pool", e)

        @block.sync
        def _(e):
            run_stream("sp", e)
            for c, n in cnt.items():
                if isinstance(c, tuple):
                    e.wait_ge(sems[c], n * 16)
        return len(ops)


class K:
    def __init__(self, debug=None):
        self.debug = debug or {}
        self.nc = bass.Bass("TRN2", target_bir_lowering=False)
        self.P = Prog()
        self.ins = {}
        self.outs = {}
        self.uid = 0

    def din(self, name, shape):
        t = self.nc.dram_tensor(name, list(shape), F32, kind="ExternalInput").ap()
        self.ins[name] = t
        return t

    def dout(self, name, shape):
        t = self.nc.dram_tensor(name, list(shape), F32, kind="ExternalOutput").ap()
        self.outs[name] = t
        return t

    def sb(self, name, shape, dt=F32):
        shape = list(shape)
        nelem = 1
        for d in shape[1:]:
            nelem *= d
        words = (nelem * (2 if dt == BF16 else 4) + 3) // 4
        words = (words + 7) // 8 * 8
        off = self.arena_off
        assert off + words <= ARENA_WORDS, ("SBUF arena overflow", name, off, words)
        self.arena_off = off + words
        self.arena_peak = max(self.arena_peak, self.arena_off)
        ap = self.arena[0:shape[0], off:off + words]
        if dt == BF16:
            ap = ap.bitcast(BF16)
        ap = ap[:, 0:nelem]
        if len(shape) == 3:
            ap = ap.rearrange("p (a b) -> p a b", b=shape[2])
        elif len(shape) == 4:
            ap = ap.rearrange("p (a b c) -> p a b c", b=shape[2], c=shape[3])
        elif len(shape) == 5:
            ap = ap.rearrange("p (a b c d) -> p a b c d", b=shape[2], c=shape[3], d=shape[4])
        return ap

    def mark(self):
        return self.arena_off

    def release(self, m):
        self.arena_off = m
        self.P.barrier()

    def build(self):
        nc, P = self.nc, self.P
        with contextlib.ExitStack() as st:
            self.stack = st
            self.arena = st.enter_context(nc.sbuf_tensor("arena", [128, ARENA_WORDS], F32))
            self.arena_off = 0
            self.arena_peak = 0
            self.declare_io()
            self.alloc_global()
            self.load_consts()
            for l in range(self.debug.get("layers", DEPTH)):
                self.layer(l)
            self.finish()
            n = P.emit(nc, st)
            self.n_ops = n
        return nc

    def declare_io(self):
        self.x_in = self.din("x", [TOK, D])
        self.cond = self.din("cond", [8, 128])
        self.ident_in = self.din("ident", [128, 128])
        self.w_mod = self.din("w_mod", [DEPTH, D, 6 * D])
        self.b_mod = self.din("b_mod", [DEPTH, 48, 128])
        self.y_out = self.dout("y", [TOK, D])
        self.consts_in = self.din("consts", [128, NCONST])
        self.keep_in = self.din("keep", [1, 1])
        self.rope_in = self.din("rope", [NT, 128, 2, 128])
        self.w_out = self.din("w_out", [DEPTH, D, D])
        self.wC = self.din("wC", [DEPTH, D, 1536])
        self.wA = self.din("wA", [DEPTH, D, 1040])
        self.ml_ib = self.din("ml_ib", [DEPTH, 8])
        self.ml_fb = self.din("ml_fb", [DEPTH, 8])
        self.ml_norm = self.din("ml_norm", [DEPTH, 256])
        self.init_mC = self.din("init_mC", [DEPTH, 2, H, HD, HD])
        self.init_mn = self.din("init_mn", [DEPTH, 2, H, HD])
        self.init_mm = self.din("init_mm", [DEPTH, 2, H])
        self.out_mC = self.dout("out_mC", [8, DEPTH, 2, H, HD, HD])
        self.out_mn = self.dout("out_mn", [8, DEPTH, 2, H, HD])
        self.out_mm = self.dout("out_mm", [8, DEPTH, 2, H])
        self.wB = self.din("wB", [DEPTH, D, 1040])
        self.dl_conv = self.din("dl_conv", [DEPTH, 128, 6, 5])
        self.dl_alog = self.din("dl_alog", [DEPTH, 8])
        self.dl_dtb = self.din("dl_dtb", [DEPTH, 8])
        self.dl_norm = self.din("dl_norm", [DEPTH, 256])
        self.init_delta = self.din("init_delta", [DEPTH, 2, H, HD, HD])
        self.out_delta = self.dout("out_delta", [8, DEPTH, 2, H, HD, HD])
        self.wD = self.din("wD", [DEPTH, D, 1152])
        self.rw_mu = self.din("rw_mu", [DEPTH, 128, 9])
        self.rw_w2p = self.din("rw_w2p", [DEPTH, 128, 2, 256])
        self.rw_a2p = self.din("rw_a2p", [DEPTH, 128, 2, 256])
        self.rw_w0 = self.din("rw_w0", [DEPTH, 512])
        self.rw_a0 = self.din("rw_a0", [DEPTH, 512])
        self.rw_kk = self.din("rw_kk", [DEPTH, 256])
        self.rw_ka = self.din("rw_ka", [DEPTH, 256])
        self.rw_rk = self.din("rw_rk", [DEPTH, 256])
        self.rw_norm = self.din("rw_norm", [DEPTH, 256])
        self.rw_g2 = self.din("rw_g2", [DEPTH, 128, 256])
        self.init_rwkv = self.din("init_rwkv", [DEPTH, 2, H, HD, HD])
        self.out_rwkv = self.dout("out_rwkv", [8, DEPTH, 2, H, HD, HD])
        self.ln1_g = self.din("ln1_g", [DEPTH, D])
        self.ln1_b = self.din("ln1_b", [DEPTH, D])
        self.ln2_g = self.din("ln2_g", [DEPTH, D])
        self.ln2_b = self.din("ln2_b", [DEPTH, D])
        self.w_ff1 = self.din("w_ff1", [DEPTH, D, 4 * D])
        self.w_ff2 = self.din("w_ff2", [DEPTH, 4 * D, D])
        self.ret_decay = self.din("ret_decay", [DEPTH, 8])
        self.ret_norm = self.din("ret_norm", [DEPTH, 256])
        self.init_ret = self.din("init_ret", [DEPTH, 2, H, HD, HD])
        self.out_ret = self.dout("out_ret", [8, DEPTH, 2, H, HD, HD])
        if "yacc" in self.debug:
            self.dbg_yacc = self.dout("dbg_yacc", [128, NT, 256])
        if "uT" in self.debug:
            self.dbg_uT = self.dout("dbg_uT", [128, 8, TOK])

    def alloc_global(self):
        nc = self.nc
        self.xres = self.sb("xres", [128, NT, D], F32)
        self.ident = self.sb("ident", [128, 128], F32)
        self.psum = self.stack.enter_context(nc.psum_tensor("psum", [128, 8, 512], F32))
        self.modT = self.sb("modT", [128, 48], F32)
        self.sc1p = self.sb("sc1p", [128, 8], F32)
        self.sc2p = self.sb("sc2p", [128, 8], F32)
        self.scT = self.sb("scT", [128, 8], F32)
        self.bmodT = self.sb("bmodT", [128, 48], F32)
        self.consts = self.sb("consts", [128, NCONST], F32)
        self.ident_bf = self.sb("ident_bf", [128, 128], BF16)
        self.keep = self.sb("keep", [128, 1], F32)
        self.g_bc = {"g1": self.sb("g1_bc", [128, D], F32), "g2": self.sb("g2_bc", [128, D], F32)}

    def load_consts(self):
        P = self.P
        xv = self.x_in.rearrange("(i p) d -> p i d", p=128)
        for q in range(4):
            P.dma("sp", lambda e, q=q: e.dma_start(out=self.xres[:, 4 * q:4 * q + 4, :], in_=xv[:, 4 * q:4 * q + 4, :]),
                  writes=[("xres", i) for i in range(4 * q, 4 * q + 4)])
        P.dma("sp", lambda e: e.dma_start(out=self.ident[:], in_=self.ident_in), writes=["ident"])
        P.dma("sp", lambda e: e.dma_start(out=self.consts[:], in_=self.consts_in), writes=["consts"])
        P.dma("sp", lambda e: e.dma_start(out=self.keep[:], in_=self.keep_in.partition_broadcast(128)), writes=["keep"])
        P.op("dve", lambda e: e.tensor_copy(self.ident_bf[:], self.ident[:]), reads=["ident"], writes=["ident_bf"])
        c8 = self.sb("c8", [8, 128], F32)
        P.dma("sp", lambda e: e.dma_start(out=c8[:], in_=self.cond), writes=["c8"])
        ps = self.psum
        P.op("pe", lambda e: e.transpose(ps[:, 0, 0:8], c8[:], self.ident[0:8, 0:8]), reads=["c8", "ident"], writes=["ps0"])
        P.op("act", lambda e: e.activation(self.scT[:], ps[:, 0, 0:8], AF.Silu), writes=["ps0", "scT"])

    def compute_mod(self, l):
        P, ps = self.P, self.psum
        m = self.mark()
        self.scbc = self.sb("scbc", [128, 8, 128], F32)
        P.op("dve", lambda e: e.tensor_copy(self.scbc[:], self.scT[:].unsqueeze(2).to_broadcast([128, 8, 128])),
             reads=["scT"], writes=["scbc"])
        b48 = self.sb("b48", [48, 128], F32)
        P.dma("sp", lambda e: e.dma_start(out=b48[:], in_=self.b_mod[l]), writes=["b48"])
        P.op("pe", lambda e: e.transpose(ps[:, 1, 0:48], b48[:], self.ident[0:48, 0:48]), reads=["b48", "ident"], writes=["ps1"])
        P.op("dve", lambda e: e.tensor_copy(self.bmodT[:], ps[:, 1, 0:48]), writes=["ps1", "bmodT"])
        wv = self.w_mod[l].rearrange("(k p) n -> p k n", p=128)
        wblk = [self.sb("wmodblk%d" % i, [128, 8, 512], F32) for i in range(2)]
        bb = self.sb("g_bb", [128, D], F32)
        for b in range(12):
            wb = wblk[b % 2]
            key = "wmodblk%d" % (b % 2)
            P.dma("sp", lambda e, b=b, wb=wb: e.dma_start(out=wb[:], in_=wv[:, :, 512 * b:512 * b + 512]), writes=[key])
            for jj in range(4):
                j = 4 * b + jj
                for k in range(8):
                    P.op("pe", lambda e, wb=wb, jj=jj, j=j, k=k: e.matmul(
                        ps[:, 2, j:j + 1], wb[:, k, 128 * jj:128 * jj + 128], self.scT[:, k:k + 1],
                        start=(k == 0), stop=(k == 7)), reads=[key, "scT"], writes=["ps2"])
            if b in (4, 5, 10, 11):
                which = "g1" if b < 6 else "g2"
                half = b % 2
                if half == 0:
                    off = 2048 if which == "g1" else 5120
                    bm = self.b_mod[l].rearrange("a b -> (a b)")[off:off + 1024].unsqueeze(0)
                    P.dma("sp", lambda e, bm=bm: e.dma_start(out=bb[:], in_=bm.partition_broadcast(128)), writes=["g_bb"])
                gt = self.g_bc[which]
                for k in range(8):
                    P.op("pe", lambda e, wb=wb, k=k: e.matmul(ps[:, 3, :], self.scbc[:, k, :], wb[:, k, :], start=(k == 0), stop=(k == 7)),
                         reads=[key, "scbc"], writes=["ps3"])
                P.op("dve", lambda e, gt=gt, half=half: e.tensor_tensor(
                    gt[:, 512 * half:512 * half + 512], ps[:, 3, :], bb[:, 512 * half:512 * half + 512], ALU.add),
                    reads=["g_bb"], writes=["ps3", which + "_bc"])
        P.op("dve", lambda e: e.tensor_tensor(self.modT[:], ps[:, 2, 0:48], self.bmodT[:], ALU.add), reads=["bmodT"], writes=["ps2", "modT"])
        P.op("dve", lambda e: e.tensor_scalar(self.sc1p[:], self.modT[:, 8:16], 1.0, None, ALU.add), reads=["modT"], writes=["sc1p"])
        P.op("dve", lambda e: e.tensor_scalar(self.sc2p[:], self.modT[:, 32:40], 1.0, None, ALU.add), reads=["modT"], writes=["sc2p"])
        self.release(m)

    def ln_mod_T(self, tile, dst_fn, scp, sh_off, tmp, dst_keys, par=0):
        P, ps = self.P, self.psum
        stats, mv, rstd, xhat = tmp
        sf = str(par)
        b0 = 4 + 2 * par
        xk = ("xres", tile)
        src = self.xres[:, tile, :]
        for hh in range(2):
            P.op("dve", lambda e, hh=hh: e.bn_stats(stats[:, hh, :], src[:, 512 * hh:512 * hh + 512]), reads=[xk], writes=["lnstats" + sf])
        P.op("dve", lambda e: e.bn_aggr(mv[:], stats[:]), reads=["lnstats" + sf], writes=["lnmv" + sf])
        P.op("act", lambda e: e.activation(rstd[:], mv[:, 1:2], AF.Ln, bias=LN_EPS, scale=1.0), reads=["lnmv" + sf], writes=["lnrstd" + sf])
        P.op("act", lambda e: e.activation(rstd[:], rstd[:], AF.Exp, scale=-0.5), reads=["lnrstd" + sf], writes=["lnrstd" + sf])
        P.op("dve", lambda e: e.tensor_scalar(xhat[:], src, mv[:, 0:1], rstd[:, 0:1], ALU.subtract, ALU.mult),
             reads=[xk, "lnmv" + sf, "lnrstd" + sf], writes=["xhat" + sf])
        for k in range(8):
            b = b0 + (k // 4)
            P.op("pe", lambda e, k=k, b=b: e.transpose(ps[:, b, 128 * (k % 4):128 * (k % 4) + 128], xhat[:, 128 * k:128 * k + 128], self.ident[:]),
                 reads=["xhat" + sf, "ident"], writes=["ps%d" % b])
        for k in range(8):
            b = b0 + (k // 4)
            P.op("act", lambda e, k=k, b=b: e.activation(dst_fn(k), ps[:, b, 128 * (k % 4):128 * (k % 4) + 128], AF.Identity,
                                                       bias=self.modT[:, sh_off + k:sh_off + k + 1], scale=scp[:, k:k + 1]),
                 reads=["modT", "sc1p", "sc2p"], writes=["ps%d" % b] + dst_keys)

    def dump(self, name, ap, keys, shape):
        want = self.debug.get("dump", ())
        if name not in want or ("dmp_" + name) in self.outs:
            return
        P = self.P
        out = self.dout("dmp_" + name, shape)
        m = self.mark()
        tmp = self.sb("dmp_" + name, shape, F32)
        P.op("dve", lambda e: e.tensor_copy(tmp[:], ap), reads=keys, writes=["dmp_" + name])
        P.dma("sp", lambda e: e.dma_start(out=out, in_=tmp[:]), reads=["dmp_" + name])
        self.release(m)

    def act_sigmoid(self, out, in_, reads, writes):
        P = self.P
        P.op("act", lambda e: e.activation(out, in_, AF.Exp, scale=-1.0), reads=reads, writes=writes)
        P.op("act", lambda e: e.activation(out, out, AF.Ln, bias=1.0, scale=1.0), reads=writes, writes=writes)
        P.op("act", lambda e: e.activation(out, out, AF.Exp, scale=-1.0), reads=writes, writes=writes)

    def cst(self, name, rows=64):
        o, w = CO[name]
        return self.consts[0:rows, o:o + w]

    def load_w(self, name, src, ncols, kchunks=8):
        P = self.P
        wb = self.sb(name, [128, kchunks, ncols], BF16)
        wv = src.rearrange("(k p) n -> p k n", p=128)
        step = 2 if kchunks % 2 == 0 else 1
        for k0 in range(0, kchunks, step):
            P.dma("pool", lambda e, k0=k0: e.dma_start(out=wb[:, k0:k0 + step, :], in_=wv[:, k0:k0 + step, :]),
                  writes=[(name, k0 // step)])
        return wb, [(name, i) for i in range(kchunks // step)]

    def emit_out_tile(self, t, osb, osb_keys, bank):
        P, ps = self.P, self.psum
        sel = self.cst("SEL").rearrange("p (q n) -> p q n", n=128)
        pk = "ps%d" % bank
        for z in range(2):
            tile = t if z == 0 else NT - 1 - t
            for c in range(2):
                P.op("pe", lambda e, z=z, c=c: e.matmul(ps[:, bank, 256 * z:256 * z + 256], sel[:, 2 * z + c, :], osb[:, c, z, :],
                                                         start=(c == 0), stop=(c == 1)),
                     reads=["consts"] + osb_keys, writes=[pk])
            yk = ("yacc", tile)
            if t < NT // 2:
                P.op("act", lambda e, z=z, tile=tile: e.activation(self.yacc[:, tile, :], ps[:, bank, 256 * z:256 * z + 256], AF.Copy),
                     writes=[pk, yk])
            else:
                P.op("dve", lambda e, z=z, tile=tile: e.tensor_tensor(self.yacc[:, tile, :], ps[:, bank, 256 * z:256 * z + 256],
                                                                     self.yacc[:, tile, :], ALU.add),
                     writes=[pk, yk])

    def mixer_ret(self, l, first):
        P, ps = self.P, self.psum
        m0 = self.mark()
        wb, wkeys = self.load_w("wC", self.wC[l], 1536)
        m1 = self.mark()
        lg = self.sb("ret_lg", [128, 8], F32)
        P.dma("sp", lambda e: e.dma_start(out=lg[:], in_=self.ret_decay[l:l + 1, :].partition_broadcast(128)), writes=["ret_lg"])
        P.op("act", lambda e: e.activation(lg[:], lg[:], AF.Exp), reads=["ret_lg"], writes=["ret_lg"])
        P.op("dve", lambda e: e.tensor_scalar(lg[:], lg[:], -1.0, None, ALU.mult), reads=["ret_lg"], writes=["ret_lg"])
        gs = self.sb("ret_gs", [128, 8], F32)
        ga = self.sb("ret_ga", [128, 8], F32)
        pidx = self.consts[:, CO["PIDX"][0]:CO["PIDX"][0] + 1]
        npidx = self.consts[:, CO["NPIDX"][0]:CO["NPIDX"][0] + 1]
        P.op("act", lambda e: e.activation(gs[:], lg[:], AF.Exp, scale=npidx), reads=["ret_lg", "consts"], writes=["ret_gs"])
        P.op("act", lambda e: e.activation(ga[:], lg[:], AF.Exp, scale=pidx), reads=["ret_lg", "consts"], writes=["ret_ga"])
        G64 = self.sb("ret_g64", [128, 4], F32)
        GAM = self.sb("ret_gam", [128, 4], F32)
        GAMI = self.sb("ret_gami", [128, 4], F32)
        for hq in range(2):
            rows = slice(64 * hq, 64 * hq + 64)
            P.op("act", lambda e, rows=rows, hq=hq: e.activation(G64[rows, :], lg[rows, hq::2], AF.Exp, scale=64.0), reads=["ret_lg"], writes=["ret_g64"])
            P.op("act", lambda e, rows=rows, hq=hq: e.activation(GAM[rows, :], lg[rows, hq::2], AF.Exp, scale=1.0), reads=["ret_lg"], writes=["ret_gam"])
            P.op("act", lambda e, rows=rows, hq=hq: e.activation(GAMI[rows, :], lg[rows, hq::2], AF.Exp, scale=-1.0), reads=["ret_lg"], writes=["ret_gami"])
        S = self.sb("ret_S", [128, 4, 64], F32)
        Sb = self.sb("ret_Sb", [128, 4, 64], BF16)
        Sout = self.sb("ret_Sout", [128, 4, 64], F32)
        P.dma("sp", lambda e: e.dma_start(out=S[:], in_=self.init_ret[l].rearrange("z (hp hq) d e -> (hq d) (z hp) e", hq=2)), writes=["ret_S"])
        P.op("dve", lambda e: e.tensor_tensor(S[:], S[:], GAM[:].unsqueeze(2).to_broadcast([128, 4, 64]), ALU.mult), reads=["ret_gam"], writes=["ret_S"])
        P.op("act", lambda e: e.activation(Sb[:], S[:], AF.Copy), reads=["ret_S"], writes=["ret_Sb"])
        rc = [self.sb("ret_rc%d" % i, [128, 2, 2, 128], F32) for i in range(2)]
        ta = self.sb("ret_ta", [128, 2, 128], F32)
        tb = self.sb("ret_tb", [128, 2, 128], F32)
        qT = self.sb("ret_qT", [128, 2, 2, 2, 128], BF16)
        P.op("dve", lambda e: e.memset(qT[:], 0.0), writes=["ret_qT"])
        kT = self.sb("ret_kT", [128, 2, 2, 128], BF16)
        vT = self.sb("ret_vT", [128, 2, 2, 128], BF16)
        ktm = self.sb("ret_ktm", [64, 8, 64], BF16)
        vtm = self.sb("ret_vtm", [64, 8, 64], BF16)
        pm = self.sb("ret_pm", [64, 8, 64], BF16)
        osb = self.sb("ret_osb", [64, 2, 2, 256], F32)
        tmpS = self.sb("ret_tmpS", [128, 4, 64], F32)
        incl = self.cst("INCL")
        psb = ps.bitcast(BF16) if False else None

        def bfview(bank):
            return ps[:, bank, :].bitcast(BF16)

        stop = self.debug.get("stop", 99)
        for t in range(NT if stop > 1 else 0):
            tiles = (t, NT - 1 - t)
            r = rc[t % 2]
            rk = "ret_rc%d" % (t % 2)
            for z in range(2):
                P.dma("sp", lambda e, z=z, r=r: e.dma_start(out=r[:, z, :, :], in_=self.rope_in[tiles[z]]), writes=[rk])
            def proj(j, bank, z):
                tok0 = 2 + 128 * tiles[z]
                for k in range(8):
                    P.op("pe", lambda e, j=j, k=k, z=z, tok0=tok0, bank=bank: e.matmul(
                        ps[:, bank, 128 * z:128 * z + 128], wb[:, k, 128 * j:128 * j + 128], self.uT[:, k, tok0:tok0 + 128],
                        start=(k == 0), stop=(k == 7)),
                        reads=wkeys + [("uT", tiles[z])], writes=["ps%d" % bank])
            for which, dst, dkey in ((0, qT, "ret_qT"), (1, kT, "ret_kT")):
                for hp in range(2):
                    j = 2 * which + hp
                    for z in range(2):
                        proj(j, 0, z)
                        proj(6 + j, 1, z)
                    scale = 0.125 if which == 0 else 1.0
                    P.op("dve", lambda e, scale=scale, r=r: e.scalar_tensor_tensor(
                        ta[:], ps[:, 0, 0:256].rearrange("p (z n) -> p z n", z=2), scale, r[:, :, 0, :], ALU.mult, ALU.mult),
                        reads=[rk], writes=["ps0", "ret_ta"])
                    P.op("dve", lambda e, scale=scale, r=r: e.scalar_tensor_tensor(
                        tb[:], ps[:, 1, 0:256].rearrange("p (z n) -> p z n", z=2), scale, r[:, :, 1, :], ALU.mult, ALU.mult),
                        reads=[rk], writes=["ps1", "ret_tb"])
                    if which == 1:
                        P.op("dve", lambda e, dst=dst, hp=hp: e.tensor_tensor(dst[:, hp, 0, :], ta[:, 0, :], tb[:, 0, :], ALU.add),
                             reads=["ret_ta", "ret_tb"], writes=[dkey])
                        P.op("dve", lambda e, dst=dst, hp=hp: e.tensor_tensor(dst[:, hp, 1, ::-1], ta[:, 1, :], tb[:, 1, :], ALU.add),
                             reads=["ret_ta", "ret_tb"], writes=[dkey])
                    else:
                        for hq in range(2):
                            rws = slice(64 * hq, 64 * hq + 64)
                            P.op("dve", lambda e, dst=dst, hp=hp, hq=hq, rws=rws: e.tensor_tensor(
                                dst[rws, hp, hq, 0, :], ta[rws, 0, :], tb[rws, 0, :], ALU.add),
                                reads=["ret_ta", "ret_tb"], writes=[dkey])
                            P.op("dve", lambda e, dst=dst, hp=hp, hq=hq, rws=rws: e.tensor_tensor(
                                dst[rws, hp, hq, 1, ::-1], ta[rws, 1, :], tb[rws, 1, :], ALU.add),
                                reads=["ret_ta", "ret_tb"], writes=[dkey])
            for hp in range(2):
                bank = hp
                for z in range(2):
                    proj(4 + hp, bank, z)
                P.op("act", lambda e, hp=hp, bank=bank: e.activation(vT[:, hp, 0, :], ps[:, bank, 0:128], AF.Copy), writes=["ps%d" % bank, "ret_vT"])
                P.op("act", lambda e, hp=hp, bank=bank: e.activation(vT[:, hp, 1, ::-1], ps[:, bank, 128:256], AF.Copy), writes=["ps%d" % bank, "ret_vT"])
            self.dump("ret_qT", qT[:], ["ret_qT"], [128, 2, 2, 2, 128])
            self.dump("ret_kT", kT[:], ["ret_kT"], [128, 2, 2, 128])
            self.dump("ret_vT", vT[:], ["ret_vT"], [128, 2, 2, 128])
            for c in range(2 if stop > 2 else 0):
                cs = slice(64 * c, 64 * c + 64)
                for (src, skey, bank) in ((kT, "ret_kT", 2), (vT, "ret_vT", 3)):
                    bv = bfview(bank)
                    for z in range(2):
                        for hp in range(2):
                            col = (4 * z + 2 * hp) * 64
                            P.op("pe", lambda e, src=src, z=z, hp=hp, col=col, bv=bv: e.transpose(
                                bv[0:64, col:col + 128], src[:, hp, z, cs], self.ident_bf[:]),
                                reads=[skey, "ident_bf"], writes=["ps%d" % bank])
                P.op("act", lambda e: e.activation(ktm[:], bfview(2)[0:64, 0:512].rearrange("p (u d) -> p u d", d=64), AF.Copy),
                     writes=["ps2", "ret_ktm"])
                P.op("dve", lambda e: e.tensor_tensor(vtm[:], bfview(3)[0:64, 0:512].rearrange("p (u d) -> p u d", d=64),
                                                      gs[0:64, :].unsqueeze(2).to_broadcast([64, 8, 64]), ALU.mult),
                     reads=["ret_gs"], writes=["ps3", "ret_vtm"])
                self.dump("ret_ktm", ktm[:], ["ret_ktm"], [64, 8, 64])
                self.dump("ret_vtm", vtm[:], ["ret_vtm"], [64, 8, 64])
                if stop <= 3:
                    continue
                for z in range(2):
                    for h in range(4):
                        hp, hq = h // 2, h % 2
                        rows = slice(64 * hq, 64 * hq + 64)
                        u = 4 * z + h
                        P.op("pe", lambda e, z=z, hp=hp, hq=hq, u=u: e.matmul(
                            ps[0:64, 4, 64 * u:64 * u + 64], kT[:, hp, z, cs], qT[:, hp, hq, z, cs], start=True, stop=True),
                            reads=["ret_kT", "ret_qT"], writes=["ps4"])
                if self.debug.get("sub") == "a":
                    continue
                P.op("dve", lambda e: e.tensor_tensor(pm[:], ps[0:64, 4, :].rearrange("p (u l) -> p u l", l=64),
                                                      incl.unsqueeze(1).to_broadcast([64, 8, 64]), ALU.mult),
                     reads=["consts"], writes=["ps4", "ret_pm"])
                self.dump("ret_pm", pm[:], ["ret_pm"], [64, 8, 64])
                if stop <= 4:
                    continue
                for z in range(2):
                    for h in range(4):
                        hp, hq = h // 2, h % 2
                        rows = slice(64 * hq, 64 * hq + 64)
                        u = 4 * z + h
                        P.op("pe", lambda e, u=u: e.matmul(ps[0:64, 5, 64 * u:64 * u + 64], pm[:, u, :], vtm[:, u, :], start=True, stop=False),
                             reads=["ret_pm", "ret_vtm"], writes=["ps5"])
                        P.op("pe", lambda e, z=z, hp=hp, hq=hq, u=u: e.matmul(
                            ps[0:64, 5, 64 * u:64 * u + 64], qT[:, hp, hq, z, cs], Sb[:, 2 * z + hp, :], start=False, stop=True),
                            reads=["ret_qT", "ret_Sb"], writes=["ps5"])
                P.op("dve", lambda e, c=c: e.tensor_tensor(
                    osb[:, c, :, :].rearrange("p z (h e) -> p (z h) e", e=64), ps[0:64, 5, :].rearrange("p (u e) -> p u e", e=64),
                    ga[0:64, :].unsqueeze(2).to_broadcast([64, 8, 64]), ALU.mult),
                    reads=["ret_ga"], writes=["ps5", ("ret_osb", c)])
                self.dump("ret_osb", osb[:, 0, :, :], [("ret_osb", 0)], [64, 2, 256])
                if stop <= 5:
                    continue
                for z in range(2):
                    for h in range(4):
                        hp, hq = h // 2, h % 2
                        rows = slice(64 * hq, 64 * hq + 64)
                        u = 4 * z + h
                        col = (2 * z + hp) * 64
                        P.op("pe", lambda e, rows=rows, u=u, col=col: e.matmul(ps[rows, 6, col:col + 64], ktm[:, u, :], vtm[:, u, :], start=True, stop=True),
                             reads=["ret_ktm", "ret_vtm"], writes=["ps6"])
                P.op("dve", lambda e: e.tensor_tensor(tmpS[:], ps[:, 6, 0:256].rearrange("p (a e) -> p a e", e=64), S[:], ALU.add),
                     reads=["ret_S"], writes=["ps6", "ret_tmpS"])
                P.op("dve", lambda e: e.tensor_tensor(S[:], tmpS[:], G64[:].unsqueeze(2).to_broadcast([128, 4, 64]), ALU.mult),
                     reads=["ret_tmpS", "ret_g64"], writes=["ret_S"])
                self.dump("ret_S1", S[:], ["ret_S"], [128, 4, 64])
                if not (t % 2 == 1 and c == 1):
                    P.op("act", lambda e: e.activation(Sb[:], S[:], AF.Copy), reads=["ret_S"], writes=["ret_Sb"])
            if stop > 6:
                self.emit_out_tile(t, osb, [("ret_osb", 0), ("ret_osb", 1)], 7)
            if t % 2 == 1 and stop > 7:
                P.op("dve", lambda e: e.tensor_tensor(Sout[:], S[:], GAMI[:].unsqueeze(2).to_broadcast([128, 4, 64]), ALU.mult),
                     reads=["ret_S", "ret_gami"], writes=["ret_Sout"])
                for z in range(2):
                    seg = (t - 1) // 2 if z == 0 else (NT - 1 - t) // 2
                    P.dma("sp", lambda e, z=z, seg=seg: e.dma_start(
                        out=self.out_ret[seg, l, z].rearrange("(hp hq) d e -> (hq d) hp e", hq=2), in_=Sout[:, 2 * z:2 * z + 2, :]),
                        reads=["ret_Sout"])
                P.op("dve", lambda e: e.tensor_scalar(S[:], S[:], self.keep[:, 0:1], None, ALU.mult), reads=["ret_S", "keep"], writes=["ret_S"])
                P.op("act", lambda e: e.activation(Sb[:], S[:], AF.Copy), reads=["ret_S"], writes=["ret_Sb"])
        if self.debug.get("yacc") == "ret":
            P.dma("sp", lambda e: e.dma_start(out=self.dbg_yacc, in_=self.yacc[:]), reads=[("yacc", i) for i in range(NT)])
        self.release(m1)
        if self.debug.get("post", True):
            self.post_simple(l, "ret", wb, wkeys, 1280, AF.Silu, self.ret_norm, True, 512, first)
        self.release(m0)

    def proj_fm(self, wb, wkeys, col, M, tiles, bank, halo=0):
        P, ps = self.P, self.psum
        W = 128 + 2 * halo
        for z in range(2):
            tok0 = 2 + 128 * tiles[z] - halo
            for k in range(8):
                P.op("pe", lambda e, k=k, z=z, tok0=tok0: e.matmul(
                    ps[0:M, bank, W * z:W * z + W], wb[:, k, col:col + M], self.uT[:, k, tok0:tok0 + W],
                    start=(k == 0), stop=(k == 7)),
                    reads=wkeys + [("uT", tiles[z])], writes=["ps%d" % bank])

    def evac_fm(self, dst_fn, bank, M, dkey, scale=1.0, W=128, eng="act", rows=None):
        P, ps = self.P, self.psum
        rs = slice(0, M) if rows is None else rows
        for z in range(2):
            src = ps[rs, bank, W * z:W * z + W]
            dst = dst_fn(z)
            if z == 1:
                dst = dst[:, ::-1]
            if eng == "act":
                P.op("act", lambda e, dst=dst, src=src: e.activation(dst, src, AF.Copy, scale=scale), writes=["ps%d" % bank, dkey])
            else:
                P.op("dve", lambda e, dst=dst, src=src: e.tensor_scalar(dst, src, scale, None, ALU.mult), writes=["ps%d" % bank, dkey])

    def tm_transposes(self, srcT, skey, cs, bank, col0):
        P, ps = self.P, self.psum
        if srcT.dtype == BF16:
            bv = ps[:, bank, :].bitcast(BF16)
            idn = self.ident_bf
        else:
            bv = ps[:, bank:bank + 2, :].rearrange("p a b -> p (a b)")
            idn = self.ident
        for z in range(2):
            for hp in range(2):
                col = col0 + (4 * z + 2 * hp) * 64
                P.op("pe", lambda e, z=z, hp=hp, col=col: e.transpose(bv[0:64, col:col + 128], srcT[:, hp, z, cs], idn[:]),
                     reads=[skey, "ident_bf", "ident"], writes=["ps%d" % bank, "ps%d" % (bank + (0 if srcT.dtype == BF16 else 1))])
        return bv[0:64, col0:col0 + 512].rearrange("p (u d) -> p u d", d=64)

    def mixer_mlstm(self, l, first):
        P, ps = self.P, self.psum
        SDT = F32 if self.debug.get("mlf32") else BF16
        m0 = self.mark()
        wb, wkeys = self.load_w("wA", self.wA[l], 1040)
        m1 = self.mark()
        incl = self.cst("INCL")
        ones = self.consts[0:64, CO["ONES"][0]:CO["ONES"][0] + 128]
        ib = self.sb("ml_ib", [128, 8], F32)
        fb = self.sb("ml_fb", [128, 8], F32)
        P.dma("sp", lambda e: e.dma_start(out=ib[:], in_=self.ml_ib[l:l + 1, :].partition_broadcast(128)), writes=["ml_ib"])
        P.dma("sp", lambda e: e.dma_start(out=fb[:], in_=self.ml_fb[l:l + 1, :].partition_broadcast(128)), writes=["ml_fb"])
        Cg = self.sb("ml_C", [128, 4, 65], F32)
        Cb = self.sb("ml_Cb", [128, 4, 65], SDT)
        Cout = self.sb("ml_Cout", [128, 4, 65], F32)
        tmpC = self.sb("ml_tmpC", [128, 4, 65], F32)
        mst = self.sb("ml_m", [8, 1], F32)
        msc = self.sb("ml_msc", [8, 4], F32)
        dg = self.sb("ml_dg", [8, 8], F32)
        esl = self.sb("ml_esl", [128, 4], F32)
        P.dma("sp", lambda e: e.dma_start(out=Cg[:, :, 0:64], in_=self.init_mC[l].rearrange("z (hp hq) d e -> (hq d) (z hp) e", hq=2)), writes=["ml_C"])
        P.dma("sp", lambda e: e.dma_start(out=Cg[:, :, 64:65], in_=self.init_mn[l].rearrange("z (hp hq) (d o) -> (hq d) (z hp) o", hq=2, o=1),
                                          allow_slow_non_contiguous=True), writes=["ml_C"])
        P.dma("sp", lambda e: e.dma_start(out=mst[:], in_=self.init_mm[l].rearrange("z (h o) -> (z h) o", o=1), allow_slow_non_contiguous=True), writes=["ml_m"])

        def bcast_units(src81, sign, dst_keys):
            P.op("dve", lambda e: e.tensor_scalar(dg[:], self.ident[0:8, 0:8], src81, None, ALU.mult), reads=["ident", "ml_m"], writes=["ml_dg"])
            P.op("pe", lambda e: e.matmul(ps[:, 7, 0:8], self.consts[0:8, CO["ONES"][0]:CO["ONES"][0] + 128], dg[:], start=True, stop=True),
                 reads=["consts", "ml_dg"], writes=["ps7"])
            for hq in range(2):
                rows = slice(64 * hq, 64 * hq + 64)
                P.op("act", lambda e, rows=rows, hq=hq: e.activation(esl[rows, :], ps[rows, 7, hq:8:2], AF.Exp, scale=sign), writes=["ps7", "ml_esl"])

        bcast_units(mst[:, 0:1], 1.0, None)
        P.op("dve", lambda e: e.tensor_tensor(Cg[:], Cg[:], esl[:].unsqueeze(2).to_broadcast([128, 4, 65]), ALU.mult), reads=["ml_esl", "ml_C"], writes=["ml_C"])
        P.op("act", lambda e: e.activation(Cb[:], Cg[:], AF.Copy), reads=["ml_C"], writes=["ml_Cb"])
        qT = self.sb("ml_qT", [128, 2, 2, 2, 128], SDT)
        P.op("dve", lambda e: e.memset(qT[:], 0.0), writes=["ml_qT"])
        kT = self.sb("ml_kT", [128, 2, 2, 128], SDT)
        vT = self.sb("ml_vT", [128, 2, 2, 128], SDT)
        gT = self.sb("ml_gT", [8, 2, 128], F32)
        ktm = self.sb("ml_ktm", [64, 8, 64], SDT)
        vaug = self.sb("ml_vaug", [64, 8, 65], SDT)
        pm = self.sb("ml_pm", [64, 8, 64], SDT)
        osb = self.sb("ml_osb", [64, 2, 2, 256], F32)
        gtm = self.sb("ml_gtm", [64, 2, 8], F32)
        li = self.sb("ml_li", [64, 8], F32)
        sp = self.sb("ml_sp", [64, 8], F32)
        lib = self.sb("ml_lib", [64, 8], F32)
        e1 = self.sb("ml_e1", [64, 8], F32)
        eb = self.sb("ml_eb", [64, 8], F32)
        wk = self.sb("ml_wk", [64, 8], F32)
        nbl = self.sb("ml_nbl", [64, 8], F32)
        ebls = self.sb("ml_ebls", [128, 4], F32)
        dn = self.sb("ml_dn", [64, 8], F32)
        for t in range(NT):
            tiles = (t, NT - 1 - t)
            for hp in range(2):
                self.proj_fm(wb, wkeys, 128 * hp, 128, tiles, 0)
                for hq in range(2):
                    rows = slice(64 * hq, 64 * hq + 64)
                    self.evac_fm(lambda z, hp=hp, hq=hq, rows=rows: qT[rows, hp, hq, z, :], 0, 128, "ml_qT", rows=rows, eng="dve" if hq else "act")
                self.proj_fm(wb, wkeys, 256 + 128 * hp, 128, tiles, 1)
                self.evac_fm(lambda z, hp=hp: kT[:, hp, z, :], 1, 128, "ml_kT", scale=0.125)
                self.proj_fm(wb, wkeys, 512 + 128 * hp, 128, tiles, 0)
                self.evac_fm(lambda z, hp=hp: vT[:, hp, z, :], 0, 128, "ml_vT", eng="dve")
            for z in range(2):
                tok0 = 2 + 128 * tiles[z]
                for k in range(8):
                    P.op("pe", lambda e, k=k, z=z, tok0=tok0: e.matmul(ps[0:8, 1, 128 * z:128 * z + 128], wb[:, k, 768 + 8 * z:768 + 8 * z + 8],
                                                                    self.uT[:, k, tok0:tok0 + 128], start=(k == 0), stop=(k == 7)),
                         reads=wkeys + [("uT", tiles[z])], writes=["ps1"])
            self.evac_fm(lambda z: gT[:, z, :], 1, 8, "ml_gT", eng="dve")
            for c in range(2):
                cs = slice(64 * c, 64 * c + 64)
                for z in range(2):
                    P.op("pe", lambda e, z=z: e.transpose(ps[0:64, 7, 8 * z:8 * z + 8], gT[:, z, cs], self.ident[0:8, 0:8]), reads=["ml_gT", "ident"], writes=["ps7"])
                P.op("dve", lambda e: e.tensor_copy(gtm[:], ps[0:64, 7, 0:16].rearrange("p (z g) -> p z g", g=8)), writes=["ps7", "ml_gtm"])
                P.op("dve", lambda e: e.tensor_tensor(li[:].rearrange("p (z h) -> p z h", h=4), gtm[:, :, 0:4],
                                                      ib[0:64, :].rearrange("p (z h) -> p z h", h=4), ALU.add), reads=["ml_gtm", "ml_ib"], writes=["ml_li"])
                P.op("dve", lambda e: e.tensor_tensor(sp[:].rearrange("p (z h) -> p z h", h=4), gtm[:, :, 4:8],
                                                      fb[0:64, :].rearrange("p (z h) -> p z h", h=4), ALU.add), reads=["ml_gtm", "ml_fb"], writes=["ml_sp"])
                P.op("act", lambda e: e.activation(sp[:], sp[:], AF.Exp, scale=-1.0), reads=["ml_sp"], writes=["ml_sp"])
                P.op("act", lambda e: e.activation(sp[:], sp[:], AF.Ln, bias=1.0, scale=1.0), reads=["ml_sp"], writes=["ml_sp"])
                P.op("pe", lambda e: e.matmul(ps[0:64, 7, 16:24], incl, sp[:], start=True, stop=True), reads=["consts", "ml_sp"], writes=["ps7"])
                P.op("pe", lambda e: e.matmul(ps[:, 7, 24:32], ones, sp[:], start=True, stop=True), reads=["consts", "ml_sp"], writes=["ps7"])
                P.op("dve", lambda e: e.tensor_tensor(lib[:], ps[0:64, 7, 16:24], li[:], ALU.add), reads=["ml_li"], writes=["ps7", "ml_lib"])
                P.op("act", lambda e: e.activation(eb[:], ps[0:64, 7, 16:24], AF.Exp, scale=-1.0), writes=["ps7", "ml_eb"])
                P.op("dve", lambda e: e.tensor_copy(nbl[:], ps[0:64, 7, 24:32]), writes=["ps7", "ml_nbl"])
                for hq in range(2):
                    rows = slice(64 * hq, 64 * hq + 64)
                    P.op("act", lambda e, rows=rows, hq=hq: e.activation(ebls[rows, :], ps[rows, 7, 24 + hq:32:2], AF.Exp, scale=-1.0), writes=["ps7", "ml_ebls"])
                P.op("act", lambda e: e.activation(e1[:], lib[:], AF.Exp), reads=["ml_lib"], writes=["ml_e1"])
                P.op("dve", lambda e: e.tensor_tensor(wk[:], lib[:], nbl[:], ALU.subtract), reads=["ml_lib", "ml_nbl"], writes=["ml_wk"])
                ktv = self.tm_transposes(kT, "ml_kT", cs, 2, 0)
                vtv = self.tm_transposes(vT, "ml_vT", cs, 2, 512)
                P.op("act", lambda e: e.activation(ktm[:], ktv, AF.Copy), writes=["ps2", "ps3", "ml_ktm"])
                P.op("dve", lambda e: e.tensor_tensor(vaug[:, :, 0:64], vtv, e1[:].unsqueeze(2).to_broadcast([64, 8, 64]), ALU.mult),
                     reads=["ml_e1"], writes=["ps2", "ps3", "ml_vaug"])
                P.op("dve", lambda e: e.tensor_copy(vaug[:, :, 64:65], e1[:].unsqueeze(2)), reads=["ml_e1"], writes=["ml_vaug"])
                for z in range(2):
                    for h in range(4):
                        hp, hq = h // 2, h % 2
                        u = 4 * z + h
                        P.op("pe", lambda e, z=z, hp=hp, hq=hq, u=u: e.matmul(ps[0:64, 4, 64 * u:64 * u + 64], kT[:, hp, z, cs], qT[:, hp, hq, z, cs], start=True, stop=True),
                             reads=["ml_kT", "ml_qT"], writes=["ps4"])
                P.op("dve", lambda e: e.tensor_tensor(pm[:], ps[0:64, 4, :].rearrange("p (u l) -> p u l", l=64), incl.unsqueeze(1).to_broadcast([64, 8, 64]), ALU.mult),
                     reads=["consts"], writes=["ps4", "ml_pm"])
                for z in range(2):
                    bank = 5 + z
                    for h in range(4):
                        hp, hq = h // 2, h % 2
                        u = 4 * z + h
                        P.op("pe", lambda e, u=u, h=h, bank=bank: e.matmul(ps[0:64, bank, 65 * h:65 * h + 65], pm[:, u, :], vaug[:, u, :], start=True, stop=False),
                             reads=["ml_pm", "ml_vaug"], writes=["ps%d" % bank])
                        P.op("pe", lambda e, z=z, hp=hp, hq=hq, h=h, bank=bank: e.matmul(ps[0:64, bank, 65 * h:65 * h + 65], qT[:, hp, hq, z, cs], Cb[:, 2 * z + hp, :], start=False, stop=True),
                             reads=["ml_qT", "ml_Cb"], writes=["ps%d" % bank])
                for z in range(2):
                    bank = 5 + z
                    o3 = ps[0:64, bank, 0:260].rearrange("p (h e) -> p h e", e=65)
                    P.op("dve", lambda e, z=z, o3=o3: e.tensor_tensor(dn[:, 4 * z:4 * z + 4], o3[:, :, 64], eb[:, 4 * z:4 * z + 4], ALU.mult),
                         reads=["ml_eb"], writes=["ps%d" % bank, "ml_dn"])
                P.op("act", lambda e: e.activation(dn[:], dn[:], AF.Abs), reads=["ml_dn"], writes=["ml_dn"])
                P.op("dve", lambda e: e.tensor_scalar(dn[:], dn[:], 1.0, None, ALU.max), reads=["ml_dn"], writes=["ml_dn"])
                P.op("dve", lambda e: e.reciprocal(dn[:], dn[:]), reads=["ml_dn"], writes=["ml_dn"])
                P.op("dve", lambda e: e.tensor_tensor(dn[:], dn[:], eb[:], ALU.mult), reads=["ml_dn", "ml_eb"], writes=["ml_dn"])
                for z in range(2):
                    bank = 5 + z
                    o3 = ps[0:64, bank, 0:260].rearrange("p (h e) -> p h e", e=65)
                    P.op("dve", lambda e, z=z, o3=o3, c=c: e.tensor_tensor(
                        osb[:, c, z, :].rearrange("p (h e) -> p h e", e=64), o3[:, :, 0:64],
                        dn[:, 4 * z:4 * z + 4].unsqueeze(2).to_broadcast([64, 4, 64]), ALU.mult),
                        reads=["ml_dn"], writes=["ps%d" % bank, ("ml_osb", c)])
                for z in range(2):
                    for h in range(4):
                        hp, hq = h // 2, h % 2
                        rows = slice(64 * hq, 64 * hq + 64)
                        u = 4 * z + h
                        col = (2 * z + hp) * 65
                        P.op("pe", lambda e, rows=rows, u=u, col=col: e.matmul(ps[rows, 3, col:col + 65], ktm[:, u, :], vaug[:, u, :], start=True, stop=True),
                             reads=["ml_ktm", "ml_vaug"], writes=["ps3"])
                P.op("dve", lambda e: e.tensor_tensor(tmpC[:], ps[:, 3, 0:260].rearrange("p (a e) -> p a e", e=65), Cg[:], ALU.add),
                     reads=["ml_C"], writes=["ps3", "ml_tmpC"])
                P.op("dve", lambda e: e.tensor_tensor(Cg[:], tmpC[:], ebls[:].unsqueeze(2).to_broadcast([128, 4, 65]), ALU.mult),
                     reads=["ml_tmpC", "ml_ebls"], writes=["ml_C"])
                if not (t % 2 == 1 and c == 1):
                    P.op("act", lambda e: e.activation(Cb[:], Cg[:], AF.Copy), reads=["ml_C"], writes=["ml_Cb"])
                P.op("pe", lambda e: e.transpose(ps[0:8, 7, 64:128], wk[:], self.ident[0:64, 0:64]), reads=["ml_wk", "ident"], writes=["ps7"])
                P.op("pe", lambda e: e.transpose(ps[0:8, 7, 128:192], nbl[:], self.ident[0:64, 0:64]), reads=["ml_nbl", "ident"], writes=["ps7"])
                P.op("dve", lambda e: e.tensor_reduce(msc[:, 0:1], ps[0:8, 7, 64:128], AX.X, ALU.max), writes=["ps7", "ml_msc"])
                P.op("dve", lambda e: e.tensor_tensor(msc[:, 1:2], mst[:], ps[0:8, 7, 128:129], ALU.subtract), reads=["ml_m"], writes=["ps7", "ml_msc"])
                P.op("dve", lambda e: e.tensor_tensor(mst[:], msc[:, 0:1], msc[:, 1:2], ALU.max), reads=["ml_msc"], writes=["ml_m"])
            self.emit_out_tile(t, osb, [("ml_osb", 0), ("ml_osb", 1)], 7)
            if t % 2 == 1:
                bcast_units(mst[:, 0:1], -1.0, None)
                P.op("dve", lambda e: e.tensor_tensor(Cout[:], Cg[:], esl[:].unsqueeze(2).to_broadcast([128, 4, 65]), ALU.mult),
                     reads=["ml_C", "ml_esl"], writes=["ml_Cout"])
                for z in range(2):
                    seg = (t - 1) // 2 if z == 0 else (NT - 1 - t) // 2
                    P.dma("sp", lambda e, z=z, seg=seg: e.dma_start(
                        out=self.out_mC[seg, l, z].rearrange("(hp hq) d e -> (hq d) hp e", hq=2), in_=Cout[:, 2 * z:2 * z + 2, 0:64]), reads=["ml_Cout"])
                    P.dma("sp", lambda e, z=z, seg=seg: e.dma_start(
                        out=self.out_mn[seg, l, z].rearrange("(hp hq) (d o) -> (hq d) hp o", hq=2, o=1), in_=Cout[:, 2 * z:2 * z + 2, 64:65],
                        allow_slow_non_contiguous=True), reads=["ml_Cout"])
                    P.dma("sp", lambda e, z=z, seg=seg: e.dma_start(
                        out=self.out_mm[seg, l, z].rearrange("(h o) -> h o", o=1), in_=mst[4 * z:4 * z + 4, :], allow_slow_non_contiguous=True), reads=["ml_m"])
                P.op("dve", lambda e: e.tensor_scalar(Cg[:], Cg[:], self.keep[:, 0:1], None, ALU.mult), reads=["ml_C", "keep"], writes=["ml_C"])
                P.op("dve", lambda e: e.tensor_scalar(mst[:], mst[:], self.keep[0:8, 0:1], None, ALU.mult), reads=["ml_m", "keep"], writes=["ml_m"])
                P.op("act", lambda e: e.activation(Cb[:], Cg[:], AF.Copy), reads=["ml_C"], writes=["ml_Cb"])
        if self.debug.get("yacc") == "mlstm":
            P.dma("sp", lambda e: e.dma_start(out=self.dbg_yacc, in_=self.yacc[:]), reads=[("yacc", i) for i in range(NT)])
        self.release(m1)
        if self.debug.get("post", True):
            self.post_simple(l, "ml", wb, wkeys, 784, AF.Sigmoid, self.ml_norm, True, 0, first)
        self.release(m0)

    def neumann_inverse(self, pfx, Pm, Qm, Rm, banks, keys=None):
        P, ps = self.P, self.psum
        bP, bQ, bR = banks
        kP, kQ, kR = keys if keys is not None else (pfx + "P", pfx + "Q", pfx + "R")
        ident64 = self.ident[0:64, 0:64]
        for u in range(8):
            P.op("pe", lambda e, u=u: e.transpose(ps[0:64, bQ, 64 * u:64 * u + 64], Pm[:, u, :], ident64), reads=[kP, "ident"], writes=["ps%d" % bQ])
        P.op("act", lambda e: e.activation(Qm[:], ps[0:64, bQ, :].rearrange("p (u l) -> p u l", l=64), AF.Copy), writes=["ps%d" % bQ, kQ])
        P.op("dve", lambda e: e.tensor_tensor(Rm[:], Pm[:], ident64.unsqueeze(1).to_broadcast([64, 8, 64]), ALU.add), reads=[kP, "ident"], writes=[kR])
        for lvl in range(5):
            last = lvl == 4
            if not last:
                for u in range(8):
                    P.op("pe", lambda e, u=u: e.matmul(ps[0:64, bP, 64 * u:64 * u + 64], Qm[:, u, :], Pm[:, u, :], start=True, stop=True),
                         reads=[kP, kQ], writes=["ps%d" % bP])
            for u in range(8):
                P.op("pe", lambda e, u=u: e.matmul(ps[0:64, bQ, 64 * u:64 * u + 64], Pm[:, u, :], Qm[:, u, :], start=True, stop=True),
                     reads=[kP, kQ], writes=["ps%d" % bQ])
            if not last:
                P.op("dve", lambda e: e.tensor_copy(Pm[:], ps[0:64, bP, :].rearrange("p (u l) -> p u l", l=64)), writes=["ps%d" % bP, kP])
            P.op("act", lambda e: e.activation(Qm[:], ps[0:64, bQ, :].rearrange("p (u l) -> p u l", l=64), AF.Copy), writes=["ps%d" % bQ, kQ])
            for u in range(8):
                P.op("pe", lambda e, u=u: e.matmul(ps[0:64, bR, 64 * u:64 * u + 64], Qm[:, u, :], Rm[:, u, :], start=True, stop=True),
                     reads=[kQ, kR], writes=["ps%d" % bR])
            P.op("dve", lambda e: e.tensor_tensor(Rm[:], ps[0:64, bR, :].rearrange("p (u l) -> p u l", l=64), Rm[:], ALU.add),
                 reads=[kR], writes=["ps%d" % bR, kR])

    def mixer_delta(self, l, first):
        P, ps = self.P, self.psum
        m0 = self.mark()
        wb, wkeys = self.load_w("wB", self.wB[l], 1040)
        m1 = self.mark()
        incl = self.cst("INCL")
        strict = self.cst("STRICT")
        ones64 = self.consts[0:64, CO["ONES"][0]:CO["ONES"][0] + 64]
        ones128 = self.consts[0:64, CO["ONES"][0]:CO["ONES"][0] + 128]
        blk = self.consts[:, CO["BLK"][0]:CO["BLK"][0] + 128]
        ident64 = self.ident[0:64, 0:64]
        cw = self.sb("dl_cw", [128, 6, 5], F32)
        P.dma("sp", lambda e: e.dma_start(out=cw[:], in_=self.dl_conv[l]), writes=["dl_cw"])
        Au = self.sb("dl_A", [128, 8], F32)
        dtb = self.sb("dl_dtb", [128, 8], F32)
        P.dma("sp", lambda e: e.dma_start(out=Au[:], in_=self.dl_alog[l:l + 1, :].partition_broadcast(128)), writes=["dl_A"])
        P.dma("sp", lambda e: e.dma_start(out=dtb[:], in_=self.dl_dtb[l:l + 1, :].partition_broadcast(128)), writes=["dl_dtb"])
        P.op("act", lambda e: e.activation(Au[:], Au[:], AF.Exp), reads=["dl_A"], writes=["dl_A"])
        S = self.sb("dl_S", [128, 4, 64], F32)
        Sb = self.sb("dl_Sb", [128, 4, 64], BF16)
        Sout = self.sb("dl_Sout", [128, 4, 64], F32)
        tmpS = self.sb("dl_tmpS", [128, 4, 64], F32)
        P.dma("sp", lambda e: e.dma_start(out=S[:], in_=self.init_delta[l].rearrange("z (hp hq) d e -> (hq d) (z hp) e", hq=2)), writes=["dl_S"])
        P.op("act", lambda e: e.activation(Sb[:], S[:], AF.Copy), reads=["dl_S"], writes=["dl_Sb"])
        pre = self.sb("dl_pre", [128, 2, 132], F32)
        acc = self.sb("dl_acc", [128, 2, 128], F32)
        sq = self.sb("dl_sq", [128, 2, 128], F32)
        rn = self.sb("dl_rn", [128, 2, 128], F32)
        qTp = self.sb("dl_qTp", [128, 2, 2, 2, 128], BF16)
        kTp = self.sb("dl_kTp", [128, 2, 2, 2, 128], BF16)
        P.op("dve", lambda e: e.memset(qTp[:], 0.0), writes=["dl_qTp"])
        P.op("dve", lambda e: e.memset(kTp[:], 0.0), writes=["dl_kTp"])
        kT = self.sb("dl_kT", [128, 2, 2, 128], BF16)
        vT = self.sb("dl_vT", [128, 2, 2, 128], BF16)
        gT = self.sb("dl_gT", [8, 2, 128], F32)
        gtm = self.sb("dl_gtm", [64, 4, 8], F32)
        beta2 = self.sb("dl_beta", [64, 2, 8], F32)
        nbeta2 = self.sb("dl_nbeta", [64, 2, 8], F32)
        ng2 = self.sb("dl_ng", [64, 2, 8], F32)
        ngc2 = self.sb("dl_ngc", [64, 2, 8], F32)
        eg2 = self.sb("dl_eg", [64, 2, 8], F32)
        eglt2 = self.sb("dl_eglt", [64, 2, 8], F32)
        egls2 = self.sb("dl_egls", [128, 2, 4], F32)
        dgm = self.sb("dl_dgm", [64, 8, 64], F32)
        dT = self.sb("dl_dT", [64, 8, 64], F32)
        dTs = self.sb("dl_dTs", [64, 8, 64], F32)
        Pm = self.sb("dl_P", [64, 8, 64], F32)
        Qm = self.sb("dl_Q", [64, 8, 64], F32)
        Rm = self.sb("dl_R", [64, 8, 64], F32)
        qkd = self.sb("dl_qkd", [64, 8, 64], BF16)
        ktm = self.sb("dl_ktm", [64, 8, 64], BF16)
        vtm = self.sb("dl_vtm", [64, 8, 64], F32)
        kd = self.sb("dl_kd", [64, 8, 64], BF16)
        rr = self.sb("dl_r", [64, 8, 64], F32)
        vnew = self.sb("dl_vnew", [64, 8, 64], BF16)
        vnf = self.sb("dl_vnf", [64, 8, 64], F32)
        t1 = self.sb("dl_t1", [64, 8, 64], F32)
        osb = self.sb("dl_osb", [64, 2, 2, 256], F32)
        for t in range(NT):
            tiles = (t, NT - 1 - t)
            edge = slice(0, 2) if t % 2 == 0 else slice(130, 132)
            for j in range(6):
                bank = j % 2
                self.proj_fm(wb, wkeys, 128 * j, 128, tiles, bank, halo=2)
                self.evac_fm(lambda z: pre[:, z, :], bank, 128, "dl_pre", W=132, eng="act")
                P.op("dve", lambda e: e.tensor_scalar(pre[:, :, edge], pre[:, :, edge], self.keep[:, 0:1], None, ALU.mult), reads=["dl_pre", "keep"], writes=["dl_pre"])
                for z in range(2):
                    eng = "dve"
                    for k in range(5):
                        wk_ = cw[:, j, k:k + 1] if z == 0 else cw[:, j, 4 - k:5 - k]
                        if k == 0:
                            P.op(eng, lambda e, z=z, wk_=wk_: e.tensor_scalar(acc[:, z, :], pre[:, z, 0:128], wk_, None, ALU.mult),
                                 reads=["dl_pre", "dl_cw"], writes=[("dl_acc", z)])
                        else:
                            P.op(eng, lambda e, z=z, k=k, wk_=wk_: e.scalar_tensor_tensor(acc[:, z, :], pre[:, z, k:k + 128], wk_, acc[:, z, :], ALU.mult, ALU.add),
                                 reads=["dl_pre", "dl_cw", ("dl_acc", z)], writes=[("dl_acc", z)])
                akeys = [("dl_acc", 0), ("dl_acc", 1)]
                self.act_sigmoid(sq[:], acc[:], akeys, ["dl_sq"])
                if j >= 4:
                    P.op("dve", lambda e, j=j: e.tensor_tensor(vT[:, j - 4, :, :], acc[:], sq[:], ALU.mult), reads=akeys + ["dl_sq"], writes=["dl_vT"])
                    continue
                P.op("dve", lambda e: e.tensor_tensor(acc[:], acc[:], sq[:], ALU.mult), reads=akeys + ["dl_sq"], writes=akeys)
                P.op("act", lambda e: e.activation(sq[:], acc[:], AF.Square), reads=akeys, writes=["dl_sq"])
                P.op("pe", lambda e: e.matmul(ps[:, 2, 0:256], blk, sq[:].rearrange("p z n -> p (z n)"), start=True, stop=True), reads=["consts", "dl_sq"], writes=["ps2"])
                P.op("act", lambda e: e.activation(rn[:], ps[:, 2, 0:256].rearrange("p (z n) -> p z n", z=2), AF.Ln, bias=1e-6, scale=1.0), writes=["ps2", "dl_rn"])
                P.op("act", lambda e: e.activation(rn[:], rn[:], AF.Exp, scale=-0.5), reads=["dl_rn"], writes=["dl_rn"])
                hp = j % 2
                if j < 2:
                    for hq in range(2):
                        rows = slice(64 * hq, 64 * hq + 64)
                        P.op("dve", lambda e, rows=rows, hp=hp, hq=hq: e.scalar_tensor_tensor(qTp[rows, hp, hq, :, :], acc[rows, :, :], 0.125, rn[rows, :, :], ALU.mult, ALU.mult),
                             reads=akeys + ["dl_rn"], writes=["dl_qTp"])
                else:
                    P.op("dve", lambda e, hp=hp: e.tensor_tensor(kT[:, hp, :, :], acc[:], rn[:], ALU.mult), reads=akeys + ["dl_rn"], writes=["dl_kT"])
                    for hq in range(2):
                        rows = slice(64 * hq, 64 * hq + 64)
                        P.op("pool", lambda e, rows=rows, hp=hp, hq=hq: e.tensor_copy(kTp[rows, hp, hq, :, :], kT[rows, hp, :, :]), reads=["dl_kT"], writes=["dl_kTp"])
            for z in range(2):
                tok0 = 2 + 128 * tiles[z]
                for k in range(8):
                    P.op("pe", lambda e, k=k, z=z, tok0=tok0: e.matmul(ps[0:8, 1, 128 * z:128 * z + 128], wb[:, k, 768 + 8 * z:768 + 8 * z + 8],
                                                                    self.uT[:, k, tok0:tok0 + 128], start=(k == 0), stop=(k == 7)),
                         reads=wkeys + [("uT", tiles[z])], writes=["ps1"])
            self.evac_fm(lambda z: gT[:, z, :], 1, 8, "dl_gT", eng="dve")
            for c in range(2):
                for z in range(2):
                    q_ = 2 * c + z
                    P.op("pe", lambda e, z=z, c=c, q_=q_: e.transpose(ps[0:64, 7, 8 * q_:8 * q_ + 8], gT[:, z, 64 * c:64 * c + 64], self.ident[0:8, 0:8]),
                         reads=["dl_gT", "ident"], writes=["ps7"])
            P.op("dve", lambda e: e.tensor_copy(gtm[:], ps[0:64, 7, 0:32].rearrange("p (q g) -> p q g", g=8)), writes=["ps7", "dl_gtm"])
            b4 = beta2[:].rearrange("p c (z h) -> p (c z) h", h=4)
            P.op("act", lambda e: e.activation(b4, gtm[:, :, 0:4], AF.Exp, scale=-1.0), reads=["dl_gtm"], writes=["dl_beta"])
            P.op("act", lambda e: e.activation(beta2[:], beta2[:], AF.Ln, bias=1.0, scale=1.0), reads=["dl_beta"], writes=["dl_beta"])
            P.op("act", lambda e: e.activation(beta2[:], beta2[:], AF.Exp, scale=-1.0), reads=["dl_beta"], writes=["dl_beta"])
            P.op("dve", lambda e: e.tensor_scalar(nbeta2[:], beta2[:], -1.0, None, ALU.mult), reads=["dl_beta"], writes=["dl_nbeta"])
            for c in range(2):
                P.op("dve", lambda e, c=c: e.tensor_tensor(ng2[:, c, :].rearrange("p (z h) -> p z h", h=4), gtm[:, 2 * c:2 * c + 2, 4:8],
                                                           dtb[0:64, :].rearrange("p (z h) -> p z h", h=4), ALU.add),
                     reads=["dl_gtm", "dl_dtb"], writes=["dl_ng"])
            P.op("act", lambda e: e.activation(ng2[:], ng2[:], AF.Exp), reads=["dl_ng"], writes=["dl_ng"])
            P.op("act", lambda e: e.activation(ng2[:], ng2[:], AF.Ln, bias=1.0, scale=1.0), reads=["dl_ng"], writes=["dl_ng"])
            P.op("dve", lambda e: e.tensor_tensor(ng2[:], ng2[:], Au[0:64, :].unsqueeze(1).to_broadcast([64, 2, 8]), ALU.mult), reads=["dl_ng", "dl_A"], writes=["dl_ng"])
            for c in range(2):
                P.op("pe", lambda e, c=c: e.matmul(ps[0:64, 7, 32 + 8 * c:40 + 8 * c], incl, ng2[:, c, :], start=True, stop=True), reads=["consts", "dl_ng"], writes=["ps7"])
                P.op("pe", lambda e, c=c: e.matmul(ps[:, 7, 48 + 8 * c:56 + 8 * c], ones128, ng2[:, c, :], start=True, stop=True), reads=["consts", "dl_ng"], writes=["ps7"])
            ngcp = ps[0:64, 7, 32:48].rearrange("p (c u) -> p c u", u=8)
            nglp = ps[0:64, 7, 48:64].rearrange("p (c u) -> p c u", u=8)
            P.op("dve", lambda e: e.tensor_copy(ngc2[:], ngcp), writes=["ps7", "dl_ngc"])
            P.op("act", lambda e: e.activation(eg2[:], ngcp, AF.Exp, scale=-1.0), writes=["ps7", "dl_eg"])
            P.op("dve", lambda e: e.tensor_tensor(eglt2[:], ngc2[:], nglp, ALU.subtract), reads=["dl_ngc"], writes=["ps7", "dl_eglt"])
            P.op("act", lambda e: e.activation(eglt2[:], eglt2[:], AF.Exp), reads=["dl_eglt"], writes=["dl_eglt"])
            for hq in range(2):
                rows = slice(64 * hq, 64 * hq + 64)
                P.op("act", lambda e, rows=rows, hq=hq: e.activation(egls2[rows, :, :], ps[rows, 7, 48:64].rearrange("p (c u) -> p c u", u=8)[:, :, hq:8:2], AF.Exp, scale=-1.0),
                     writes=["ps7", "dl_egls"])
            for c in range(2):
                cs = slice(64 * c, 64 * c + 64)
                beta, nbeta, ngc, eg, eglt, egls = beta2[:, c, :], nbeta2[:, c, :], ngc2[:, c, :], eg2[:, c, :], eglt2[:, c, :], egls2[:, c, :]
                P.op("dve", lambda e: e.tensor_tensor(dgm[:], ident64.unsqueeze(1).to_broadcast([64, 8, 64]), ngc[:].unsqueeze(2).to_broadcast([64, 8, 64]), ALU.mult),
                     reads=["ident", "dl_ngc"], writes=["dl_dgm"])
                P.op("pe", lambda e: e.matmul(ps[0:64, 3, :], ones64, dgm[:].rearrange("p u l -> p (u l)"), start=True, stop=True), reads=["consts", "dl_dgm"], writes=["ps3"])
                P.op("dve", lambda e: e.tensor_tensor(dT[:], ngc[:].unsqueeze(2).to_broadcast([64, 8, 64]), ps[0:64, 3, :].rearrange("p (u l) -> p u l", l=64), ALU.subtract),
                     reads=["dl_ngc"], writes=["ps3", "dl_dT"])
                P.op("dve", lambda e: e.tensor_scalar(dT[:], dT[:], 0.0, None, ALU.min), reads=["dl_dT"], writes=["dl_dT"])
                P.op("act", lambda e: e.activation(dT[:], dT[:], AF.Exp), reads=["dl_dT"], writes=["dl_dT"])
                P.op("pool", lambda e: e.tensor_tensor(dTs[:], dT[:], strict.unsqueeze(1).to_broadcast([64, 8, 64]), ALU.mult), reads=["dl_dT", "consts"], writes=["dl_dTs"])
                P.op("pool", lambda e: e.tensor_tensor(dT[:], dT[:], incl.unsqueeze(1).to_broadcast([64, 8, 64]), ALU.mult), reads=["dl_dT", "consts"], writes=["dl_dT"])
                ktv = self.tm_transposes(kT, "dl_kT", cs, 2, 0)
                vtv = self.tm_transposes(vT, "dl_vT", cs, 2, 512)
                P.op("act", lambda e: e.activation(ktm[:], ktv, AF.Copy), writes=["ps2", "dl_ktm"])
                P.op("dve", lambda e: e.tensor_copy(vtm[:], vtv), writes=["ps2", "dl_vtm"])
                P.op("dve", lambda e: e.tensor_tensor(kd[:], ktm[:], eglt[:].unsqueeze(2).to_broadcast([64, 8, 64]), ALU.mult), reads=["dl_ktm", "dl_eglt"], writes=["dl_kd"])
                for z in range(2):
                    for h in range(4):
                        hp, hq = h // 2, h % 2
                        u = 4 * z + h
                        P.op("pe", lambda e, z=z, hp=hp, hq=hq, u=u: e.matmul(ps[0:64, 4, 64 * u:64 * u + 64], kT[:, hp, z, cs], kTp[:, hp, hq, z, cs], start=True, stop=True),
                             reads=["dl_kT", "dl_kTp"], writes=["ps4"])
                        P.op("pe", lambda e, z=z, hp=hp, hq=hq, u=u: e.matmul(ps[0:64, 5, 64 * u:64 * u + 64], kT[:, hp, z, cs], qTp[:, hp, hq, z, cs], start=True, stop=True),
                             reads=["dl_kT", "dl_qTp"], writes=["ps5"])
                P.op("dve", lambda e: e.tensor_tensor(Pm[:], ps[0:64, 4, :].rearrange("p (u l) -> p u l", l=64), dTs[:], ALU.mult), reads=["dl_dTs"], writes=["ps4", "dl_P"])
                P.op("dve", lambda e: e.tensor_tensor(Pm[:], Pm[:], nbeta[:].unsqueeze(2).to_broadcast([64, 8, 64]), ALU.mult), reads=["dl_P", "dl_nbeta"], writes=["dl_P"])
                P.op("dve", lambda e: e.tensor_tensor(qkd[:], ps[0:64, 5, :].rearrange("p (u l) -> p u l", l=64), dT[:], ALU.mult), reads=["dl_dT"], writes=["ps5", "dl_qkd"])
                self.neumann_inverse("dl_", Pm, Qm, Rm, (4, 5, 6))
                for z in range(2):
                    bank = 3 + z
                    for h in range(4):
                        hp, hq = h // 2, h % 2
                        P.op("pe", lambda e, z=z, hp=hp, hq=hq, h=h, bank=bank: e.matmul(ps[0:64, bank, 128 * h:128 * h + 64], kTp[:, hp, hq, z, cs], Sb[:, 2 * z + hp, :], start=True, stop=True),
                             reads=["dl_kTp", "dl_Sb"], writes=["ps%d" % bank])
                        P.op("pe", lambda e, z=z, hp=hp, hq=hq, h=h, bank=bank: e.matmul(ps[0:64, bank, 128 * h + 64:128 * h + 128], qTp[:, hp, hq, z, cs], Sb[:, 2 * z + hp, :], start=True, stop=True),
                             reads=["dl_qTp", "dl_Sb"], writes=["ps%d" % bank])
                for z in range(2):
                    bank = 3 + z
                    ks4 = ps[0:64, bank, :].rearrange("p (h two e) -> p h two e", two=2, e=64)
                    us = slice(4 * z, 4 * z + 4)
                    P.op("dve", lambda e, ks4=ks4, us=us: e.tensor_tensor(t1[:, us, :], ks4[:, :, 0, :], eg[:, us].unsqueeze(2).to_broadcast([64, 4, 64]), ALU.mult),
                         reads=["dl_eg"], writes=["ps%d" % bank, ("dl_t1", z)])
                    P.op("dve", lambda e, us=us, z=z: e.tensor_tensor(rr[:, us, :], vtm[:, us, :], t1[:, us, :], ALU.subtract), reads=["dl_vtm", ("dl_t1", z)], writes=[("dl_r", z)])
                    P.op("dve", lambda e, ks4=ks4, us=us: e.tensor_tensor(t1[:, us, :], ks4[:, :, 1, :], eg[:, us].unsqueeze(2).to_broadcast([64, 4, 64]), ALU.mult),
                         reads=["dl_eg", ("dl_r", z)], writes=["ps%d" % bank, ("dl_t1", z)])
                rkeys = [("dl_r", 0), ("dl_r", 1)]
                for u in range(8):
                    P.op("pe", lambda e, u=u: e.matmul(ps[0:64, 5, 64 * u:64 * u + 64], Rm[:, u, :], rr[:, u, :], start=True, stop=True), reads=["dl_R"] + rkeys, writes=["ps5"])
                P.op("dve", lambda e: e.tensor_tensor(vnew[:], ps[0:64, 5, :].rearrange("p (u l) -> p u l", l=64), beta[:].unsqueeze(2).to_broadcast([64, 8, 64]), ALU.mult),
                     reads=["dl_beta"], writes=["ps5", "dl_vnew"])
                for u in range(8):
                    P.op("pe", lambda e, u=u: e.matmul(ps[0:64, 6, 64 * u:64 * u + 64], qkd[:, u, :], vnew[:, u, :], start=True, stop=True), reads=["dl_qkd", "dl_vnew"], writes=["ps6"])
                P.op("dve", lambda e, c=c: e.tensor_tensor(osb[:, c, :, :].rearrange("p z (h e) -> p (z h) e", e=64), ps[0:64, 6, :].rearrange("p (u e) -> p u e", e=64), t1[:], ALU.add),
                     reads=[("dl_t1", 0), ("dl_t1", 1)], writes=["ps6", ("dl_osb", c)])
                for z in range(2):
                    for h in range(4):
                        hp, hq = h // 2, h % 2
                        rows = slice(64 * hq, 64 * hq + 64)
                        u = 4 * z + h
                        col = (2 * z + hp) * 64
                        P.op("pe", lambda e, rows=rows, u=u, col=col: e.matmul(ps[rows, 3, col:col + 64], kd[:, u, :], vnew[:, u, :], start=True, stop=True),
                             reads=["dl_kd", "dl_vnew"], writes=["ps3"])
                P.op("dve", lambda e: e.tensor_tensor(tmpS[:], S[:], egls[:].unsqueeze(2).to_broadcast([128, 4, 64]), ALU.mult), reads=["dl_S", "dl_egls"], writes=["dl_tmpS"])
                P.op("dve", lambda e: e.tensor_tensor(S[:], ps[:, 3, 0:256].rearrange("p (a e) -> p a e", e=64), tmpS[:], ALU.add), reads=["dl_tmpS"], writes=["ps3", "dl_S"])
                if not (t % 2 == 1 and c == 1):
                    P.op("act", lambda e: e.activation(Sb[:], S[:], AF.Copy), reads=["dl_S"], writes=["dl_Sb"])
            self.emit_out_tile(t, osb, [("dl_osb", 0), ("dl_osb", 1)], 7)
            if t % 2 == 1:
                P.op("dve", lambda e: e.tensor_copy(Sout[:], S[:]), reads=["dl_S"], writes=["dl_Sout"])
                for z in range(2):
                    seg = (t - 1) // 2 if z == 0 else (NT - 1 - t) // 2
                    P.dma("sp", lambda e, z=z, seg=seg: e.dma_start(
                        out=self.out_delta[seg, l, z].rearrange("(hp hq) d e -> (hq d) hp e", hq=2), in_=Sout[:, 2 * z:2 * z + 2, :]), reads=["dl_Sout"])
                P.op("dve", lambda e: e.tensor_scalar(S[:], S[:], self.keep[:, 0:1], None, ALU.mult), reads=["dl_S", "keep"], writes=["dl_S"])
                P.op("act", lambda e: e.activation(Sb[:], S[:], AF.Copy), reads=["dl_S"], writes=["dl_Sb"])
        if self.debug.get("yacc") == "delta":
            P.dma("sp", lambda e: e.dma_start(out=self.dbg_yacc, in_=self.yacc[:]), reads=[("yacc", i) for i in range(NT)])
        self.release(m1)
        if self.debug.get("post", True):
            self.post_simple(l, "dl", wb, wkeys, 784, AF.Silu, self.dl_norm, False, 256, first)
        self.release(m0)

    def rwkv_mix_fm(self, pre, mu_col, dst, keys_r, keys_w, eng="dve"):
        P = self.P
        n = dst.shape[-1]
        P.op("dve", lambda e: e.tensor_tensor(dst, pre[:, :, 0:n], pre[:, :, 2:n + 2], ALU.add), reads=keys_r, writes=keys_w)
        P.op("dve", lambda e: e.scalar_tensor_tensor(dst, dst, 0.5, pre[:, :, 1:n + 1], ALU.mult, ALU.subtract), reads=keys_r + keys_w, writes=keys_w)
        P.op("dve", lambda e: e.scalar_tensor_tensor(dst, dst, mu_col, pre[:, :, 1:n + 1], ALU.mult, ALU.add), reads=keys_r + keys_w + ["rw_mu"], writes=keys_w)

    def mixer_rwkv(self, l, first):
        P, ps = self.P, self.psum
        m0 = self.mark()
        wb, wkeys = self.load_w("wD", self.wD[l], 1152)
        incl = self.cst("INCL")
        strict = self.cst("STRICT")
        ones64 = self.consts[0:64, CO["ONES"][0]:CO["ONES"][0] + 64]
        ident64 = self.ident[0:64, 0:64]
        bacc = self.sb("rw_bacc", [128, NT, 256], BF16)
        mu = self.sb("rw_mu", [128, 9], F32)
        P.dma("sp", lambda e: e.dma_start(out=mu[:], in_=self.rw_mu[l]), writes=["rw_mu"])
        m1 = self.mark()
        w2p = self.sb("rw_w2p", [128, 2, 256], F32)
        a2p = self.sb("rw_a2p", [128, 2, 256], F32)
        P.dma("sp", lambda e: e.dma_start(out=w2p[:], in_=self.rw_w2p[l]), writes=["rw_w2p"])
        P.dma("sp", lambda e: e.dma_start(out=a2p[:], in_=self.rw_a2p[l]), writes=["rw_a2p"])
        bcs = {}
        for nm, src, width in (("w0", self.rw_w0, 512), ("a0", self.rw_a0, 512), ("kk", self.rw_kk, 256), ("ka", self.rw_ka, 256), ("rk", self.rw_rk, 256)):
            bcs[nm] = self.sb("rw_bc_" + nm, [64, width], F32)
            P.dma("sp", lambda e, nm=nm, src=src: e.dma_start(out=bcs[nm][:], in_=src[l:l + 1, :].partition_broadcast(64)), writes=["rw_bc"])
        omka = self.sb("rw_omka", [64, 256], F32)
        P.op("dve", lambda e: e.tensor_scalar(omka[:], bcs["ka"][:], -1.0, 1.0, ALU.mult, ALU.add), reads=["rw_bc"], writes=["rw_omka"])
        M = self.sb("rw_M", [64, 8, 64], F32)
        pre = self.sb("rw_pre", [128, 2, 130], F32)
        rT = self.sb("rw_rT", [128, 2, 2, 128], BF16)
        kT = self.sb("rw_kT", [128, 2, 2, 128], BF16)
        vT = self.sb("rw_vT", [128, 2, 2, 128], BF16)
        twT = self.sb("rw_twT", [128, 2, 128], F32)
        daT = self.sb("rw_daT", [128, 2, 128], F32)
        rtm = self.sb("rw_rtm", [64, 2, 256], F32)
        ktm = self.sb("rw_ktm", [64, 2, 256], F32)
        vtm = self.sb("rw_vtm", [64, 2, 256], F32)
        lw = self.sb("rw_lw", [64, 2, 256], F32)
        av = self.sb("rw_a", [64, 2, 256], F32)
        ecl = self.sb("rw_ecl", [64, 2, 256], F32)
        encl = self.sb("rw_encl", [64, 2, 256], F32)
        ecw = self.sb("rw_ecw", [64, 2, 256], F32)
        kh = self.sb("rw_kh", [64, 2, 256], F32)
        kx = self.sb("rw_kx", [64, 2, 256], F32)
        ss8 = self.sb("rw_ss8", [64, 8], F32)
        bs8 = self.sb("rw_bs8", [64, 8], F32)
        ktz = self.sb("rw_ktz", [64, 2, 256], F32)
        al = self.sb("rw_al", [64, 2, 256], F32)
        be = self.sb("rw_be", [64, 2, 256], F32)
        kti = self.sb("rw_kti", [64, 2, 256], F32)
        rti = self.sb("rw_rti", [64, 2, 256], F32)
        beT = self.sb("rw_beT", [64, 8, 64], F32)
        Pm = lw[:].rearrange("p z (h j) -> p (z h) j", j=64)
        Qm = av[:].rearrange("p z (h j) -> p (z h) j", j=64)
        Aak = ecl[:].rearrange("p z (h j) -> p (z h) j", j=64)
        Ara = encl[:].rearrange("p z (h j) -> p (z h) j", j=64)
        Ark = ecw[:].rearrange("p z (h j) -> p (z h) j", j=64)
        X1 = kh[:].rearrange("p z (h j) -> p (z h) j", j=64)
        Uu = kx[:].rearrange("p z (h j) -> p (z h) j", j=64)
        tmpM = ktz[:].rearrange("p z (h j) -> p (z h) j", j=64)
        alT = rtm[:].rearrange("p z (h j) -> p (z h) j", j=64)
        ktT = ktm[:].rearrange("p z (h j) -> p (z h) j", j=64)
        rtT = be[:].rearrange("p z (h j) -> p (z h) j", j=64)
        Mt = kh[:].rearrange("p z (h j) -> p (z h) j", j=64)
        Rm = rti[:].rearrange("p z (h j) -> p (z h) j", j=64)
        WL = self.sb("rw_WL", [64, 8], F32)
        P.dma("sp", lambda e: e.dma_start(out=Mt[:], in_=self.init_rwkv[l].rearrange("z h i j -> i (z h) j")), writes=["rw_kh"])
        for u in range(8):
            P.op("pe", lambda e, u=u: e.transpose(ps[0:64, 2, 64 * u:64 * u + 64], Mt[:, u, :], ident64), reads=["rw_kh", "ident"], writes=["ps2"])
        P.op("dve", lambda e: e.tensor_copy(M[:], ps[0:64, 2, :].rearrange("p (u e) -> p u e", e=64)), writes=["ps2", "rw_M"])
        osb = self.sb("rw_osb", [64, 2, 512], F32)
        v3 = lambda x_: x_[:].rearrange("p z (h j) -> p (z h) j", j=64)
        for t in range(NT):
            tiles = (t, NT - 1 - t)
            edge = slice(0, 1) if t % 2 == 0 else slice(129, 130)
            for j in range(9):
                bank = j % 2
                self.proj_fm(wb, wkeys, 128 * j, 128, tiles, bank, halo=1)
                self.evac_fm(lambda z: pre[:, z, :], bank, 128, "rw_pre", W=130, eng="act")
                P.op("dve", lambda e: e.tensor_scalar(pre[:, :, edge], pre[:, :, edge], self.keep[:, 0:1], None, ALU.mult), reads=["rw_pre", "keep"], writes=["rw_pre"])
                if j < 6:
                    dst, dk = ((rT, "rw_rT"), (kT, "rw_kT"), (vT, "rw_vT"))[j // 2]
                    dst = dst[:, j % 2, :, :]
                elif j == 6:
                    dst, dk = twT[:], "rw_twT"
                elif j == 7:
                    dst, dk = daT[:], "rw_daT"
                else:
                    continue
                self.rwkv_mix_fm(pre, mu[:, j:j + 1], dst, ["rw_pre"], [dk])
                if j == 6:
                    P.op("act", lambda e: e.activation(twT[:], twT[:], AF.Tanh), reads=["rw_twT"], writes=["rw_twT"])
            for c in range(2):
                cs = slice(64 * c, 64 * c + 64)
                for z in range(2):
                    P.op("pe", lambda e, z=z: e.matmul(ps[0:64, 2, 256 * z:256 * z + 256], twT[:, z, cs], w2p[:, z, :], start=True, stop=True), reads=["rw_twT", "rw_w2p"], writes=["ps2"])
                    P.op("pe", lambda e, z=z: e.matmul(ps[0:64, 3, 256 * z:256 * z + 256], daT[:, z, cs], a2p[:, z, :], start=True, stop=True), reads=["rw_daT", "rw_a2p"], writes=["ps3"])
                lwf = lw[:].rearrange("p z n -> p (z n)")
                avf = av[:].rearrange("p z n -> p (z n)")
                P.op("dve", lambda e: e.tensor_tensor(lwf, ps[0:64, 2, :], bcs["w0"][:], ALU.add), reads=["rw_bc"], writes=["ps2", "rw_lw"])
                self.act_sigmoid(lwf, lwf, ["rw_lw"], ["rw_lw"])
                P.op("dve", lambda e: e.tensor_scalar(lwf, lwf, -float(np.exp(-0.5)), None, ALU.mult), reads=["rw_lw"], writes=["rw_lw"])
                P.op("dve", lambda e: e.tensor_tensor(avf, ps[0:64, 3, :], bcs["a0"][:], ALU.add), reads=["rw_bc"], writes=["ps3", "rw_a"])
                self.act_sigmoid(avf, avf, ["rw_a"], ["rw_a"])
                P.op("pe", lambda e: e.matmul(ps[0:64, 4, :], incl, lwf, start=True, stop=True), reads=["consts", "rw_lw"], writes=["ps4"])
                for u in range(8):
                    z, h = u // 4, u % 4
                    P.op("pe", lambda e, u=u, z=z, h=h: e.matmul(ps[0:64, 7, 64 + u:65 + u], lw[:, z, 64 * h:64 * h + 64], ones64[:, 0:1], start=True, stop=True),
                         reads=["rw_lw", "consts"], writes=["ps7"])
                P.op("act", lambda e: e.activation(WL[:], ps[0:64, 7, 64:72], AF.Exp), writes=["ps7", "rw_WL"])
                eclf = ecl[:].rearrange("p z n -> p (z n)")
                P.op("act", lambda e: e.activation(eclf, ps[0:64, 4, :], AF.Exp), writes=["ps4", "rw_ecl"])
                P.op("act", lambda e: e.activation(encl[:].rearrange("p z n -> p (z n)"), ps[0:64, 4, :], AF.Exp, scale=-1.0), writes=["ps4", "rw_encl"])
                P.op("dve", lambda e: e.tensor_tensor(ecw[:].rearrange("p z n -> p (z n)"), ps[0:64, 4, :], lwf, ALU.subtract), reads=["rw_lw"], writes=["ps4", "rw_ecw"])
                P.op("act", lambda e: e.activation(ecw[:], ecw[:], AF.Exp), reads=["rw_ecw"], writes=["rw_ecw"])
                for (src, skey, dstt, dkey, bank) in ((rT, "rw_rT", rtm, "rw_rtm", 5), (kT, "rw_kT", ktm, "rw_ktm", 6), (vT, "rw_vT", vtm, "rw_vtm", 5)):
                    bv = ps[:, bank, :].bitcast(BF16)
                    for z in range(2):
                        for hp in range(2):
                            col = (4 * z + 2 * hp) * 64
                            P.op("pe", lambda e, src=src, z=z, hp=hp, col=col, bv=bv: e.transpose(bv[0:64, col:col + 128], src[:, hp, z, cs], self.ident_bf[:]),
                                 reads=[skey, "ident_bf"], writes=["ps%d" % bank])
                    P.op("act", lambda e, dstt=dstt, bv=bv: e.activation(dstt[:].rearrange("p z n -> p (z n)"), bv[0:64, 0:512], AF.Copy), writes=["ps%d" % bank, dkey])
                kk2 = bcs["kk"][:].unsqueeze(1).to_broadcast([64, 2, 256])
                ka2 = bcs["ka"][:].unsqueeze(1).to_broadcast([64, 2, 256])
                P.op("dve", lambda e: e.tensor_tensor(kx[:], ktm[:], kk2, ALU.mult), reads=["rw_ktm", "rw_bc"], writes=["rw_kx"])
                P.op("act", lambda e: e.activation(kh[:], kx[:], AF.Square), reads=["rw_kx"], writes=["rw_kh"])
                P.op("dve", lambda e: e.tensor_reduce(ss8[:], v3(kh), AX.X, ALU.add), reads=["rw_kh"], writes=["rw_ss8"])
                P.op("act", lambda e: e.activation(ss8[:], ss8[:], AF.Ln, bias=1e-6, scale=1.0), reads=["rw_ss8"], writes=["rw_ss8"])
                P.op("act", lambda e: e.activation(ss8[:], ss8[:], AF.Exp, scale=-0.5), reads=["rw_ss8"], writes=["rw_ss8"])
                P.op("dve", lambda e: e.tensor_tensor(v3(kh), v3(kx), ss8[:].unsqueeze(2).to_broadcast([64, 8, 64]), ALU.mult), reads=["rw_kx", "rw_ss8"], writes=["rw_kh"])
                P.op("dve", lambda e: e.tensor_tensor(ktz[:], av[:], ka2, ALU.mult), reads=["rw_a", "rw_bc"], writes=["rw_ktz"])
                P.op("dve", lambda e: e.tensor_tensor(ktz[:], ktz[:], omka[:].unsqueeze(1).to_broadcast([64, 2, 256]), ALU.add), reads=["rw_ktz", "rw_omka"], writes=["rw_ktz"])
                P.op("dve", lambda e: e.tensor_tensor(ktz[:], ktz[:], ktm[:], ALU.mult), reads=["rw_ktz", "rw_ktm"], writes=["rw_ktz"])
                P.op("dve", lambda e: e.tensor_tensor(al[:], av[:], kh[:], ALU.mult), reads=["rw_a", "rw_kh"], writes=["rw_al"])
                P.op("dve", lambda e: e.scalar_tensor_tensor(al[:], al[:], -1.0, encl[:], ALU.mult, ALU.mult), reads=["rw_al", "rw_encl"], writes=["rw_al"])
                P.op("dve", lambda e: e.tensor_tensor(be[:], kh[:], ecw[:], ALU.mult), reads=["rw_kh", "rw_ecw"], writes=["rw_be"])
                P.op("dve", lambda e: e.tensor_tensor(kti[:], ktz[:], encl[:], ALU.mult), reads=["rw_ktz", "rw_encl"], writes=["rw_kti"])
                P.op("dve", lambda e: e.tensor_tensor(rti[:], rtm[:], ecl[:], ALU.mult), reads=["rw_rtm", "rw_ecl"], writes=["rw_rti"])
                P.op("dve", lambda e: e.tensor_tensor(kx[:], rtm[:], ktz[:], ALU.mult), reads=["rw_rtm", "rw_ktz"], writes=["rw_kx"])
                P.op("dve", lambda e: e.tensor_tensor(kx[:], kx[:], bcs["rk"][:].unsqueeze(1).to_broadcast([64, 2, 256]), ALU.mult), reads=["rw_kx", "rw_bc"], writes=["rw_kx"])
                P.op("dve", lambda e: e.tensor_reduce(bs8[:], v3(kx), AX.X, ALU.add), reads=["rw_kx"], writes=["rw_bs8"])
                for z in range(2):
                    P.op("dve", lambda e, z=z, c=c: e.tensor_tensor(osb[:, z, 256:512].rearrange("p (h j) -> p h j", j=64), vtm[:, z, :].rearrange("p (h j) -> p h j", j=64),
                                                               bs8[:, 4 * z:4 * z + 4].unsqueeze(2).to_broadcast([64, 4, 64]), ALU.mult),
                         reads=["rw_vtm", "rw_bs8"], writes=["rw_osb"])
                for (src, skey, dstT, dkey, bank) in ((al, "rw_al", alT, "rw_rtm", 2), (be, "rw_be", beT, "rw_beT", 3), (kti, "rw_kti", ktT, "rw_ktm", 4), (rti, "rw_rti", rtT, "rw_be", 5)):
                    for u in range(8):
                        z, h = u // 4, u % 4
                        P.op("pe", lambda e, src=src, u=u, z=z, h=h, bank=bank: e.transpose(ps[0:64, bank, 64 * u:64 * u + 64], src[:, z, 64 * h:64 * h + 64], ident64),
                             reads=[skey, "ident"], writes=["ps%d" % bank])
                    P.op("act", lambda e, dstT=dstT, bank=bank: e.activation(dstT[:], ps[0:64, bank, :].rearrange("p (u l) -> p u l", l=64), AF.Copy), writes=["ps%d" % bank, dkey])
                for (lhs, lkey, rhs, rkey, bank, msk, dst, dkey) in ((alT, "rw_rtm", beT, "rw_beT", 2, strict, Pm, "rw_lw"), (ktT, "rw_ktm", beT, "rw_beT", 3, strict, Aak, "rw_ecl"),
                                                                      (alT, "rw_rtm", rtT, "rw_be", 4, incl, Ara, "rw_encl"), (ktT, "rw_ktm", rtT, "rw_be", 5, incl, Ark, "rw_ecw")):
                    for u in range(8):
                        P.op("pe", lambda e, lhs=lhs, rhs=rhs, u=u, bank=bank: e.matmul(ps[0:64, bank, 64 * u:64 * u + 64], lhs[:, u, :], rhs[:, u, :], start=True, stop=True),
                             reads=[lkey, rkey], writes=["ps%d" % bank])
                    P.op("dve", lambda e, bank=bank, msk=msk, dst=dst: e.tensor_tensor(dst[:], ps[0:64, bank, :].rearrange("p (u l) -> p u l", l=64),
                                                                                     msk.unsqueeze(1).to_broadcast([64, 8, 64]), ALU.mult),
                         reads=["consts"], writes=["ps%d" % bank, dkey])
                self.neumann_inverse("rw_", Pm, Qm, Rm, (2, 3, 4), keys=("rw_lw", "rw_a", "rw_rti"))
                for u in range(8):
                    z, h = u // 4, u % 4
                    P.op("pe", lambda e, u=u: e.matmul(ps[0:64, 5, 64 * u:64 * u + 64], beT[:, u, :], M[:, u, :], start=True, stop=False), reads=["rw_beT", "rw_M"], writes=["ps5"])
                    P.op("pe", lambda e, u=u, z=z, h=h: e.matmul(ps[0:64, 5, 64 * u:64 * u + 64], Aak[:, u, :], vtm[:, z, 64 * h:64 * h + 64], start=False, stop=True),
                         reads=["rw_ecl", "rw_vtm"], writes=["ps5"])
                P.op("act", lambda e: e.activation(X1[:], ps[0:64, 5, :].rearrange("p (u e) -> p u e", e=64), AF.Copy), writes=["ps5", "rw_kh"])
                for u in range(8):
                    P.op("pe", lambda e, u=u: e.matmul(ps[0:64, 6, 64 * u:64 * u + 64], Rm[:, u, :], X1[:, u, :], start=True, stop=True), reads=["rw_rti", "rw_kh"], writes=["ps6"])
                P.op("act", lambda e: e.activation(Uu[:], ps[0:64, 6, :].rearrange("p (u e) -> p u e", e=64), AF.Copy), writes=["ps6", "rw_kx"])
                for u in range(8):
                    z, h = u // 4, u % 4
                    vu = vtm[:, z, 64 * h:64 * h + 64]
                    P.op("pe", lambda e, u=u: e.matmul(ps[0:64, 5, 64 * u:64 * u + 64], rtT[:, u, :], M[:, u, :], start=True, stop=False), reads=["rw_be", "rw_M"], writes=["ps5"])
                    P.op("pe", lambda e, u=u: e.matmul(ps[0:64, 5, 64 * u:64 * u + 64], Ara[:, u, :], Uu[:, u, :], start=False, stop=False), reads=["rw_encl", "rw_kx"], writes=["ps5"])
                    P.op("pe", lambda e, u=u, vu=vu: e.matmul(ps[0:64, 5, 64 * u:64 * u + 64], Ark[:, u, :], vu, start=False, stop=True), reads=["rw_ecw", "rw_vtm"], writes=["ps5"])
                for z in range(2):
                    P.op("act", lambda e, z=z, c=c: e.activation(osb[:, z, 0:256], ps[0:64, 5, 256 * z:256 * z + 256], AF.Copy), writes=["ps5", "rw_osb"])
                for u in range(8):
                    z, h = u // 4, u % 4
                    vu = vtm[:, z, 64 * h:64 * h + 64]
                    P.op("pe", lambda e, u=u, z=z, h=h: e.matmul(ps[0:64, 6, 64 * u:64 * u + 64], al[:, z, 64 * h:64 * h + 64], Uu[:, u, :], start=True, stop=False), reads=["rw_al", "rw_kx"], writes=["ps6"])
                    P.op("pe", lambda e, u=u, z=z, h=h, vu=vu: e.matmul(ps[0:64, 6, 64 * u:64 * u + 64], kti[:, z, 64 * h:64 * h + 64], vu, start=False, stop=True), reads=["rw_kti", "rw_vtm"], writes=["ps6"])
                P.op("dve", lambda e: e.tensor_tensor(tmpM[:], ps[0:64, 6, :].rearrange("p (u e) -> p u e", e=64), M[:], ALU.add), reads=["rw_M"], writes=["ps6", "rw_ktz"])
                P.op("dve", lambda e: e.tensor_tensor(M[:], tmpM[:], WL[:].unsqueeze(2).to_broadcast([64, 8, 64]), ALU.mult), reads=["rw_ktz", "rw_WL"], writes=["rw_M"])
                self.emit_out_tile2(t, c, osb, ["rw_osb"], bacc)
            if t % 2 == 1:
                for u in range(8):
                    P.op("pe", lambda e, u=u: e.transpose(ps[0:64, 2, 64 * u:64 * u + 64], M[:, u, :], ident64), reads=["rw_M", "ident"], writes=["ps2"])
                P.op("dve", lambda e: e.tensor_copy(Mt[:], ps[0:64, 2, :].rearrange("p (u e) -> p u e", e=64)), writes=["ps2", "rw_kh"])
                for z in range(2):
                    seg = (t - 1) // 2 if z == 0 else (NT - 1 - t) // 2
                    P.dma("sp", lambda e, z=z, seg=seg: e.dma_start(out=self.out_rwkv[seg, l, z].rearrange("h i j -> i h j"), in_=Mt[:, 4 * z:4 * z + 4, :]), reads=["rw_kh"])
                P.op("dve", lambda e: e.tensor_scalar(M[:], M[:], self.keep[0:64, 0:1], None, ALU.mult), reads=["rw_M", "keep"], writes=["rw_M"])
        if self.debug.get("yacc") == "rwkv":
            P.dma("sp", lambda e: e.dma_start(out=self.dbg_yacc, in_=self.yacc[:]), reads=[("yacc", i) for i in range(NT)])
        if self.debug.get("yacc") == "rwkv_bonus":
            P.dma("sp", lambda e: e.dma_start(out=self.dbg_yacc, in_=bacc[:]), reads=[("bacc", i) for i in range(NT)])
        self.release(m1)
        if self.debug.get("post", True):
            self.post_rwkv(l, wb, wkeys, bacc, mu, first)
        self.release(m0)

    def emit_out_tile2(self, t, c, osb, osb_keys, bacc):
        P, ps = self.P, self.psum
        sel = self.cst("SEL").rearrange("p (q n) -> p q n", n=128)
        for z in range(2):
            bank = z
            pk = "ps%d" % bank
            tile = t if z == 0 else NT - 1 - t
            P.op("pe", lambda e, z=z, bank=bank: e.matmul(ps[:, bank, :], sel[:, 2 * z + c, :], osb[:, z, :], start=(c == 0), stop=(c == 1)),
                 reads=["consts"] + osb_keys, writes=[pk])
            if c == 0:
                continue
            yk = ("yacc", tile)
            bk = ("bacc", tile)
            if t < NT // 2:
                P.op("act", lambda e, tile=tile, bank=bank: e.activation(self.yacc[:, tile, :], ps[:, bank, 0:256], AF.Copy), writes=[pk, yk])
                P.op("act", lambda e, tile=tile, bank=bank: e.activation(bacc[:, tile, :], ps[:, bank, 256:512], AF.Copy), writes=[pk, bk])
            else:
                P.op("dve", lambda e, tile=tile, bank=bank: e.tensor_tensor(self.yacc[:, tile, :], ps[:, bank, 0:256], self.yacc[:, tile, :], ALU.add), writes=[pk, yk])
                P.op("dve", lambda e, tile=tile, bank=bank: e.tensor_tensor(bacc[:, tile, :], ps[:, bank, 256:512], bacc[:, tile, :], ALU.add), writes=[pk, bk])

    def post_rwkv(self, l, wb, wkeys, bacc, mu, first):
        P, ps = self.P, self.psum
        m0 = self.mark()
        wo, wokeys = self.load_w("wo_rw", self.w_out[l][768:1024, :], D, kchunks=2)
        gain_bc = self.sb("hn_gain", [128, 256], F32)
        P.dma("sp", lambda e: e.dma_start(out=gain_bc[:], in_=self.rw_norm[l:l + 1, :].partition_broadcast(128)), writes=["hn_gain"])
        g2 = self.sb("rw_g2", [128, 256], F32)
        P.dma("sp", lambda e: e.dma_start(out=g2[:], in_=self.rw_g2[l]), writes=["rw_g2"])
        tmp = (self.sb("hn_yc", [128, 256], F32), self.sb("hn_sq", [128, 256], F32), self.sb("hn_st", [128, 2, 4], F32),
               self.sb("yact", [128, 256], F32))
        gate = self.sb("hn_gate", [128, 256], F32)
        otmp = (self.sb("yTm", [128, 2, 128], BF16), self.sb("gtmp", [128, D], F32))
        pre1 = self.sb("rwp_pre", [128, 1, 130], F32)
        sg = self.sb("rwp_sg", [128, 1, 128], F32)
        for i in range(NT):
            tok0 = 2 + 128 * i - 1
            for k in range(8):
                P.op("pe", lambda e, k=k, tok0=tok0: e.matmul(ps[:, 0, 0:130], wb[:, k, 1024:1152], self.uT[:, k, tok0:tok0 + 130], start=(k == 0), stop=(k == 7)),
                     reads=wkeys + [("uT", i)] + ([("uT", i - 1)] if i > 0 else []) + ([("uT", i + 1)] if i < NT - 1 else []), writes=["ps0"])
            P.op("act", lambda e: e.activation(pre1[:, 0, :], ps[:, 0, 0:130], AF.Copy), writes=["ps0", "rwp_pre"])
            edge = slice(0, 1) if i % 2 == 0 else slice(129, 130)
            P.op("dve", lambda e, edge=edge: e.tensor_scalar(pre1[:, :, edge], pre1[:, :, edge], self.keep[:, 0:1], None, ALU.mult), reads=["rwp_pre", "keep"], writes=["rwp_pre"])
            self.rwkv_mix_fm(pre1, mu[:, 8:9], sg[:], ["rwp_pre"], ["rwp_sg"])
            self.act_sigmoid(sg[:], sg[:], ["rwp_sg"], ["rwp_sg"])
            P.op("pe", lambda e: e.matmul(ps[:, 1, 0:256], sg[:, 0, :], g2[:], start=True, stop=True), reads=["rwp_sg", "rw_g2"], writes=["ps1"])
            P.op("act", lambda e: e.activation(gate[:], ps[:, 1, 0:256], AF.Copy), writes=["ps1", "hn_gate"])
            self.head_norm_tile(i, True, gain_bc, gate[:], ["hn_gate"], tmp, extra=(bacc[:, i, :], [("bacc", i)]))
            self.out_proj_tile(i, tmp[3], wo, wokeys, first, otmp)
        self.release(m0)

    def head_norm_tile(self, i, center, gain_bc, gate_ap, gate_keys, tmp, extra=None, sfx=""):
        P = self.P
        yc, sq, st4, yact = tmp
        kst, kst1, kyc, ksq, kya = "hn_st" + sfx, "hn_st1" + sfx, "hn_yc" + sfx, "hn_sq" + sfx, "yact" + sfx
        y3 = self.yacc[:, i, :].rearrange("p (h e) -> p h e", e=64)
        yk = ("yacc", i)
        yc3 = yc[:].rearrange("p (h e) -> p h e", e=64)
        if center:
            P.op("dve", lambda e: e.tensor_reduce(st4[:, 0, :], y3, AX.X, ALU.add), reads=[yk], writes=[kst])
            P.op("dve", lambda e: e.tensor_scalar(st4[:, 0, :], st4[:, 0, :], -1.0 / 64, None, ALU.mult), reads=[kst], writes=[kst])
            P.op("dve", lambda e: e.tensor_tensor(yc3, y3, st4[:, 0, :].unsqueeze(2).to_broadcast([128, 4, 64]), ALU.add),
                 reads=[yk, kst], writes=[kyc])
        else:
            P.op("dve", lambda e: e.tensor_copy(yc[:], self.yacc[:, i, :]), reads=[yk], writes=[kyc])
        P.op("act", lambda e: e.activation(sq[:], yc[:], AF.Square), reads=[kyc], writes=[ksq])
        P.op("dve", lambda e: e.tensor_reduce(st4[:, 1, :], sq[:].rearrange("p (h e) -> p h e", e=64), AX.X, ALU.add), reads=[ksq], writes=[kst1])
        P.op("act", lambda e: e.activation(st4[:, 1, :], st4[:, 1, :], AF.Ln, bias=LN_EPS, scale=1.0 / 64), reads=[kst1], writes=[kst1])
        P.op("act", lambda e: e.activation(st4[:, 1, :], st4[:, 1, :], AF.Exp, scale=-0.5), reads=[kst1], writes=[kst1])
        P.op("dve", lambda e: e.tensor_tensor(yc3, yc3, st4[:, 1, :].unsqueeze(2).to_broadcast([128, 4, 64]), ALU.mult),
             reads=[kyc, kst1], writes=[kyc])
        P.op("dve", lambda e: e.tensor_tensor(yc[:], yc[:], gain_bc[:], ALU.mult), reads=[kyc, "hn_gain"], writes=[kyc])
        if extra is not None:
            P.op("dve", lambda e: e.tensor_tensor(yc[:], yc[:], extra[0], ALU.add), reads=[kyc] + extra[1], writes=[kyc])
        P.op("dve", lambda e: e.tensor_tensor(yact[:], yc[:], gate_ap, ALU.mult), reads=[kyc] + gate_keys, writes=[kya])

    def out_proj_tile(self, i, yact, wo, wokeys, first, tmp, sfx="", banks=(2, 4, 5)):
        P, ps = self.P, self.psum
        yTm, gtmp = tmp
        bT, b0, b1 = banks
        kya, kyT, kgt = "yact" + sfx, "yTm" + sfx, "gtmp" + sfx
        for kk in range(2):
            P.op("pe", lambda e, kk=kk: e.transpose(ps[:, bT, 128 * kk:128 * kk + 128], yact[:, 128 * kk:128 * kk + 128], self.ident[:]),
                 reads=[kya, "ident"], writes=["ps%d" % bT])
        P.op("act", lambda e: e.activation(yTm[:], ps[:, bT, 0:256].rearrange("p (k n) -> p k n", n=128), AF.Copy), writes=["ps%d" % bT, kyT])
        for n, bn in enumerate((b0, b1)):
            for kk in range(2):
                P.op("pe", lambda e, n=n, kk=kk, bn=bn: e.matmul(ps[:, bn, :], yTm[:, kk, :], wo[:, kk, 512 * n:512 * n + 512],
                                                                 start=(kk == 0), stop=(kk == 1)),
                     reads=[kyT] + wokeys, writes=["ps%d" % bn])
        xk = ("xres", i)
        for n, bn in enumerate((b0, b1)):
            cols = slice(512 * n, 512 * n + 512)
            P.op("dve", lambda e, bn=bn, cols=cols: e.tensor_tensor(gtmp[:, cols], ps[:, bn, :], self.g_bc["g1"][:, cols], ALU.mult),
                 reads=["g1_bc"], writes=["ps%d" % bn, kgt])
        P.op("dve", lambda e: e.scalar_tensor_tensor(self.xres[:, i, :], self.xres[:, i, :], ALPHA if first else 1.0, gtmp[:], ALU.mult, ALU.add),
             reads=[kgt], writes=[xk])

    def post_simple(self, l, name, wb, wkeys, gcol0, gate_func, norm_dram, center, wo_row0, first):
        P, ps = self.P, self.psum
        m0 = self.mark()
        wo, wokeys = self.load_w("wo_" + name, self.w_out[l][wo_row0:wo_row0 + 256, :], D, kchunks=2)
        gain_bc = self.sb("hn_gain", [128, 256], F32)
        P.dma("sp", lambda e: e.dma_start(out=gain_bc[:], in_=norm_dram[l:l + 1, :].partition_broadcast(128)), writes=["hn_gain"])
        tmps = [(self.sb("hn_yc%d" % q, [128, 256], F32), self.sb("hn_sq%d" % q, [128, 256], F32), self.sb("hn_st%d" % q, [128, 2, 4], F32),
                 self.sb("yact%d" % q, [128, 256], F32)) for q in range(2)]
        gates = [self.sb("hn_gate%d" % q, [128, 256], F32) for q in range(3)]
        otmps = [(self.sb("yTm%d" % q, [128, 2, 128], BF16), self.sb("gtmp%d" % q, [128, D], F32)) for q in range(2)]

        def stage_a(i):
            gate = gates[i % 3]
            gk = "hn_gate%d" % (i % 3)
            bank = i % 2
            for k in range(8):
                P.op("pe", lambda e, k=k: e.matmul(ps[:, bank, 0:256], self.uT[:, k, 2 + 128 * i:2 + 128 * i + 128], wb[:, k, gcol0:gcol0 + 256],
                                                   start=(k == 0), stop=(k == 7)),
                     reads=wkeys + [("uT", i)], writes=["ps%d" % bank])
            self.act_sigmoid(gate[:], ps[:, bank, 0:256], [], ["ps%d" % bank, gk])
            if gate_func == AF.Silu:
                P.op("dve", lambda e: e.tensor_tensor(gate[:], ps[:, bank, 0:256], gate[:], ALU.mult), writes=["ps%d" % bank, gk])

        def stage_b(i):
            self.head_norm_tile(i, center, gain_bc, gates[i % 3][:], ["hn_gate%d" % (i % 3)], tmps[i % 2], sfx=str(i % 2))

        def stage_c(i):
            self.out_proj_tile(i, tmps[i % 2][3], wo, wokeys, first, otmps[i % 2], sfx=str(i % 2), banks=((2, 4, 5) if i % 2 == 0 else (3, 6, 7)))

        for step in range(NT + 2):
            if step < NT:
                stage_a(step)
            if 1 <= step <= NT:
                stage_b(step - 1)
            if step >= 2:
                stage_c(step - 2)
        self.release(m0)

    def ln_affine_tile(self, i, g_bc, b_bc, par=0):
        P = self.P
        stats, mv, rstd, xhat = self.lntmp2[par]
        sf = str(par)
        xk = ("xres", i)
        src = self.xres[:, i, :]
        for hh in range(2):
            P.op("dve", lambda e, hh=hh: e.bn_stats(stats[:, hh, :], src[:, 512 * hh:512 * hh + 512]), reads=[xk], writes=["lnstats" + sf])
        P.op("dve", lambda e: e.bn_aggr(mv[:], stats[:]), reads=["lnstats" + sf], writes=["lnmv" + sf])
        P.op("act", lambda e: e.activation(rstd[:], mv[:, 1:2], AF.Ln, bias=LN_EPS, scale=1.0), reads=["lnmv" + sf], writes=["lnrstd" + sf])
        P.op("act", lambda e: e.activation(rstd[:], rstd[:], AF.Exp, scale=-0.5), reads=["lnrstd" + sf], writes=["lnrstd" + sf])
        P.op("dve", lambda e: e.tensor_scalar(xhat[:], src, mv[:, 0:1], rstd[:, 0:1], ALU.subtract, ALU.mult),
             reads=[xk, "lnmv" + sf, "lnrstd" + sf], writes=["xhat" + sf])
        P.op("pool", lambda e: e.tensor_tensor(xhat[:], xhat[:], g_bc[:], ALU.mult), reads=["xhat" + sf, "lnbc"], writes=["xhat" + sf])
        P.op("pool", lambda e: e.tensor_tensor(src, xhat[:], b_bc[:], ALU.add), reads=["xhat" + sf, "lnbc"], writes=[xk])

    def phase_c(self, l):
        P, ps = self.P, self.psum
        m0 = self.mark()
        self.alloc_lntmp()
        bc = {}
        for nm, src in (("ln1_g", self.ln1_g), ("ln1_b", self.ln1_b), ("ln2_g", self.ln2_g), ("ln2_b", self.ln2_b)):
            bc[nm] = self.sb("bc_" + nm, [128, D], F32)
            P.dma("sp", lambda e, nm=nm, src=src: e.dma_start(out=bc[nm][:], in_=src[l:l + 1, :].partition_broadcast(128)), writes=["lnbc"])
        u2T = self.sb("u2T", [128, 8, TOK], BF16)
        hT = [self.sb("hT%d" % i, [128, 4, 512], BF16) for i in range(2)]
        rtmp = [self.sb("ffn_rtmp%d" % i, [128, 512], F32) for i in range(2)]
        gtmp = [self.sb("ffn_gtmp%d" % i, [128, 512], F32) for i in range(2)]
        w1b = [self.sb("w1b%d" % i, [128, 8, 512], BF16) for i in range(2)]
        w2b = [self.sb("w2b%d" % i, [128, 4, 1024], BF16) for i in range(2)]
        w1v = self.w_ff1[l].rearrange("(k p) n -> p k n", p=128)
        w2v = self.w_ff2[l].rearrange("(c p) n -> p c n", p=128)
        for i in range(NT):
            self.ln_affine_tile(i, bc["ln1_g"], bc["ln1_b"], par=i % 2)
            self.ln_mod_T(i, lambda k, i=i: u2T[:, k, 128 * i:128 * i + 128], self.sc2p, 24, self.lntmp2[i % 2], [("u2T", i)], par=i % 2)
        nh = 0
        ng_ = 0
        for sl in range(8):
            wa = w1b[sl % 2]
            wbk = w2b[sl % 2]
            ka = "w1b%d" % (sl % 2)
            kb = "w2b%d" % (sl % 2)
            for kh in range(2):
                P.dma("pool", lambda e, wa=wa, sl=sl, kh=kh: e.dma_start(out=wa[:, 4 * kh:4 * kh + 4, :], in_=w1v[:, 4 * kh:4 * kh + 4, 512 * sl:512 * sl + 512]), writes=[(ka, kh)])
            for kh in range(2):
                P.dma("pool", lambda e, wbk=wbk, sl=sl, kh=kh: e.dma_start(out=wbk[:, 2 * kh:2 * kh + 2, :], in_=w2v[:, 4 * sl + 2 * kh:4 * sl + 2 * kh + 2, :]), writes=[(kb, kh)])
            wakeys = [(ka, 0), (ka, 1)]
            wbkeys = [(kb, 0), (kb, 1)]
            for blk in range(4):
                hb = hT[nh % 2]
                hk = "hT%d" % (nh % 2)
                nh += 1
                u2keys = [("u2T", i) for i in range(4 * blk, 4 * blk + 4)]
                for cc in range(4):
                    bank = cc % 2
                    for k in range(8):
                        P.op("pe", lambda e, wa=wa, cc=cc, k=k, bank=bank, blk=blk: e.matmul(
                            ps[:, bank, :], wa[:, k, 128 * cc:128 * cc + 128], u2T[:, k, 512 * blk:512 * blk + 512], start=(k == 0), stop=(k == 7)),
                            reads=wakeys + u2keys, writes=["ps%d" % bank])
                    rt = rtmp[cc % 2]
                    rk = "ffn_rtmp%d" % (cc % 2)
                    P.op("act", lambda e, rt=rt, bank=bank: e.activation(rt[:], ps[:, bank, :], AF.Relu), writes=["ps%d" % bank, rk])
                    P.op("dve", lambda e, rt=rt, cc=cc, hb=hb: e.tensor_tensor(hb[:, cc, :], rt[:], rt[:], ALU.mult), reads=[rk], writes=[(hk, cc)])
                hkeys = [(hk, cc) for cc in range(4)]
                for j in range(4):
                    i = 4 * blk + j
                    for n in range(2):
                        bank = 2 + (ng_ % 4)
                        gt = gtmp[ng_ % 2]
                        gk = "ffn_gtmp%d" % (ng_ % 2)
                        ng_ += 1
                        cols = slice(512 * n, 512 * n + 512)
                        for hc in range(4):
                            P.op("pe", lambda e, wbk=wbk, hc=hc, j=j, bank=bank, hb=hb, cols=cols: e.matmul(
                                ps[:, bank, :], hb[:, hc, 128 * j:128 * j + 128], wbk[:, hc, cols], start=(hc == 0), stop=(hc == 3)),
                                reads=wbkeys + hkeys, writes=["ps%d" % bank])
                        P.op("dve", lambda e, bank=bank, cols=cols, gt=gt: e.tensor_tensor(gt[:], ps[:, bank, :], self.g_bc["g2"][:, cols], ALU.mult),
                             reads=["g2_bc"], writes=["ps%d" % bank, gk])
                        P.op("dve", lambda e, i=i, cols=cols, gt=gt, sl=sl: e.scalar_tensor_tensor(
                            self.xres[:, i, cols], self.xres[:, i, cols], ALPHA if sl == 0 else 1.0, gt[:], ALU.mult, ALU.add),
                            reads=[gk], writes=[("xres", i)])
        for i in range(NT):
            self.ln_affine_tile(i, bc["ln2_g"], bc["ln2_b"], par=i % 2)
        self.release(m0)

    def alloc_lntmp(self):
        self.lntmp2 = [(self.sb("lnstats%d" % q, [128, 2, 6], F32), self.sb("lnmv%d" % q, [128, 2], F32),
                        self.sb("lnrstd%d" % q, [128, 1], F32), self.sb("xhat%d" % q, [128, D], F32)) for q in range(2)]
        self.lntmp = self.lntmp2[0]

    def layer(self, l):
        P = self.P
        self.compute_mod(l)
        mL = self.mark()
        self.uT = self.sb("uT", [128, 8, TOK + 4], BF16)
        self.yacc = self.sb("yacc", [128, NT, 256], F32)
        P.op("dve", lambda e: e.memset(self.uT[:, :, 0:2], 0.0), writes=["uTpadL"])
        P.op("dve", lambda e: e.memset(self.uT[:, :, TOK + 2:TOK + 4], 0.0), writes=["uTpadR"])
        mA = self.mark()
        self.alloc_lntmp()
        for i in range(NT):
            self.ln_mod_T(i, lambda k, i=i: self.uT[:, k, 2 + 128 * i:2 + 128 * i + 128], self.sc1p, 0, self.lntmp2[i % 2], [("uT", i)], par=i % 2)
        self.release(mA)
        if "uT" in self.debug and l == self.debug["uT"]:
            mm_ = self.mark()
            dbgf = self.sb("dbgf", [128, 8, 512], F32)
            for q in range(4):
                P.op("dve", lambda e, q=q: e.tensor_copy(dbgf[:], self.uT[:, :, 2 + 512 * q:2 + 512 * q + 512]),
                     reads=[("uT", i) for i in range(4 * q, 4 * q + 4)], writes=["dbgf"])
                P.dma("sp", lambda e, q=q: e.dma_start(out=self.dbg_uT[:, :, 512 * q:512 * q + 512], in_=dbgf[:]), reads=["dbgf"])
            self.release(mm_)
        mixers = self.debug.get("mixers", ["mlstm", "delta", "ret", "rwkv"])
        first = True
        for mx in mixers:
            getattr(self, "mixer_" + mx)(l, first)
            first = False
        self.release(mL)
        if self.debug.get("phase_c", True):
            self.phase_c(l)

    def finish(self):
        P = self.P
        yv = self.y_out.rearrange("(i p) d -> p i d", p=128)
        for q in range(4):
            P.dma("sp", lambda e, q=q: e.dma_start(out=yv[:, 4 * q:4 * q + 4, :], in_=self.xres[:, 4 * q:4 * q + 4, :]),
                  reads=[("xres", i) for i in range(4 * q, 4 * q + 4)])


PROMPT_ASSIGN = [[0, 1, 2], [3, 4, 5], [6, 7, 8], [9, 10, 11], [12, 13], [14, 15]]

OFF_A, OFF_B, OFF_C, OFF_D = 0, 1040, 2080, 3104


def rope_tables(is_sample):
    tab = np.zeros((TOK, 64, 2), np.float32)
    tab[:, :, 0] = 1.0
    if is_sample:
        n = np.arange(TOK)
        posv = (n // 64, n % 64)
        inv = 10000.0 ** (-np.arange(16, dtype=np.float32) / 16)
        for half in range(2):
            ang = posv[half].astype(np.float32)[:, None] * inv[None, :]
            cos, sin = np.cos(ang), np.sin(ang)
            base = 32 * half
            tab[:, base:base + 16, 0] = cos
            tab[:, base + 16:base + 32, 0] = cos
            tab[:, base:base + 16, 1] = -sin
            tab[:, base + 16:base + 32, 1] = sin
    t = tab.reshape(NT, 128, 64, 2).transpose(0, 2, 3, 1)
    t = np.concatenate([t, t], axis=1)
    return np.ascontiguousarray(t.astype(np.float32))


def swap_perm():
    idx = []
    for h in range(4):
        for half in range(2):
            b = 64 * h + 32 * half
            idx += list(range(b + 16, b + 32)) + list(range(b, b + 16))
    return np.array(idx)


def make_in_maps(inp, kern):
    f32 = np.float32
    g = lambda k: np.asarray(inp[k], f32)
    maps = []
    ident = np.eye(128, dtype=f32)
    consts = build_consts()
    b_mod = np.ascontiguousarray(g("b_mod").reshape(DEPTH, 48, 128))
    w_mod = np.ascontiguousarray(g("w_mod"))
    w_in = g("w_in")
    sw = swap_perm()
    cq = w_in[:, :, OFF_C:OFF_C + 256]
    ck = w_in[:, :, OFF_C + 256:OFF_C + 512]
    cv = w_in[:, :, OFF_C + 512:OFF_C + 768]
    cg = w_in[:, :, OFF_C + 768:OFF_C + 1024]
    aq = w_in[:, :, OFF_A:OFF_A + 256]
    ak = w_in[:, :, OFF_A + 256:OFF_A + 512]
    av = w_in[:, :, OFF_A + 512:OFF_A + 768]
    ao = w_in[:, :, OFF_A + 768:OFF_A + 1024]
    ai = w_in[:, :, OFF_A + 1024:OFF_A + 1032]
    af = w_in[:, :, OFF_A + 1032:OFF_A + 1040]
    agate = np.concatenate([ai[:, :, 0:4], af[:, :, 0:4], ai[:, :, 4:8], af[:, :, 4:8]], axis=2)
    bqkv = w_in[:, :, OFF_B:OFF_B + 768]
    bz = w_in[:, :, OFF_B + 768:OFF_B + 1024]
    bbeta = w_in[:, :, OFF_B + 1024:OFF_B + 1032]
    balpha = w_in[:, :, OFF_B + 1032:OFF_B + 1040]
    bgate = np.concatenate([bbeta[:, :, 0:4], balpha[:, :, 0:4], bbeta[:, :, 4:8], balpha[:, :, 4:8]], axis=2)
    dconv = g("delta_conv")
    dconv = np.ascontiguousarray(dconv.reshape(DEPTH, 5, 6, 128).transpose(0, 3, 2, 1))
    w2 = g("rwkv_w2")
    a2 = g("rwkv_a2")
    w2p = np.zeros((DEPTH, 128, 2, 256), f32)
    a2p = np.zeros((DEPTH, 128, 2, 256), f32)
    for z_ in range(2):
        w2p[:, 64 * z_:64 * z_ + 64, z_, :] = w2[:, z_]
        a2p[:, 64 * z_:64 * z_ + 64, z_, :] = a2[:, z_]
    shared = {
        "wD": np.ascontiguousarray(w_in[:, :, OFF_D:OFF_D + 1152]),
        "rw_mu": np.ascontiguousarray(g("rwkv_mu").reshape(DEPTH, 9, 128).transpose(0, 2, 1)),
        "rw_w2p": w2p, "rw_a2p": a2p,
        "rw_w0": np.ascontiguousarray(g("rwkv_w0").reshape(DEPTH, 512)),
        "rw_a0": np.ascontiguousarray(g("rwkv_a0").reshape(DEPTH, 512)),
        "rw_kk": g("rwkv_kk"), "rw_ka": g("rwkv_ka"),
        "rw_rk": np.ascontiguousarray(g("rwkv_rk").reshape(DEPTH, 256)),
        "rw_norm": g("rwkv_norm"), "rw_g2": np.ascontiguousarray(g("rwkv_g2")),
        "wB": np.ascontiguousarray(np.concatenate([bqkv, bgate, bz], axis=2)),
        "dl_conv": dconv,
        "dl_alog": np.ascontiguousarray(g("delta_a_log").reshape(DEPTH, 8)),
        "dl_dtb": np.ascontiguousarray(g("delta_dt_bias").reshape(DEPTH, 8)),
        "dl_norm": np.ascontiguousarray(g("delta_norm")),
        "wA": np.ascontiguousarray(np.concatenate([aq, ak, av, agate, ao], axis=2)),
        "ml_ib": np.ascontiguousarray(g("mlstm_i_bias").reshape(DEPTH, 8)),
        "ml_fb": np.ascontiguousarray(g("mlstm_f_bias").reshape(DEPTH, 8)),
        "ml_norm": np.ascontiguousarray(g("mlstm_norm")),
        "ident": ident, "consts": consts, "w_mod": w_mod, "b_mod": b_mod,
        "w_out": np.ascontiguousarray(g("w_out")),
        "wC": np.ascontiguousarray(np.concatenate([cq, ck, cv, cq[:, :, sw], ck[:, :, sw], cg], axis=2)),
        "ret_decay": np.ascontiguousarray(g("ret_decay").reshape(DEPTH, 8)),
        "ret_norm": np.ascontiguousarray(g("ret_norm")),
        "ln1_g": g("ln1_g"), "ln1_b": g("ln1_b"), "ln2_g": g("ln2_g"), "ln2_b": g("ln2_b"),
        "w_ff1": np.ascontiguousarray(g("w_ff1")), "w_ff2": np.ascontiguousarray(g("w_ff2")),
    }
    ropes = {True: rope_tables(True), False: rope_tables(False)}
    zeros_mat = np.zeros((DEPTH, 2, H, HD, HD), f32)
    for c in range(N_CORES):
        m = dict(shared)
        if c < 2:
            x = g("x_sample")[c]
            cond = g("c")[c]
            m["init_ret"] = np.ascontiguousarray(g("state_ret")[c])
            m["init_mC"] = np.ascontiguousarray(g("state_mlstm_C")[c])
            m["init_delta"] = np.ascontiguousarray(g("state_delta")[c])
            m["init_rwkv"] = np.ascontiguousarray(g("state_rwkv")[c])
            m["init_mn"] = np.ascontiguousarray(g("state_mlstm_n")[c])
            m["init_mm"] = np.ascontiguousarray(g("state_mlstm_m")[c])
        else:
            x = np.zeros((TOK, D), f32)
            mine = PROMPT_ASSIGN[c - 2]
            for s_ in range(8):
                x[256 * s_:256 * s_ + 256] = inp["x_prompt"][mine[s_ % len(mine)]]
            cond = g("c_ctx")
            m["init_ret"] = zeros_mat
            m["init_mC"] = zeros_mat
            m["init_delta"] = zeros_mat
            m["init_rwkv"] = zeros_mat
            m["init_mn"] = np.zeros((DEPTH, 2, H, HD), f32)
            m["init_mm"] = np.zeros((DEPTH, 2, H), f32)
        m["x"] = np.ascontiguousarray(x)
        m["cond"] = np.ascontiguousarray(cond.reshape(8, 128))
        m["keep"] = np.full((1, 1), 1.0 if c < 2 else 0.0, f32)
        m["rope"] = ropes[c < 2]
        missing = [k for k in kern.ins if k not in m]
        assert not missing, missing
        maps.append({k: m[k] for k in kern.ins})
    return maps


def run(inp, debug=None, trace=False):
    kern = K(debug)
    nc = kern.build()
    maps = make_in_maps(inp, kern)
    res = run_bass_kernel_spmd(nc, maps, core_ids=list(range(N_CORES)), trace=trace)
    return kern, res


def gather_states(r, name, shape_tail):
    out = np.zeros((16, DEPTH, 2) + shape_tail, np.float32)
    for c in range(2, N_CORES):
        for s_, b in enumerate(PROMPT_ASSIGN[c - 2]):
            out[b] = r[c][name][s_]
    return out


def kernel(**inp):
    kern, res = run(inp)
    r = res.results
    BATCH, SEQ = 16, 256
    y_prompt = np.zeros((BATCH, SEQ, D), np.float32)
    y_sample = np.zeros((2, TOK, D), np.float32)
    for c in range(2):
        y_sample[c] = r[c]["y"]
    for c in range(2, N_CORES):
        for s_, b in enumerate(PROMPT_ASSIGN[c - 2]):
            y_prompt[b] = r[c]["y"][256 * s_:256 * s_ + 256]
    new_ret = gather_states(r, "out_ret", (H, HD, HD))
    new_mC = gather_states(r, "out_mC", (H, HD, HD))
    new_mn = gather_states(r, "out_mn", (H, HD))
    new_mm = gather_states(r, "out_mm", (H,))
    new_delta = gather_states(r, "out_delta", (H, HD, HD)) if "out_delta" in r[0] else np.zeros_like(new_ret)
    new_rwkv = gather_states(r, "out_rwkv", (H, HD, HD)) if "out_rwkv" in r[0] else np.zeros_like(new_ret)
    return (y_prompt, y_sample, new_mC, new_mn, new_mm, new_delta, new_ret, new_rwkv)
```
